# Optimizing a Trainium2 kernel written in Bass

```python
import jax, jax.numpy as jnp
from jax import lax
import numpy as np

D_MODEL = 1024
BATCH = 32
SEQ = 2048
DEPTH = 1
DEC_BATCH = 8
DEC_SEQ = 2048
PAST_LEN = 128

MLA_HEADS = 8
MLA_Q_LORA = 256
MLA_KV_LORA = 128
MLA_NOPE = 64
MLA_ROPE = 32
MLA_V = 64
ROPE_THETA = 10000.0
Q_BLOCK = 128
GLA_HEADS = 4
GLA_DK = 64
GLA_DV = 128
GLA_GATE_RANK = 16
GLA_GATE_NORM = 16.0
GLA_CHUNK = 64
MIX_WIDTH = MLA_HEADS * MLA_V + GLA_HEADS * GLA_DV
IN_SIZES = (MLA_Q_LORA, MLA_KV_LORA, MLA_ROPE,
            GLA_HEADS * GLA_DK, GLA_HEADS * GLA_DK, GLA_HEADS * GLA_DV,
            GLA_GATE_RANK, GLA_GATE_RANK, GLA_HEADS * GLA_DV)
IN_WIDTH = sum(IN_SIZES)
IN_SPLITS = tuple(int(s) for s in np.cumsum(IN_SIZES)[:-1])
PEER_HEADS = 8
PEER_NKEYS = 128
PEER_NEXP = PEER_NKEYS * PEER_NKEYS
PEER_QDIM = 256
PEER_TOPK = 16
PEER_TOKEN_BLOCK = 128
EPS = 1e-6

kernel_name = "hymba_mla_gla_peer_encoder"


def rms_norm(x, g):
    xf = x.astype(jnp.float32)
    y = xf * lax.rsqrt(jnp.mean(xf * xf, axis=-1, keepdims=True) + EPS)
    return (y * g.astype(jnp.float32)).astype(x.dtype)


def rope_tables(seq_len):
    half = MLA_ROPE // 2
    freqs = ROPE_THETA ** (-jnp.arange(half, dtype=jnp.float32) * 2.0 / MLA_ROPE)
    ang = jnp.arange(seq_len, dtype=jnp.float32)[:, None] * freqs[None, :]
    return jnp.cos(ang), jnp.sin(ang)


def apply_rope(x, cos, sin):
    xf = x.astype(jnp.float32)
    x1, x2 = jnp.split(xf, 2, axis=-1)
    return jnp.concatenate([x1 * cos - x2 * sin, x2 * cos + x1 * sin], axis=-1).astype(x.dtype)


def mla_attention(q_nope, q_rope, k_nope, k_rope, v):
    B, S, H, _ = q_nope.shape
    nq = S // Q_BLOCK
    scale = (MLA_NOPE + MLA_ROPE) ** -0.5
    qn = q_nope.reshape(B, nq, Q_BLOCK, H, MLA_NOPE).transpose(1, 0, 2, 3, 4)
    qr = q_rope.reshape(B, nq, Q_BLOCK, H, MLA_ROPE).transpose(1, 0, 2, 3, 4)

    def block(args):
        qn_b, qr_b = args
        s = (jnp.einsum("bqhd,bkhd->bhqk", qn_b, k_nope)
             + jnp.einsum("bqhd,bkd->bhqk", qr_b, k_rope))
        p = jax.nn.softmax(s.astype(jnp.float32) * scale, axis=-1).astype(v.dtype)
        return jnp.einsum("bhqk,bkhd->bqhd", p, v)

    o = lax.map(block, (qn, qr))
    return o.transpose(1, 0, 2, 3, 4).reshape(B, S, H * MLA_V)


def gla_scan(q, k, v, g):
    B, S, H, DK = q.shape
    DV = v.shape[-1]
    n = S // GLA_CHUNK
    out_dtype = v.dtype

    def to_chunks(a):
        return a.astype(jnp.float32).reshape(B, n, GLA_CHUNK, H, a.shape[-1]).transpose(1, 0, 3, 2, 4)

    qc, kc, vc, gc = to_chunks(q), to_chunks(k), to_chunks(v), to_chunks(g)
    mask = jnp.tril(jnp.ones((GLA_CHUNK, GLA_CHUNK), dtype=bool))

    def step(state, inp):
        qi, ki, vi, gi = inp
        b = jnp.cumsum(gi, axis=-2)
        q_dec = qi * jnp.exp(b)
        att = jnp.einsum("bhid,bhjd->bhij", q_dec, ki * jnp.exp(-b))
        att = jnp.where(mask, att, 0.0)
        o = jnp.einsum("bhij,bhjv->bhiv", att, vi) + jnp.einsum("bhid,bhdv->bhiv", q_dec, state)
        b_last = b[..., -1:, :]
        state = (state * jnp.exp(b_last)[..., 0, :, None]
                 + jnp.einsum("bhjd,bhjv->bhdv", ki * jnp.exp(b_last - b), vi))
        return state, o

    state0 = jnp.zeros((B, H, DK, DV), jnp.float32)
    _, o = lax.scan(step, state0, (qc, kc, vc, gc))
    return o.transpose(1, 0, 3, 2, 4).reshape(B, S, H, DV).astype(out_dtype)


def peer(xn, w_q, sub_keys, u_tab, v_tab):
    B, S, D = xn.shape
    xt = xn.reshape(-1, PEER_TOKEN_BLOCK, D)

    def block(xb):
        T = xb.shape[0]
        q = (xb @ w_q).reshape(T, PEER_HEADS, 2, PEER_QDIM // 2)
        s = jnp.einsum("thpd,hpkd->thpk", q, sub_keys).astype(jnp.float32)
        s1, i1 = lax.top_k(s[:, :, 0], PEER_TOPK)
        s2, i2 = lax.top_k(s[:, :, 1], PEER_TOPK)
        cand = (s1[..., :, None] + s2[..., None, :]).reshape(T, PEER_HEADS, PEER_TOPK * PEER_TOPK)
        cidx = (i1[..., :, None] * PEER_NKEYS + i2[..., None, :]).reshape(T, PEER_HEADS, PEER_TOPK * PEER_TOPK)
        top_s, pos = lax.top_k(cand, PEER_TOPK)
        idx = jnp.take_along_axis(cidx, pos, axis=-1)
        gate = jax.nn.softmax(top_s, axis=-1)
        u = u_tab[idx]
        vv = v_tab[idx]
        act = jax.nn.gelu(jnp.einsum("thkd,td->thk", u, xb).astype(jnp.float32))
        w = (gate * act).astype(xb.dtype)
        return jnp.einsum("thk,thkd->td", w, vv)

    return lax.map(block, xt).reshape(B, S, D)


def encoder_layer(x, norm1_g, w_in, q_norm_g, w_uq, kv_norm_g, w_ukv,
                  gate_fwd_w, gate_fwd_b, gate_bwd_w, gate_bwd_b, gla_norm_g, w_o,
                  norm2_g, peer_wq, peer_subkeys, peer_u, peer_v):
    B, S, D = x.shape
    n1 = rms_norm(x, norm1_g)
    proj = n1 @ w_in
    c_q, c_kv, k_r, gq, gk, gv, glr_f, glr_b, gout = jnp.split(proj, IN_SPLITS, axis=-1)

    qf = (rms_norm(c_q, q_norm_g) @ w_uq).reshape(B, S, MLA_HEADS, MLA_NOPE + MLA_ROPE)
    q_nope, q_rope = qf[..., :MLA_NOPE], qf[..., MLA_NOPE:]
    kvf = (rms_norm(c_kv, kv_norm_g) @ w_ukv).reshape(B, S, MLA_HEADS, MLA_NOPE + MLA_V)
    k_nope, v_mla = kvf[..., :MLA_NOPE], kvf[..., MLA_NOPE:]
    cos, sin = rope_tables(S)
    q_rope = apply_rope(q_rope, cos[:, None, :], sin[:, None, :])
    k_rope = apply_rope(k_r, cos, sin)
    attn = mla_attention(q_nope, q_rope, k_nope, k_rope, v_mla)

    qg = gq.reshape(B, S, GLA_HEADS, GLA_DK) * (GLA_DK ** -0.5)
    kg = gk.reshape(B, S, GLA_HEADS, GLA_DK)
    vg = gv.reshape(B, S, GLA_HEADS, GLA_DV)
    g_f = (jax.nn.log_sigmoid((glr_f @ gate_fwd_w + gate_fwd_b).astype(jnp.float32))
           / GLA_GATE_NORM).reshape(B, S, GLA_HEADS, GLA_DK)
    g_b = (jax.nn.log_sigmoid((glr_b @ gate_bwd_w + gate_bwd_b).astype(jnp.float32))
           / GLA_GATE_NORM).reshape(B, S, GLA_HEADS, GLA_DK)
    o_f = gla_scan(qg, kg, vg, g_f)
    o_b = jnp.flip(gla_scan(jnp.flip(qg, 1), jnp.flip(kg, 1), jnp.flip(vg, 1), jnp.flip(g_b, 1)), 1)
    o_gla = rms_norm(o_f + o_b, gla_norm_g) * jax.nn.silu(gout.reshape(B, S, GLA_HEADS, GLA_DV))

    mix = jnp.concatenate([attn, o_gla.reshape(B, S, GLA_HEADS * GLA_DV)], axis=-1) @ w_o
    h = x + mix
    return h + peer(rms_norm(h, norm2_g), peer_wq, peer_subkeys, peer_u, peer_v)


def setup_inputs(seed: int = 0) -> dict:
    key = jax.random.key(seed)
    ks = jax.random.split(key, 24)
    f32 = jnp.float32

    def nrm(k, shape, scale):
        return jax.random.normal(k, shape, f32) * scale

    def gain(k, shape):
        return 1.0 + 0.02 * jax.random.normal(k, shape, f32)

    L = DEPTH
    return {
        "x_prompt": jax.random.normal(ks[0], (BATCH, SEQ, D_MODEL), f32),
        "x_sample": jax.random.normal(ks[1], (DEC_BATCH, DEC_SEQ, D_MODEL), f32),
        "norm1_g": gain(ks[2], (L, D_MODEL)),
        "w_in": nrm(ks[3], (L, D_MODEL, IN_WIDTH), D_MODEL ** -0.5),
        "q_norm_g": gain(ks[4], (L, MLA_Q_LORA)),
        "w_uq": nrm(ks[5], (L, MLA_Q_LORA, MLA_HEADS * (MLA_NOPE + MLA_ROPE)), MLA_Q_LORA ** -0.5),
        "kv_norm_g": gain(ks[6], (L, MLA_KV_LORA)),
        "w_ukv": nrm(ks[7], (L, MLA_KV_LORA, MLA_HEADS * (MLA_NOPE + MLA_V)), MLA_KV_LORA ** -0.5),
        "gate_fwd_w": nrm(ks[8], (L, GLA_GATE_RANK, GLA_HEADS * GLA_DK), GLA_GATE_RANK ** -0.5),
        "gate_fwd_b": nrm(ks[9], (L, GLA_HEADS * GLA_DK), 0.1),
        "gate_bwd_w": nrm(ks[10], (L, GLA_GATE_RANK, GLA_HEADS * GLA_DK), GLA_GATE_RANK ** -0.5),
        "gate_bwd_b": nrm(ks[11], (L, GLA_HEADS * GLA_DK), 0.1),
        "gla_norm_g": gain(ks[12], (L, GLA_DV)),
        "w_o": nrm(ks[13], (L, MIX_WIDTH, D_MODEL), MIX_WIDTH ** -0.5),
        "norm2_g": gain(ks[14], (L, D_MODEL)),
        "peer_wq": nrm(ks[15], (L, D_MODEL, PEER_HEADS * PEER_QDIM), D_MODEL ** -0.5),
        "peer_subkeys": nrm(ks[16], (L, PEER_HEADS, 2, PEER_NKEYS, PEER_QDIM // 2), (PEER_QDIM // 2) ** -0.5),
        "peer_u": nrm(ks[17], (L, PEER_NEXP, D_MODEL), D_MODEL ** -0.5),
        "peer_v": nrm(ks[18], (L, PEER_NEXP, D_MODEL), (PEER_HEADS * PEER_TOPK) ** -0.5),
        "final_norm_g": gain(ks[19], (D_MODEL,)),
    }


def reference(x_prompt, x_sample, norm1_g, w_in, q_norm_g, w_uq, kv_norm_g, w_ukv,
              gate_fwd_w, gate_fwd_b, gate_bwd_w, gate_bwd_b, gla_norm_g, w_o,
              norm2_g, peer_wq, peer_subkeys, peer_u, peer_v, final_norm_g):
    def trunk(x):
        for l in range(DEPTH):
            x = encoder_layer(x, norm1_g[l], w_in[l], q_norm_g[l], w_uq[l], kv_norm_g[l], w_ukv[l],
                              gate_fwd_w[l], gate_fwd_b[l], gate_bwd_w[l], gate_bwd_b[l],
                              gla_norm_g[l], w_o[l], norm2_g[l], peer_wq[l], peer_subkeys[l],
                              peer_u[l], peer_v[l])
        return rms_norm(x, final_norm_g)

    y_prompt = trunk(x_prompt)
    y_sample = trunk(x_sample)
    return (y_prompt, y_sample)
```

```python
import numpy as np
from contextlib import ExitStack
import concourse.bass as bass
import concourse.mybir as mybir
from concourse.bass_utils import run_bass_kernel_spmd

F32 = mybir.dt.float32
BF16 = mybir.dt.bfloat16
U32 = mybir.dt.uint32
AF = mybir.ActivationFunctionType
ALU = mybir.AluOpType
AX = mybir.AxisListType

NSEQ = 5
DBG = False
EPOCH = 60000
S_LEN = 2048
NTILE = 16
EPS = 1e-6


class DSem:
    def __init__(self, h):
        self.h = h
        self.cnt = 0


class T:
    def __init__(self, ap, name, dsem=None):
        self.ap = ap
        self.name = name
        self.lw = None
        self.rd = {}
        self.dsem = dsem

    def __getitem__(self, k):
        return self.ap[k]


class Sched:
    def __init__(self, nc, stack):
        self.nc = nc
        self.stack = stack
        self.eng = {"pe": nc.tensor, "act": nc.scalar, "dve": nc.vector, "pool": nc.gpsimd, "sp": nc.sync}
        self.cnt = {k: 0 for k in self.eng}
        self.sem = {}
        self.nsem = 0
        for k in self.eng:
            self.sem[k] = self._newsem(k)
        self.waited = {k: {} for k in self.eng}
        self.ninstr = 0
        self.dpool = {"sw": [], "hw": []}
        self.dall = []

    def _newsem(self, name):
        self.nsem += 1
        return self.stack.enter_context(self.nc.semaphore(f"s{self.nsem}_{name}"))

    def getd(self, kind="hw"):
        if self.dpool[kind]:
            return self.dpool[kind].pop()
        d = DSem(self._newsem("d" + kind))
        d.kind = kind
        self.dall.append(d)
        return d

    def sb(self, st, name, shape, dtype, dma=False):
        self.nalloc = getattr(self, "nalloc", 0) + 1
        name = f"{name}_{self.nalloc}"
        t = st.enter_context(self.nc.sbuf_tensor(name, shape, dtype))
        kind = "hw" if dma is True else dma
        tt = T(t, name, self.getd(kind) if dma else None)
        if dma:
            st.callback(lambda d=tt.dsem: self.dpool[d.kind].append(d))
        return tt

    def ps(self, st, name, shape, dtype):
        t = st.enter_context(self.nc.psum_tensor(name, shape, dtype))
        tt = T(t, name)
        tt.excl = True
        return tt

    def dram(self, ap, name):
        return T(ap, name, self.getd())

    def _wait(self, e, deps):
        w = self.waited[e]
        for (sem, val) in deps:
            key = id(sem)
            if w.get(key, 0) >= val:
                continue
            self.eng[e].wait_ge(sem, val)
            w[key] = val
            self.ninstr += 1

    def replay(self, lst, k):
        d, self.defer = self.defer, None
        for _ in range(min(k, len(lst))):
            kind, a = lst.pop(0)
            (self.op if kind == "op" else self.dma)(*a)
        self.defer = d

    def op(self, e, fn, reads=(), writes=()):
        if getattr(self, "defer", None) is not None:
            self.defer.append(("op", (e, fn, list(reads), list(writes))))
            return None
        ex = [t for t in reads if getattr(t, "excl", False)]
        if ex:
            reads = [t for t in reads if not getattr(t, "excl", False)]
            writes = list(writes) + ex
        deps = []
        for t in reads:
            if t.lw is not None:
                deps.append(t.lw[1:])
        for t in writes:
            if t.lw is not None and t.lw[0] != e:
                deps.append(t.lw[1:])
            for en, d in t.rd.items():
                if en != e:
                    deps.append(d)
        self._wait(e, deps)
        ins = fn(self.eng[e])
        self.cnt[e] += 1
        if self.cnt[e] > EPOCH:
            self.sem[e] = self._newsem(e)
            self.cnt[e] = 1
        ins.then_inc(self.sem[e], 1)
        self.ninstr += 1
        rec = (self.sem[e], self.cnt[e])
        for t in reads:
            t.rd[e] = rec
        for t in writes:
            t.lw = (e,) + rec
            t.rd = {}
        return ins

    def dma(self, q, fn, reads=(), writes=(), semt=None):
        if getattr(self, "defer", None) is not None:
            self.defer.append(("dma", (q, fn, list(reads), list(writes), semt)))
            return None
        deps = []
        for t in reads:
            if t.lw is not None:
                deps.append(t.lw[1:])
        for t in writes:
            if t.lw is not None:
                deps.append(t.lw[1:])
            for en, d in t.rd.items():
                deps.append(d)
        self._wait(q, deps)
        ins = fn(self.eng[q])
        ds = semt.dsem
        ds.cnt += 16
        ins.then_inc(ds.h, 16)
        self.ninstr += 1
        rec = (ds.h, ds.cnt)
        for t in reads:
            t.rd[("dma", id(ds))] = rec
        for t in writes:
            t.lw = ("dma",) + rec
            t.rd = {}
        return ins

    def barrier(self):
        deps = [(self.sem[k], self.cnt[k]) for k in self.eng if self.cnt[k] > 0]
        deps += [(d.h, d.cnt) for d in self.dall if d.cnt > 0]
        for e in self.eng:
            self._wait(e, deps)


def build_program(nseq, dbg, stop=None):
    nc = bass.Bass("TRN2", target_bir_lowering=False)
    ntok = nseq * S_LEN

    def din(name, shape, dt=F32):
        return nc.dram_tensor(name, shape, dt, kind="ExternalInput").ap()

    x_d = din("x", [ntok, 1024])
    winA_d = din("w_inA", [1024, 448])
    winB_d = din("w_inB", [1024, 1568])
    g1_d = din("norm1_g", [1024])
    gq_d = din("q_norm_g", [256])
    wuq_d = din("w_uq", [256, 768])
    gkv_d = din("kv_norm_g", [128])
    wukv_d = din("w_ukv", [128, 1024])
    wg_d = din("gate_w", [33, 512])
    gla_d = din("gla_norm_g", [128])
    wo_d = din("w_o", [1024, 1024])
    g2_d = din("norm2_g", [1024])
    wq_d = din("peer_wq", [1024, 2048])
    subk_d = din("subkT", [128, 16, 128])
    u_d = din("peer_u", [16384, 1024])
    v_d = din("peer_v", [16384, 1024])
    gf_d = din("final_norm_g", [1024])
    cos_d = din("rope_cos", [S_LEN, 16])
    sin_d = din("rope_sin", [S_LEN, 16])
    y_d = nc.dram_tensor("y", [ntok, 1024], F32, kind="ExternalOutput").ap()
    uvbf_d = nc.dram_tensor("uv_bf", [16384, 2048], BF16, kind="Internal").ap()
    if dbg:
        dbgh_d = nc.dram_tensor("dbg_h", [ntok, 1024], F32, kind="ExternalOutput").ap()
        dbgi_d = nc.dram_tensor("dbg_i", [ntok, 128], U32, kind="ExternalOutput").ap()
        dbgg_d = nc.dram_tensor("dbg_g", [ntok, 128], F32, kind="ExternalOutput").ap()
        dbga_d = nc.dram_tensor("dbg_a", [ntok, 128], F32, kind="ExternalOutput").ap()

    with ExitStack() as gst:
        S = Sched(nc, gst)
        y_t = S.dram(y_d, "y")
        if dbg:
            dbgh_t = S.dram(dbgh_d, "dbgh")
            dbgi_t = S.dram(dbgi_d, "dbgi")
            dbgg_t = S.dram(dbgg_d, "dbgg")
            dbga_t = S.dram(dbga_d, "dbga")

        ident_bf = S.sb(gst, "ident_bf", [128, 128], BF16)
        ident_f = S.sb(gst, "ident_f", [128, 128], F32)
        triL = S.sb(gst, "triL", [128, 128], F32)
        triU = S.sb(gst, "triU", [128, 128], F32)
        eye16 = S.sb(gst, "eye16", [16, 16], F32)
        ones16 = S.sb(gst, "ones16", [16, 128], F32)
        iota16 = S.sb(gst, "iota16", [128, 16], F32)
        for tt in (ident_bf, ident_f):
            S.op("pool", lambda e, tt=tt: e.memset(tt[:], 0.0), writes=[tt])
            S.op("pool", lambda e, tt=tt: e.affine_select(out=tt[:], in_=tt[:], pattern=[[-1, 128]], compare_op=ALU.not_equal,
                                                          fill=1.0, base=0, channel_multiplier=1), reads=[tt], writes=[tt])
        S.op("pool", lambda e: e.memset(eye16[:], 0.0), writes=[eye16])
        S.op("pool", lambda e: e.affine_select(out=eye16[:], in_=eye16[:], pattern=[[-1, 16]], compare_op=ALU.not_equal,
                                               fill=1.0, base=0, channel_multiplier=1), reads=[eye16], writes=[eye16])
        S.op("pool", lambda e: e.memset(ones16[:], 1.0), writes=[ones16])
        S.op("pool", lambda e: e.memset(triL[:], 1.0), writes=[triL])
        S.op("pool", lambda e: e.affine_select(out=triL[:], in_=triL[:], pattern=[[1, 128]], compare_op=ALU.is_ge,
                                               fill=0.0, base=0, channel_multiplier=-1), reads=[triL], writes=[triL])
        S.op("pool", lambda e: e.memset(triU[:], 1.0), writes=[triU])
        S.op("pool", lambda e: e.affine_select(out=triU[:], in_=triU[:], pattern=[[-1, 128]], compare_op=ALU.is_ge,
                                               fill=0.0, base=0, channel_multiplier=1), reads=[triU], writes=[triU])
        S.op("pool", lambda e: e.iota(iota16[:], pattern=[[1, 16]], base=0, channel_multiplier=0,
                                      allow_small_or_imprecise_dtypes=True), writes=[iota16])
        attnT = S.sb(gst, "attnT", [128, 4, S_LEN], BF16)
        oglaT = S.sb(gst, "oglaT", [128, 4, S_LEN], BF16)
        PB = [S.ps(gst, f"pb{i}", [128, 512], F32) for i in range(8)]

        def pbf(i):
            return PB[i][:].bitcast(BF16)

        stat = S.sb(gst, "stat", [128, 8], F32)
        DrowG = S.sb(gst, "DrowG", [16, 512], F32)

        def rstd_of(src_ap, src_ts, width, junk, col):
            S.op("act", lambda e: e.activation(out=junk, in_=src_ap, func=AF.Square, accum_out=stat[:, col:col + 1]),
                 reads=src_ts, writes=[stat, junk_t])
            S.op("act", lambda e: e.activation(out=stat[:, col + 1:col + 2], in_=stat[:, col:col + 1], func=AF.Sqrt,
                                               scale=1.0 / width, bias=EPS), reads=[stat], writes=[stat])
            S.op("dve", lambda e: e.reciprocal(out=stat[:, col + 1:col + 2], in_=stat[:, col + 1:col + 2]),
                 reads=[stat], writes=[stat])
            return stat[:, col + 1:col + 2]

        junk_t = S.sb(gst, "junk", [128, 1024], BF16)

        def load_norm_T(st_tiles, row0, gb, xt, n1, n1T, bank):
            S.dma("sp", lambda q: q.dma_start(out=xt[:], in_=x_d[row0:row0 + 128, :]), writes=[xt], semt=xt)
            r = rstd_of(xt[:], [xt], 1024, junk_t[:], 0)
            S.op("dve", lambda e: e.scalar_tensor_tensor(out=n1[:], in0=xt[:], scalar=r, in1=gb[:], op0=ALU.mult, op1=ALU.mult),
                 reads=[xt, stat, gb], writes=[n1])
            transpose_to(n1, lambda kc: n1[:, kc * 128:(kc + 1) * 128], 8, bank, n1T, lambda: n1T[:])

        def transpose_to(src_t, src_fn, nblk, bank, dst_t, dst_fn, rows=128, eng="act"):
            pv = pbf(bank)
            for k in range(nblk):
                S.op("pe", lambda e, k=k: e.transpose(out=pv[0:rows, k * 128:(k + 1) * 128], in_=src_fn(k), identity=ident_bf[:]),
                     reads=[src_t, ident_bf], writes=[PB[bank]])
            srcv = pv[0:rows, 0:nblk * 128].rearrange("p (a b) -> p a b", a=nblk)
            if eng == "act":
                S.op("act", lambda e: e.copy(out=dst_fn(), in_=srcv), reads=[PB[bank]], writes=[dst_t])
            else:
                S.op("dve", lambda e: e.tensor_copy(out=dst_fn(), in_=srcv), reads=[PB[bank]], writes=[dst_t])

        def bcast_load(st, name, vec_d, n):
            t = S.sb(st, name, [128, n], F32, dma=True)
            S.dma("sp", lambda q: q.dma_start(out=t[:], in_=vec_d.partition_broadcast(128)), writes=[t], semt=t)
            return t

        uv_t = S.dram(uvbf_d, "uvbf")
        with ExitStack() as pp:
            cb = [S.sb(pp, f"cb{i}", [128, 4, 1024], BF16, dma="sw") for i in range(3)]
            k = 0
            for (src, c0) in ((u_d, 0), (v_d, 1024)):
                for ch in range(32):
                    b = cb[k % 3]
                    k += 1
                    S.dma("pool", lambda q, b=b, src=src, ch=ch: q.dma_start(out=b[:], in_=src[ch * 512:(ch + 1) * 512, :].rearrange("(p r) d -> p r d", r=4)),
                          writes=[b], semt=b)
                    S.dma("sp", lambda q, b=b, c0=c0, ch=ch: q.dma_start(out=uvbf_d[ch * 512:(ch + 1) * 512, c0:c0 + 1024].rearrange("(p r) d -> p r d", r=4), in_=b[:]),
                          reads=[b], writes=[uv_t], semt=uv_t)
            S.barrier()

        for s in range(nseq):
            base = s * S_LEN
            with ExitStack() as p1:
                WinA = S.sb(p1, "WinA", [128, 8, 448], BF16, dma="sw")
                Wuq = S.sb(p1, "Wuq", [128, 2, 768], BF16, dma="sw")
                Wukv = S.sb(p1, "Wukv", [128, 1024], BF16, dma="sw")
                Wg = S.sb(p1, "Wg", [33, 512], F32, dma=True)
                S.dma("pool", lambda q: q.dma_start(out=WinA[:], in_=winA_d.rearrange("(c p) n -> p c n", p=128)), writes=[WinA], semt=WinA)
                S.dma("pool", lambda q: q.dma_start(out=Wuq[:], in_=wuq_d.rearrange("(c p) n -> p c n", p=128)), writes=[Wuq], semt=Wuq)
                S.dma("pool", lambda q: q.dma_start(out=Wukv[:], in_=wukv_d), writes=[Wukv], semt=Wukv)
                S.dma("sp", lambda q: q.dma_start(out=Wg[:], in_=wg_d), writes=[Wg], semt=Wg)
                g1b = bcast_load(p1, "g1b", g1_d, 1024)
                gqb = bcast_load(p1, "gqb", gq_d, 256)
                gkvb = bcast_load(p1, "gkvb", gkv_d, 128)
                cosT = S.sb(p1, "cosT", [128, NTILE, 16], F32, dma=True)
                sinT = S.sb(p1, "sinT", [128, NTILE, 16], F32, dma=True)
                S.dma("sp", lambda q: q.dma_start(out=cosT[:], in_=cos_d.rearrange("(n p) d -> p n d", p=128)), writes=[cosT], semt=cosT)
                S.dma("sp", lambda q: q.dma_start(out=sinT[:], in_=sin_d.rearrange("(n p) d -> p n d", p=128)), writes=[sinT], semt=sinT)

                KT = S.sb(p1, "KT", [128, 8, S_LEN], BF16)
                Vaug = S.sb(p1, "Vaug", [128, NTILE, 8, 65], BF16)
                cqT = S.sb(p1, "cqT", [128, 2, S_LEN], BF16)
                TTs = S.sb(p1, "TTs", [128, 4, NTILE], F32)
                xt = S.sb(p1, "xt", [128, 1024], F32, dma=True)
                n1 = S.sb(p1, "n1", [128, 1024], BF16)
                n1T = S.sb(p1, "n1T", [128, 8, 128], BF16)
                pa = S.sb(p1, "pa", [128, 448], F32)
                cqn = S.sb(p1, "cqn", [128, 384], BF16)
                lat = S.sb(p1, "latT", [128, 3, 128], BF16)
                glrA = S.sb(p1, "glrA", [33, 128], F32)
                Kt = S.sb(p1, "Kt", [128, 8, 96], BF16)
                rp = S.sb(p1, "rp", [128, 6, 8, 16], F32)
                sp_t = S.sb(p1, "sp", [128, 512], F32)
                S.op("pool", lambda e: e.memset(glrA[:], 1.0), writes=[glrA])
                S.op("pool", lambda e: e.memset(Vaug[:], 1.0), writes=[Vaug])

                def gate_sp(pa_t, glr_ap, bank_tr, bank_z):
                    S.op("pe", lambda e: e.transpose(out=PB[bank_tr][0:32, 0:128], in_=glr_ap, identity=ident_f[:]),
                         reads=[pa_t, ident_f], writes=[PB[bank_tr]])
                    S.op("act", lambda e: e.copy(out=glrA[0:32, :], in_=PB[bank_tr][0:32, 0:128]), reads=[PB[bank_tr]], writes=[glrA])
                    S.op("pe", lambda e: e.matmul(PB[bank_z][:, :], lhsT=glrA[:, :], rhs=Wg[:, :], start=True, stop=True),
                         reads=[glrA, Wg], writes=[PB[bank_z]])
                    S.op("act", lambda e: e.activation(out=sp_t[:], in_=PB[bank_z][:, :], func=AF.Exp, scale=-1.0),
                         reads=[PB[bank_z]], writes=[sp_t])
                    S.op("act", lambda e: e.activation(out=sp_t[:], in_=sp_t[:], func=AF.Ln, bias=1.0), reads=[sp_t], writes=[sp_t])

                def rope(dst_t, dst1, dst2, x1, x2, src_ts, n, shp):
                    c = cosT[:, n, :]
                    sn = sinT[:, n, :]
                    if len(shp) == 3:
                        c = c.unsqueeze(1).to_broadcast(shp)
                        sn = sn.unsqueeze(1).to_broadcast(shp)
                        t = [rp[:, i, 0:shp[1], :] for i in range(4)]
                    else:
                        t = [rp[:, i, 0, :] for i in range(4)]
                    S.op("dve", lambda e: e.tensor_tensor(out=t[0], in0=x1, in1=c, op=ALU.mult), reads=src_ts + [cosT], writes=[rp])
                    S.op("dve", lambda e: e.tensor_tensor(out=t[1], in0=x2, in1=sn, op=ALU.mult), reads=src_ts + [sinT], writes=[rp])
                    S.op("dve", lambda e: e.tensor_tensor(out=t[2], in0=x2, in1=c, op=ALU.mult), reads=src_ts + [cosT], writes=[rp])
                    S.op("dve", lambda e: e.tensor_tensor(out=t[3], in0=x1, in1=sn, op=ALU.mult), reads=src_ts + [sinT], writes=[rp])
                    S.op("dve", lambda e: e.tensor_tensor(out=dst1, in0=t[0], in1=t[1], op=ALU.subtract), reads=[rp], writes=[dst_t])
                    S.op("dve", lambda e: e.tensor_tensor(out=dst2, in0=t[2], in1=t[3], op=ALU.add), reads=[rp], writes=[dst_t])

                krr = S.sb(p1, "krr", [128, 32], BF16)
                for n in range(NTILE):
                    load_norm_T(p1, base + n * 128, g1b, xt, n1, n1T, 0)
                    for kc in range(8):
                        S.op("pe", lambda e, kc=kc: e.matmul(PB[1][:, 0:448], lhsT=n1T[:, kc, :], rhs=WinA[:, kc, :], start=(kc == 0), stop=(kc == 7)),
                             reads=[n1T, WinA], writes=[PB[1]])
                    S.op("act", lambda e: e.copy(out=pa[:], in_=PB[1][:, 0:448]), reads=[PB[1]], writes=[pa])
                    r = rstd_of(pa[:, 0:256], [pa], 256, junk_t[:, 0:256], 2)
                    S.op("dve", lambda e: e.scalar_tensor_tensor(out=cqn[:, 0:256], in0=pa[:, 0:256], scalar=r, in1=gqb[:], op0=ALU.mult, op1=ALU.mult),
                         reads=[pa, stat, gqb], writes=[cqn])
                    r = rstd_of(pa[:, 256:384], [pa], 128, junk_t[:, 0:128], 4)
                    S.op("dve", lambda e: e.scalar_tensor_tensor(out=cqn[:, 256:384], in0=pa[:, 256:384], scalar=r, in1=gkvb[:], op0=ALU.mult, op1=ALU.mult),
                         reads=[pa, stat, gkvb], writes=[cqn])
                    transpose_to(cqn, lambda k: cqn[:, k * 128:(k + 1) * 128], 3, 2, lat, lambda: lat[:], eng="dve")
                    S.op("dve", lambda e, n=n: e.tensor_copy(out=cqT[:, :, n * 128:(n + 1) * 128], in_=lat[:, 0:2, :]), reads=[lat], writes=[cqT])
                    for j in range(2):
                        S.op("pe", lambda e, j=j: e.matmul(PB[4 + j][:, :], lhsT=lat[:, 2, :], rhs=Wukv[:, j * 512:(j + 1) * 512], start=True, stop=True),
                             reads=[lat, Wukv], writes=[PB[4 + j]])
                        kv = PB[4 + j][:, :].rearrange("p (h c) -> p h c", h=4)
                        S.op("act", lambda e, j=j, kv=kv, n=n: e.copy(out=Vaug[:, n, 4 * j:4 * j + 4, 0:64], in_=kv[:, :, 64:128]),
                             reads=[PB[4 + j]], writes=[Vaug])
                        S.op("dve", lambda e, j=j, kv=kv: e.tensor_copy(out=Kt[:, 4 * j:4 * j + 4, 0:64], in_=kv[:, :, 0:64]),
                             reads=[PB[4 + j]], writes=[Kt])
                    rope(krr, krr[:, 0:16], krr[:, 16:32], pa[:, 384:400], pa[:, 400:416], [pa], n, [128, 16])
                    S.op("dve", lambda e: e.tensor_copy(out=Kt[:, :, 64:96], in_=krr[:].unsqueeze(1).to_broadcast([128, 8, 32])),
                         reads=[krr], writes=[Kt])
                    transpose_to(Kt, lambda h: Kt[:, h, :], 8, 6, KT,
                                 lambda n=n: KT[0:96, :, n * 128:(n + 1) * 128], rows=96)
                    gate_sp(pa, pa[:, 416:448], 3, 7)
                    for cc in range(4):
                        S.op("pe", lambda e, cc=cc: e.matmul(PB[3][:, 256 + cc:257 + cc], lhsT=sp_t[:, cc * 128:(cc + 1) * 128], rhs=triU[:, 0:1],
                                                             start=True, stop=True), reads=[sp_t, triU], writes=[PB[3]])
                    S.op("dve", lambda e, n=n: e.tensor_copy(out=TTs[:, :, n], in_=PB[3][:, 256:260]), reads=[PB[3]], writes=[TTs])

                if stop == "A1":
                    S.barrier()
                    return nc
                Inc = S.sb(p1, "Inc", [128, 4, NTILE], F32)
                Dc = S.sb(p1, "Dc", [128, 4, NTILE], F32)
                S.op("dve", lambda e: e.tensor_copy(out=Inc[:, :, 0:1], in_=TTs[:, :, 0:1]), reads=[TTs], writes=[Inc])
                for n in range(1, NTILE):
                    S.op("dve", lambda e, n=n: e.tensor_tensor(out=Inc[:, :, n:n + 1], in0=Inc[:, :, n - 1:n], in1=TTs[:, :, n:n + 1], op=ALU.add),
                         reads=[Inc, TTs], writes=[Inc])
                S.op("dve", lambda e: e.tensor_tensor(out=Dc[:, 0:2, :], in0=Inc[:, 0:2, :], in1=TTs[:, 0:2, :], op=ALU.subtract), reads=[Inc, TTs], writes=[Dc])
                S.op("dve", lambda e: e.tensor_tensor(out=Dc[:, 0:2, :], in0=Dc[:, 0:2, :], in1=Inc[:, 0:2, 7:8].to_broadcast([128, 2, NTILE]), op=ALU.subtract),
                     reads=[Dc, Inc], writes=[Dc])
                S.op("dve", lambda e: e.tensor_tensor(out=Dc[:, 2:4, :], in0=Inc[:, 2:4, 7:8].to_broadcast([128, 2, NTILE]), in1=Inc[:, 2:4, :], op=ALU.subtract),
                     reads=[Inc], writes=[Dc])
                for cc in range(4):
                    S.op("pe", lambda e, cc=cc: e.transpose(out=PB[3][0:16, cc * 128:(cc + 1) * 128], in_=Dc[:, cc, :], identity=ident_f[:]),
                         reads=[Dc, ident_f], writes=[PB[3]])
                S.op("act", lambda e: e.copy(out=DrowG[:], in_=PB[3][0:16, :]), reads=[PB[3]], writes=[DrowG])

                if stop == "A1b":
                    S.barrier()
                    return nc
                Qt = S.sb(p1, "Qt", [128, 8, 96], BF16)
                QTc = S.sb(p1, "QTc", [128, 8, 512], BF16)
                Eb = [S.sb(p1, f"Eb{i}", [128, 512], BF16) for i in range(2)]
                Oacc = S.sb(p1, "Oacc", [128, 4, 8, 65], F32)
                rc = S.sb(p1, "rc", [128, 4, 8], F32)
                atok = S.sb(p1, "atok", [128, 4, 512], BF16)
                sc = 96.0 ** -0.5
                for qc in range(4):
                    for i in range(4):
                        n = qc * 4 + i
                        for j in range(2):
                            for kc in range(2):
                                S.op("pe", lambda e, j=j, kc=kc, n=n: e.matmul(PB[6 + j][:, 0:384], lhsT=cqT[:, kc, n * 128:(n + 1) * 128],
                                                                                 rhs=Wuq[:, kc, j * 384:(j + 1) * 384], start=(kc == 0), stop=(kc == 1)),
                                     reads=[cqT, Wuq], writes=[PB[6 + j]])
                            qv = PB[6 + j][:, 0:384].rearrange("p (h c) -> p h c", h=4)
                            S.op("act", lambda e, j=j, qv=qv: e.copy(out=Qt[:, 4 * j:4 * j + 4, 0:64], in_=qv[:, :, 0:64]), reads=[PB[6 + j]], writes=[Qt])
                            rope(Qt, Qt[:, 4 * j:4 * j + 4, 64:80], Qt[:, 4 * j:4 * j + 4, 80:96], qv[:, :, 64:80], qv[:, :, 80:96], [PB[6 + j]], n, [128, 4, 16])
                        transpose_to(Qt, lambda h: Qt[:, h, :], 8, 0, QTc, lambda i=i: QTc[0:96, :, i * 128:(i + 1) * 128], rows=96)
                    for h in range(8):
                        pv0 = 2 + 2 * (h % 2)
                        for kt in range(NTILE):
                            sb_ = kt % 2
                            S.op("pe", lambda e, h=h, kt=kt, sb_=sb_: e.matmul(PB[sb_][:, :], lhsT=KT[0:96, h, kt * 128:(kt + 1) * 128], rhs=QTc[0:96, h, :],
                                                                               start=True, stop=True), reads=[KT, QTc], writes=[PB[sb_]])
                            S.op("act", lambda e, sb_=sb_: e.activation(out=Eb[sb_][:], in_=PB[sb_][:, :], func=AF.Exp, scale=sc),
                                 reads=[PB[sb_]], writes=[Eb[sb_]])
                            for qt in range(4):
                                bk = pv0 + qt // 2
                                c0 = (qt % 2) * 128
                                S.op("pe", lambda e, h=h, kt=kt, sb_=sb_, qt=qt, bk=bk, c0=c0: e.matmul(
                                    PB[bk][:, c0:c0 + 65], lhsT=Eb[sb_][:, qt * 128:(qt + 1) * 128], rhs=Vaug[:, kt, h, :],
                                    start=(kt == 0 and qt % 2 == 0), stop=(kt == NTILE - 1), skip_group_check=True),
                                    reads=[Eb[sb_], Vaug], writes=[PB[bk]])
                        for half in range(2):
                            bk = pv0 + half
                            src = PB[bk][:, 0:256].rearrange("p (a c) -> p a c", a=2)[:, :, 0:65]
                            S.op("dve", lambda e, h=h, half=half, src=src: e.tensor_copy(out=Oacc[:, 2 * half:2 * half + 2, h, :], in_=src),
                                 reads=[PB[bk]], writes=[Oacc])
                    S.op("dve", lambda e: e.reciprocal(out=rc[:], in_=Oacc[:, :, :, 64]), reads=[Oacc], writes=[rc])
                    for i in range(4):
                        S.op("dve", lambda e, i=i: e.tensor_tensor(out=atok[:, i, :].rearrange("p (h c) -> p h c", h=8), in0=Oacc[:, i, :, 0:64],
                                                                   in1=rc[:, i, :].unsqueeze(2).to_broadcast([128, 8, 64]), op=ALU.mult),
                             reads=[Oacc, rc], writes=[atok])
                        n = qc * 4 + i
                        transpose_to(atok, lambda c, i=i: atok[:, i, c * 128:(c + 1) * 128], 4, 6 + (i % 2), attnT,
                                     lambda n=n: attnT[:, :, n * 128:(n + 1) * 128], eng="dve")
                S.barrier()
            if stop == "B":
                return nc

            with ExitStack() as p2:
                qdT = S.sb(p2, "qdT", [128, 4, S_LEN], BF16)
                kdT = S.sb(p2, "kdT", [128, 4, S_LEN], BF16)
                GV = S.sb(p2, "GV", [128, NTILE, 512], BF16)
                SG = S.sb(p2, "SG", [128, NTILE, 512], BF16)
                glab = bcast_load(p2, "glab", gla_d, 128)
                with ExitStack() as p2a:
                    WinB = S.sb(p2a, "WinB", [128, 8, 1568], BF16, dma="sw")
                    Wg = S.sb(p2a, "Wg2", [33, 512], F32, dma=True)
                    S.dma("pool", lambda q: q.dma_start(out=WinB[:], in_=winB_d.rearrange("(c p) n -> p c n", p=128)), writes=[WinB], semt=WinB)
                    S.dma("sp", lambda q: q.dma_start(out=Wg[:], in_=wg_d), writes=[Wg], semt=Wg)
                    g1b = bcast_load(p2a, "g1b2", g1_d, 1024)
                    xt = S.sb(p2a, "xt2", [128, 1024], F32, dma=True)
                    n1 = S.sb(p2a, "n12", [128, 1024], BF16)
                    n1T = S.sb(p2a, "n1T2", [128, 8, 128], BF16)
                    glr = S.sb(p2a, "glr", [128, 32], F32)
                    glrA = S.sb(p2a, "glrA2", [33, 128], F32)
                    sp_t = S.sb(p2a, "sp2", [128, 512], F32)
                    Ep = S.sb(p2a, "Ep", [128, 512], F32)
                    Em = S.sb(p2a, "Em", [128, 512], F32)
                    qd = S.sb(p2a, "qd", [128, 512], BF16)
                    kd = S.sb(p2a, "kd", [128, 512], BF16)
                    DrM = S.sb(p2a, "DrM", [16, 512], F32)
                    S.op("pool", lambda e: e.memset(glrA[:], 1.0), writes=[glrA])
                    for n in range(NTILE):
                        load_norm_T(p2a, base + n * 128, g1b, xt, n1, n1T, 0)
                        for (bk, c0, w) in ((1, 0, 512), (2, 512, 512), (3, 1024, 512), (4, 1536, 32)):
                            for kc in range(8):
                                S.op("pe", lambda e, kc=kc, bk=bk, c0=c0, w=w: e.matmul(PB[bk][:, 0:w], lhsT=n1T[:, kc, :], rhs=WinB[:, kc, c0:c0 + w],
                                                                                         start=(kc == 0), stop=(kc == 7)), reads=[n1T, WinB], writes=[PB[bk]])
                        S.op("act", lambda e, n=n: e.copy(out=GV[:, n, :], in_=PB[2][:, :]), reads=[PB[2]], writes=[GV])
                        S.op("act", lambda e, n=n: e.activation(out=SG[:, n, :], in_=PB[3][:, :], func=AF.Silu), reads=[PB[3]], writes=[SG])
                        S.op("dve", lambda e: e.tensor_copy(out=glr[:], in_=PB[4][:, 0:32]), reads=[PB[4]], writes=[glr])
                        S.op("pe", lambda e: e.transpose(out=PB[5][0:32, 0:128], in_=glr[:], identity=ident_f[:]), reads=[glr, ident_f], writes=[PB[5]])
                        S.op("act", lambda e: e.copy(out=glrA[0:32, :], in_=PB[5][0:32, 0:128]), reads=[PB[5]], writes=[glrA])
                        S.op("pe", lambda e: e.matmul(PB[6][:, :], lhsT=glrA[:, :], rhs=Wg[:, :], start=True, stop=True), reads=[glrA, Wg], writes=[PB[6]])
                        S.op("act", lambda e: e.activation(out=sp_t[:], in_=PB[6][:, :], func=AF.Exp, scale=-1.0), reads=[PB[6]], writes=[sp_t])
                        S.op("act", lambda e: e.activation(out=sp_t[:], in_=sp_t[:], func=AF.Ln, bias=1.0), reads=[sp_t], writes=[sp_t])
                        S.op("dve", lambda e, n=n: e.tensor_scalar_mul(out=DrM[:], in0=DrowG[:], scalar1=eye16[:, n:n + 1]), reads=[DrowG, eye16], writes=[DrM])
                        S.op("pe", lambda e: e.matmul(PB[7][:, 0:256], lhsT=triL[:], rhs=sp_t[:, 0:256], start=True, stop=False, skip_group_check=True),
                             reads=[triL, sp_t], writes=[PB[7]])
                        S.op("pe", lambda e: e.matmul(PB[7][:, 256:512], lhsT=triU[:], rhs=sp_t[:, 256:512], start=False, stop=False, skip_group_check=True),
                             reads=[triU, sp_t], writes=[PB[7]])
                        S.op("pe", lambda e: e.matmul(PB[7][:, :], lhsT=ones16[:], rhs=DrM[:], start=False, stop=True, skip_group_check=True),
                             reads=[ones16, DrM], writes=[PB[7]])
                        S.op("act", lambda e: e.activation(out=Ep[:], in_=PB[7][:, :], func=AF.Exp, scale=-1.0 / 16), reads=[PB[7]], writes=[Ep])
                        S.op("act", lambda e: e.activation(out=Em[:], in_=PB[7][:, :], func=AF.Exp, scale=1.0 / 16), reads=[PB[7]], writes=[Em])
                        gqv = PB[1][:, 0:256].unsqueeze(1).to_broadcast([128, 2, 256])
                        gkv_ = PB[1][:, 256:512].unsqueeze(1).to_broadcast([128, 2, 256])
                        S.op("dve", lambda e, gqv=gqv: e.scalar_tensor_tensor(out=qd[:].rearrange("p (d c) -> p d c", d=2), in0=gqv, scalar=0.125,
                                                                              in1=Ep[:].rearrange("p (d c) -> p d c", d=2), op0=ALU.mult, op1=ALU.mult),
                             reads=[PB[1], Ep], writes=[qd])
                        S.op("dve", lambda e, gkv_=gkv_: e.tensor_tensor(out=kd[:].rearrange("p (d c) -> p d c", d=2), in0=gkv_,
                                                                         in1=Em[:].rearrange("p (d c) -> p d c", d=2), op=ALU.mult),
                             reads=[PB[1], Em], writes=[kd])
                        transpose_to(qd, lambda k: qd[:, k * 128:(k + 1) * 128], 4, 5, qdT, lambda n=n: qdT[:, :, n * 128:(n + 1) * 128], eng="dve")
                        transpose_to(kd, lambda k: kd[:, k * 128:(k + 1) * 128], 4, 4, kdT, lambda n=n: kdT[:, :, n * 128:(n + 1) * 128], eng="act")
                    S.barrier()
                if stop == "A2":
                    return nc

                with ExitStack() as p2c:
                    mf = S.sb(p2c, "mf", [128, 4, 512], BF16)
                    mb = S.sb(p2c, "mb", [128, 4, 512], BF16)
                    S.op("pool", lambda e: e.memset(mf[:], 1.0), writes=[mf])
                    S.op("pool", lambda e: e.memset(mb[:], 1.0), writes=[mb])
                    for r in range(4):
                        S.op("pool", lambda e, r=r: e.affine_select(out=mf[:, r, :], in_=mf[:, r, :], pattern=[[1, 512]], compare_op=ALU.is_ge, fill=0.0,
                                                                    base=-128 * r, channel_multiplier=-1), reads=[mf], writes=[mf])
                        S.op("pool", lambda e, r=r: e.affine_select(out=mb[:, r, :], in_=mb[:, r, :], pattern=[[-1, 512]], compare_op=ALU.is_ge, fill=0.0,
                                                                    base=128 * r, channel_multiplier=1), reads=[mb], writes=[mb])
                    At = [S.sb(p2c, f"At{i}", [128, 512], BF16) for i in range(3)]
                    Ogc = S.sb(p2c, "Ogc", [128, 4, 4, 128], F32)
                    sq = S.sb(p2c, "sq", [128, 16, 128], F32)
                    ssq = S.sb(p2c, "ssq", [128, 16], F32)
                    ogt = S.sb(p2c, "ogt", [128, 4, 512], BF16)
                    ai = 0
                    for tc in range(4):
                        for h in range(4):
                            ob = 4 + (h % 2)
                            hp, hl = h // 2, (h % 2) * 64
                            first = True
                            jobs = [(0, jt) for jt in range(0, 4 * tc + 4)] + [(1, jt) for jt in range(4 * tc, NTILE)]
                            for ji, (d, jt) in enumerate(jobs):
                                sbk = ji % 2
                                blk = d * 2 + hp
                                S.op("pe", lambda e, blk=blk, jt=jt, tc=tc, sbk=sbk, hl=hl: e.matmul(
                                    PB[sbk][:, :], lhsT=kdT[hl:hl + 64, blk, jt * 128:(jt + 1) * 128], rhs=qdT[hl:hl + 64, blk, tc * 512:(tc + 1) * 512],
                                    start=True, stop=True), reads=[kdT, qdT], writes=[PB[sbk]])
                                A = At[ai % 3]
                                ai += 1
                                r = jt - 4 * tc
                                if 0 <= r < 4:
                                    mk = mf if d == 0 else mb
                                    S.op("dve", lambda e, A=A, sbk=sbk, mk=mk, r=r: e.tensor_tensor(out=A[:], in0=PB[sbk][:, :], in1=mk[:, r, :], op=ALU.mult),
                                         reads=[PB[sbk], mk], writes=[A])
                                else:
                                    S.op("act", lambda e, A=A, sbk=sbk: e.copy(out=A[:], in_=PB[sbk][:, :]), reads=[PB[sbk]], writes=[A])
                                for i in range(4):
                                    tt_ = 4 * tc + i
                                    if (d == 0 and jt > tt_) or (d == 1 and jt < tt_):
                                        continue
                                    S.op("pe", lambda e, A=A, i=i, jt=jt, h=h, ob=ob, first=first: e.matmul(
                                        PB[ob][:, i * 128:(i + 1) * 128], lhsT=A[:, i * 128:(i + 1) * 128], rhs=GV[:, jt, h * 128:(h + 1) * 128],
                                        start=first, stop=False, skip_group_check=True), reads=[A, GV], writes=[PB[ob]])
                                    first = False
                            S.op("act", lambda e, h=h, ob=ob: e.copy(out=Ogc[:, :, h, :], in_=PB[ob][:, :].rearrange("p (i c) -> p i c", i=4)),
                                 reads=[PB[ob]], writes=[Ogc])
                        og2 = Ogc[:].rearrange("p i h c -> p (i h) c")
                        S.op("dve", lambda e: e.tensor_tensor(out=sq[:], in0=og2, in1=og2, op=ALU.mult), reads=[Ogc], writes=[sq])
                        S.op("dve", lambda e: e.tensor_reduce(out=ssq[:], in_=sq[:], axis=AX.X, op=ALU.add), reads=[sq], writes=[ssq])
                        S.op("act", lambda e: e.activation(out=ssq[:], in_=ssq[:], func=AF.Sqrt, scale=1.0 / 128, bias=EPS), reads=[ssq], writes=[ssq])
                        S.op("dve", lambda e: e.reciprocal(out=ssq[:], in_=ssq[:]), reads=[ssq], writes=[ssq])
                        S.op("dve", lambda e: e.tensor_tensor(out=sq[:], in0=og2, in1=ssq[:].unsqueeze(2).to_broadcast([128, 16, 128]), op=ALU.mult),
                             reads=[Ogc, ssq], writes=[sq])
                        S.op("dve", lambda e: e.tensor_tensor(out=sq[:], in0=sq[:], in1=glab[:].unsqueeze(1).to_broadcast([128, 16, 128]), op=ALU.mult),
                             reads=[sq, glab], writes=[sq])
                        S.op("dve", lambda e, tc=tc: e.tensor_tensor(out=ogt[:].rearrange("p i c -> p (i c)"), in0=sq[:].rearrange("p a c -> p (a c)"),
                                                                     in1=SG[:, 4 * tc:4 * tc + 4, :].rearrange("p i c -> p (i c)"), op=ALU.mult),
                             reads=[sq, SG], writes=[ogt])
                        for i in range(4):
                            n = 4 * tc + i
                            transpose_to(ogt, lambda c, i=i: ogt[:, i, c * 128:(c + 1) * 128], 4, 6 + (i % 2), oglaT,
                                         lambda n=n: oglaT[:, :, n * 128:(n + 1) * 128], eng="act")
                    S.barrier()

            if stop == "C":
                return nc
            with ExitStack() as p3:
                Wo = S.sb(p3, "Wo", [128, 8, 1024], BF16, dma="sw")
                Wq = S.sb(p3, "Wq", [128, 8, 2048], BF16, dma="sw")
                SubT = S.sb(p3, "SubT", [128, 16, 128], BF16, dma="sw")
                S.dma("pool", lambda q: q.dma_start(out=Wo[:], in_=wo_d.rearrange("(c p) n -> p c n", p=128)), writes=[Wo], semt=Wo)
                S.dma("pool", lambda q: q.dma_start(out=Wq[:], in_=wq_d.rearrange("(c p) n -> p c n", p=128)), writes=[Wq], semt=Wq)
                S.dma("pool", lambda q: q.dma_start(out=SubT[:], in_=subk_d), writes=[SubT], semt=SubT)
                g2b = bcast_load(p3, "g2b", g2_d, 1024)
                gfb = bcast_load(p3, "gfb", gf_d, 1024)
                htP = [S.sb(p3, f"ht{i}", [128, 1024], F32, dma=True) for i in range(2)]
                hnP = [S.sb(p3, f"hn{i}", [128, 1024], F32) for i in range(2)]
                idxP = [S.sb(p3, f"idxu{i}", [128, 128], U32) for i in range(2)]
                gateP = [S.sb(p3, f"gate{i}", [128, 8, 16], F32) for i in range(2)]
                hnb = S.sb(p3, "hnb", [128, 1024], BF16)
                hnT = S.sb(p3, "hnT", [128, 8, 128], BF16)
                qT = S.sb(p3, "qT", [128, 16, 128], BF16)
                ssb = S.sb(p3, "ssb", [128, 16, 128], F32)
                wk = S.sb(p3, "wk", [128, 256], F32)
                m16 = S.sb(p3, "m16", [128, 16, 16], F32)
                i16 = S.sb(p3, "i16", [128, 16, 16], U32)
                i16f = S.sb(p3, "i16f", [128, 16, 16], F32)
                cand = S.sb(p3, "cand", [128, 8, 256], F32)
                tops = S.sb(p3, "tops", [128, 8, 16], F32)
                pos = S.sb(p3, "pos", [128, 8, 16], U32)
                pa_ = S.sb(p3, "posa", [128, 8, 16], U32)
                pb_ = S.sb(p3, "posb", [128, 8, 16], U32)
                paf = S.sb(p3, "paf", [128, 8, 16], F32)
                pbf_ = S.sb(p3, "pbf", [128, 8, 16], F32)
                eq = S.sb(p3, "eq", [128, 8, 16, 16], BF16)
                sel1 = S.sb(p3, "sel1", [128, 8, 16], F32)
                sel2 = S.sb(p3, "sel2", [128, 8, 16], F32)
                idxf = S.sb(p3, "idxf", [128, 128], F32)
                gsum = S.sb(p3, "gsum", [128, 8], F32)
                stat2 = S.sb(p3, "stat2", [128, 8], F32)
                junk_a = S.sb(p3, "junk_a", [128, 1024], BF16)
                actv = S.sb(p3, "actv", [128, 128], F32)
                glt = S.sb(p3, "glt", [128, 128], F32)
                wct = S.sb(p3, "wct", [128, 128], F32)
                actC = [T(actv[:, i:i + 1], f"actc{i}") for i in range(128)]
                glC = [T(glt[:, i:i + 1], f"glc{i}") for i in range(128)]
                NUV = 10
                UV = [S.sb(p3, f"UV{i}", [128, 2048], BF16, dma="sw") for i in range(NUV)]
                NDG = 4
                Dg = [S.sb(p3, f"Dg{i}", [128, 128], BF16) for i in range(NDG)]

                def rstd2(src_t, col):
                    S.op("act", lambda e: e.activation(out=junk_a[:], in_=src_t[:], func=AF.Square, accum_out=stat2[:, col:col + 1]),
                         reads=[src_t], writes=[stat2, junk_a])
                    S.op("act", lambda e: e.activation(out=stat2[:, col + 1:col + 2], in_=stat2[:, col:col + 1], func=AF.Sqrt,
                                                       scale=1.0 / 1024, bias=EPS), reads=[stat2], writes=[stat2])
                    S.op("dve", lambda e: e.reciprocal(out=stat2[:, col + 1:col + 2], in_=stat2[:, col + 1:col + 2]), reads=[stat2], writes=[stat2])
                    return stat2[:, col + 1:col + 2]

                def top16(src_ap, src_t, n_el, mv, iv, mvt, ivt):
                    S.op("dve", lambda e: e.max(out=mv[:, 0:8], in_=src_ap), reads=[src_t], writes=[mvt])
                    S.op("dve", lambda e: e.max_index(out=iv[:, 0:8], in_max=mv[:, 0:8], in_values=src_ap), reads=[src_t, mvt], writes=[ivt])
                    S.op("dve", lambda e: e.match_replace(out=wk[:, 0:n_el], in_to_replace=mv[:, 0:8], in_values=src_ap, imm_value=-1e30),
                         reads=[src_t, mvt], writes=[wk])
                    S.op("dve", lambda e: e.max(out=mv[:, 8:16], in_=wk[:, 0:n_el]), reads=[wk], writes=[mvt])
                    S.op("dve", lambda e: e.max_index(out=iv[:, 8:16], in_max=mv[:, 8:16], in_values=wk[:, 0:n_el]), reads=[wk, mvt], writes=[ivt])

                def emit_d12(n):
                    par = n % 2
                    ht, hn, idxu, gate = htP[par], hnP[par], idxP[par], gateP[par]
                    r0 = base + n * 128
                    S.dma("sp", lambda q: q.dma_start(out=ht[:], in_=x_d[r0:r0 + 128, :]), writes=[ht], semt=ht)
                    for j in range(2):
                        for c in range(8):
                            src = attnT if c < 4 else oglaT
                            S.op("pe", lambda e, j=j, c=c, src=src: e.matmul(PB[j][:, :], lhsT=src[:, c % 4, n * 128:(n + 1) * 128],
                                                                             rhs=Wo[:, c, j * 512:(j + 1) * 512], start=(c == 0), stop=(c == 7)),
                                 reads=[src, Wo], writes=[PB[j]])
                        S.op("dve", lambda e, j=j: e.tensor_tensor(out=ht[:, j * 512:(j + 1) * 512], in0=PB[j][:, :], in1=ht[:, j * 512:(j + 1) * 512], op=ALU.add),
                             reads=[PB[j], ht], writes=[ht])
                    if dbg:
                        S.dma("sp", lambda q: q.dma_start(out=dbgh_d[r0:r0 + 128, :], in_=ht[:]), reads=[ht], writes=[dbgh_t], semt=dbgh_t)
                    r = rstd2(ht, 0)
                    S.op("dve", lambda e: e.scalar_tensor_tensor(out=hn[:], in0=ht[:], scalar=r, in1=g2b[:], op0=ALU.mult, op1=ALU.mult),
                         reads=[ht, stat2, g2b], writes=[hn])
                    S.op("act", lambda e: e.copy(out=hnb[:], in_=hn[:]), reads=[hn], writes=[hnb])
                    transpose_to(hnb, lambda kc: hnb[:, kc * 128:(kc + 1) * 128], 8, 2, hnT, lambda: hnT[:])
                    for g4 in range(4):
                        bk = 3 + (g4 % 2)
                        for q4 in range(4):
                            hp = g4 * 4 + q4
                            for kc in range(8):
                                S.op("pe", lambda e, hp=hp, kc=kc, bk=bk, q4=q4: e.matmul(PB[bk][:, q4 * 128:(q4 + 1) * 128], lhsT=Wq[:, kc, hp * 128:(hp + 1) * 128],
                                                                                            rhs=hnT[:, kc, :], start=(kc == 0), stop=(kc == 7)),
                                     reads=[Wq, hnT], writes=[PB[bk]])
                        S.op("act", lambda e, g4=g4, bk=bk: e.copy(out=qT[:, g4 * 4:(g4 + 1) * 4, :].rearrange("p a b -> p (a b)"), in_=PB[bk][:, :]),
                             reads=[PB[bk]], writes=[qT])
                    for g4 in range(4):
                        bk = 5 if g4 % 2 == 0 else 2
                        for q4 in range(4):
                            hp = g4 * 4 + q4
                            S.op("pe", lambda e, hp=hp, bk=bk, q4=q4: e.matmul(PB[bk][:, q4 * 128:(q4 + 1) * 128], lhsT=qT[:, hp, :], rhs=SubT[:, hp, :],
                                                                                start=True, stop=True), reads=[qT, SubT], writes=[PB[bk]])
                        S.op("act", lambda e, g4=g4, bk=bk: e.copy(out=ssb[:, g4 * 4:(g4 + 1) * 4, :].rearrange("p a b -> p (a b)"), in_=PB[bk][:, :]),
                             reads=[PB[bk]], writes=[ssb])
                    for hp in range(16):
                        top16(ssb[:, hp, :], ssb, 128, m16[:, hp, :], i16[:, hp, :], m16, i16)
                    m4 = m16[:].rearrange("p (h a) i -> p h a i", a=2)
                    S.op("dve", lambda e: e.tensor_tensor(out=cand[:].rearrange("p h (i j) -> p h i j", i=16),
                                                          in0=m4[:, :, 0, :].unsqueeze(3).to_broadcast([128, 8, 16, 16]),
                                                          in1=m4[:, :, 1, :].unsqueeze(2).to_broadcast([128, 8, 16, 16]), op=ALU.add),
                         reads=[m16], writes=[cand])
                    for h in range(8):
                        top16(cand[:, h, :], cand, 256, tops[:, h, :], pos[:, h, :], tops, pos)
                    S.op("dve", lambda e: e.tensor_single_scalar(out=pa_[:], in_=pos[:], scalar=4, op=ALU.logical_shift_right), reads=[pos], writes=[pa_])
                    S.op("dve", lambda e: e.tensor_single_scalar(out=pb_[:], in_=pos[:], scalar=15, op=ALU.bitwise_and), reads=[pos], writes=[pb_])
                    S.op("dve", lambda e: e.tensor_copy(out=paf[:], in_=pa_[:]), reads=[pa_], writes=[paf])
                    S.op("dve", lambda e: e.tensor_copy(out=pbf_[:], in_=pb_[:]), reads=[pb_], writes=[pbf_])
                    S.op("dve", lambda e: e.tensor_copy(out=i16f[:], in_=i16[:]), reads=[i16], writes=[i16f])
                    i4 = i16f[:].rearrange("p (h a) i -> p h a i", a=2)
                    iob = iota16[:].unsqueeze(1).unsqueeze(1).to_broadcast([128, 8, 16, 16])
                    for (pf, a, sel) in ((paf, 0, sel1), (pbf_, 1, sel2)):
                        S.op("dve", lambda e, pf=pf: e.tensor_tensor(out=eq[:], in0=pf[:].unsqueeze(3).to_broadcast([128, 8, 16, 16]), in1=iob, op=ALU.is_equal),
                             reads=[pf, iota16], writes=[eq])
                        S.op("dve", lambda e, a=a: e.tensor_tensor(out=eq[:], in0=eq[:], in1=i4[:, :, a, :].unsqueeze(2).to_broadcast([128, 8, 16, 16]), op=ALU.mult),
                             reads=[eq, i16f], writes=[eq])
                        S.op("dve", lambda e, sel=sel: e.tensor_reduce(out=sel[:], in_=eq[:], axis=AX.X, op=ALU.add), reads=[eq], writes=[sel])
                    S.op("dve", lambda e: e.scalar_tensor_tensor(out=idxf[:], in0=sel1[:].rearrange("p h k -> p (h k)"), scalar=128.0,
                                                                 in1=sel2[:].rearrange("p h k -> p (h k)"), op0=ALU.mult, op1=ALU.add),
                         reads=[sel1, sel2], writes=[idxf])
                    S.op("dve", lambda e: e.tensor_copy(out=idxu[:], in_=idxf[:]), reads=[idxf], writes=[idxu])
                    S.op("dve", lambda e: e.tensor_tensor(out=gate[:], in0=tops[:], in1=tops[:, :, 0:1].to_broadcast([128, 8, 16]), op=ALU.subtract),
                         reads=[tops], writes=[gate])
                    S.op("act", lambda e: e.activation(out=gate[:], in_=gate[:], func=AF.Exp), reads=[gate], writes=[gate])
                    S.op("dve", lambda e: e.tensor_reduce(out=gsum[:], in_=gate[:], axis=AX.X, op=ALU.add), reads=[gate], writes=[gsum])
                    S.op("dve", lambda e: e.reciprocal(out=gsum[:], in_=gsum[:]), reads=[gsum], writes=[gsum])
                    S.op("dve", lambda e: e.tensor_tensor(out=gate[:], in0=gate[:], in1=gsum[:].unsqueeze(2).to_broadcast([128, 8, 16]), op=ALU.mult),
                         reads=[gate, gsum], writes=[gate])

                def emit_loop(n, pending):
                    par = n % 2
                    ht, hn, idxu, gate = htP[par], hnP[par], idxP[par], gateP[par]
                    r0 = base + n * 128
                    gflat = gate[:].rearrange("p h k -> p (h k)")
                    LAG = 2

                    def emit_acc(sl):
                        U = UV[sl % NUV]
                        D = Dg[sl % NDG]
                        S.op("act", lambda e: e.activation(out=glt[:, sl:sl + 1], in_=actv[:, sl:sl + 1], func=AF.Gelu_apprx_tanh),
                             reads=[actC[sl]], writes=[glC[sl]])
                        S.op("act", lambda e: e.activation(out=wct[:, sl:sl + 1], in_=gflat[:, sl:sl + 1], func=AF.Copy, scale=glt[:, sl:sl + 1]),
                             reads=[glC[sl], gate], writes=[glC[sl]])
                        S.op("act", lambda e: e.activation(out=D[:], in_=ident_bf[:], func=AF.Copy, scale=wct[:, sl:sl + 1]),
                             reads=[ident_bf, glC[sl]], writes=[D])
                        for j in range(2):
                            S.op("pe", lambda e, j=j: e.matmul(PB[6 + j][:, :], lhsT=D[:], rhs=U[:, 1024 + j * 512:1024 + (j + 1) * 512],
                                                               start=(sl == 0), stop=(sl == 127)), reads=[D, U], writes=[PB[6 + j]])

                    for sl in range(128):
                        U = UV[sl % NUV]
                        S.dma("pool", lambda q, U=U, sl=sl: q.indirect_dma_start(out=U[:], out_offset=None, in_=uvbf_d,
                                                                                 in_offset=bass.IndirectOffsetOnAxis(ap=idxu[:, sl:sl + 1], axis=0)),
                              reads=[idxu, uv_t], writes=[U], semt=U)
                        S.op("dve", lambda e, U=U, sl=sl: e.scalar_tensor_tensor(out=junk_t[:], in0=U[:, 0:1024], scalar=1.0, in1=hn[:], op0=ALU.mult, op1=ALU.mult,
                                                                                 accum_out=actv[:, sl:sl + 1]), reads=[U, hn], writes=[junk_t, actC[sl]])
                        if sl >= LAG:
                            emit_acc(sl - LAG)
                        if pending:
                            S.replay(pending, 2)
                    for sl in range(128 - LAG, 128):
                        emit_acc(sl)
                    if pending:
                        S.replay(pending, len(pending))
                    for j in range(2):
                        S.op("dve", lambda e, j=j: e.tensor_tensor(out=ht[:, j * 512:(j + 1) * 512], in0=PB[6 + j][:, :], in1=ht[:, j * 512:(j + 1) * 512], op=ALU.add),
                             reads=[PB[6 + j], ht], writes=[ht])
                    r = rstd2(ht, 2)
                    S.op("dve", lambda e: e.scalar_tensor_tensor(out=hn[:], in0=ht[:], scalar=r, in1=gfb[:], op0=ALU.mult, op1=ALU.mult),
                         reads=[ht, stat2, gfb], writes=[hn])
                    S.dma("sp", lambda q: q.dma_start(out=y_d[r0:r0 + 128, :], in_=hn[:]), reads=[hn], writes=[y_t], semt=y_t)

                emit_d12(0)
                for n in range(NTILE):
                    pending = []
                    if n + 1 < NTILE:
                        S.defer = pending
                        emit_d12(n + 1)
                        S.defer = None
                    emit_loop(n, pending)
                S.barrier()
        S.barrier()
        print("ninstr", S.ninstr, "nsem", S.nsem)
    return nc


def _host_inputs(inputs):
    f = np.float32
    w_in = np.asarray(inputs["w_in"])[0]
    wA = np.ascontiguousarray(np.concatenate([w_in[:, 0:416], w_in[:, 1440:1472]], axis=1))
    wB = np.ascontiguousarray(np.concatenate([w_in[:, 416:1440], w_in[:, 1472:1984], w_in[:, 1440:1472]], axis=1))
    gw = np.zeros((33, 512), f)
    gw[0:16, 0:256] = np.asarray(inputs["gate_fwd_w"])[0]
    gw[16:32, 256:512] = np.asarray(inputs["gate_bwd_w"])[0]
    gw[32, 0:256] = np.asarray(inputs["gate_fwd_b"])[0]
    gw[32, 256:512] = np.asarray(inputs["gate_bwd_b"])[0]
    subk = np.asarray(inputs["peer_subkeys"])[0].reshape(16, 128, 128)
    subkT = np.ascontiguousarray(np.transpose(subk, (2, 0, 1)))
    half = 16
    freqs = (np.float32(10000.0) ** (-np.arange(half, dtype=f) * f(2.0) / f(32))).astype(f)
    ang = (np.arange(S_LEN, dtype=f)[:, None] * freqs[None, :]).astype(f)
    shared = {
        "w_inA": wA, "w_inB": wB,
        "norm1_g": np.ascontiguousarray(np.asarray(inputs["norm1_g"])[0]),
        "q_norm_g": np.ascontiguousarray(np.asarray(inputs["q_norm_g"])[0]),
        "w_uq": np.ascontiguousarray(np.asarray(inputs["w_uq"])[0]),
        "kv_norm_g": np.ascontiguousarray(np.asarray(inputs["kv_norm_g"])[0]),
        "w_ukv": np.ascontiguousarray(np.asarray(inputs["w_ukv"])[0]),
        "gate_w": gw,
        "gla_norm_g": np.ascontiguousarray(np.asarray(inputs["gla_norm_g"])[0]),
        "w_o": np.ascontiguousarray(np.asarray(inputs["w_o"])[0]),
        "norm2_g": np.ascontiguousarray(np.asarray(inputs["norm2_g"])[0]),
        "peer_wq": np.ascontiguousarray(np.asarray(inputs["peer_wq"])[0]),
        "subkT": subkT,
        "peer_u": np.ascontiguousarray(np.asarray(inputs["peer_u"])[0]),
        "peer_v": np.ascontiguousarray(np.asarray(inputs["peer_v"])[0]),
        "final_norm_g": np.ascontiguousarray(np.asarray(inputs["final_norm_g"])),
        "rope_cos": np.cos(ang).astype(f), "rope_sin": np.sin(ang).astype(f),
    }
    return shared


def kernel(**inputs):
    xp = np.asarray(inputs["x_prompt"], dtype=np.float32)
    xs = np.asarray(inputs["x_sample"], dtype=np.float32)
    xall = np.concatenate([xp, xs], axis=0)
    nb = xall.shape[0]
    ncore = 8
    per = nb // ncore
    shared = _host_inputs(inputs)
    nc = build_program(per, False)
    in_maps = []
    for c in range(ncore):
        m = dict(shared)
        m["x"] = np.ascontiguousarray(xall[c * per:(c + 1) * per].reshape(per * S_LEN, 1024))
        in_maps.append(m)
    res = run_bass_kernel_spmd(nc, in_maps, core_ids=list(range(ncore)))
    ys = [np.asarray(r["y"]).reshape(per, S_LEN, 1024) for r in res.results]
    yall = np.concatenate(ys, axis=0).astype(np.float32)
    return (yall[:xp.shape[0]], yall[xp.shape[0]:])
```

```python
import numpy as np
from contextlib import ExitStack
import concourse.bass as bass
import concourse.mybir as mybir
from concourse.bass_utils import run_bass_kernel_spmd

F32 = mybir.dt.float32
BF16 = mybir.dt.bfloat16
U32 = mybir.dt.uint32
AF = mybir.ActivationFunctionType
ALU = mybir.AluOpType
AX = mybir.AxisListType

NSEQ = 5
DBG = False
EPOCH = 60000
S_LEN = 2048
NTILE = 16
EPS = 1e-6


class DSem:
    def __init__(self, h):
        self.h = h
        self.cnt = 0


class T:
    def __init__(self, ap, name, dsem=None):
        self.ap = ap
        self.name = name
        self.lw = None
        self.rd = {}
        self.dsem = dsem

    def __getitem__(self, k):
        return self.ap[k]


class Sched:
    def __init__(self, nc, stack):
        self.nc = nc
        self.stack = stack
        self.eng = {"pe": nc.tensor, "act": nc.scalar, "dve": nc.vector, "pool": nc.gpsimd, "sp": nc.sync}
        self.cnt = {k: 0 for k in self.eng}
        self.sem = {}
        self.nsem = 0
        for k in self.eng:
            self.sem[k] = self._newsem(k)
        self.waited = {k: {} for k in self.eng}
        self.ninstr = 0
        self.dpool = {"sw": [], "hw": []}
        self.dall = []

    def _newsem(self, name):
        self.nsem += 1
        return self.stack.enter_context(self.nc.semaphore(f"s{self.nsem}_{name}"))

    def getd(self, kind="hw"):
        if self.dpool[kind]:
            return self.dpool[kind].pop()
        d = DSem(self._newsem("d" + kind))
        d.kind = kind
        self.dall.append(d)
        return d

    def sb(self, st, name, shape, dtype, dma=False):
        self.nalloc = getattr(self, "nalloc", 0) + 1
        name = f"{name}_{self.nalloc}"
        t = st.enter_context(self.nc.sbuf_tensor(name, shape, dtype))
        kind = "hw" if dma is True else dma
        tt = T(t, name, self.getd(kind) if dma else None)
        if dma:
            st.callback(lambda d=tt.dsem: self.dpool[d.kind].append(d))
        return tt

    def ps(self, st, name, shape, dtype):
        t = st.enter_context(self.nc.psum_tensor(name, shape, dtype))
        tt = T(t, name)
        tt.excl = True
        return tt

    def dram(self, ap, name):
        return T(ap, name, self.getd())

    def _wait(self, e, deps):
        w = self.waited[e]
        for (sem, val) in deps:
            key = id(sem)
            if w.get(key, 0) >= val:
                continue
            self.eng[e].wait_ge(sem, val)
            w[key] = val
            self.ninstr += 1

    def replay(self, lst, k):
        d, self.defer = self.defer, None
        for _ in range(min(k, len(lst))):
            kind, a = lst.pop(0)
            (self.op if kind == "op" else self.dma)(*a)
        self.defer = d

    def op(self, e, fn, reads=(), writes=()):
        if getattr(self, "defer", None) is not None:
            self.defer.append(("op", (e, fn, list(reads), list(writes))))
            return None
        ex = [t for t in reads if getattr(t, "excl", False)]
        if ex:
            reads = [t for t in reads if not getattr(t, "excl", False)]
            writes = list(writes) + ex
        deps = []
        for t in reads:
            if t.lw is not None:
                deps.append(t.lw[1:])
        strict = (e != "pe")
        for t in writes:
            if t.lw is not None and (t.lw[0] != e or strict):
                deps.append(t.lw[1:])
            for en, d in t.rd.items():
                if en != e or strict:
                    deps.append(d)
        self._wait(e, deps)
        ins = fn(self.eng[e])
        self.cnt[e] += 1
        if self.cnt[e] > EPOCH:
            self.sem[e] = self._newsem(e)
            self.cnt[e] = 1
        ins.then_inc(self.sem[e], 1)
        self.ninstr += 1
        rec = (self.sem[e], self.cnt[e])
        for t in reads:
            t.rd[e] = rec
        for t in writes:
            t.lw = (e,) + rec
            t.rd = {}
        return ins

    def dma(self, q, fn, reads=(), writes=(), semt=None):
        if getattr(self, "defer", None) is not None:
            self.defer.append(("dma", (q, fn, list(reads), list(writes), semt)))
            return None
        deps = []
        for t in reads:
            if t.lw is not None:
                deps.append(t.lw[1:])
        for t in writes:
            if t.lw is not None:
                deps.append(t.lw[1:])
            for en, d in t.rd.items():
                deps.append(d)
        self._wait(q, deps)
        ins = fn(self.eng[q])
        ds = semt.dsem
        ds.cnt += 16
        ins.then_inc(ds.h, 16)
        self.ninstr += 1
        rec = (ds.h, ds.cnt)
        for t in reads:
            t.rd[("dma", id(ds))] = rec
        for t in writes:
            t.lw = ("dma",) + rec
            t.rd = {}
        return ins

    def barrier(self):
        deps = [(self.sem[k], self.cnt[k]) for k in self.eng if self.cnt[k] > 0]
        deps += [(d.h, d.cnt) for d in self.dall if d.cnt > 0]
        for e in self.eng:
            self._wait(e, deps)


def build_program(nseq, dbg, stop=None):
    nc = bass.Bass("TRN2", target_bir_lowering=False)
    ntok = nseq * S_LEN

    def din(name, shape, dt=F32):
        return nc.dram_tensor(name, shape, dt, kind="ExternalInput").ap()

    x_d = din("x", [ntok, 1024])
    winA_d = din("w_inA", [1024, 448])
    winB_d = din("w_inB", [1024, 1568])
    g1_d = din("norm1_g", [1024])
    gq_d = din("q_norm_g", [256])
    wuq_d = din("w_uq", [256, 768])
    gkv_d = din("kv_norm_g", [128])
    wukv_d = din("w_ukv", [128, 1024])
    wg_d = din("gate_w", [33, 512])
    gla_d = din("gla_norm_g", [128])
    wo_d = din("w_o", [1024, 1024])
    g2_d = din("norm2_g", [1024])
    wq_d = din("peer_wq", [1024, 2048])
    subk_d = din("subkT", [128, 16, 128])
    u_d = din("peer_u", [16384, 1024])
    v_d = din("peer_v", [16384, 1024])
    gf_d = din("final_norm_g", [1024])
    cos_d = din("rope_cos", [S_LEN, 16])
    sin_d = din("rope_sin", [S_LEN, 16])
    y_d = nc.dram_tensor("y", [ntok, 1024], F32, kind="ExternalOutput").ap()
    uvbf_d = nc.dram_tensor("uv_bf", [16384, 2048], BF16, kind="Internal").ap()
    if dbg:
        dbgh_d = nc.dram_tensor("dbg_h", [ntok, 1024], F32, kind="ExternalOutput").ap()
        dbgi_d = nc.dram_tensor("dbg_i", [ntok, 128], U32, kind="ExternalOutput").ap()
        dbgg_d = nc.dram_tensor("dbg_g", [ntok, 128], F32, kind="ExternalOutput").ap()
        dbga_d = nc.dram_tensor("dbg_a", [ntok, 128], F32, kind="ExternalOutput").ap()

    with ExitStack() as gst:
        S = Sched(nc, gst)
        y_t = S.dram(y_d, "y")
        if dbg:
            dbgh_t = S.dram(dbgh_d, "dbgh")
            dbgi_t = S.dram(dbgi_d, "dbgi")
            dbgg_t = S.dram(dbgg_d, "dbgg")
            dbga_t = S.dram(dbga_d, "dbga")

        ident_bf = S.sb(gst, "ident_bf", [128, 128], BF16)
        ident_f = S.sb(gst, "ident_f", [128, 128], F32)
        triL = S.sb(gst, "triL", [128, 128], F32)
        triU = S.sb(gst, "triU", [128, 128], F32)
        eye16 = S.sb(gst, "eye16", [16, 16], F32)
        ones16 = S.sb(gst, "ones16", [16, 128], F32)
        iota16 = S.sb(gst, "iota16", [128, 16], F32)
        for tt in (ident_bf, ident_f):
            S.op("pool", lambda e, tt=tt: e.memset(tt[:], 0.0), writes=[tt])
            S.op("pool", lambda e, tt=tt: e.affine_select(out=tt[:], in_=tt[:], pattern=[[-1, 128]], compare_op=ALU.not_equal,
                                                          fill=1.0, base=0, channel_multiplier=1), reads=[tt], writes=[tt])
        S.op("pool", lambda e: e.memset(eye16[:], 0.0), writes=[eye16])
        S.op("pool", lambda e: e.affine_select(out=eye16[:], in_=eye16[:], pattern=[[-1, 16]], compare_op=ALU.not_equal,
                                               fill=1.0, base=0, channel_multiplier=1), reads=[eye16], writes=[eye16])
        S.op("pool", lambda e: e.memset(ones16[:], 1.0), writes=[ones16])
        S.op("pool", lambda e: e.memset(triL[:], 1.0), writes=[triL])
        S.op("pool", lambda e: e.affine_select(out=triL[:], in_=triL[:], pattern=[[1, 128]], compare_op=ALU.is_ge,
                                               fill=0.0, base=0, channel_multiplier=-1), reads=[triL], writes=[triL])
        S.op("pool", lambda e: e.memset(triU[:], 1.0), writes=[triU])
        S.op("pool", lambda e: e.affine_select(out=triU[:], in_=triU[:], pattern=[[-1, 128]], compare_op=ALU.is_ge,
                                               fill=0.0, base=0, channel_multiplier=1), reads=[triU], writes=[triU])
        S.op("pool", lambda e: e.iota(iota16[:], pattern=[[1, 16]], base=0, channel_multiplier=0,
                                      allow_small_or_imprecise_dtypes=True), writes=[iota16])
        attnT = S.sb(gst, "attnT", [128, 4, S_LEN], BF16)
        oglaT = S.sb(gst, "oglaT", [128, 4, S_LEN], BF16)
        PB = [S.ps(gst, f"pb{i}", [128, 512], F32) for i in range(8)]

        def pbf(i):
            return PB[i][:].bitcast(BF16)

        stat = S.sb(gst, "stat", [128, 8], F32)
        DrowG = S.sb(gst, "DrowG", [16, 512], F32)

        def rstd_of(src_ap, src_ts, width, junk, col):
            S.op("act", lambda e: e.activation(out=junk, in_=src_ap, func=AF.Square, accum_out=stat[:, col:col + 1]),
                 reads=src_ts, writes=[stat, junk_t])
            S.op("act", lambda e: e.activation(out=stat[:, col + 1:col + 2], in_=stat[:, col:col + 1], func=AF.Sqrt,
                                               scale=1.0 / width, bias=EPS), reads=[stat], writes=[stat])
            S.op("dve", lambda e: e.reciprocal(out=stat[:, col + 1:col + 2], in_=stat[:, col + 1:col + 2]),
                 reads=[stat], writes=[stat])
            return stat[:, col + 1:col + 2]

        junk_t = S.sb(gst, "junk", [128, 1024], BF16)

        def load_norm_T(st_tiles, row0, gb, xt, n1, n1T, bank):
            S.dma("sp", lambda q: q.dma_start(out=xt[:], in_=x_d[row0:row0 + 128, :]), writes=[xt], semt=xt)
            r = rstd_of(xt[:], [xt], 1024, junk_t[:], 0)
            S.op("dve", lambda e: e.scalar_tensor_tensor(out=n1[:], in0=xt[:], scalar=r, in1=gb[:], op0=ALU.mult, op1=ALU.mult),
                 reads=[xt, stat, gb], writes=[n1])
            transpose_to(n1, lambda kc: n1[:, kc * 128:(kc + 1) * 128], 8, bank, n1T, lambda: n1T[:])

        def transpose_to(src_t, src_fn, nblk, bank, dst_t, dst_fn, rows=128, eng="act"):
            pv = pbf(bank)
            for k in range(nblk):
                S.op("pe", lambda e, k=k: e.transpose(out=pv[0:rows, k * 128:(k + 1) * 128], in_=src_fn(k), identity=ident_bf[:]),
                     reads=[src_t, ident_bf], writes=[PB[bank]])
            srcv = pv[0:rows, 0:nblk * 128].rearrange("p (a b) -> p a b", a=nblk)
            if eng == "act":
                S.op("act", lambda e: e.copy(out=dst_fn(), in_=srcv), reads=[PB[bank]], writes=[dst_t])
            else:
                S.op("dve", lambda e: e.tensor_copy(out=dst_fn(), in_=srcv), reads=[PB[bank]], writes=[dst_t])

        def bcast_load(st, name, vec_d, n):
            t = S.sb(st, name, [128, n], F32, dma=True)
            S.dma("sp", lambda q: q.dma_start(out=t[:], in_=vec_d.partition_broadcast(128)), writes=[t], semt=t)
            return t

        uv_t = S.dram(uvbf_d, "uvbf")
        with ExitStack() as pp:
            cb = [S.sb(pp, f"cb{i}", [128, 4, 1024], BF16, dma="sw") for i in range(3)]
            k = 0
            for (src, c0) in ((u_d, 0), (v_d, 1024)):
                for ch in range(32):
                    b = cb[k % 3]
                    k += 1
                    S.dma("pool", lambda q, b=b, src=src, ch=ch: q.dma_start(out=b[:], in_=src[ch * 512:(ch + 1) * 512, :].rearrange("(p r) d -> p r d", r=4)),
                          writes=[b], semt=b)
                    S.dma("sp", lambda q, b=b, c0=c0, ch=ch: q.dma_start(out=uvbf_d[ch * 512:(ch + 1) * 512, c0:c0 + 1024].rearrange("(p r) d -> p r d", r=4), in_=b[:]),
                          reads=[b], writes=[uv_t], semt=uv_t)
            S.barrier()

        for s in range(nseq):
            base = s * S_LEN
            with ExitStack() as p1:
                WinA = S.sb(p1, "WinA", [128, 8, 448], BF16, dma="sw")
                Wuq = S.sb(p1, "Wuq", [128, 2, 768], BF16, dma="sw")
                Wukv = S.sb(p1, "Wukv", [128, 1024], BF16, dma="sw")
                Wg = S.sb(p1, "Wg", [33, 512], F32, dma=True)
                S.dma("pool", lambda q: q.dma_start(out=WinA[:], in_=winA_d.rearrange("(c p) n -> p c n", p=128)), writes=[WinA], semt=WinA)
                S.dma("pool", lambda q: q.dma_start(out=Wuq[:], in_=wuq_d.rearrange("(c p) n -> p c n", p=128)), writes=[Wuq], semt=Wuq)
                S.dma("pool", lambda q: q.dma_start(out=Wukv[:], in_=wukv_d), writes=[Wukv], semt=Wukv)
                S.dma("sp", lambda q: q.dma_start(out=Wg[:], in_=wg_d), writes=[Wg], semt=Wg)
                g1b = bcast_load(p1, "g1b", g1_d, 1024)
                gqb = bcast_load(p1, "gqb", gq_d, 256)
                gkvb = bcast_load(p1, "gkvb", gkv_d, 128)
                cosT = S.sb(p1, "cosT", [128, NTILE, 16], F32, dma=True)
                sinT = S.sb(p1, "sinT", [128, NTILE, 16], F32, dma=True)
                S.dma("sp", lambda q: q.dma_start(out=cosT[:], in_=cos_d.rearrange("(n p) d -> p n d", p=128)), writes=[cosT], semt=cosT)
                S.dma("sp", lambda q: q.dma_start(out=sinT[:], in_=sin_d.rearrange("(n p) d -> p n d", p=128)), writes=[sinT], semt=sinT)

                KT = S.sb(p1, "KT", [128, 8, S_LEN], BF16)
                Vaug = S.sb(p1, "Vaug", [128, NTILE, 8, 65], BF16)
                cqT = S.sb(p1, "cqT", [128, 2, S_LEN], BF16)
                TTs = S.sb(p1, "TTs", [128, 4, NTILE], F32)
                xt = S.sb(p1, "xt", [128, 1024], F32, dma=True)
                n1 = S.sb(p1, "n1", [128, 1024], BF16)
                n1T = S.sb(p1, "n1T", [128, 8, 128], BF16)
                pa = S.sb(p1, "pa", [128, 448], F32)
                cqn = S.sb(p1, "cqn", [128, 384], BF16)
                lat = S.sb(p1, "latT", [128, 3, 128], BF16)
                glrA = S.sb(p1, "glrA", [33, 128], F32)
                Kt = S.sb(p1, "Kt", [128, 8, 96], BF16)
                rp = S.sb(p1, "rp", [128, 6, 8, 16], F32)
                sp_t = S.sb(p1, "sp", [128, 512], F32)
                S.op("pool", lambda e: e.memset(glrA[:], 1.0), writes=[glrA])
                S.op("pool", lambda e: e.memset(Vaug[:], 1.0), writes=[Vaug])

                def gate_sp(pa_t, glr_ap, bank_tr, bank_z):
                    S.op("pe", lambda e: e.transpose(out=PB[bank_tr][0:32, 0:128], in_=glr_ap, identity=ident_f[:]),
                         reads=[pa_t, ident_f], writes=[PB[bank_tr]])
                    S.op("act", lambda e: e.copy(out=glrA[0:32, :], in_=PB[bank_tr][0:32, 0:128]), reads=[PB[bank_tr]], writes=[glrA])
                    S.op("pe", lambda e: e.matmul(PB[bank_z][:, :], lhsT=glrA[:, :], rhs=Wg[:, :], start=True, stop=True),
                         reads=[glrA, Wg], writes=[PB[bank_z]])
                    S.op("act", lambda e: e.activation(out=sp_t[:], in_=PB[bank_z][:, :], func=AF.Exp, scale=-1.0),
                         reads=[PB[bank_z]], writes=[sp_t])
                    S.op("act", lambda e: e.activation(out=sp_t[:], in_=sp_t[:], func=AF.Ln, bias=1.0), reads=[sp_t], writes=[sp_t])

                def rope(dst_t, dst1, dst2, x1, x2, src_ts, n, shp):
                    c = cosT[:, n, :]
                    sn = sinT[:, n, :]
                    if len(shp) == 3:
                        c = c.unsqueeze(1).to_broadcast(shp)
                        sn = sn.unsqueeze(1).to_broadcast(shp)
                        t = [rp[:, i, 0:shp[1], :] for i in range(4)]
                    else:
                        t = [rp[:, i, 0, :] for i in range(4)]
                    S.op("dve", lambda e: e.tensor_tensor(out=t[0], in0=x1, in1=c, op=ALU.mult), reads=src_ts + [cosT], writes=[rp])
                    S.op("dve", lambda e: e.tensor_tensor(out=t[1], in0=x2, in1=sn, op=ALU.mult), reads=src_ts + [sinT], writes=[rp])
                    S.op("dve", lambda e: e.tensor_tensor(out=t[2], in0=x2, in1=c, op=ALU.mult), reads=src_ts + [cosT], writes=[rp])
                    S.op("dve", lambda e: e.tensor_tensor(out=t[3], in0=x1, in1=sn, op=ALU.mult), reads=src_ts + [sinT], writes=[rp])
                    S.op("dve", lambda e: e.tensor_tensor(out=dst1, in0=t[0], in1=t[1], op=ALU.subtract), reads=[rp], writes=[dst_t])
                    S.op("dve", lambda e: e.tensor_tensor(out=dst2, in0=t[2], in1=t[3], op=ALU.add), reads=[rp], writes=[dst_t])

                krr = S.sb(p1, "krr", [128, 32], BF16)
                for n in range(NTILE):
                    load_norm_T(p1, base + n * 128, g1b, xt, n1, n1T, 0)
                    for kc in range(8):
                        S.op("pe", lambda e, kc=kc: e.matmul(PB[1][:, 0:448], lhsT=n1T[:, kc, :], rhs=WinA[:, kc, :], start=(kc == 0), stop=(kc == 7)),
                             reads=[n1T, WinA], writes=[PB[1]])
                    S.op("act", lambda e: e.copy(out=pa[:], in_=PB[1][:, 0:448]), reads=[PB[1]], writes=[pa])
                    r = rstd_of(pa[:, 0:256], [pa], 256, junk_t[:, 0:256], 2)
                    S.op("dve", lambda e: e.scalar_tensor_tensor(out=cqn[:, 0:256], in0=pa[:, 0:256], scalar=r, in1=gqb[:], op0=ALU.mult, op1=ALU.mult),
                         reads=[pa, stat, gqb], writes=[cqn])
                    r = rstd_of(pa[:, 256:384], [pa], 128, junk_t[:, 0:128], 4)
                    S.op("dve", lambda e: e.scalar_tensor_tensor(out=cqn[:, 256:384], in0=pa[:, 256:384], scalar=r, in1=gkvb[:], op0=ALU.mult, op1=ALU.mult),
                         reads=[pa, stat, gkvb], writes=[cqn])
                    transpose_to(cqn, lambda k: cqn[:, k * 128:(k + 1) * 128], 3, 2, lat, lambda: lat[:], eng="dve")
                    S.op("dve", lambda e, n=n: e.tensor_copy(out=cqT[:, :, n * 128:(n + 1) * 128], in_=lat[:, 0:2, :]), reads=[lat], writes=[cqT])
                    for j in range(2):
                        S.op("pe", lambda e, j=j: e.matmul(PB[4 + j][:, :], lhsT=lat[:, 2, :], rhs=Wukv[:, j * 512:(j + 1) * 512], start=True, stop=True),
                             reads=[lat, Wukv], writes=[PB[4 + j]])
                        kv = PB[4 + j][:, :].rearrange("p (h c) -> p h c", h=4)
                        S.op("act", lambda e, j=j, kv=kv, n=n: e.copy(out=Vaug[:, n, 4 * j:4 * j + 4, 0:64], in_=kv[:, :, 64:128]),
                             reads=[PB[4 + j]], writes=[Vaug])
                        S.op("dve", lambda e, j=j, kv=kv: e.tensor_copy(out=Kt[:, 4 * j:4 * j + 4, 0:64], in_=kv[:, :, 0:64]),
                             reads=[PB[4 + j]], writes=[Kt])
                    rope(krr, krr[:, 0:16], krr[:, 16:32], pa[:, 384:400], pa[:, 400:416], [pa], n, [128, 16])
                    S.op("dve", lambda e: e.tensor_copy(out=Kt[:, :, 64:96], in_=krr[:].unsqueeze(1).to_broadcast([128, 8, 32])),
                         reads=[krr], writes=[Kt])
                    transpose_to(Kt, lambda h: Kt[:, h, :], 8, 6, KT,
                                 lambda n=n: KT[0:96, :, n * 128:(n + 1) * 128], rows=96)
                    gate_sp(pa, pa[:, 416:448], 3, 7)
                    for cc in range(4):
                        S.op("pe", lambda e, cc=cc: e.matmul(PB[3][:, 256 + cc:257 + cc], lhsT=sp_t[:, cc * 128:(cc + 1) * 128], rhs=triU[:, 0:1],
                                                             start=True, stop=True), reads=[sp_t, triU], writes=[PB[3]])
                    S.op("dve", lambda e, n=n: e.tensor_copy(out=TTs[:, :, n], in_=PB[3][:, 256:260]), reads=[PB[3]], writes=[TTs])

                if stop == "A1":
                    S.barrier()
                    return nc
                Inc = S.sb(p1, "Inc", [128, 4, NTILE], F32)
                Dc = S.sb(p1, "Dc", [128, 4, NTILE], F32)
                S.op("dve", lambda e: e.tensor_copy(out=Inc[:, :, 0:1], in_=TTs[:, :, 0:1]), reads=[TTs], writes=[Inc])
                for n in range(1, NTILE):
                    S.op("dve", lambda e, n=n: e.tensor_tensor(out=Inc[:, :, n:n + 1], in0=Inc[:, :, n - 1:n], in1=TTs[:, :, n:n + 1], op=ALU.add),
                         reads=[Inc, TTs], writes=[Inc])
                S.op("dve", lambda e: e.tensor_tensor(out=Dc[:, 0:2, :], in0=Inc[:, 0:2, :], in1=TTs[:, 0:2, :], op=ALU.subtract), reads=[Inc, TTs], writes=[Dc])
                S.op("dve", lambda e: e.tensor_tensor(out=Dc[:, 0:2, :], in0=Dc[:, 0:2, :], in1=Inc[:, 0:2, 7:8].to_broadcast([128, 2, NTILE]), op=ALU.subtract),
                     reads=[Dc, Inc], writes=[Dc])
                S.op("dve", lambda e: e.tensor_tensor(out=Dc[:, 2:4, :], in0=Inc[:, 2:4, 7:8].to_broadcast([128, 2, NTILE]), in1=Inc[:, 2:4, :], op=ALU.subtract),
                     reads=[Inc], writes=[Dc])
                for cc in range(4):
                    S.op("pe", lambda e, cc=cc: e.transpose(out=PB[3][0:16, cc * 128:(cc + 1) * 128], in_=Dc[:, cc, :], identity=ident_f[:]),
                         reads=[Dc, ident_f], writes=[PB[3]])
                S.op("act", lambda e: e.copy(out=DrowG[:], in_=PB[3][0:16, :]), reads=[PB[3]], writes=[DrowG])

                if stop == "A1b":
                    S.barrier()
                    return nc
                Qt = S.sb(p1, "Qt", [128, 8, 96], BF16)
                QTc = S.sb(p1, "QTc", [128, 8, 512], BF16)
                Eb = [S.sb(p1, f"Eb{i}", [128, 512], BF16) for i in range(2)]
                Oacc = S.sb(p1, "Oacc", [128, 4, 8, 65], F32)
                rc = S.sb(p1, "rc", [128, 4, 8], F32)
                atok = S.sb(p1, "atok", [128, 4, 512], BF16)
                sc = 96.0 ** -0.5
                for qc in range(4):
                    for i in range(4):
                        n = qc * 4 + i
                        for j in range(2):
                            for kc in range(2):
                                S.op("pe", lambda e, j=j, kc=kc, n=n: e.matmul(PB[6 + j][:, 0:384], lhsT=cqT[:, kc, n * 128:(n + 1) * 128],
                                                                                 rhs=Wuq[:, kc, j * 384:(j + 1) * 384], start=(kc == 0), stop=(kc == 1)),
                                     reads=[cqT, Wuq], writes=[PB[6 + j]])
                            qv = PB[6 + j][:, 0:384].rearrange("p (h c) -> p h c", h=4)
                            S.op("act", lambda e, j=j, qv=qv: e.copy(out=Qt[:, 4 * j:4 * j + 4, 0:64], in_=qv[:, :, 0:64]), reads=[PB[6 + j]], writes=[Qt])
                            rope(Qt, Qt[:, 4 * j:4 * j + 4, 64:80], Qt[:, 4 * j:4 * j + 4, 80:96], qv[:, :, 64:80], qv[:, :, 80:96], [PB[6 + j]], n, [128, 4, 16])
                        transpose_to(Qt, lambda h: Qt[:, h, :], 8, 0, QTc, lambda i=i: QTc[0:96, :, i * 128:(i + 1) * 128], rows=96)
                    jobsB = [(h, kt) for h in range(8) for kt in range(NTILE)]

                    def b_score(i):
                        h, kt = jobsB[i]
                        sb_ = i % 2
                        S.op("pe", lambda e: e.matmul(PB[sb_][:, :], lhsT=KT[0:96, h, kt * 128:(kt + 1) * 128], rhs=QTc[0:96, h, :],
                                                      start=True, stop=True), reads=[KT, QTc], writes=[PB[sb_]])
                        S.op("act", lambda e: e.activation(out=Eb[sb_][:], in_=PB[sb_][:, :], func=AF.Exp, scale=sc),
                             reads=[PB[sb_]], writes=[Eb[sb_]])

                    def b_pv(i):
                        h, kt = jobsB[i]
                        sb_ = i % 2
                        pv0 = 2 + 2 * (h % 2)
                        for qt in range(4):
                            bk = pv0 + qt // 2
                            c0 = (qt % 2) * 128
                            S.op("pe", lambda e, qt=qt, bk=bk, c0=c0: e.matmul(
                                PB[bk][:, c0:c0 + 65], lhsT=Eb[sb_][:, qt * 128:(qt + 1) * 128], rhs=Vaug[:, kt, h, :],
                                start=(kt == 0 and qt % 2 == 0), stop=(kt == NTILE - 1), skip_group_check=True),
                                reads=[Eb[sb_], Vaug], writes=[PB[bk]])
                        if kt == NTILE - 1:
                            for half in range(2):
                                bk = pv0 + half
                                src = PB[bk][:, 0:256].rearrange("p (a c) -> p a c", a=2)[:, :, 0:65]
                                S.op("dve", lambda e, half=half, src=src: e.tensor_copy(out=Oacc[:, 2 * half:2 * half + 2, h, :], in_=src),
                                     reads=[PB[bk]], writes=[Oacc])

                    for i in range(len(jobsB) + 1):
                        if i < len(jobsB):
                            b_score(i)
                        if i >= 1:
                            b_pv(i - 1)
                    S.op("dve", lambda e: e.reciprocal(out=rc[:], in_=Oacc[:, :, :, 64]), reads=[Oacc], writes=[rc])
                    for i in range(4):
                        S.op("dve", lambda e, i=i: e.tensor_tensor(out=atok[:, i, :].rearrange("p (h c) -> p h c", h=8), in0=Oacc[:, i, :, 0:64],
                                                                   in1=rc[:, i, :].unsqueeze(2).to_broadcast([128, 8, 64]), op=ALU.mult),
                             reads=[Oacc, rc], writes=[atok])
                        n = qc * 4 + i
                        transpose_to(atok, lambda c, i=i: atok[:, i, c * 128:(c + 1) * 128], 4, 6 + (i % 2), attnT,
                                     lambda n=n: attnT[:, :, n * 128:(n + 1) * 128], eng="dve")
                S.barrier()
            if stop == "B":
                return nc

            with ExitStack() as p2:
                qdT = S.sb(p2, "qdT", [128, 4, S_LEN], BF16)
                kdT = S.sb(p2, "kdT", [128, 4, S_LEN], BF16)
                GV = S.sb(p2, "GV", [128, NTILE, 512], BF16)
                SG = S.sb(p2, "SG", [128, NTILE, 512], BF16)
                glab = bcast_load(p2, "glab", gla_d, 128)
                with ExitStack() as p2a:
                    WinB = S.sb(p2a, "WinB", [128, 8, 1568], BF16, dma="sw")
                    Wg = S.sb(p2a, "Wg2", [33, 512], F32, dma=True)
                    S.dma("pool", lambda q: q.dma_start(out=WinB[:], in_=winB_d.rearrange("(c p) n -> p c n", p=128)), writes=[WinB], semt=WinB)
                    S.dma("sp", lambda q: q.dma_start(out=Wg[:], in_=wg_d), writes=[Wg], semt=Wg)
                    g1b = bcast_load(p2a, "g1b2", g1_d, 1024)
                    xt = S.sb(p2a, "xt2", [128, 1024], F32, dma=True)
                    n1 = S.sb(p2a, "n12", [128, 1024], BF16)
                    n1T = S.sb(p2a, "n1T2", [128, 8, 128], BF16)
                    glr = S.sb(p2a, "glr", [128, 32], F32)
                    glrA = S.sb(p2a, "glrA2", [33, 128], F32)
                    sp_t = S.sb(p2a, "sp2", [128, 512], F32)
                    Ep = S.sb(p2a, "Ep", [128, 512], F32)
                    Em = S.sb(p2a, "Em", [128, 512], F32)
                    qd = S.sb(p2a, "qd", [128, 512], BF16)
                    kd = S.sb(p2a, "kd", [128, 512], BF16)
                    DrM = S.sb(p2a, "DrM", [16, 512], F32)
                    S.op("pool", lambda e: e.memset(glrA[:], 1.0), writes=[glrA])
                    for n in range(NTILE):
                        load_norm_T(p2a, base + n * 128, g1b, xt, n1, n1T, 0)
                        for (bk, c0, w) in ((1, 0, 512), (2, 512, 512), (3, 1024, 512), (4, 1536, 32)):
                            for kc in range(8):
                                S.op("pe", lambda e, kc=kc, bk=bk, c0=c0, w=w: e.matmul(PB[bk][:, 0:w], lhsT=n1T[:, kc, :], rhs=WinB[:, kc, c0:c0 + w],
                                                                                         start=(kc == 0), stop=(kc == 7)), reads=[n1T, WinB], writes=[PB[bk]])
                        S.op("act", lambda e, n=n: e.copy(out=GV[:, n, :], in_=PB[2][:, :]), reads=[PB[2]], writes=[GV])
                        S.op("act", lambda e, n=n: e.activation(out=SG[:, n, :], in_=PB[3][:, :], func=AF.Silu), reads=[PB[3]], writes=[SG])
                        S.op("dve", lambda e: e.tensor_copy(out=glr[:], in_=PB[4][:, 0:32]), reads=[PB[4]], writes=[glr])
                        S.op("pe", lambda e: e.transpose(out=PB[5][0:32, 0:128], in_=glr[:], identity=ident_f[:]), reads=[glr, ident_f], writes=[PB[5]])
                        S.op("act", lambda e: e.copy(out=glrA[0:32, :], in_=PB[5][0:32, 0:128]), reads=[PB[5]], writes=[glrA])
                        S.op("pe", lambda e: e.matmul(PB[6][:, :], lhsT=glrA[:, :], rhs=Wg[:, :], start=True, stop=True), reads=[glrA, Wg], writes=[PB[6]])
                        S.op("act", lambda e: e.activation(out=sp_t[:], in_=PB[6][:, :], func=AF.Exp, scale=-1.0), reads=[PB[6]], writes=[sp_t])
                        S.op("act", lambda e: e.activation(out=sp_t[:], in_=sp_t[:], func=AF.Ln, bias=1.0), reads=[sp_t], writes=[sp_t])
                        S.op("dve", lambda e, n=n: e.tensor_scalar_mul(out=DrM[:], in0=DrowG[:], scalar1=eye16[:, n:n + 1]), reads=[DrowG, eye16], writes=[DrM])
                        S.op("pe", lambda e: e.matmul(PB[7][:, 0:256], lhsT=triL[:], rhs=sp_t[:, 0:256], start=True, stop=False, skip_group_check=True),
                             reads=[triL, sp_t], writes=[PB[7]])
                        S.op("pe", lambda e: e.matmul(PB[7][:, 256:512], lhsT=triU[:], rhs=sp_t[:, 256:512], start=False, stop=False, skip_group_check=True),
                             reads=[triU, sp_t], writes=[PB[7]])
                        S.op("pe", lambda e: e.matmul(PB[7][:, :], lhsT=ones16[:], rhs=DrM[:], start=False, stop=True, skip_group_check=True),
                             reads=[ones16, DrM], writes=[PB[7]])
                        S.op("act", lambda e: e.activation(out=Ep[:], in_=PB[7][:, :], func=AF.Exp, scale=-1.0 / 16), reads=[PB[7]], writes=[Ep])
                        S.op("act", lambda e: e.activation(out=Em[:], in_=PB[7][:, :], func=AF.Exp, scale=1.0 / 16), reads=[PB[7]], writes=[Em])
                        gqv = PB[1][:, 0:256].unsqueeze(1).to_broadcast([128, 2, 256])
                        gkv_ = PB[1][:, 256:512].unsqueeze(1).to_broadcast([128, 2, 256])
                        S.op("dve", lambda e, gqv=gqv: e.scalar_tensor_tensor(out=qd[:].rearrange("p (d c) -> p d c", d=2), in0=gqv, scalar=0.125,
                                                                              in1=Ep[:].rearrange("p (d c) -> p d c", d=2), op0=ALU.mult, op1=ALU.mult),
                             reads=[PB[1], Ep], writes=[qd])
                        S.op("dve", lambda e, gkv_=gkv_: e.tensor_tensor(out=kd[:].rearrange("p (d c) -> p d c", d=2), in0=gkv_,
                                                                         in1=Em[:].rearrange("p (d c) -> p d c", d=2), op=ALU.mult),
                             reads=[PB[1], Em], writes=[kd])
                        transpose_to(qd, lambda k: qd[:, k * 128:(k + 1) * 128], 4, 5, qdT, lambda n=n: qdT[:, :, n * 128:(n + 1) * 128], eng="dve")
                        transpose_to(kd, lambda k: kd[:, k * 128:(k + 1) * 128], 4, 4, kdT, lambda n=n: kdT[:, :, n * 128:(n + 1) * 128], eng="act")
                    S.barrier()
                if stop == "A2":
                    return nc

                with ExitStack() as p2c:
                    mf = S.sb(p2c, "mf", [128, 4, 512], BF16)
                    mb = S.sb(p2c, "mb", [128, 4, 512], BF16)
                    S.op("pool", lambda e: e.memset(mf[:], 1.0), writes=[mf])
                    S.op("pool", lambda e: e.memset(mb[:], 1.0), writes=[mb])
                    for r in range(4):
                        S.op("pool", lambda e, r=r: e.affine_select(out=mf[:, r, :], in_=mf[:, r, :], pattern=[[1, 512]], compare_op=ALU.is_ge, fill=0.0,
                                                                    base=-128 * r, channel_multiplier=-1), reads=[mf], writes=[mf])
                        S.op("pool", lambda e, r=r: e.affine_select(out=mb[:, r, :], in_=mb[:, r, :], pattern=[[-1, 512]], compare_op=ALU.is_ge, fill=0.0,
                                                                    base=128 * r, channel_multiplier=1), reads=[mb], writes=[mb])
                    At = [S.sb(p2c, f"At{i}", [128, 512], BF16) for i in range(3)]
                    Ogc = S.sb(p2c, "Ogc", [128, 4, 4, 128], F32)
                    sq = S.sb(p2c, "sq", [128, 16, 128], F32)
                    ssq = S.sb(p2c, "ssq", [128, 16], F32)
                    ogt = S.sb(p2c, "ogt", [128, 4, 512], BF16)
                    for tc in range(4):
                        jobsC = []
                        for h in range(4):
                            jl = [(h, 0, jt) for jt in range(0, 4 * tc + 4)] + [(h, 1, jt) for jt in range(4 * tc, NTILE)]
                            jobsC += [(h, d, jt, k == 0, k == len(jl) - 1) for k, (h, d, jt) in enumerate(jl)]

                        def c_score(i):
                            h, d, jt, isf, isl = jobsC[i]
                            sbk = i % 2
                            hp, hl = h // 2, (h % 2) * 64
                            blk = d * 2 + hp
                            S.op("pe", lambda e: e.matmul(
                                PB[sbk][:, :], lhsT=kdT[hl:hl + 64, blk, jt * 128:(jt + 1) * 128], rhs=qdT[hl:hl + 64, blk, tc * 512:(tc + 1) * 512],
                                start=True, stop=True), reads=[kdT, qdT], writes=[PB[sbk]])
                            A = At[i % 3]
                            r = jt - 4 * tc
                            if 0 <= r < 4:
                                mk = mf if d == 0 else mb
                                S.op("dve", lambda e: e.tensor_tensor(out=A[:], in0=PB[sbk][:, :], in1=mk[:, r, :], op=ALU.mult),
                                     reads=[PB[sbk], mk], writes=[A])
                            else:
                                S.op("act", lambda e: e.copy(out=A[:], in_=PB[sbk][:, :]), reads=[PB[sbk]], writes=[A])

                        def c_pv(i):
                            h, d, jt, isf, isl = jobsC[i]
                            ob = 4 + (h % 2)
                            A = At[i % 3]
                            first = isf
                            for ii in range(4):
                                tt_ = 4 * tc + ii
                                if (d == 0 and jt > tt_) or (d == 1 and jt < tt_):
                                    continue
                                S.op("pe", lambda e, ii=ii, first=first: e.matmul(
                                    PB[ob][:, ii * 128:(ii + 1) * 128], lhsT=A[:, ii * 128:(ii + 1) * 128], rhs=GV[:, jt, h * 128:(h + 1) * 128],
                                    start=first, stop=False, skip_group_check=True), reads=[A, GV], writes=[PB[ob]])
                                first = False
                            if isl:
                                S.op("act", lambda e: e.copy(out=Ogc[:, :, h, :], in_=PB[ob][:, :].rearrange("p (i c) -> p i c", i=4)),
                                     reads=[PB[ob]], writes=[Ogc])

                        for i in range(len(jobsC) + 1):
                            if i < len(jobsC):
                                c_score(i)
                            if i >= 1:
                                c_pv(i - 1)
                        og2 = Ogc[:].rearrange("p i h c -> p (i h) c")
                        S.op("dve", lambda e: e.tensor_tensor(out=sq[:], in0=og2, in1=og2, op=ALU.mult), reads=[Ogc], writes=[sq])
                        S.op("dve", lambda e: e.tensor_reduce(out=ssq[:], in_=sq[:], axis=AX.X, op=ALU.add), reads=[sq], writes=[ssq])
                        S.op("act", lambda e: e.activation(out=ssq[:], in_=ssq[:], func=AF.Sqrt, scale=1.0 / 128, bias=EPS), reads=[ssq], writes=[ssq])
                        S.op("dve", lambda e: e.reciprocal(out=ssq[:], in_=ssq[:]), reads=[ssq], writes=[ssq])
                        S.op("dve", lambda e: e.tensor_tensor(out=sq[:], in0=og2, in1=ssq[:].unsqueeze(2).to_broadcast([128, 16, 128]), op=ALU.mult),
                             reads=[Ogc, ssq], writes=[sq])
                        S.op("dve", lambda e: e.tensor_tensor(out=sq[:], in0=sq[:], in1=glab[:].unsqueeze(1).to_broadcast([128, 16, 128]), op=ALU.mult),
                             reads=[sq, glab], writes=[sq])
                        S.op("dve", lambda e, tc=tc: e.tensor_tensor(out=ogt[:].rearrange("p i c -> p (i c)"), in0=sq[:].rearrange("p a c -> p (a c)"),
                                                                     in1=SG[:, 4 * tc:4 * tc + 4, :].rearrange("p i c -> p (i c)"), op=ALU.mult),
                             reads=[sq, SG], writes=[ogt])
                        for i in range(4):
                            n = 4 * tc + i
                            transpose_to(ogt, lambda c, i=i: ogt[:, i, c * 128:(c + 1) * 128], 4, 6 + (i % 2), oglaT,
                                         lambda n=n: oglaT[:, :, n * 128:(n + 1) * 128], eng="act")
                    S.barrier()

            if stop == "C":
                return nc
            with ExitStack() as p3:
                Wo = S.sb(p3, "Wo", [128, 8, 1024], BF16, dma="sw")
                Wq = S.sb(p3, "Wq", [128, 8, 2048], BF16, dma="sw")
                SubT = S.sb(p3, "SubT", [128, 16, 128], BF16, dma="sw")
                S.dma("pool", lambda q: q.dma_start(out=Wo[:], in_=wo_d.rearrange("(c p) n -> p c n", p=128)), writes=[Wo], semt=Wo)
                S.dma("pool", lambda q: q.dma_start(out=Wq[:], in_=wq_d.rearrange("(c p) n -> p c n", p=128)), writes=[Wq], semt=Wq)
                S.dma("pool", lambda q: q.dma_start(out=SubT[:], in_=subk_d), writes=[SubT], semt=SubT)
                g2b = bcast_load(p3, "g2b", g2_d, 1024)
                gfb = bcast_load(p3, "gfb", gf_d, 1024)
                htP = [S.sb(p3, f"ht{i}", [128, 1024], F32, dma=True) for i in range(2)]
                hnP = [S.sb(p3, f"hn{i}", [128, 1024], F32) for i in range(2)]
                idxP = [S.sb(p3, f"idxu{i}", [128, 128], U32) for i in range(2)]
                gateP = [S.sb(p3, f"gate{i}", [128, 8, 16], F32) for i in range(2)]
                hnb = S.sb(p3, "hnb", [128, 1024], BF16)
                hnT = S.sb(p3, "hnT", [128, 8, 128], BF16)
                qT = S.sb(p3, "qT", [128, 16, 128], BF16)
                ssb = S.sb(p3, "ssb", [128, 16, 128], F32)
                wk = S.sb(p3, "wk", [128, 256], F32)
                m16 = S.sb(p3, "m16", [128, 16, 16], F32)
                i16 = S.sb(p3, "i16", [128, 16, 16], U32)
                i16f = S.sb(p3, "i16f", [128, 16, 16], F32)
                cand = S.sb(p3, "cand", [128, 8, 256], F32)
                tops = S.sb(p3, "tops", [128, 8, 16], F32)
                pos = S.sb(p3, "pos", [128, 8, 16], U32)
                pa_ = S.sb(p3, "posa", [128, 8, 16], U32)
                pb_ = S.sb(p3, "posb", [128, 8, 16], U32)
                paf = S.sb(p3, "paf", [128, 8, 16], F32)
                pbf_ = S.sb(p3, "pbf", [128, 8, 16], F32)
                eq = S.sb(p3, "eq", [128, 8, 16, 16], BF16)
                sel1 = S.sb(p3, "sel1", [128, 8, 16], F32)
                sel2 = S.sb(p3, "sel2", [128, 8, 16], F32)
                idxf = S.sb(p3, "idxf", [128, 128], F32)
                gsum = S.sb(p3, "gsum", [128, 8], F32)
                stat2 = S.sb(p3, "stat2", [128, 8], F32)
                junk_a = S.sb(p3, "junk_a", [128, 1024], BF16)
                junkD = [S.sb(p3, f"junkD{i}", [128, 1024], BF16) for i in range(2)]
                actv = S.sb(p3, "actv", [128, 128], F32)
                glt = S.sb(p3, "glt", [128, 128], F32)
                wct = S.sb(p3, "wct", [128, 128], F32)
                actC = [T(actv[:, i:i + 1], f"actc{i}") for i in range(128)]
                glC = [T(glt[:, i:i + 1], f"glc{i}") for i in range(128)]
                NUV = 10
                UV = [S.sb(p3, f"UV{i}", [128, 2048], BF16, dma="sw") for i in range(NUV)]
                NDG = 4
                Dg = [S.sb(p3, f"Dg{i}", [128, 128], BF16) for i in range(NDG)]

                def rstd2(src_t, col):
                    S.op("act", lambda e: e.activation(out=junk_a[:], in_=src_t[:], func=AF.Square, accum_out=stat2[:, col:col + 1]),
                         reads=[src_t], writes=[stat2, junk_a])
                    S.op("act", lambda e: e.activation(out=stat2[:, col + 1:col + 2], in_=stat2[:, col:col + 1], func=AF.Sqrt,
                                                       scale=1.0 / 1024, bias=EPS), reads=[stat2], writes=[stat2])
                    S.op("dve", lambda e: e.reciprocal(out=stat2[:, col + 1:col + 2], in_=stat2[:, col + 1:col + 2]), reads=[stat2], writes=[stat2])
                    return stat2[:, col + 1:col + 2]

                def top16(src_ap, src_t, n_el, mv, iv, mvt, ivt):
                    S.op("dve", lambda e: e.max(out=mv[:, 0:8], in_=src_ap), reads=[src_t], writes=[mvt])
                    S.op("dve", lambda e: e.max_index(out=iv[:, 0:8], in_max=mv[:, 0:8], in_values=src_ap), reads=[src_t, mvt], writes=[ivt])
                    S.op("dve", lambda e: e.match_replace(out=wk[:, 0:n_el], in_to_replace=mv[:, 0:8], in_values=src_ap, imm_value=-1e30),
                         reads=[src_t, mvt], writes=[wk])
                    S.op("dve", lambda e: e.max(out=mv[:, 8:16], in_=wk[:, 0:n_el]), reads=[wk], writes=[mvt])
                    S.op("dve", lambda e: e.max_index(out=iv[:, 8:16], in_max=mv[:, 8:16], in_values=wk[:, 0:n_el]), reads=[wk, mvt], writes=[ivt])

                def emit_d12(n):
                    par = n % 2
                    ht, hn, idxu, gate = htP[par], hnP[par], idxP[par], gateP[par]
                    r0 = base + n * 128
                    S.dma("sp", lambda q: q.dma_start(out=ht[:], in_=x_d[r0:r0 + 128, :]), writes=[ht], semt=ht)
                    for j in range(2):
                        for c in range(8):
                            src = attnT if c < 4 else oglaT
                            S.op("pe", lambda e, j=j, c=c, src=src: e.matmul(PB[j][:, :], lhsT=src[:, c % 4, n * 128:(n + 1) * 128],
                                                                             rhs=Wo[:, c, j * 512:(j + 1) * 512], start=(c == 0), stop=(c == 7)),
                                 reads=[src, Wo], writes=[PB[j]])
                        S.op("dve", lambda e, j=j: e.tensor_tensor(out=ht[:, j * 512:(j + 1) * 512], in0=PB[j][:, :], in1=ht[:, j * 512:(j + 1) * 512], op=ALU.add),
                             reads=[PB[j], ht], writes=[ht])
                    if dbg:
                        S.dma("sp", lambda q: q.dma_start(out=dbgh_d[r0:r0 + 128, :], in_=ht[:]), reads=[ht], writes=[dbgh_t], semt=dbgh_t)
                    r = rstd2(ht, 0)
                    S.op("dve", lambda e: e.scalar_tensor_tensor(out=hn[:], in0=ht[:], scalar=r, in1=g2b[:], op0=ALU.mult, op1=ALU.mult),
                         reads=[ht, stat2, g2b], writes=[hn])
                    S.op("act", lambda e: e.copy(out=hnb[:], in_=hn[:]), reads=[hn], writes=[hnb])
                    transpose_to(hnb, lambda kc: hnb[:, kc * 128:(kc + 1) * 128], 8, 2, hnT, lambda: hnT[:])
                    for g4 in range(4):
                        bk = 3 + (g4 % 2)
                        for q4 in range(4):
                            hp = g4 * 4 + q4
                            for kc in range(8):
                                S.op("pe", lambda e, hp=hp, kc=kc, bk=bk, q4=q4: e.matmul(PB[bk][:, q4 * 128:(q4 + 1) * 128], lhsT=Wq[:, kc, hp * 128:(hp + 1) * 128],
                                                                                            rhs=hnT[:, kc, :], start=(kc == 0), stop=(kc == 7)),
                                     reads=[Wq, hnT], writes=[PB[bk]])
                        S.op("act", lambda e, g4=g4, bk=bk: e.copy(out=qT[:, g4 * 4:(g4 + 1) * 4, :].rearrange("p a b -> p (a b)"), in_=PB[bk][:, :]),
                             reads=[PB[bk]], writes=[qT])
                    for g4 in range(4):
                        bk = 5 if g4 % 2 == 0 else 2
                        for q4 in range(4):
                            hp = g4 * 4 + q4
                            S.op("pe", lambda e, hp=hp, bk=bk, q4=q4: e.matmul(PB[bk][:, q4 * 128:(q4 + 1) * 128], lhsT=qT[:, hp, :], rhs=SubT[:, hp, :],
                                                                                start=True, stop=True), reads=[qT, SubT], writes=[PB[bk]])
                        S.op("act", lambda e, g4=g4, bk=bk: e.copy(out=ssb[:, g4 * 4:(g4 + 1) * 4, :].rearrange("p a b -> p (a b)"), in_=PB[bk][:, :]),
                             reads=[PB[bk]], writes=[ssb])
                    for hp in range(16):
                        top16(ssb[:, hp, :], ssb, 128, m16[:, hp, :], i16[:, hp, :], m16, i16)
                    m4 = m16[:].rearrange("p (h a) i -> p h a i", a=2)
                    S.op("dve", lambda e: e.tensor_tensor(out=cand[:].rearrange("p h (i j) -> p h i j", i=16),
                                                          in0=m4[:, :, 0, :].unsqueeze(3).to_broadcast([128, 8, 16, 16]),
                                                          in1=m4[:, :, 1, :].unsqueeze(2).to_broadcast([128, 8, 16, 16]), op=ALU.add),
                         reads=[m16], writes=[cand])
                    for h in range(8):
                        top16(cand[:, h, :], cand, 256, tops[:, h, :], pos[:, h, :], tops, pos)
                    S.op("dve", lambda e: e.tensor_single_scalar(out=pa_[:], in_=pos[:], scalar=4, op=ALU.logical_shift_right), reads=[pos], writes=[pa_])
                    S.op("dve", lambda e: e.tensor_single_scalar(out=pb_[:], in_=pos[:], scalar=15, op=ALU.bitwise_and), reads=[pos], writes=[pb_])
                    S.op("dve", lambda e: e.tensor_copy(out=paf[:], in_=pa_[:]), reads=[pa_], writes=[paf])
                    S.op("dve", lambda e: e.tensor_copy(out=pbf_[:], in_=pb_[:]), reads=[pb_], writes=[pbf_])
                    S.op("dve", lambda e: e.tensor_copy(out=i16f[:], in_=i16[:]), reads=[i16], writes=[i16f])
                    i4 = i16f[:].rearrange("p (h a) i -> p h a i", a=2)
                    iob = iota16[:].unsqueeze(1).unsqueeze(1).to_broadcast([128, 8, 16, 16])
                    for (pf, a, sel) in ((paf, 0, sel1), (pbf_, 1, sel2)):
                        S.op("dve", lambda e, pf=pf: e.tensor_tensor(out=eq[:], in0=pf[:].unsqueeze(3).to_broadcast([128, 8, 16, 16]), in1=iob, op=ALU.is_equal),
                             reads=[pf, iota16], writes=[eq])
                        S.op("dve", lambda e, a=a: e.tensor_tensor(out=eq[:], in0=eq[:], in1=i4[:, :, a, :].unsqueeze(2).to_broadcast([128, 8, 16, 16]), op=ALU.mult),
                             reads=[eq, i16f], writes=[eq])
                        S.op("dve", lambda e, sel=sel: e.tensor_reduce(out=sel[:], in_=eq[:], axis=AX.X, op=ALU.add), reads=[eq], writes=[sel])
                    S.op("dve", lambda e: e.scalar_tensor_tensor(out=idxf[:], in0=sel1[:].rearrange("p h k -> p (h k)"), scalar=128.0,
                                                                 in1=sel2[:].rearrange("p h k -> p (h k)"), op0=ALU.mult, op1=ALU.add),
                         reads=[sel1, sel2], writes=[idxf])
                    S.op("dve", lambda e: e.tensor_copy(out=idxu[:], in_=idxf[:]), reads=[idxf], writes=[idxu])
                    S.op("dve", lambda e: e.tensor_tensor(out=gate[:], in0=tops[:], in1=tops[:, :, 0:1].to_broadcast([128, 8, 16]), op=ALU.subtract),
                         reads=[tops], writes=[gate])
                    S.op("act", lambda e: e.activation(out=gate[:], in_=gate[:], func=AF.Exp), reads=[gate], writes=[gate])
                    S.op("dve", lambda e: e.tensor_reduce(out=gsum[:], in_=gate[:], axis=AX.X, op=ALU.add), reads=[gate], writes=[gsum])
                    S.op("dve", lambda e: e.reciprocal(out=gsum[:], in_=gsum[:]), reads=[gsum], writes=[gsum])
                    S.op("dve", lambda e: e.tensor_tensor(out=gate[:], in0=gate[:], in1=gsum[:].unsqueeze(2).to_broadcast([128, 8, 16]), op=ALU.mult),
                         reads=[gate, gsum], writes=[gate])

                def emit_loop(n, pending):
                    par = n % 2
                    ht, hn, idxu, gate = htP[par], hnP[par], idxP[par], gateP[par]
                    r0 = base + n * 128
                    gflat = gate[:].rearrange("p h k -> p (h k)")
                    LAG = 2

                    def emit_acc(sl):
                        U = UV[sl % NUV]
                        D = Dg[sl % NDG]
                        S.op("act", lambda e: e.activation(out=glt[:, sl:sl + 1], in_=actv[:, sl:sl + 1], func=AF.Gelu_apprx_tanh),
                             reads=[actC[sl]], writes=[glC[sl]])
                        S.op("act", lambda e: e.activation(out=wct[:, sl:sl + 1], in_=gflat[:, sl:sl + 1], func=AF.Copy, scale=glt[:, sl:sl + 1]),
                             reads=[glC[sl], gate], writes=[glC[sl]])
                        S.op("act", lambda e: e.activation(out=D[:], in_=ident_bf[:], func=AF.Copy, scale=wct[:, sl:sl + 1]),
                             reads=[ident_bf, glC[sl]], writes=[D])
                        for j in range(2):
                            S.op("pe", lambda e, j=j: e.matmul(PB[6 + j][:, :], lhsT=D[:], rhs=U[:, 1024 + j * 512:1024 + (j + 1) * 512],
                                                               start=(sl == 0), stop=(sl == 127)), reads=[D, U], writes=[PB[6 + j]])

                    for sl in range(128):
                        U = UV[sl % NUV]
                        S.dma("pool", lambda q, U=U, sl=sl: q.indirect_dma_start(out=U[:], out_offset=None, in_=uvbf_d,
                                                                                 in_offset=bass.IndirectOffsetOnAxis(ap=idxu[:, sl:sl + 1], axis=0)),
                              reads=[idxu, uv_t], writes=[U], semt=U)
                        S.op("dve", lambda e, U=U, sl=sl: e.scalar_tensor_tensor(out=junkD[sl % 2][:], in0=U[:, 0:1024], scalar=1.0, in1=hn[:], op0=ALU.mult, op1=ALU.mult,
                                                                                 accum_out=actv[:, sl:sl + 1]), reads=[U, hn], writes=[junkD[sl % 2], actC[sl]])
                        if sl >= LAG:
                            emit_acc(sl - LAG)
                        if pending:
                            S.replay(pending, 3)
                    for sl in range(128 - LAG, 128):
                        emit_acc(sl)
                    if pending:
                        S.replay(pending, len(pending))
                    for j in range(2):
                        S.op("dve", lambda e, j=j: e.tensor_tensor(out=ht[:, j * 512:(j + 1) * 512], in0=PB[6 + j][:, :], in1=ht[:, j * 512:(j + 1) * 512], op=ALU.add),
                             reads=[PB[6 + j], ht], writes=[ht])
                    r = rstd2(ht, 2)
                    S.op("dve", lambda e: e.scalar_tensor_tensor(out=hn[:], in0=ht[:], scalar=r, in1=gfb[:], op0=ALU.mult, op1=ALU.mult),
                         reads=[ht, stat2, gfb], writes=[hn])
                    S.dma("sp", lambda q: q.dma_start(out=y_d[r0:r0 + 128, :], in_=hn[:]), reads=[hn], writes=[y_t], semt=y_t)

                emit_d12(0)
                for n in range(NTILE):
                    pending = []
                    if n + 1 < NTILE:
                        S.defer = pending
                        emit_d12(n + 1)
                        S.defer = None
                    emit_loop(n, pending)
                S.barrier()
        S.barrier()
        print("ninstr", S.ninstr, "nsem", S.nsem)
    return nc


def _host_inputs(inputs):
    f = np.float32
    w_in = np.asarray(inputs["w_in"])[0]
    wA = np.ascontiguousarray(np.concatenate([w_in[:, 0:416], w_in[:, 1440:1472]], axis=1))
    wB = np.ascontiguousarray(np.concatenate([w_in[:, 416:1440], w_in[:, 1472:1984], w_in[:, 1440:1472]], axis=1))
    gw = np.zeros((33, 512), f)
    gw[0:16, 0:256] = np.asarray(inputs["gate_fwd_w"])[0]
    gw[16:32, 256:512] = np.asarray(inputs["gate_bwd_w"])[0]
    gw[32, 0:256] = np.asarray(inputs["gate_fwd_b"])[0]
    gw[32, 256:512] = np.asarray(inputs["gate_bwd_b"])[0]
    subk = np.asarray(inputs["peer_subkeys"])[0].reshape(16, 128, 128)
    subkT = np.ascontiguousarray(np.transpose(subk, (2, 0, 1)))
    half = 16
    freqs = (np.float32(10000.0) ** (-np.arange(half, dtype=f) * f(2.0) / f(32))).astype(f)
    ang = (np.arange(S_LEN, dtype=f)[:, None] * freqs[None, :]).astype(f)
    shared = {
        "w_inA": wA, "w_inB": wB,
        "norm1_g": np.ascontiguousarray(np.asarray(inputs["norm1_g"])[0]),
        "q_norm_g": np.ascontiguousarray(np.asarray(inputs["q_norm_g"])[0]),
        "w_uq": np.ascontiguousarray(np.asarray(inputs["w_uq"])[0]),
        "kv_norm_g": np.ascontiguousarray(np.asarray(inputs["kv_norm_g"])[0]),
        "w_ukv": np.ascontiguousarray(np.asarray(inputs["w_ukv"])[0]),
        "gate_w": gw,
        "gla_norm_g": np.ascontiguousarray(np.asarray(inputs["gla_norm_g"])[0]),
        "w_o": np.ascontiguousarray(np.asarray(inputs["w_o"])[0]),
        "norm2_g": np.ascontiguousarray(np.asarray(inputs["norm2_g"])[0]),
        "peer_wq": np.ascontiguousarray(np.asarray(inputs["peer_wq"])[0]),
        "subkT": subkT,
        "peer_u": np.ascontiguousarray(np.asarray(inputs["peer_u"])[0]),
        "peer_v": np.ascontiguousarray(np.asarray(inputs["peer_v"])[0]),
        "final_norm_g": np.ascontiguousarray(np.asarray(inputs["final_norm_g"])),
        "rope_cos": np.cos(ang).astype(f), "rope_sin": np.sin(ang).astype(f),
    }
    return shared


def kernel(**inputs):
    xp = np.asarray(inputs["x_prompt"], dtype=np.float32)
    xs = np.asarray(inputs["x_sample"], dtype=np.float32)
    xall = np.concatenate([xp, xs], axis=0)
    nb = xall.shape[0]
    ncore = 8
    per = nb // ncore
    shared = _host_inputs(inputs)
    nc = build_program(per, False)
    in_maps = []
    for c in range(ncore):
        m = dict(shared)
        m["x"] = np.ascontiguousarray(xall[c * per:(c + 1) * per].reshape(per * S_LEN, 1024))
        in_maps.append(m)
    res = run_bass_kernel_spmd(nc, in_maps, core_ids=list(range(ncore)))
    ys = [np.asarray(r["y"]).reshape(per, S_LEN, 1024) for r in res.results]
    yall = np.concatenate(ys, axis=0).astype(np.float32)
    return (yall[:xp.shape[0]], yall[xp.shape[0]:])
```

```python
import numpy as np
from contextlib import ExitStack
import concourse.bass as bass
import concourse.mybir as mybir
from concourse.bass_utils import run_bass_kernel_spmd

F32 = mybir.dt.float32
BF16 = mybir.dt.bfloat16
U32 = mybir.dt.uint32
AF = mybir.ActivationFunctionType
ALU = mybir.AluOpType
AX = mybir.AxisListType

NSEQ = 5
DBG = False
EPOCH = 60000
S_LEN = 2048
NTILE = 16
EPS = 1e-6


class DSem:
    def __init__(self, h):
        self.h = h
        self.cnt = 0


class T:
    def __init__(self, ap, name, dsem=None):
        self.ap = ap
        self.name = name
        self.lw = None
        self.rd = {}
        self.dsem = dsem

    def __getitem__(self, k):
        return self.ap[k]


class Sched:
    def __init__(self, nc, stack):
        self.nc = nc
        self.stack = stack
        self.eng = {"pe": nc.tensor, "act": nc.scalar, "dve": nc.vector, "pool": nc.gpsimd, "sp": nc.sync}
        self.cnt = {k: 0 for k in self.eng}
        self.sem = {}
        self.nsem = 0
        for k in self.eng:
            self.sem[k] = self._newsem(k)
        self.waited = {k: {} for k in self.eng}
        self.ninstr = 0
        self.dpool = {"sw": [], "hw": []}
        self.dall = []

    def _newsem(self, name):
        self.nsem += 1
        return self.stack.enter_context(self.nc.semaphore(f"s{self.nsem}_{name}"))

    def getd(self, kind="hw"):
        if self.dpool[kind]:
            return self.dpool[kind].pop()
        d = DSem(self._newsem("d" + kind))
        d.kind = kind
        self.dall.append(d)
        return d

    def sb(self, st, name, shape, dtype, dma=False):
        self.nalloc = getattr(self, "nalloc", 0) + 1
        name = f"{name}_{self.nalloc}"
        t = st.enter_context(self.nc.sbuf_tensor(name, shape, dtype))
        kind = "hw" if dma is True else dma
        tt = T(t, name, self.getd(kind) if dma else None)
        if dma:
            st.callback(lambda d=tt.dsem: self.dpool[d.kind].append(d))
        return tt

    def ps(self, st, name, shape, dtype):
        t = st.enter_context(self.nc.psum_tensor(name, shape, dtype))
        tt = T(t, name)
        tt.excl = True
        return tt

    def dram(self, ap, name):
        return T(ap, name, self.getd())

    def _wait(self, e, deps):
        w = self.waited[e]
        for (sem, val) in deps:
            key = id(sem)
            if w.get(key, 0) >= val:
                continue
            self.eng[e].wait_ge(sem, val)
            w[key] = val
            self.ninstr += 1

    def replay(self, lst, k):
        d, self.defer = self.defer, None
        for _ in range(min(k, len(lst))):
            kind, a = lst.pop(0)
            (self.op if kind == "op" else self.dma)(*a)
        self.defer = d

    def op(self, e, fn, reads=(), writes=()):
        if getattr(self, "defer", None) is not None:
            self.defer.append(("op", (e, fn, list(reads), list(writes))))
            return None
        ex = [t for t in reads if getattr(t, "excl", False)]
        if ex:
            reads = [t for t in reads if not getattr(t, "excl", False)]
            writes = list(writes) + ex
        deps = []
        for t in reads:
            if t.lw is not None:
                deps.append(t.lw[1:])
        strict = (e != "pe")
        for t in writes:
            if t.lw is not None and (t.lw[0] != e or strict):
                deps.append(t.lw[1:])
            for en, d in t.rd.items():
                if en != e or strict:
                    deps.append(d)
        self._wait(e, deps)
        ins = fn(self.eng[e])
        self.cnt[e] += 1
        if self.cnt[e] > EPOCH:
            self.sem[e] = self._newsem(e)
            self.cnt[e] = 1
        ins.then_inc(self.sem[e], 1)
        self.ninstr += 1
        rec = (self.sem[e], self.cnt[e])
        for t in reads:
            t.rd[e] = rec
        for t in writes:
            t.lw = (e,) + rec
            t.rd = {}
        return ins

    def dma(self, q, fn, reads=(), writes=(), semt=None):
        if getattr(self, "defer", None) is not None:
            self.defer.append(("dma", (q, fn, list(reads), list(writes), semt)))
            return None
        deps = []
        for t in reads:
            if t.lw is not None:
                deps.append(t.lw[1:])
        for t in writes:
            if t.lw is not None:
                deps.append(t.lw[1:])
            for en, d in t.rd.items():
                deps.append(d)
        self._wait(q, deps)
        ins = fn(self.eng[q])
        ds = semt.dsem
        ds.cnt += 16
        ins.then_inc(ds.h, 16)
        self.ninstr += 1
        rec = (ds.h, ds.cnt)
        for t in reads:
            t.rd[("dma", id(ds))] = rec
        for t in writes:
            t.lw = ("dma",) + rec
            t.rd = {}
        return ins

    def barrier(self):
        deps = [(self.sem[k], self.cnt[k]) for k in self.eng if self.cnt[k] > 0]
        deps += [(d.h, d.cnt) for d in self.dall if d.cnt > 0]
        for e in self.eng:
            self._wait(e, deps)


def build_program(nseq, dbg, stop=None):
    nc = bass.Bass("TRN2", target_bir_lowering=False)
    ntok = nseq * S_LEN

    def din(name, shape, dt=F32):
        return nc.dram_tensor(name, shape, dt, kind="ExternalInput").ap()

    x_d = din("x", [ntok, 1024])
    winA_d = din("w_inA", [1024, 448])
    winB_d = din("w_inB", [1024, 1568])
    g1_d = din("norm1_g", [1024])
    gq_d = din("q_norm_g", [256])
    wuq_d = din("w_uq", [256, 768])
    gkv_d = din("kv_norm_g", [128])
    wukv_d = din("w_ukv", [128, 1024])
    wg_d = din("gate_w", [33, 512])
    gla_d = din("gla_norm_g", [128])
    wo_d = din("w_o", [1024, 1024])
    g2_d = din("norm2_g", [1024])
    wq_d = din("peer_wq", [1024, 2048])
    subk_d = din("subkT", [128, 16, 128])
    u_d = din("peer_u", [16384, 1024])
    v_d = din("peer_v", [16384, 1024])
    gf_d = din("final_norm_g", [1024])
    cos_d = din("rope_cos", [S_LEN, 16])
    sin_d = din("rope_sin", [S_LEN, 16])
    y_d = nc.dram_tensor("y", [ntok, 1024], F32, kind="ExternalOutput").ap()
    uvbf_d = nc.dram_tensor("uv_bf", [16384, 2048], BF16, kind="Internal").ap()
    if dbg:
        dbgh_d = nc.dram_tensor("dbg_h", [ntok, 1024], F32, kind="ExternalOutput").ap()
        dbgi_d = nc.dram_tensor("dbg_i", [ntok, 128], U32, kind="ExternalOutput").ap()
        dbgg_d = nc.dram_tensor("dbg_g", [ntok, 128], F32, kind="ExternalOutput").ap()
        dbga_d = nc.dram_tensor("dbg_a", [ntok, 128], F32, kind="ExternalOutput").ap()

    with ExitStack() as gst:
        S = Sched(nc, gst)
        y_t = S.dram(y_d, "y")
        if dbg:
            dbgh_t = S.dram(dbgh_d, "dbgh")
            dbgi_t = S.dram(dbgi_d, "dbgi")
            dbgg_t = S.dram(dbgg_d, "dbgg")
            dbga_t = S.dram(dbga_d, "dbga")

        ident_bf = S.sb(gst, "ident_bf", [128, 128], BF16)
        ident_f = S.sb(gst, "ident_f", [128, 128], F32)
        triL = S.sb(gst, "triL", [128, 128], F32)
        triU = S.sb(gst, "triU", [128, 128], F32)
        eye16 = S.sb(gst, "eye16", [16, 16], F32)
        ones16 = S.sb(gst, "ones16", [16, 128], F32)
        iota16 = S.sb(gst, "iota16", [128, 16], F32)
        for tt in (ident_bf, ident_f):
            S.op("pool", lambda e, tt=tt: e.memset(tt[:], 0.0), writes=[tt])
            S.op("pool", lambda e, tt=tt: e.affine_select(out=tt[:], in_=tt[:], pattern=[[-1, 128]], compare_op=ALU.not_equal,
                                                          fill=1.0, base=0, channel_multiplier=1), reads=[tt], writes=[tt])
        S.op("pool", lambda e: e.memset(eye16[:], 0.0), writes=[eye16])
        S.op("pool", lambda e: e.affine_select(out=eye16[:], in_=eye16[:], pattern=[[-1, 16]], compare_op=ALU.not_equal,
                                               fill=1.0, base=0, channel_multiplier=1), reads=[eye16], writes=[eye16])
        S.op("pool", lambda e: e.memset(ones16[:], 1.0), writes=[ones16])
        S.op("pool", lambda e: e.memset(triL[:], 1.0), writes=[triL])
        S.op("pool", lambda e: e.affine_select(out=triL[:], in_=triL[:], pattern=[[1, 128]], compare_op=ALU.is_ge,
                                               fill=0.0, base=0, channel_multiplier=-1), reads=[triL], writes=[triL])
        S.op("pool", lambda e: e.memset(triU[:], 1.0), writes=[triU])
        S.op("pool", lambda e: e.affine_select(out=triU[:], in_=triU[:], pattern=[[-1, 128]], compare_op=ALU.is_ge,
                                               fill=0.0, base=0, channel_multiplier=1), reads=[triU], writes=[triU])
        S.op("pool", lambda e: e.iota(iota16[:], pattern=[[1, 16]], base=0, channel_multiplier=0,
                                      allow_small_or_imprecise_dtypes=True), writes=[iota16])
        attnT = S.sb(gst, "attnT", [128, 4, S_LEN], BF16)
        oglaT = S.sb(gst, "oglaT", [128, 4, S_LEN], BF16)
        PB = [S.ps(gst, f"pb{i}", [128, 512], F32) for i in range(8)]

        def pbf(i):
            return PB[i][:].bitcast(BF16)

        stat = S.sb(gst, "stat", [128, 8], F32)
        DrowG = S.sb(gst, "DrowG", [16, 512], F32)

        def rstd_of(src_ap, src_ts, width, junk, col, st_=None):
            stt = stat if st_ is None else st_
            jap = junk[:, 0:width]
            S.op("act", lambda e: e.activation(out=jap, in_=src_ap, func=AF.Square, accum_out=stt[:, col:col + 1]),
                 reads=src_ts, writes=[stt, junk])
            S.op("act", lambda e: e.activation(out=stt[:, col + 1:col + 2], in_=stt[:, col:col + 1], func=AF.Ln,
                                               scale=1.0 / width, bias=EPS), reads=[stt], writes=[stt])
            S.op("act", lambda e: e.activation(out=stt[:, col + 1:col + 2], in_=stt[:, col + 1:col + 2], func=AF.Exp, scale=-0.5),
                 reads=[stt], writes=[stt])
            return stt[:, col + 1:col + 2]

        def run_pipelined(tile_fn, ntile):
            lists = []
            for n in range(ntile):
                S.defer = L = []
                tile_fn(n)
                S.defer = None
                lists.append(L)
            cur = lists[0]
            S.replay(cur, len(cur) // 2)
            for n in range(1, ntile):
                nxt = lists[n]
                half = len(nxt) // 2
                k = 0
                while cur or k < half:
                    if cur:
                        S.replay(cur, 1)
                    if k < half:
                        S.replay(nxt, 1)
                        k += 1
                cur = nxt
            S.replay(cur, len(cur))

        junk_t = S.sb(gst, "junk", [128, 1024], BF16)

        def load_norm_T(row0, gb, xt, n1, n1T, bank, st_, jk):
            S.dma("sp", lambda q: q.dma_start(out=xt[:], in_=x_d[row0:row0 + 128, :]), writes=[xt], semt=xt)
            r = rstd_of(xt[:], [xt], 1024, jk, 0, st_)
            S.op("dve", lambda e: e.scalar_tensor_tensor(out=n1[:], in0=xt[:], scalar=r, in1=gb[:], op0=ALU.mult, op1=ALU.mult),
                 reads=[xt, st_, gb], writes=[n1])
            transpose_to(n1, lambda kc: n1[:, kc * 128:(kc + 1) * 128], 8, bank, n1T, lambda: n1T[:])

        def transpose_to(src_t, src_fn, nblk, bank, dst_t, dst_fn, rows=128, eng="act"):
            pv = pbf(bank)
            for k in range(nblk):
                S.op("pe", lambda e, k=k: e.transpose(out=pv[0:rows, k * 128:(k + 1) * 128], in_=src_fn(k), identity=ident_bf[:]),
                     reads=[src_t, ident_bf], writes=[PB[bank]])
            srcv = pv[0:rows, 0:nblk * 128].rearrange("p (a b) -> p a b", a=nblk)
            if eng == "act":
                S.op("act", lambda e: e.copy(out=dst_fn(), in_=srcv), reads=[PB[bank]], writes=[dst_t])
            else:
                S.op("dve", lambda e: e.tensor_copy(out=dst_fn(), in_=srcv), reads=[PB[bank]], writes=[dst_t])

        def bcast_load(st, name, vec_d, n):
            t = S.sb(st, name, [128, n], F32, dma=True)
            S.dma("sp", lambda q: q.dma_start(out=t[:], in_=vec_d.partition_broadcast(128)), writes=[t], semt=t)
            return t

        uv_t = S.dram(uvbf_d, "uvbf")
        with ExitStack() as pp:
            cb = [S.sb(pp, f"cb{i}", [128, 4, 1024], BF16, dma="sw") for i in range(3)]
            k = 0
            for (src, c0) in ((u_d, 0), (v_d, 1024)):
                for ch in range(32):
                    b = cb[k % 3]
                    k += 1
                    S.dma("pool", lambda q, b=b, src=src, ch=ch: q.dma_start(out=b[:], in_=src[ch * 512:(ch + 1) * 512, :].rearrange("(p r) d -> p r d", r=4)),
                          writes=[b], semt=b)
                    S.dma("sp", lambda q, b=b, c0=c0, ch=ch: q.dma_start(out=uvbf_d[ch * 512:(ch + 1) * 512, c0:c0 + 1024].rearrange("(p r) d -> p r d", r=4), in_=b[:]),
                          reads=[b], writes=[uv_t], semt=uv_t)
            S.barrier()

        for s in range(nseq):
            base = s * S_LEN
            with ExitStack() as p1:
                WinA = S.sb(p1, "WinA", [128, 8, 448], BF16, dma="sw")
                Wuq = S.sb(p1, "Wuq", [128, 2, 768], BF16, dma="sw")
                Wukv = S.sb(p1, "Wukv", [128, 1024], BF16, dma="sw")
                Wg = S.sb(p1, "Wg", [33, 512], F32, dma=True)
                S.dma("pool", lambda q: q.dma_start(out=WinA[:], in_=winA_d.rearrange("(c p) n -> p c n", p=128)), writes=[WinA], semt=WinA)
                S.dma("pool", lambda q: q.dma_start(out=Wuq[:], in_=wuq_d.rearrange("(c p) n -> p c n", p=128)), writes=[Wuq], semt=Wuq)
                S.dma("pool", lambda q: q.dma_start(out=Wukv[:], in_=wukv_d), writes=[Wukv], semt=Wukv)
                S.dma("sp", lambda q: q.dma_start(out=Wg[:], in_=wg_d), writes=[Wg], semt=Wg)
                g1b = bcast_load(p1, "g1b", g1_d, 1024)
                gqb = bcast_load(p1, "gqb", gq_d, 256)
                gkvb = bcast_load(p1, "gkvb", gkv_d, 128)
                cosT = S.sb(p1, "cosT", [128, NTILE, 16], F32, dma=True)
                sinT = S.sb(p1, "sinT", [128, NTILE, 16], F32, dma=True)
                S.dma("sp", lambda q: q.dma_start(out=cosT[:], in_=cos_d.rearrange("(n p) d -> p n d", p=128)), writes=[cosT], semt=cosT)
                S.dma("sp", lambda q: q.dma_start(out=sinT[:], in_=sin_d.rearrange("(n p) d -> p n d", p=128)), writes=[sinT], semt=sinT)

                KT = S.sb(p1, "KT", [128, 8, S_LEN], BF16)
                Vaug = S.sb(p1, "Vaug", [128, NTILE, 8, 65], BF16)
                cqT = S.sb(p1, "cqT", [128, 2, S_LEN], BF16)
                TTs = S.sb(p1, "TTs", [128, 4, NTILE], F32)
                S.op("pool", lambda e: e.memset(Vaug[:], 1.0), writes=[Vaug])
                with ExitStack() as p1a:
                    def mk(name, shape, dt, **kw):
                        return [S.sb(p1a, f"{name}{i}", shape, dt, **kw) for i in range(2)]
                    xtP = mk("xt", [128, 1024], F32, dma=True)
                    n1P = mk("n1", [128, 1024], BF16)
                    n1TP = mk("n1T", [128, 8, 128], BF16)
                    paP = mk("pa", [128, 448], F32)
                    cqnP = mk("cqn", [128, 384], BF16)
                    latP = mk("latT", [128, 3, 128], BF16)
                    glrAP = mk("glrA", [33, 128], F32)
                    KtP = mk("Kt", [128, 8, 96], BF16)
                    rpP = mk("rp", [128, 4, 16], F32)
                    spP = mk("sp", [128, 512], F32)
                    krrP = mk("krr", [128, 32], BF16)
                    stP = mk("st", [128, 8], F32)
                    jkP = mk("jk", [128, 1024], BF16)
                    for i in range(2):
                        S.op("pool", lambda e, i=i: e.memset(glrAP[i][:], 1.0), writes=[glrAP[i]])

                    def a1_tile(n):
                        p = n % 2
                        xt, n1, n1T, pa, cqn, lat, glrA, Kt, rp, sp_t, krr, st_, jk = (xtP[p], n1P[p], n1TP[p], paP[p], cqnP[p], latP[p], glrAP[p],
                                                                                      KtP[p], rpP[p], spP[p], krrP[p], stP[p], jkP[p])
                        X0, X1, X2, X3 = 4 * p, 4 * p + 1, 4 * p + 2, 4 * p + 3
                        load_norm_T(base + n * 128, g1b, xt, n1, n1T, X0, st_, jk)
                        for kc in range(8):
                            S.op("pe", lambda e, kc=kc: e.matmul(PB[X1][:, 0:448], lhsT=n1T[:, kc, :], rhs=WinA[:, kc, :], start=(kc == 0), stop=(kc == 7)),
                                 reads=[n1T, WinA], writes=[PB[X1]])
                        S.op("act", lambda e: e.copy(out=pa[:], in_=PB[X1][:, 0:448]), reads=[PB[X1]], writes=[pa])
                        r = rstd_of(pa[:, 0:256], [pa], 256, jk, 2, st_)
                        S.op("dve", lambda e: e.scalar_tensor_tensor(out=cqn[:, 0:256], in0=pa[:, 0:256], scalar=r, in1=gqb[:], op0=ALU.mult, op1=ALU.mult),
                             reads=[pa, st_, gqb], writes=[cqn])
                        r2 = rstd_of(pa[:, 256:384], [pa], 128, jk, 4, st_)
                        S.op("dve", lambda e: e.scalar_tensor_tensor(out=cqn[:, 256:384], in0=pa[:, 256:384], scalar=r2, in1=gkvb[:], op0=ALU.mult, op1=ALU.mult),
                             reads=[pa, st_, gkvb], writes=[cqn])
                        transpose_to(cqn, lambda k: cqn[:, k * 128:(k + 1) * 128], 3, X0, lat, lambda: lat[:], eng="dve")
                        S.op("dve", lambda e: e.tensor_copy(out=cqT[:, :, n * 128:(n + 1) * 128], in_=lat[:, 0:2, :]), reads=[lat], writes=[cqT])
                        for j in range(2):
                            bk = X2 + j
                            S.op("pe", lambda e, j=j, bk=bk: e.matmul(PB[bk][:, :], lhsT=lat[:, 2, :], rhs=Wukv[:, j * 512:(j + 1) * 512], start=True, stop=True),
                                 reads=[lat, Wukv], writes=[PB[bk]])
                            kv = PB[bk][:, :].rearrange("p (h c) -> p h c", h=4)
                            S.op("act", lambda e, j=j, kv=kv, bk=bk: e.copy(out=Vaug[:, n, 4 * j:4 * j + 4, 0:64], in_=kv[:, :, 64:128]),
                                 reads=[PB[bk]], writes=[Vaug])
                            S.op("dve", lambda e, j=j, kv=kv, bk=bk: e.tensor_copy(out=Kt[:, 4 * j:4 * j + 4, 0:64], in_=kv[:, :, 0:64]),
                                 reads=[PB[bk]], writes=[Kt])
                        c_ = cosT[:, n, :]
                        s_ = sinT[:, n, :]
                        x1, x2 = pa[:, 384:400], pa[:, 400:416]
                        S.op("dve", lambda e: e.tensor_tensor(out=rp[:, 0, :], in0=x1, in1=c_, op=ALU.mult), reads=[pa, cosT], writes=[rp])
                        S.op("dve", lambda e: e.tensor_tensor(out=rp[:, 1, :], in0=x2, in1=s_, op=ALU.mult), reads=[pa, sinT], writes=[rp])
                        S.op("dve", lambda e: e.tensor_tensor(out=rp[:, 2, :], in0=x2, in1=c_, op=ALU.mult), reads=[pa, cosT], writes=[rp])
                        S.op("dve", lambda e: e.tensor_tensor(out=rp[:, 3, :], in0=x1, in1=s_, op=ALU.mult), reads=[pa, sinT], writes=[rp])
                        S.op("dve", lambda e: e.tensor_tensor(out=krr[:, 0:16], in0=rp[:, 0, :], in1=rp[:, 1, :], op=ALU.subtract), reads=[rp], writes=[krr])
                        S.op("dve", lambda e: e.tensor_tensor(out=krr[:, 16:32], in0=rp[:, 2, :], in1=rp[:, 3, :], op=ALU.add), reads=[rp], writes=[krr])
                        S.op("dve", lambda e: e.tensor_copy(out=Kt[:, :, 64:96], in_=krr[:].unsqueeze(1).to_broadcast([128, 8, 32])),
                             reads=[krr], writes=[Kt])
                        transpose_to(Kt, lambda h: Kt[:, h, :], 8, X0, KT, lambda: KT[0:96, :, n * 128:(n + 1) * 128], rows=96)
                        S.op("pe", lambda e: e.transpose(out=PB[X1][0:32, 0:128], in_=pa[:, 416:448], identity=ident_f[:]),
                             reads=[pa, ident_f], writes=[PB[X1]])
                        S.op("act", lambda e: e.copy(out=glrA[0:32, :], in_=PB[X1][0:32, 0:128]), reads=[PB[X1]], writes=[glrA])
                        S.op("pe", lambda e: e.matmul(PB[X2][:, :], lhsT=glrA[:, :], rhs=Wg[:, :], start=True, stop=True),
                             reads=[glrA, Wg], writes=[PB[X2]])
                        S.op("act", lambda e: e.activation(out=sp_t[:], in_=PB[X2][:, :], func=AF.Exp, scale=-1.0), reads=[PB[X2]], writes=[sp_t])
                        S.op("act", lambda e: e.activation(out=sp_t[:], in_=sp_t[:], func=AF.Ln, bias=1.0), reads=[sp_t], writes=[sp_t])
                        for cc in range(4):
                            S.op("pe", lambda e, cc=cc: e.matmul(PB[X1][:, 256 + cc:257 + cc], lhsT=sp_t[:, cc * 128:(cc + 1) * 128], rhs=triU[:, 0:1],
                                                                 start=True, stop=True), reads=[sp_t, triU], writes=[PB[X1]])
                        S.op("dve", lambda e: e.tensor_copy(out=TTs[:, :, n], in_=PB[X1][:, 256:260]), reads=[PB[X1]], writes=[TTs])

                    run_pipelined(a1_tile, NTILE)
                    S.barrier()
                if stop == "A1":
                    S.barrier()
                    return nc
                Inc = S.sb(p1, "Inc", [128, 4, NTILE], F32)
                Dc = S.sb(p1, "Dc", [128, 4, NTILE], F32)
                S.op("dve", lambda e: e.tensor_copy(out=Inc[:, :, 0:1], in_=TTs[:, :, 0:1]), reads=[TTs], writes=[Inc])
                for n in range(1, NTILE):
                    S.op("dve", lambda e, n=n: e.tensor_tensor(out=Inc[:, :, n:n + 1], in0=Inc[:, :, n - 1:n], in1=TTs[:, :, n:n + 1], op=ALU.add),
                         reads=[Inc, TTs], writes=[Inc])
                S.op("dve", lambda e: e.tensor_tensor(out=Dc[:, 0:2, :], in0=Inc[:, 0:2, :], in1=TTs[:, 0:2, :], op=ALU.subtract), reads=[Inc, TTs], writes=[Dc])
                S.op("dve", lambda e: e.tensor_tensor(out=Dc[:, 0:2, :], in0=Dc[:, 0:2, :], in1=Inc[:, 0:2, 7:8].to_broadcast([128, 2, NTILE]), op=ALU.subtract),
                     reads=[Dc, Inc], writes=[Dc])
                S.op("dve", lambda e: e.tensor_tensor(out=Dc[:, 2:4, :], in0=Inc[:, 2:4, 7:8].to_broadcast([128, 2, NTILE]), in1=Inc[:, 2:4, :], op=ALU.subtract),
                     reads=[Inc], writes=[Dc])
                for cc in range(4):
                    S.op("pe", lambda e, cc=cc: e.transpose(out=PB[3][0:16, cc * 128:(cc + 1) * 128], in_=Dc[:, cc, :], identity=ident_f[:]),
                         reads=[Dc, ident_f], writes=[PB[3]])
                S.op("act", lambda e: e.copy(out=DrowG[:], in_=PB[3][0:16, :]), reads=[PB[3]], writes=[DrowG])

                if stop == "A1b":
                    S.barrier()
                    return nc
                Qt = S.sb(p1, "Qt", [128, 8, 96], BF16)
                rpq = S.sb(p1, "rpq", [128, 4, 4, 16], F32)
                QTc = S.sb(p1, "QTc", [128, 8, 512], BF16)
                Eb = [S.sb(p1, f"Eb{i}", [128, 512], BF16) for i in range(2)]
                Oacc = S.sb(p1, "Oacc", [128, 4, 8, 65], F32)
                rc = S.sb(p1, "rc", [128, 4, 8], F32)
                atok = S.sb(p1, "atok", [128, 4, 512], BF16)
                sc = 96.0 ** -0.5
                for qc in range(4):
                    for i in range(4):
                        n = qc * 4 + i
                        for j in range(2):
                            for kc in range(2):
                                S.op("pe", lambda e, j=j, kc=kc, n=n: e.matmul(PB[6 + j][:, 0:384], lhsT=cqT[:, kc, n * 128:(n + 1) * 128],
                                                                                 rhs=Wuq[:, kc, j * 384:(j + 1) * 384], start=(kc == 0), stop=(kc == 1)),
                                     reads=[cqT, Wuq], writes=[PB[6 + j]])
                            qv = PB[6 + j][:, 0:384].rearrange("p (h c) -> p h c", h=4)
                            S.op("act", lambda e, j=j, qv=qv: e.copy(out=Qt[:, 4 * j:4 * j + 4, 0:64], in_=qv[:, :, 0:64]), reads=[PB[6 + j]], writes=[Qt])
                            cb_ = cosT[:, n, :].unsqueeze(1).to_broadcast([128, 4, 16])
                            sb2_ = sinT[:, n, :].unsqueeze(1).to_broadcast([128, 4, 16])
                            x1, x2 = qv[:, :, 64:80], qv[:, :, 80:96]
                            bkq = PB[6 + j]
                            S.op("dve", lambda e, x1=x1, cb_=cb_: e.tensor_tensor(out=rpq[:, 0], in0=x1, in1=cb_, op=ALU.mult), reads=[bkq, cosT], writes=[rpq])
                            S.op("dve", lambda e, x2=x2, sb2_=sb2_: e.tensor_tensor(out=rpq[:, 1], in0=x2, in1=sb2_, op=ALU.mult), reads=[bkq, sinT], writes=[rpq])
                            S.op("dve", lambda e, x2=x2, cb_=cb_: e.tensor_tensor(out=rpq[:, 2], in0=x2, in1=cb_, op=ALU.mult), reads=[bkq, cosT], writes=[rpq])
                            S.op("dve", lambda e, x1=x1, sb2_=sb2_: e.tensor_tensor(out=rpq[:, 3], in0=x1, in1=sb2_, op=ALU.mult), reads=[bkq, sinT], writes=[rpq])
                            S.op("dve", lambda e, j=j: e.tensor_tensor(out=Qt[:, 4 * j:4 * j + 4, 64:80], in0=rpq[:, 0], in1=rpq[:, 1], op=ALU.subtract), reads=[rpq], writes=[Qt])
                            S.op("dve", lambda e, j=j: e.tensor_tensor(out=Qt[:, 4 * j:4 * j + 4, 80:96], in0=rpq[:, 2], in1=rpq[:, 3], op=ALU.add), reads=[rpq], writes=[Qt])
                        transpose_to(Qt, lambda h: Qt[:, h, :], 8, 0, QTc, lambda i=i: QTc[0:96, :, i * 128:(i + 1) * 128], rows=96)
                    jobsB = [(h, kt) for h in range(8) for kt in range(NTILE)]

                    def b_score(i):
                        h, kt = jobsB[i]
                        sb_ = i % 2
                        S.op("pe", lambda e: e.matmul(PB[sb_][:, :], lhsT=KT[0:96, h, kt * 128:(kt + 1) * 128], rhs=QTc[0:96, h, :],
                                                      start=True, stop=True), reads=[KT, QTc], writes=[PB[sb_]])
                        S.op("act", lambda e: e.activation(out=Eb[sb_][:], in_=PB[sb_][:, :], func=AF.Exp, scale=sc),
                             reads=[PB[sb_]], writes=[Eb[sb_]])

                    def b_pv(i):
                        h, kt = jobsB[i]
                        sb_ = i % 2
                        pv0 = 2 + 2 * (h % 2)
                        for qt in range(4):
                            bk = pv0 + qt // 2
                            c0 = (qt % 2) * 128
                            S.op("pe", lambda e, qt=qt, bk=bk, c0=c0: e.matmul(
                                PB[bk][:, c0:c0 + 65], lhsT=Eb[sb_][:, qt * 128:(qt + 1) * 128], rhs=Vaug[:, kt, h, :],
                                start=(kt == 0 and qt % 2 == 0), stop=(kt == NTILE - 1), skip_group_check=True),
                                reads=[Eb[sb_], Vaug], writes=[PB[bk]])
                        if kt == NTILE - 1:
                            for half in range(2):
                                bk = pv0 + half
                                src = PB[bk][:, 0:256].rearrange("p (a c) -> p a c", a=2)[:, :, 0:65]
                                S.op("dve", lambda e, half=half, src=src: e.tensor_copy(out=Oacc[:, 2 * half:2 * half + 2, h, :], in_=src),
                                     reads=[PB[bk]], writes=[Oacc])

                    for i in range(len(jobsB) + 1):
                        if i < len(jobsB):
                            b_score(i)
                        if i >= 1:
                            b_pv(i - 1)
                    S.op("dve", lambda e: e.reciprocal(out=rc[:], in_=Oacc[:, :, :, 64]), reads=[Oacc], writes=[rc])
                    for i in range(4):
                        S.op("dve", lambda e, i=i: e.tensor_tensor(out=atok[:, i, :].rearrange("p (h c) -> p h c", h=8), in0=Oacc[:, i, :, 0:64],
                                                                   in1=rc[:, i, :].unsqueeze(2).to_broadcast([128, 8, 64]), op=ALU.mult),
                             reads=[Oacc, rc], writes=[atok])
                        n = qc * 4 + i
                        transpose_to(atok, lambda c, i=i: atok[:, i, c * 128:(c + 1) * 128], 4, 6 + (i % 2), attnT,
                                     lambda n=n: attnT[:, :, n * 128:(n + 1) * 128], eng="dve")
                S.barrier()
            if stop == "B":
                return nc

            with ExitStack() as p2:
                qdT = S.sb(p2, "qdT", [128, 4, S_LEN], BF16)
                kdT = S.sb(p2, "kdT", [128, 4, S_LEN], BF16)
                GV = S.sb(p2, "GV", [128, NTILE, 512], BF16)
                SG = S.sb(p2, "SG", [128, NTILE, 512], BF16)
                glab = bcast_load(p2, "glab", gla_d, 128)
                with ExitStack() as p2a:
                    WinB = S.sb(p2a, "WinB", [128, 8, 1568], BF16, dma="sw")
                    Wg = S.sb(p2a, "Wg2", [33, 512], F32, dma=True)
                    S.dma("pool", lambda q: q.dma_start(out=WinB[:], in_=winB_d.rearrange("(c p) n -> p c n", p=128)), writes=[WinB], semt=WinB)
                    S.dma("sp", lambda q: q.dma_start(out=Wg[:], in_=wg_d), writes=[Wg], semt=Wg)
                    g1b = bcast_load(p2a, "g1b2", g1_d, 1024)

                    def mk(name, shape, dt, **kw):
                        return [S.sb(p2a, f"{name}{i}", shape, dt, **kw) for i in range(2)]
                    xtP = mk("xt2", [128, 1024], F32, dma=True)
                    n1P = mk("n12", [128, 1024], BF16)
                    n1TP = mk("n1T2", [128, 8, 128], BF16)
                    glrP = mk("glr", [128, 32], F32)
                    glrAP = mk("glrA2", [33, 128], F32)
                    spP = mk("sp2", [128, 512], F32)
                    EpP = mk("Ep", [128, 512], F32)
                    EmP = mk("Em", [128, 512], F32)
                    qdP = mk("qd", [128, 512], BF16)
                    kdP = mk("kd", [128, 512], BF16)
                    DrMP = mk("DrM", [16, 512], F32)
                    stP = mk("st2", [128, 8], F32)
                    jkP = mk("jk2", [128, 1024], BF16)
                    for i in range(2):
                        S.op("pool", lambda e, i=i: e.memset(glrAP[i][:], 1.0), writes=[glrAP[i]])

                    def a2_tile(n):
                        p = n % 2
                        xt, n1, n1T, glr, glrA, sp_t, Ep, Em, qd, kd, DrM, st_, jk = (xtP[p], n1P[p], n1TP[p], glrP[p], glrAP[p], spP[p], EpP[p], EmP[p],
                                                                                      qdP[p], kdP[p], DrMP[p], stP[p], jkP[p])
                        X0, X1, X2, X3 = 4 * p, 4 * p + 1, 4 * p + 2, 4 * p + 3
                        load_norm_T(base + n * 128, g1b, xt, n1, n1T, X0, st_, jk)
                        for (bk, c0, w) in ((X1, 0, 512), (X2, 512, 512), (X3, 1024, 512)):
                            for kc in range(8):
                                S.op("pe", lambda e, kc=kc, bk=bk, c0=c0, w=w: e.matmul(PB[bk][:, 0:w], lhsT=n1T[:, kc, :], rhs=WinB[:, kc, c0:c0 + w],
                                                                                         start=(kc == 0), stop=(kc == 7)), reads=[n1T, WinB], writes=[PB[bk]])
                        S.op("act", lambda e: e.copy(out=GV[:, n, :], in_=PB[X2][:, :]), reads=[PB[X2]], writes=[GV])
                        S.op("act", lambda e: e.activation(out=SG[:, n, :], in_=PB[X3][:, :], func=AF.Silu), reads=[PB[X3]], writes=[SG])
                        for kc in range(8):
                            S.op("pe", lambda e, kc=kc: e.matmul(PB[X2][:, 0:32], lhsT=n1T[:, kc, :], rhs=WinB[:, kc, 1536:1568],
                                                                 start=(kc == 0), stop=(kc == 7)), reads=[n1T, WinB], writes=[PB[X2]])
                        S.op("dve", lambda e: e.tensor_copy(out=glr[:], in_=PB[X2][:, 0:32]), reads=[PB[X2]], writes=[glr])
                        S.op("pe", lambda e: e.transpose(out=PB[X3][0:32, 0:128], in_=glr[:], identity=ident_f[:]), reads=[glr, ident_f], writes=[PB[X3]])
                        S.op("act", lambda e: e.copy(out=glrA[0:32, :], in_=PB[X3][0:32, 0:128]), reads=[PB[X3]], writes=[glrA])
                        S.op("pe", lambda e: e.matmul(PB[X2][:, :], lhsT=glrA[:, :], rhs=Wg[:, :], start=True, stop=True), reads=[glrA, Wg], writes=[PB[X2]])
                        S.op("act", lambda e: e.activation(out=sp_t[:], in_=PB[X2][:, :], func=AF.Exp, scale=-1.0), reads=[PB[X2]], writes=[sp_t])
                        S.op("act", lambda e: e.activation(out=sp_t[:], in_=sp_t[:], func=AF.Ln, bias=1.0), reads=[sp_t], writes=[sp_t])
                        S.op("dve", lambda e: e.tensor_scalar_mul(out=DrM[:], in0=DrowG[:], scalar1=eye16[:, n:n + 1]), reads=[DrowG, eye16], writes=[DrM])
                        S.op("pe", lambda e: e.matmul(PB[X3][:, 0:256], lhsT=triL[:], rhs=sp_t[:, 0:256], start=True, stop=False, skip_group_check=True),
                             reads=[triL, sp_t], writes=[PB[X3]])
                        S.op("pe", lambda e: e.matmul(PB[X3][:, 256:512], lhsT=triU[:], rhs=sp_t[:, 256:512], start=False, stop=False, skip_group_check=True),
                             reads=[triU, sp_t], writes=[PB[X3]])
                        S.op("pe", lambda e: e.matmul(PB[X3][:, :], lhsT=ones16[:], rhs=DrM[:], start=False, stop=True, skip_group_check=True),
                             reads=[ones16, DrM], writes=[PB[X3]])
                        S.op("act", lambda e: e.activation(out=Ep[:], in_=PB[X3][:, :], func=AF.Exp, scale=-1.0 / 16), reads=[PB[X3]], writes=[Ep])
                        S.op("act", lambda e: e.activation(out=Em[:], in_=PB[X3][:, :], func=AF.Exp, scale=1.0 / 16), reads=[PB[X3]], writes=[Em])
                        gqv = PB[X1][:, 0:256].unsqueeze(1).to_broadcast([128, 2, 256])
                        gkv_ = PB[X1][:, 256:512].unsqueeze(1).to_broadcast([128, 2, 256])
                        S.op("dve", lambda e: e.scalar_tensor_tensor(out=qd[:].rearrange("p (d c) -> p d c", d=2), in0=gqv, scalar=0.125,
                                                                     in1=Ep[:].rearrange("p (d c) -> p d c", d=2), op0=ALU.mult, op1=ALU.mult),
                             reads=[PB[X1], Ep], writes=[qd])
                        S.op("dve", lambda e: e.tensor_tensor(out=kd[:].rearrange("p (d c) -> p d c", d=2), in0=gkv_,
                                                              in1=Em[:].rearrange("p (d c) -> p d c", d=2), op=ALU.mult),
                             reads=[PB[X1], Em], writes=[kd])
                        transpose_to(qd, lambda k: qd[:, k * 128:(k + 1) * 128], 4, X0, qdT, lambda: qdT[:, :, n * 128:(n + 1) * 128], eng="dve")
                        transpose_to(kd, lambda k: kd[:, k * 128:(k + 1) * 128], 4, X2, kdT, lambda: kdT[:, :, n * 128:(n + 1) * 128], eng="act")

                    run_pipelined(a2_tile, NTILE)
                    S.barrier()
                if stop == "A2":
                    return nc

                with ExitStack() as p2c:
                    mf = S.sb(p2c, "mf", [128, 4, 512], BF16)
                    mb = S.sb(p2c, "mb", [128, 4, 512], BF16)
                    S.op("pool", lambda e: e.memset(mf[:], 1.0), writes=[mf])
                    S.op("pool", lambda e: e.memset(mb[:], 1.0), writes=[mb])
                    for r in range(4):
                        S.op("pool", lambda e, r=r: e.affine_select(out=mf[:, r, :], in_=mf[:, r, :], pattern=[[1, 512]], compare_op=ALU.is_ge, fill=0.0,
                                                                    base=-128 * r, channel_multiplier=-1), reads=[mf], writes=[mf])
                        S.op("pool", lambda e, r=r: e.affine_select(out=mb[:, r, :], in_=mb[:, r, :], pattern=[[-1, 512]], compare_op=ALU.is_ge, fill=0.0,
                                                                    base=128 * r, channel_multiplier=1), reads=[mb], writes=[mb])
                    At = [S.sb(p2c, f"At{i}", [128, 512], BF16) for i in range(3)]
                    Ogc = S.sb(p2c, "Ogc", [128, 4, 4, 128], F32)
                    sq = S.sb(p2c, "sq", [128, 16, 128], F32)
                    ssq = S.sb(p2c, "ssq", [128, 16], F32)
                    ogt = S.sb(p2c, "ogt", [128, 4, 512], BF16)
                    for tc in range(4):
                        jobsC = []
                        for h in range(4):
                            jl = [(h, 0, jt) for jt in range(0, 4 * tc + 4)] + [(h, 1, jt) for jt in range(4 * tc, NTILE)]
                            jobsC += [(h, d, jt, k == 0, k == len(jl) - 1) for k, (h, d, jt) in enumerate(jl)]

                        def c_score(i):
                            h, d, jt, isf, isl = jobsC[i]
                            sbk = i % 2
                            hp, hl = h // 2, (h % 2) * 64
                            blk = d * 2 + hp
                            S.op("pe", lambda e: e.matmul(
                                PB[sbk][:, :], lhsT=kdT[hl:hl + 64, blk, jt * 128:(jt + 1) * 128], rhs=qdT[hl:hl + 64, blk, tc * 512:(tc + 1) * 512],
                                start=True, stop=True), reads=[kdT, qdT], writes=[PB[sbk]])
                            A = At[i % 3]
                            r = jt - 4 * tc
                            if 0 <= r < 4:
                                mk = mf if d == 0 else mb
                                S.op("dve", lambda e: e.tensor_tensor(out=A[:], in0=PB[sbk][:, :], in1=mk[:, r, :], op=ALU.mult),
                                     reads=[PB[sbk], mk], writes=[A])
                            else:
                                S.op("act", lambda e: e.copy(out=A[:], in_=PB[sbk][:, :]), reads=[PB[sbk]], writes=[A])

                        def c_pv(i):
                            h, d, jt, isf, isl = jobsC[i]
                            ob = 4 + (h % 2)
                            A = At[i % 3]
                            first = isf
                            for ii in range(4):
                                tt_ = 4 * tc + ii
                                if (d == 0 and jt > tt_) or (d == 1 and jt < tt_):
                                    continue
                                S.op("pe", lambda e, ii=ii, first=first: e.matmul(
                                    PB[ob][:, ii * 128:(ii + 1) * 128], lhsT=A[:, ii * 128:(ii + 1) * 128], rhs=GV[:, jt, h * 128:(h + 1) * 128],
                                    start=first, stop=False, skip_group_check=True), reads=[A, GV], writes=[PB[ob]])
                                first = False
                            if isl:
                                S.op("act", lambda e: e.copy(out=Ogc[:, :, h, :], in_=PB[ob][:, :].rearrange("p (i c) -> p i c", i=4)),
                                     reads=[PB[ob]], writes=[Ogc])

                        for i in range(len(jobsC) + 1):
                            if i < len(jobsC):
                                c_score(i)
                            if i >= 1:
                                c_pv(i - 1)
                        og2 = Ogc[:].rearrange("p i h c -> p (i h) c")
                        S.op("dve", lambda e: e.tensor_tensor(out=sq[:], in0=og2, in1=og2, op=ALU.mult), reads=[Ogc], writes=[sq])
                        S.op("dve", lambda e: e.tensor_reduce(out=ssq[:], in_=sq[:], axis=AX.X, op=ALU.add), reads=[sq], writes=[ssq])
                        S.op("act", lambda e: e.activation(out=ssq[:], in_=ssq[:], func=AF.Sqrt, scale=1.0 / 128, bias=EPS), reads=[ssq], writes=[ssq])
                        S.op("dve", lambda e: e.reciprocal(out=ssq[:], in_=ssq[:]), reads=[ssq], writes=[ssq])
                        S.op("dve", lambda e: e.tensor_tensor(out=sq[:], in0=og2, in1=ssq[:].unsqueeze(2).to_broadcast([128, 16, 128]), op=ALU.mult),
                             reads=[Ogc, ssq], writes=[sq])
                        S.op("dve", lambda e: e.tensor_tensor(out=sq[:], in0=sq[:], in1=glab[:].unsqueeze(1).to_broadcast([128, 16, 128]), op=ALU.mult),
                             reads=[sq, glab], writes=[sq])
                        S.op("dve", lambda e, tc=tc: e.tensor_tensor(out=ogt[:].rearrange("p i c -> p (i c)"), in0=sq[:].rearrange("p a c -> p (a c)"),
                                                                     in1=SG[:, 4 * tc:4 * tc + 4, :].rearrange("p i c -> p (i c)"), op=ALU.mult),
                             reads=[sq, SG], writes=[ogt])
                        for i in range(4):
                            n = 4 * tc + i
                            transpose_to(ogt, lambda c, i=i: ogt[:, i, c * 128:(c + 1) * 128], 4, 6 + (i % 2), oglaT,
                                         lambda n=n: oglaT[:, :, n * 128:(n + 1) * 128], eng="act")
                    S.barrier()

            if stop == "C":
                return nc
            with ExitStack() as p3:
                Wo = S.sb(p3, "Wo", [128, 8, 1024], BF16, dma="sw")
                Wq = S.sb(p3, "Wq", [128, 8, 2048], BF16, dma="sw")
                SubT = S.sb(p3, "SubT", [128, 16, 128], BF16, dma="sw")
                S.dma("pool", lambda q: q.dma_start(out=Wo[:], in_=wo_d.rearrange("(c p) n -> p c n", p=128)), writes=[Wo], semt=Wo)
                S.dma("pool", lambda q: q.dma_start(out=Wq[:], in_=wq_d.rearrange("(c p) n -> p c n", p=128)), writes=[Wq], semt=Wq)
                S.dma("pool", lambda q: q.dma_start(out=SubT[:], in_=subk_d), writes=[SubT], semt=SubT)
                g2b = bcast_load(p3, "g2b", g2_d, 1024)
                gfb = bcast_load(p3, "gfb", gf_d, 1024)
                htP = [S.sb(p3, f"ht{i}", [128, 1024], F32, dma=True) for i in range(2)]
                hnP = [S.sb(p3, f"hn{i}", [128, 1024], F32) for i in range(2)]
                idxP = [S.sb(p3, f"idxu{i}", [128, 128], U32) for i in range(2)]
                gateP = [S.sb(p3, f"gate{i}", [128, 8, 16], F32) for i in range(2)]
                hnb = S.sb(p3, "hnb", [128, 1024], BF16)
                hnT = S.sb(p3, "hnT", [128, 8, 128], BF16)
                qT = S.sb(p3, "qT", [128, 16, 128], BF16)
                ssb = S.sb(p3, "ssb", [128, 16, 128], F32)
                wk = S.sb(p3, "wk", [128, 256], F32)
                m16 = S.sb(p3, "m16", [128, 16, 16], F32)
                i16 = S.sb(p3, "i16", [128, 16, 16], U32)
                i16f = S.sb(p3, "i16f", [128, 16, 16], F32)
                cand = S.sb(p3, "cand", [128, 8, 256], F32)
                tops = S.sb(p3, "tops", [128, 8, 16], F32)
                pos = S.sb(p3, "pos", [128, 8, 16], U32)
                pa_ = S.sb(p3, "posa", [128, 8, 16], U32)
                pb_ = S.sb(p3, "posb", [128, 8, 16], U32)
                paf = S.sb(p3, "paf", [128, 8, 16], F32)
                pbf_ = S.sb(p3, "pbf", [128, 8, 16], F32)
                eq = S.sb(p3, "eq", [128, 8, 16, 16], BF16)
                sel1 = S.sb(p3, "sel1", [128, 8, 16], F32)
                sel2 = S.sb(p3, "sel2", [128, 8, 16], F32)
                idxf = S.sb(p3, "idxf", [128, 128], F32)
                gsum = S.sb(p3, "gsum", [128, 8], F32)
                stat2 = S.sb(p3, "stat2", [128, 8], F32)
                junk_a = S.sb(p3, "junk_a", [128, 1024], BF16)
                junkD = [S.sb(p3, f"junkD{i}", [128, 1024], BF16) for i in range(2)]
                actv = S.sb(p3, "actv", [128, 128], F32)
                glt = S.sb(p3, "glt", [128, 128], F32)
                wct = S.sb(p3, "wct", [128, 128], F32)
                actC = [T(actv[:, i:i + 1], f"actc{i}") for i in range(128)]
                glC = [T(glt[:, i:i + 1], f"glc{i}") for i in range(128)]
                NUV = 10
                UV = [S.sb(p3, f"UV{i}", [128, 2048], BF16, dma="sw") for i in range(NUV)]
                NDG = 4
                Dg = [S.sb(p3, f"Dg{i}", [128, 128], BF16) for i in range(NDG)]

                def rstd2(src_t, col):
                    S.op("act", lambda e: e.activation(out=junk_a[:], in_=src_t[:], func=AF.Square, accum_out=stat2[:, col:col + 1]),
                         reads=[src_t], writes=[stat2, junk_a])
                    S.op("act", lambda e: e.activation(out=stat2[:, col + 1:col + 2], in_=stat2[:, col:col + 1], func=AF.Sqrt,
                                                       scale=1.0 / 1024, bias=EPS), reads=[stat2], writes=[stat2])
                    S.op("dve", lambda e: e.reciprocal(out=stat2[:, col + 1:col + 2], in_=stat2[:, col + 1:col + 2]), reads=[stat2], writes=[stat2])
                    return stat2[:, col + 1:col + 2]

                def top16(src_ap, src_t, n_el, mv, iv, mvt, ivt):
                    S.op("dve", lambda e: e.max(out=mv[:, 0:8], in_=src_ap), reads=[src_t], writes=[mvt])
                    S.op("dve", lambda e: e.max_index(out=iv[:, 0:8], in_max=mv[:, 0:8], in_values=src_ap), reads=[src_t, mvt], writes=[ivt])
                    S.op("dve", lambda e: e.match_replace(out=wk[:, 0:n_el], in_to_replace=mv[:, 0:8], in_values=src_ap, imm_value=-1e30),
                         reads=[src_t, mvt], writes=[wk])
                    S.op("dve", lambda e: e.max(out=mv[:, 8:16], in_=wk[:, 0:n_el]), reads=[wk], writes=[mvt])
                    S.op("dve", lambda e: e.max_index(out=iv[:, 8:16], in_max=mv[:, 8:16], in_values=wk[:, 0:n_el]), reads=[wk, mvt], writes=[ivt])

                def emit_d12(n):
                    par = n % 2
                    ht, hn, idxu, gate = htP[par], hnP[par], idxP[par], gateP[par]
                    r0 = base + n * 128
                    S.dma("sp", lambda q: q.dma_start(out=ht[:], in_=x_d[r0:r0 + 128, :]), writes=[ht], semt=ht)
                    for j in range(2):
                        for c in range(8):
                            src = attnT if c < 4 else oglaT
                            S.op("pe", lambda e, j=j, c=c, src=src: e.matmul(PB[j][:, :], lhsT=src[:, c % 4, n * 128:(n + 1) * 128],
                                                                             rhs=Wo[:, c, j * 512:(j + 1) * 512], start=(c == 0), stop=(c == 7)),
                                 reads=[src, Wo], writes=[PB[j]])
                        S.op("dve", lambda e, j=j: e.tensor_tensor(out=ht[:, j * 512:(j + 1) * 512], in0=PB[j][:, :], in1=ht[:, j * 512:(j + 1) * 512], op=ALU.add),
                             reads=[PB[j], ht], writes=[ht])
                    if dbg:
                        S.dma("sp", lambda q: q.dma_start(out=dbgh_d[r0:r0 + 128, :], in_=ht[:]), reads=[ht], writes=[dbgh_t], semt=dbgh_t)
                    r = rstd2(ht, 0)
                    S.op("dve", lambda e: e.scalar_tensor_tensor(out=hn[:], in0=ht[:], scalar=r, in1=g2b[:], op0=ALU.mult, op1=ALU.mult),
                         reads=[ht, stat2, g2b], writes=[hn])
                    S.op("act", lambda e: e.copy(out=hnb[:], in_=hn[:]), reads=[hn], writes=[hnb])
                    transpose_to(hnb, lambda kc: hnb[:, kc * 128:(kc + 1) * 128], 8, 2, hnT, lambda: hnT[:])
                    for g4 in range(4):
                        bk = 3 + (g4 % 2)
                        for q4 in range(4):
                            hp = g4 * 4 + q4
                            for kc in range(8):
                                S.op("pe", lambda e, hp=hp, kc=kc, bk=bk, q4=q4: e.matmul(PB[bk][:, q4 * 128:(q4 + 1) * 128], lhsT=Wq[:, kc, hp * 128:(hp + 1) * 128],
                                                                                            rhs=hnT[:, kc, :], start=(kc == 0), stop=(kc == 7)),
                                     reads=[Wq, hnT], writes=[PB[bk]])
                        S.op("act", lambda e, g4=g4, bk=bk: e.copy(out=qT[:, g4 * 4:(g4 + 1) * 4, :].rearrange("p a b -> p (a b)"), in_=PB[bk][:, :]),
                             reads=[PB[bk]], writes=[qT])
                    for g4 in range(4):
                        bk = 5 if g4 % 2 == 0 else 2
                        for q4 in range(4):
                            hp = g4 * 4 + q4
                            S.op("pe", lambda e, hp=hp, bk=bk, q4=q4: e.matmul(PB[bk][:, q4 * 128:(q4 + 1) * 128], lhsT=qT[:, hp, :], rhs=SubT[:, hp, :],
                                                                                start=True, stop=True), reads=[qT, SubT], writes=[PB[bk]])
                        S.op("act", lambda e, g4=g4, bk=bk: e.copy(out=ssb[:, g4 * 4:(g4 + 1) * 4, :].rearrange("p a b -> p (a b)"), in_=PB[bk][:, :]),
                             reads=[PB[bk]], writes=[ssb])
                    for hp in range(16):
                        top16(ssb[:, hp, :], ssb, 128, m16[:, hp, :], i16[:, hp, :], m16, i16)
                    m4 = m16[:].rearrange("p (h a) i -> p h a i", a=2)
                    S.op("dve", lambda e: e.tensor_tensor(out=cand[:].rearrange("p h (i j) -> p h i j", i=16),
                                                          in0=m4[:, :, 0, :].unsqueeze(3).to_broadcast([128, 8, 16, 16]),
                                                          in1=m4[:, :, 1, :].unsqueeze(2).to_broadcast([128, 8, 16, 16]), op=ALU.add),
                         reads=[m16], writes=[cand])
                    for h in range(8):
                        top16(cand[:, h, :], cand, 256, tops[:, h, :], pos[:, h, :], tops, pos)
                    S.op("dve", lambda e: e.tensor_single_scalar(out=pa_[:], in_=pos[:], scalar=4, op=ALU.logical_shift_right), reads=[pos], writes=[pa_])
                    S.op("dve", lambda e: e.tensor_single_scalar(out=pb_[:], in_=pos[:], scalar=15, op=ALU.bitwise_and), reads=[pos], writes=[pb_])
                    S.op("dve", lambda e: e.tensor_copy(out=paf[:], in_=pa_[:]), reads=[pa_], writes=[paf])
                    S.op("dve", lambda e: e.tensor_copy(out=pbf_[:], in_=pb_[:]), reads=[pb_], writes=[pbf_])
                    S.op("dve", lambda e: e.tensor_copy(out=i16f[:], in_=i16[:]), reads=[i16], writes=[i16f])
                    i4 = i16f[:].rearrange("p (h a) i -> p h a i", a=2)
                    iob = iota16[:].unsqueeze(1).unsqueeze(1).to_broadcast([128, 8, 16, 16])
                    for (pf, a, sel) in ((paf, 0, sel1), (pbf_, 1, sel2)):
                        S.op("dve", lambda e, pf=pf: e.tensor_tensor(out=eq[:], in0=pf[:].unsqueeze(3).to_broadcast([128, 8, 16, 16]), in1=iob, op=ALU.is_equal),
                             reads=[pf, iota16], writes=[eq])
                        S.op("dve", lambda e, a=a: e.tensor_tensor(out=eq[:], in0=eq[:], in1=i4[:, :, a, :].unsqueeze(2).to_broadcast([128, 8, 16, 16]), op=ALU.mult),
                             reads=[eq, i16f], writes=[eq])
                        S.op("dve", lambda e, sel=sel: e.tensor_reduce(out=sel[:], in_=eq[:], axis=AX.X, op=ALU.add), reads=[eq], writes=[sel])
                    S.op("dve", lambda e: e.scalar_tensor_tensor(out=idxf[:], in0=sel1[:].rearrange("p h k -> p (h k)"), scalar=128.0,
                                                                 in1=sel2[:].rearrange("p h k -> p (h k)"), op0=ALU.mult, op1=ALU.add),
                         reads=[sel1, sel2], writes=[idxf])
                    S.op("dve", lambda e: e.tensor_copy(out=idxu[:], in_=idxf[:]), reads=[idxf], writes=[idxu])
                    S.op("dve", lambda e: e.tensor_tensor(out=gate[:], in0=tops[:], in1=tops[:, :, 0:1].to_broadcast([128, 8, 16]), op=ALU.subtract),
                         reads=[tops], writes=[gate])
                    S.op("act", lambda e: e.activation(out=gate[:], in_=gate[:], func=AF.Exp), reads=[gate], writes=[gate])
                    S.op("dve", lambda e: e.tensor_reduce(out=gsum[:], in_=gate[:], axis=AX.X, op=ALU.add), reads=[gate], writes=[gsum])
                    S.op("dve", lambda e: e.reciprocal(out=gsum[:], in_=gsum[:]), reads=[gsum], writes=[gsum])
                    S.op("dve", lambda e: e.tensor_tensor(out=gate[:], in0=gate[:], in1=gsum[:].unsqueeze(2).to_broadcast([128, 8, 16]), op=ALU.mult),
                         reads=[gate, gsum], writes=[gate])

                def emit_loop(n, pending):
                    par = n % 2
                    ht, hn, idxu, gate = htP[par], hnP[par], idxP[par], gateP[par]
                    r0 = base + n * 128
                    gflat = gate[:].rearrange("p h k -> p (h k)")
                    LAG = 2

                    def emit_acc(sl):
                        U = UV[sl % NUV]
                        D = Dg[sl % NDG]
                        S.op("act", lambda e: e.activation(out=glt[:, sl:sl + 1], in_=actv[:, sl:sl + 1], func=AF.Gelu_apprx_tanh),
                             reads=[actC[sl]], writes=[glC[sl]])
                        S.op("act", lambda e: e.activation(out=wct[:, sl:sl + 1], in_=gflat[:, sl:sl + 1], func=AF.Copy, scale=glt[:, sl:sl + 1]),
                             reads=[glC[sl], gate], writes=[glC[sl]])
                        S.op("act", lambda e: e.activation(out=D[:], in_=ident_bf[:], func=AF.Copy, scale=wct[:, sl:sl + 1]),
                             reads=[ident_bf, glC[sl]], writes=[D])
                        for j in range(2):
                            S.op("pe", lambda e, j=j: e.matmul(PB[6 + j][:, :], lhsT=D[:], rhs=U[:, 1024 + j * 512:1024 + (j + 1) * 512],
                                                               start=(sl == 0), stop=(sl == 127)), reads=[D, U], writes=[PB[6 + j]])

                    for sl in range(128):
                        U = UV[sl % NUV]
                        S.dma("pool", lambda q, U=U, sl=sl: q.indirect_dma_start(out=U[:], out_offset=None, in_=uvbf_d,
                                                                                 in_offset=bass.IndirectOffsetOnAxis(ap=idxu[:, sl:sl + 1], axis=0)),
                              reads=[idxu, uv_t], writes=[U], semt=U)
                        S.op("dve", lambda e, U=U, sl=sl: e.scalar_tensor_tensor(out=junkD[sl % 2][:], in0=U[:, 0:1024], scalar=1.0, in1=hn[:], op0=ALU.mult, op1=ALU.mult,
                                                                                 accum_out=actv[:, sl:sl + 1]), reads=[U, hn], writes=[junkD[sl % 2], actC[sl]])
                        if sl >= LAG:
                            emit_acc(sl - LAG)
                        if pending:
                            S.replay(pending, 3)
                    for sl in range(128 - LAG, 128):
                        emit_acc(sl)
                    if pending:
                        S.replay(pending, len(pending))
                    for j in range(2):
                        S.op("dve", lambda e, j=j: e.tensor_tensor(out=ht[:, j * 512:(j + 1) * 512], in0=PB[6 + j][:, :], in1=ht[:, j * 512:(j + 1) * 512], op=ALU.add),
                             reads=[PB[6 + j], ht], writes=[ht])
                    r = rstd2(ht, 2)
                    S.op("dve", lambda e: e.scalar_tensor_tensor(out=hn[:], in0=ht[:], scalar=r, in1=gfb[:], op0=ALU.mult, op1=ALU.mult),
                         reads=[ht, stat2, gfb], writes=[hn])
                    S.dma("sp", lambda q: q.dma_start(out=y_d[r0:r0 + 128, :], in_=hn[:]), reads=[hn], writes=[y_t], semt=y_t)

                emit_d12(0)
                for n in range(NTILE):
                    pending = []
                    if n + 1 < NTILE:
                        S.defer = pending
                        emit_d12(n + 1)
                        S.defer = None
                    emit_loop(n, pending)
                S.barrier()
        S.barrier()
        print("ninstr", S.ninstr, "nsem", S.nsem)
    return nc


def _host_inputs(inputs):
    f = np.float32
    w_in = np.asarray(inputs["w_in"])[0]
    wA = np.ascontiguousarray(np.concatenate([w_in[:, 0:416], w_in[:, 1440:1472]], axis=1))
    wB = np.ascontiguousarray(np.concatenate([w_in[:, 416:1440], w_in[:, 1472:1984], w_in[:, 1440:1472]], axis=1))
    gw = np.zeros((33, 512), f)
    gw[0:16, 0:256] = np.asarray(inputs["gate_fwd_w"])[0]
    gw[16:32, 256:512] = np.asarray(inputs["gate_bwd_w"])[0]
    gw[32, 0:256] = np.asarray(inputs["gate_fwd_b"])[0]
    gw[32, 256:512] = np.asarray(inputs["gate_bwd_b"])[0]
    subk = np.asarray(inputs["peer_subkeys"])[0].reshape(16, 128, 128)
    subkT = np.ascontiguousarray(np.transpose(subk, (2, 0, 1)))
    half = 16
    freqs = (np.float32(10000.0) ** (-np.arange(half, dtype=f) * f(2.0) / f(32))).astype(f)
    ang = (np.arange(S_LEN, dtype=f)[:, None] * freqs[None, :]).astype(f)
    shared = {
        "w_inA": wA, "w_inB": wB,
        "norm1_g": np.ascontiguousarray(np.asarray(inputs["norm1_g"])[0]),
        "q_norm_g": np.ascontiguousarray(np.asarray(inputs["q_norm_g"])[0]),
        "w_uq": np.ascontiguousarray(np.asarray(inputs["w_uq"])[0]),
        "kv_norm_g": np.ascontiguousarray(np.asarray(inputs["kv_norm_g"])[0]),
        "w_ukv": np.ascontiguousarray(np.asarray(inputs["w_ukv"])[0]),
        "gate_w": gw,
        "gla_norm_g": np.ascontiguousarray(np.asarray(inputs["gla_norm_g"])[0]),
        "w_o": np.ascontiguousarray(np.asarray(inputs["w_o"])[0]),
        "norm2_g": np.ascontiguousarray(np.asarray(inputs["norm2_g"])[0]),
        "peer_wq": np.ascontiguousarray(np.asarray(inputs["peer_wq"])[0]),
        "subkT": subkT,
        "peer_u": np.ascontiguousarray(np.asarray(inputs["peer_u"])[0]),
        "peer_v": np.ascontiguousarray(np.asarray(inputs["peer_v"])[0]),
        "final_norm_g": np.ascontiguousarray(np.asarray(inputs["final_norm_g"])),
        "rope_cos": np.cos(ang).astype(f), "rope_sin": np.sin(ang).astype(f),
    }
    return shared


def kernel(**inputs):
    xp = np.asarray(inputs["x_prompt"], dtype=np.float32)
    xs = np.asarray(inputs["x_sample"], dtype=np.float32)
    xall = np.concatenate([xp, xs], axis=0)
    nb = xall.shape[0]
    ncore = 8
    per = nb // ncore
    shared = _host_inputs(inputs)
    nc = build_program(per, False)
    in_maps = []
    for c in range(ncore):
        m = dict(shared)
        m["x"] = np.ascontiguousarray(xall[c * per:(c + 1) * per].reshape(per * S_LEN, 1024))
        in_maps.append(m)
    res = run_bass_kernel_spmd(nc, in_maps, core_ids=list(range(ncore)))
    ys = [np.asarray(r["y"]).reshape(per, S_LEN, 1024) for r in res.results]
    yall = np.concatenate(ys, axis=0).astype(np.float32)
    return (yall[:xp.shape[0]], yall[xp.shape[0]:])
```

```python
import numpy as np
from contextlib import ExitStack
import concourse.bass as bass
import concourse.mybir as mybir
from concourse.bass_utils import run_bass_kernel_spmd

F32 = mybir.dt.float32
BF16 = mybir.dt.bfloat16
U32 = mybir.dt.uint32
AF = mybir.ActivationFunctionType
ALU = mybir.AluOpType
AX = mybir.AxisListType

NSEQ = 5
DBG = False
EPOCH = 60000
S_LEN = 2048
NTILE = 16
EPS = 1e-6


class DSem:
    def __init__(self, h):
        self.h = h
        self.cnt = 0


class T:
    def __init__(self, ap, name, dsem=None):
        self.ap = ap
        self.name = name
        self.lw = None
        self.rd = {}
        self.dsem = dsem

    def __getitem__(self, k):
        return self.ap[k]


class Sched:
    def __init__(self, nc, stack):
        self.nc = nc
        self.stack = stack
        self.eng = {"pe": nc.tensor, "act": nc.scalar, "dve": nc.vector, "pool": nc.gpsimd, "sp": nc.sync}
        self.cnt = {k: 0 for k in self.eng}
        self.sem = {}
        self.nsem = 0
        for k in self.eng:
            self.sem[k] = self._newsem(k)
        self.waited = {k: {} for k in self.eng}
        self.ninstr = 0
        self.dpool = {"sw": [], "hw": []}
        self.dall = []

    def _newsem(self, name):
        self.nsem += 1
        return self.stack.enter_context(self.nc.semaphore(f"s{self.nsem}_{name}"))

    def getd(self, kind="hw"):
        if self.dpool[kind]:
            return self.dpool[kind].pop()
        d = DSem(self._newsem("d" + kind))
        d.kind = kind
        self.dall.append(d)
        return d

    def sb(self, st, name, shape, dtype, dma=False):
        self.nalloc = getattr(self, "nalloc", 0) + 1
        name = f"{name}_{self.nalloc}"
        t = st.enter_context(self.nc.sbuf_tensor(name, shape, dtype))
        kind = "hw" if dma is True else dma
        tt = T(t, name, self.getd(kind) if dma else None)
        if dma:
            st.callback(lambda d=tt.dsem: self.dpool[d.kind].append(d))
        return tt

    def ps(self, st, name, shape, dtype):
        t = st.enter_context(self.nc.psum_tensor(name, shape, dtype))
        tt = T(t, name)
        tt.excl = True
        return tt

    def dram(self, ap, name):
        return T(ap, name, self.getd())

    def _wait(self, e, deps):
        w = self.waited[e]
        for (sem, val) in deps:
            key = id(sem)
            if w.get(key, 0) >= val:
                continue
            self.eng[e].wait_ge(sem, val)
            w[key] = val
            self.ninstr += 1

    def replay(self, lst, k):
        d, self.defer = self.defer, None
        for _ in range(min(k, len(lst))):
            kind, a = lst.pop(0)
            (self.op if kind == "op" else self.dma)(*a)
        self.defer = d

    def op(self, e, fn, reads=(), writes=()):
        if getattr(self, "defer", None) is not None:
            self.defer.append(("op", (e, fn, list(reads), list(writes))))
            return None
        ex = [t for t in reads if getattr(t, "excl", False)]
        if ex:
            reads = [t for t in reads if not getattr(t, "excl", False)]
            writes = list(writes) + ex
        deps = []
        for t in reads:
            if t.lw is not None:
                deps.append(t.lw[1:])
        strict = (e != "pe")
        for t in writes:
            if t.lw is not None and (t.lw[0] != e or strict):
                deps.append(t.lw[1:])
            for en, d in t.rd.items():
                if en != e or strict:
                    deps.append(d)
        self._wait(e, deps)
        ins = fn(self.eng[e])
        self.cnt[e] += 1
        if self.cnt[e] > EPOCH:
            self.sem[e] = self._newsem(e)
            self.cnt[e] = 1
        ins.then_inc(self.sem[e], 1)
        self.ninstr += 1
        rec = (self.sem[e], self.cnt[e])
        for t in reads:
            t.rd[e] = rec
        for t in writes:
            t.lw = (e,) + rec
            t.rd = {}
        return ins

    def dma(self, q, fn, reads=(), writes=(), semt=None):
        if getattr(self, "defer", None) is not None:
            self.defer.append(("dma", (q, fn, list(reads), list(writes), semt)))
            return None
        deps = []
        for t in reads:
            if t.lw is not None:
                deps.append(t.lw[1:])
        for t in writes:
            if t.lw is not None:
                deps.append(t.lw[1:])
            for en, d in t.rd.items():
                deps.append(d)
        self._wait(q, deps)
        ins = fn(self.eng[q])
        ds = semt.dsem
        ds.cnt += 16
        ins.then_inc(ds.h, 16)
        self.ninstr += 1
        rec = (ds.h, ds.cnt)
        for t in reads:
            t.rd[("dma", id(ds))] = rec
        for t in writes:
            t.lw = ("dma",) + rec
            t.rd = {}
        return ins

    def barrier(self):
        deps = [(self.sem[k], self.cnt[k]) for k in self.eng if self.cnt[k] > 0]
        deps += [(d.h, d.cnt) for d in self.dall if d.cnt > 0]
        for e in self.eng:
            self._wait(e, deps)


def build_program(nseq, dbg, stop=None):
    nc = bass.Bass("TRN2", target_bir_lowering=False)
    ntok = nseq * S_LEN

    def din(name, shape, dt=F32):
        return nc.dram_tensor(name, shape, dt, kind="ExternalInput").ap()

    x_d = din("x", [ntok, 1024])
    winA_d = din("w_inA", [1024, 448])
    winB_d = din("w_inB", [1024, 1568])
    g1_d = din("norm1_g", [1024])
    gq_d = din("q_norm_g", [256])
    wuq_d = din("w_uq", [256, 768])
    gkv_d = din("kv_norm_g", [128])
    wukv_d = din("w_ukv", [128, 1024])
    wg_d = din("gate_w", [33, 512])
    gla_d = din("gla_norm_g", [128])
    wo_d = din("w_o", [1024, 1024])
    g2_d = din("norm2_g", [1024])
    wq_d = din("peer_wq", [1024, 2048])
    subk_d = din("subkT", [128, 16, 128])
    u_d = din("peer_u", [16384, 1024])
    v_d = din("peer_v", [16384, 1024])
    gf_d = din("final_norm_g", [1024])
    cos_d = din("rope_cos", [S_LEN, 16])
    sin_d = din("rope_sin", [S_LEN, 16])
    y_d = nc.dram_tensor("y", [ntok, 1024], F32, kind="ExternalOutput").ap()
    uvbf_d = nc.dram_tensor("uv_bf", [16384, 2048], BF16, kind="Internal").ap()
    if dbg:
        dbgh_d = nc.dram_tensor("dbg_h", [ntok, 1024], F32, kind="ExternalOutput").ap()
        dbgi_d = nc.dram_tensor("dbg_i", [ntok, 128], U32, kind="ExternalOutput").ap()
        dbgg_d = nc.dram_tensor("dbg_g", [ntok, 128], F32, kind="ExternalOutput").ap()
        dbga_d = nc.dram_tensor("dbg_a", [ntok, 128], F32, kind="ExternalOutput").ap()

    with ExitStack() as gst:
        S = Sched(nc, gst)
        y_t = S.dram(y_d, "y")
        if dbg:
            dbgh_t = S.dram(dbgh_d, "dbgh")
            dbgi_t = S.dram(dbgi_d, "dbgi")
            dbgg_t = S.dram(dbgg_d, "dbgg")
            dbga_t = S.dram(dbga_d, "dbga")

        ident_bf = S.sb(gst, "ident_bf", [128, 128], BF16)
        ident_f = S.sb(gst, "ident_f", [128, 128], F32)
        triL = S.sb(gst, "triL", [128, 128], F32)
        triU = S.sb(gst, "triU", [128, 128], F32)
        eye16 = S.sb(gst, "eye16", [16, 16], F32)
        ones16 = S.sb(gst, "ones16", [16, 128], F32)
        iota16 = S.sb(gst, "iota16", [128, 16], F32)
        for tt in (ident_bf, ident_f):
            S.op("pool", lambda e, tt=tt: e.memset(tt[:], 0.0), writes=[tt])
            S.op("pool", lambda e, tt=tt: e.affine_select(out=tt[:], in_=tt[:], pattern=[[-1, 128]], compare_op=ALU.not_equal,
                                                          fill=1.0, base=0, channel_multiplier=1), reads=[tt], writes=[tt])
        S.op("pool", lambda e: e.memset(eye16[:], 0.0), writes=[eye16])
        S.op("pool", lambda e: e.affine_select(out=eye16[:], in_=eye16[:], pattern=[[-1, 16]], compare_op=ALU.not_equal,
                                               fill=1.0, base=0, channel_multiplier=1), reads=[eye16], writes=[eye16])
        S.op("pool", lambda e: e.memset(ones16[:], 1.0), writes=[ones16])
        S.op("pool", lambda e: e.memset(triL[:], 1.0), writes=[triL])
        S.op("pool", lambda e: e.affine_select(out=triL[:], in_=triL[:], pattern=[[1, 128]], compare_op=ALU.is_ge,
                                               fill=0.0, base=0, channel_multiplier=-1), reads=[triL], writes=[triL])
        S.op("pool", lambda e: e.memset(triU[:], 1.0), writes=[triU])
        S.op("pool", lambda e: e.affine_select(out=triU[:], in_=triU[:], pattern=[[-1, 128]], compare_op=ALU.is_ge,
                                               fill=0.0, base=0, channel_multiplier=1), reads=[triU], writes=[triU])
        S.op("pool", lambda e: e.iota(iota16[:], pattern=[[1, 16]], base=0, channel_multiplier=0,
                                      allow_small_or_imprecise_dtypes=True), writes=[iota16])
        attnT = S.sb(gst, "attnT", [128, 4, S_LEN], BF16)
        oglaT = S.sb(gst, "oglaT", [128, 4, S_LEN], BF16)
        PB = [S.ps(gst, f"pb{i}", [128, 512], F32) for i in range(8)]

        def pbf(i):
            return PB[i][:].bitcast(BF16)

        stat = S.sb(gst, "stat", [128, 8], F32)
        DrowG = S.sb(gst, "DrowG", [16, 512], F32)

        def rstd_of(src_ap, src_ts, width, junk, col, st_=None):
            stt = stat if st_ is None else st_
            jap = junk[:, 0:width]
            S.op("act", lambda e: e.activation(out=jap, in_=src_ap, func=AF.Square, accum_out=stt[:, col:col + 1]),
                 reads=src_ts, writes=[stt, junk])
            S.op("act", lambda e: e.activation(out=stt[:, col + 1:col + 2], in_=stt[:, col:col + 1], func=AF.Ln,
                                               scale=1.0 / width, bias=EPS), reads=[stt], writes=[stt])
            S.op("act", lambda e: e.activation(out=stt[:, col + 1:col + 2], in_=stt[:, col + 1:col + 2], func=AF.Exp, scale=-0.5),
                 reads=[stt], writes=[stt])
            return stt[:, col + 1:col + 2]

        def run_pipelined(tile_fn, ntile):
            lists = []
            for n in range(ntile):
                S.defer = L = []
                tile_fn(n)
                S.defer = None
                lists.append(L)
            cur = lists[0]
            S.replay(cur, len(cur) // 2)
            for n in range(1, ntile):
                nxt = lists[n]
                half = len(nxt) // 2
                k = 0
                while cur or k < half:
                    if cur:
                        S.replay(cur, 1)
                    if k < half:
                        S.replay(nxt, 1)
                        k += 1
                cur = nxt
            S.replay(cur, len(cur))

        junk_t = S.sb(gst, "junk", [128, 1024], BF16)

        def load_norm_T(row0, gb, xt, n1, n1T, bank, st_, jk):
            S.dma("sp", lambda q: q.dma_start(out=xt[:], in_=x_d[row0:row0 + 128, :]), writes=[xt], semt=xt)
            r = rstd_of(xt[:], [xt], 1024, jk, 0, st_)
            S.op("dve", lambda e: e.scalar_tensor_tensor(out=n1[:], in0=xt[:], scalar=r, in1=gb[:], op0=ALU.mult, op1=ALU.mult),
                 reads=[xt, st_, gb], writes=[n1])
            transpose_to(n1, lambda kc: n1[:, kc * 128:(kc + 1) * 128], 8, bank, n1T, lambda: n1T[:])

        def transpose_to(src_t, src_fn, nblk, bank, dst_t, dst_fn, rows=128, eng="act"):
            pv = pbf(bank)
            for k in range(nblk):
                S.op("pe", lambda e, k=k: e.transpose(out=pv[0:rows, k * 128:(k + 1) * 128], in_=src_fn(k), identity=ident_bf[:]),
                     reads=[src_t, ident_bf], writes=[PB[bank]])
            srcv = pv[0:rows, 0:nblk * 128].rearrange("p (a b) -> p a b", a=nblk)
            if eng == "act":
                S.op("act", lambda e: e.copy(out=dst_fn(), in_=srcv), reads=[PB[bank]], writes=[dst_t])
            else:
                S.op("dve", lambda e: e.tensor_copy(out=dst_fn(), in_=srcv), reads=[PB[bank]], writes=[dst_t])

        def bcast_load(st, name, vec_d, n):
            t = S.sb(st, name, [128, n], F32, dma=True)
            S.dma("sp", lambda q: q.dma_start(out=t[:], in_=vec_d.partition_broadcast(128)), writes=[t], semt=t)
            return t

        uv_t = S.dram(uvbf_d, "uvbf")
        with ExitStack() as pp:
            cb = [S.sb(pp, f"cb{i}", [128, 4, 1024], BF16, dma="sw") for i in range(3)]
            k = 0
            for (src, c0) in ((u_d, 0), (v_d, 1024)):
                for ch in range(32):
                    b = cb[k % 3]
                    k += 1
                    S.dma("pool", lambda q, b=b, src=src, ch=ch: q.dma_start(out=b[:], in_=src[ch * 512:(ch + 1) * 512, :].rearrange("(p r) d -> p r d", r=4)),
                          writes=[b], semt=b)
                    S.dma("sp", lambda q, b=b, c0=c0, ch=ch: q.dma_start(out=uvbf_d[ch * 512:(ch + 1) * 512, c0:c0 + 1024].rearrange("(p r) d -> p r d", r=4), in_=b[:]),
                          reads=[b], writes=[uv_t], semt=uv_t)
            S.barrier()

        for s in range(nseq):
            base = s * S_LEN
            with ExitStack() as p1:
                WinA = S.sb(p1, "WinA", [128, 8, 448], BF16, dma="sw")
                Wuq = S.sb(p1, "Wuq", [128, 2, 768], BF16, dma="sw")
                Wukv = S.sb(p1, "Wukv", [128, 1024], BF16, dma="sw")
                Wg = S.sb(p1, "Wg", [33, 512], F32, dma=True)
                S.dma("pool", lambda q: q.dma_start(out=WinA[:], in_=winA_d.rearrange("(c p) n -> p c n", p=128)), writes=[WinA], semt=WinA)
                S.dma("pool", lambda q: q.dma_start(out=Wuq[:], in_=wuq_d.rearrange("(c p) n -> p c n", p=128)), writes=[Wuq], semt=Wuq)
                S.dma("pool", lambda q: q.dma_start(out=Wukv[:], in_=wukv_d), writes=[Wukv], semt=Wukv)
                S.dma("sp", lambda q: q.dma_start(out=Wg[:], in_=wg_d), writes=[Wg], semt=Wg)
                g1b = bcast_load(p1, "g1b", g1_d, 1024)
                gqb = bcast_load(p1, "gqb", gq_d, 256)
                gkvb = bcast_load(p1, "gkvb", gkv_d, 128)
                cosT = S.sb(p1, "cosT", [128, NTILE, 16], F32, dma=True)
                sinT = S.sb(p1, "sinT", [128, NTILE, 16], F32, dma=True)
                S.dma("sp", lambda q: q.dma_start(out=cosT[:], in_=cos_d.rearrange("(n p) d -> p n d", p=128)), writes=[cosT], semt=cosT)
                S.dma("sp", lambda q: q.dma_start(out=sinT[:], in_=sin_d.rearrange("(n p) d -> p n d", p=128)), writes=[sinT], semt=sinT)

                KT = S.sb(p1, "KT", [128, 8, S_LEN], BF16)
                Vaug = S.sb(p1, "Vaug", [128, NTILE, 8, 65], BF16)
                cqT = S.sb(p1, "cqT", [128, 2, S_LEN], BF16)
                TTs = S.sb(p1, "TTs", [128, 4, NTILE], F32)
                S.op("pool", lambda e: e.memset(Vaug[:], 1.0), writes=[Vaug])
                with ExitStack() as p1a:
                    def mk(name, shape, dt, **kw):
                        return [S.sb(p1a, f"{name}{i}", shape, dt, **kw) for i in range(2)]
                    xtP = mk("xt", [128, 1024], F32, dma=True)
                    n1P = mk("n1", [128, 1024], BF16)
                    n1TP = mk("n1T", [128, 8, 128], BF16)
                    paP = mk("pa", [128, 448], F32)
                    cqnP = mk("cqn", [128, 384], BF16)
                    latP = mk("latT", [128, 3, 128], BF16)
                    glrAP = mk("glrA", [33, 128], F32)
                    KtP = mk("Kt", [128, 8, 96], BF16)
                    rpP = mk("rp", [128, 4, 16], F32)
                    spP = mk("sp", [128, 512], F32)
                    krrP = mk("krr", [128, 32], BF16)
                    stP = mk("st", [128, 8], F32)
                    jkP = mk("jk", [128, 1024], BF16)
                    for i in range(2):
                        S.op("pool", lambda e, i=i: e.memset(glrAP[i][:], 1.0), writes=[glrAP[i]])

                    def a1_tile(n):
                        p = n % 2
                        xt, n1, n1T, pa, cqn, lat, glrA, Kt, rp, sp_t, krr, st_, jk = (xtP[p], n1P[p], n1TP[p], paP[p], cqnP[p], latP[p], glrAP[p],
                                                                                      KtP[p], rpP[p], spP[p], krrP[p], stP[p], jkP[p])
                        X0, X1, X2, X3 = 4 * p, 4 * p + 1, 4 * p + 2, 4 * p + 3
                        load_norm_T(base + n * 128, g1b, xt, n1, n1T, X0, st_, jk)
                        for kc in range(8):
                            S.op("pe", lambda e, kc=kc: e.matmul(PB[X1][:, 0:448], lhsT=n1T[:, kc, :], rhs=WinA[:, kc, :], start=(kc == 0), stop=(kc == 7)),
                                 reads=[n1T, WinA], writes=[PB[X1]])
                        S.op("act", lambda e: e.copy(out=pa[:], in_=PB[X1][:, 0:448]), reads=[PB[X1]], writes=[pa])
                        r = rstd_of(pa[:, 0:256], [pa], 256, jk, 2, st_)
                        S.op("dve", lambda e: e.scalar_tensor_tensor(out=cqn[:, 0:256], in0=pa[:, 0:256], scalar=r, in1=gqb[:], op0=ALU.mult, op1=ALU.mult),
                             reads=[pa, st_, gqb], writes=[cqn])
                        r2 = rstd_of(pa[:, 256:384], [pa], 128, jk, 4, st_)
                        S.op("dve", lambda e: e.scalar_tensor_tensor(out=cqn[:, 256:384], in0=pa[:, 256:384], scalar=r2, in1=gkvb[:], op0=ALU.mult, op1=ALU.mult),
                             reads=[pa, st_, gkvb], writes=[cqn])
                        transpose_to(cqn, lambda k: cqn[:, k * 128:(k + 1) * 128], 3, X0, lat, lambda: lat[:], eng="dve")
                        S.op("dve", lambda e: e.tensor_copy(out=cqT[:, :, n * 128:(n + 1) * 128], in_=lat[:, 0:2, :]), reads=[lat], writes=[cqT])
                        for j in range(2):
                            bk = X2 + j
                            S.op("pe", lambda e, j=j, bk=bk: e.matmul(PB[bk][:, :], lhsT=lat[:, 2, :], rhs=Wukv[:, j * 512:(j + 1) * 512], start=True, stop=True),
                                 reads=[lat, Wukv], writes=[PB[bk]])
                            kv = PB[bk][:, :].rearrange("p (h c) -> p h c", h=4)
                            S.op("act", lambda e, j=j, kv=kv, bk=bk: e.copy(out=Vaug[:, n, 4 * j:4 * j + 4, 0:64], in_=kv[:, :, 64:128]),
                                 reads=[PB[bk]], writes=[Vaug])
                            S.op("dve", lambda e, j=j, kv=kv, bk=bk: e.tensor_copy(out=Kt[:, 4 * j:4 * j + 4, 0:64], in_=kv[:, :, 0:64]),
                                 reads=[PB[bk]], writes=[Kt])
                        c_ = cosT[:, n, :]
                        s_ = sinT[:, n, :]
                        x1, x2 = pa[:, 384:400], pa[:, 400:416]
                        S.op("dve", lambda e: e.tensor_tensor(out=rp[:, 0, :], in0=x1, in1=c_, op=ALU.mult), reads=[pa, cosT], writes=[rp])
                        S.op("dve", lambda e: e.tensor_tensor(out=rp[:, 1, :], in0=x2, in1=s_, op=ALU.mult), reads=[pa, sinT], writes=[rp])
                        S.op("dve", lambda e: e.tensor_tensor(out=rp[:, 2, :], in0=x2, in1=c_, op=ALU.mult), reads=[pa, cosT], writes=[rp])
                        S.op("dve", lambda e: e.tensor_tensor(out=rp[:, 3, :], in0=x1, in1=s_, op=ALU.mult), reads=[pa, sinT], writes=[rp])
                        S.op("dve", lambda e: e.tensor_tensor(out=krr[:, 0:16], in0=rp[:, 0, :], in1=rp[:, 1, :], op=ALU.subtract), reads=[rp], writes=[krr])
                        S.op("dve", lambda e: e.tensor_tensor(out=krr[:, 16:32], in0=rp[:, 2, :], in1=rp[:, 3, :], op=ALU.add), reads=[rp], writes=[krr])
                        S.op("dve", lambda e: e.tensor_copy(out=Kt[:, :, 64:96], in_=krr[:].unsqueeze(1).to_broadcast([128, 8, 32])),
                             reads=[krr], writes=[Kt])
                        transpose_to(Kt, lambda h: Kt[:, h, :], 8, X0, KT, lambda: KT[0:96, :, n * 128:(n + 1) * 128], rows=96)
                        S.op("pe", lambda e: e.transpose(out=PB[X1][0:32, 0:128], in_=pa[:, 416:448], identity=ident_f[:]),
                             reads=[pa, ident_f], writes=[PB[X1]])
                        S.op("act", lambda e: e.copy(out=glrA[0:32, :], in_=PB[X1][0:32, 0:128]), reads=[PB[X1]], writes=[glrA])
                        S.op("pe", lambda e: e.matmul(PB[X2][:, :], lhsT=glrA[:, :], rhs=Wg[:, :], start=True, stop=True),
                             reads=[glrA, Wg], writes=[PB[X2]])
                        S.op("act", lambda e: e.activation(out=sp_t[:], in_=PB[X2][:, :], func=AF.Exp, scale=-1.0), reads=[PB[X2]], writes=[sp_t])
                        S.op("act", lambda e: e.activation(out=sp_t[:], in_=sp_t[:], func=AF.Ln, bias=1.0), reads=[sp_t], writes=[sp_t])
                        for cc in range(4):
                            S.op("pe", lambda e, cc=cc: e.matmul(PB[X1][:, 256 + cc:257 + cc], lhsT=sp_t[:, cc * 128:(cc + 1) * 128], rhs=triU[:, 0:1],
                                                                 start=True, stop=True), reads=[sp_t, triU], writes=[PB[X1]])
                        S.op("dve", lambda e: e.tensor_copy(out=TTs[:, :, n], in_=PB[X1][:, 256:260]), reads=[PB[X1]], writes=[TTs])

                    run_pipelined(a1_tile, NTILE)
                    S.barrier()
                if stop == "A1":
                    S.barrier()
                    return nc
                Inc = S.sb(p1, "Inc", [128, 4, NTILE], F32)
                Dc = S.sb(p1, "Dc", [128, 4, NTILE], F32)
                S.op("dve", lambda e: e.tensor_copy(out=Inc[:, :, 0:1], in_=TTs[:, :, 0:1]), reads=[TTs], writes=[Inc])
                for n in range(1, NTILE):
                    S.op("dve", lambda e, n=n: e.tensor_tensor(out=Inc[:, :, n:n + 1], in0=Inc[:, :, n - 1:n], in1=TTs[:, :, n:n + 1], op=ALU.add),
                         reads=[Inc, TTs], writes=[Inc])
                S.op("dve", lambda e: e.tensor_tensor(out=Dc[:, 0:2, :], in0=Inc[:, 0:2, :], in1=TTs[:, 0:2, :], op=ALU.subtract), reads=[Inc, TTs], writes=[Dc])
                S.op("dve", lambda e: e.tensor_tensor(out=Dc[:, 0:2, :], in0=Dc[:, 0:2, :], in1=Inc[:, 0:2, 7:8].to_broadcast([128, 2, NTILE]), op=ALU.subtract),
                     reads=[Dc, Inc], writes=[Dc])
                S.op("dve", lambda e: e.tensor_tensor(out=Dc[:, 2:4, :], in0=Inc[:, 2:4, 7:8].to_broadcast([128, 2, NTILE]), in1=Inc[:, 2:4, :], op=ALU.subtract),
                     reads=[Inc], writes=[Dc])
                for cc in range(4):
                    S.op("pe", lambda e, cc=cc: e.transpose(out=PB[3][0:16, cc * 128:(cc + 1) * 128], in_=Dc[:, cc, :], identity=ident_f[:]),
                         reads=[Dc, ident_f], writes=[PB[3]])
                S.op("act", lambda e: e.copy(out=DrowG[:], in_=PB[3][0:16, :]), reads=[PB[3]], writes=[DrowG])

                if stop == "A1b":
                    S.barrier()
                    return nc
                Qt = S.sb(p1, "Qt", [128, 8, 96], BF16)
                rpq = S.sb(p1, "rpq", [128, 4, 4, 16], F32)
                QTc = S.sb(p1, "QTc", [128, 8, 512], BF16)
                Eb = [S.sb(p1, f"Eb{i}", [128, 512], BF16) for i in range(2)]
                Oacc = S.sb(p1, "Oacc", [128, 4, 8, 65], F32)
                rc = S.sb(p1, "rc", [128, 4, 8], F32)
                atok = S.sb(p1, "atok", [128, 4, 512], BF16)
                sc = 96.0 ** -0.5
                for qc in range(4):
                    for i in range(4):
                        n = qc * 4 + i
                        for j in range(2):
                            for kc in range(2):
                                S.op("pe", lambda e, j=j, kc=kc, n=n: e.matmul(PB[6 + j][:, 0:384], lhsT=cqT[:, kc, n * 128:(n + 1) * 128],
                                                                                 rhs=Wuq[:, kc, j * 384:(j + 1) * 384], start=(kc == 0), stop=(kc == 1)),
                                     reads=[cqT, Wuq], writes=[PB[6 + j]])
                            qv = PB[6 + j][:, 0:384].rearrange("p (h c) -> p h c", h=4)
                            S.op("act", lambda e, j=j, qv=qv: e.copy(out=Qt[:, 4 * j:4 * j + 4, 0:64], in_=qv[:, :, 0:64]), reads=[PB[6 + j]], writes=[Qt])
                            cb_ = cosT[:, n, :].unsqueeze(1).to_broadcast([128, 4, 16])
                            sb2_ = sinT[:, n, :].unsqueeze(1).to_broadcast([128, 4, 16])
                            x1, x2 = qv[:, :, 64:80], qv[:, :, 80:96]
                            bkq = PB[6 + j]
                            S.op("dve", lambda e, x1=x1, cb_=cb_: e.tensor_tensor(out=rpq[:, 0], in0=x1, in1=cb_, op=ALU.mult), reads=[bkq, cosT], writes=[rpq])
                            S.op("dve", lambda e, x2=x2, sb2_=sb2_: e.tensor_tensor(out=rpq[:, 1], in0=x2, in1=sb2_, op=ALU.mult), reads=[bkq, sinT], writes=[rpq])
                            S.op("dve", lambda e, x2=x2, cb_=cb_: e.tensor_tensor(out=rpq[:, 2], in0=x2, in1=cb_, op=ALU.mult), reads=[bkq, cosT], writes=[rpq])
                            S.op("dve", lambda e, x1=x1, sb2_=sb2_: e.tensor_tensor(out=rpq[:, 3], in0=x1, in1=sb2_, op=ALU.mult), reads=[bkq, sinT], writes=[rpq])
                            S.op("dve", lambda e, j=j: e.tensor_tensor(out=Qt[:, 4 * j:4 * j + 4, 64:80], in0=rpq[:, 0], in1=rpq[:, 1], op=ALU.subtract), reads=[rpq], writes=[Qt])
                            S.op("dve", lambda e, j=j: e.tensor_tensor(out=Qt[:, 4 * j:4 * j + 4, 80:96], in0=rpq[:, 2], in1=rpq[:, 3], op=ALU.add), reads=[rpq], writes=[Qt])
                        transpose_to(Qt, lambda h: Qt[:, h, :], 8, 0, QTc, lambda i=i: QTc[0:96, :, i * 128:(i + 1) * 128], rows=96)
                    jobsB = [(h, kt) for h in range(8) for kt in range(NTILE)]

                    def b_score(i):
                        h, kt = jobsB[i]
                        sb_ = i % 2
                        S.op("pe", lambda e: e.matmul(PB[sb_][:, :], lhsT=KT[0:96, h, kt * 128:(kt + 1) * 128], rhs=QTc[0:96, h, :],
                                                      start=True, stop=True), reads=[KT, QTc], writes=[PB[sb_]])
                        S.op("act", lambda e: e.activation(out=Eb[sb_][:], in_=PB[sb_][:, :], func=AF.Exp, scale=sc),
                             reads=[PB[sb_]], writes=[Eb[sb_]])

                    def b_pv(i):
                        h, kt = jobsB[i]
                        sb_ = i % 2
                        pv0 = 2 + 2 * (h % 2)
                        for qt in range(4):
                            bk = pv0 + qt // 2
                            c0 = (qt % 2) * 128
                            S.op("pe", lambda e, qt=qt, bk=bk, c0=c0: e.matmul(
                                PB[bk][:, c0:c0 + 65], lhsT=Eb[sb_][:, qt * 128:(qt + 1) * 128], rhs=Vaug[:, kt, h, :],
                                start=(kt == 0 and qt % 2 == 0), stop=(kt == NTILE - 1), skip_group_check=True),
                                reads=[Eb[sb_], Vaug], writes=[PB[bk]])
                        if kt == NTILE - 1:
                            for half in range(2):
                                bk = pv0 + half
                                src = PB[bk][:, 0:256].rearrange("p (a c) -> p a c", a=2)[:, :, 0:65]
                                S.op("dve", lambda e, half=half, src=src: e.tensor_copy(out=Oacc[:, 2 * half:2 * half + 2, h, :], in_=src),
                                     reads=[PB[bk]], writes=[Oacc])

                    for i in range(len(jobsB) + 1):
                        if i < len(jobsB):
                            b_score(i)
                        if i >= 1:
                            b_pv(i - 1)
                    S.op("dve", lambda e: e.reciprocal(out=rc[:], in_=Oacc[:, :, :, 64]), reads=[Oacc], writes=[rc])
                    for i in range(4):
                        S.op("dve", lambda e, i=i: e.tensor_tensor(out=atok[:, i, :].rearrange("p (h c) -> p h c", h=8), in0=Oacc[:, i, :, 0:64],
                                                                   in1=rc[:, i, :].unsqueeze(2).to_broadcast([128, 8, 64]), op=ALU.mult),
                             reads=[Oacc, rc], writes=[atok])
                        n = qc * 4 + i
                        transpose_to(atok, lambda c, i=i: atok[:, i, c * 128:(c + 1) * 128], 4, 6 + (i % 2), attnT,
                                     lambda n=n: attnT[:, :, n * 128:(n + 1) * 128], eng="dve")
                S.barrier()
            if stop == "B":
                return nc

            with ExitStack() as p2:
                qdT = S.sb(p2, "qdT", [128, 4, S_LEN], BF16)
                kdT = S.sb(p2, "kdT", [128, 4, S_LEN], BF16)
                GV = S.sb(p2, "GV", [128, NTILE, 512], BF16)
                SG = S.sb(p2, "SG", [128, NTILE, 512], BF16)
                glab = bcast_load(p2, "glab", gla_d, 128)
                with ExitStack() as p2a:
                    WinB = S.sb(p2a, "WinB", [128, 8, 1568], BF16, dma="sw")
                    Wg = S.sb(p2a, "Wg2", [33, 512], F32, dma=True)
                    S.dma("pool", lambda q: q.dma_start(out=WinB[:], in_=winB_d.rearrange("(c p) n -> p c n", p=128)), writes=[WinB], semt=WinB)
                    S.dma("sp", lambda q: q.dma_start(out=Wg[:], in_=wg_d), writes=[Wg], semt=Wg)
                    g1b = bcast_load(p2a, "g1b2", g1_d, 1024)

                    def mk(name, shape, dt, **kw):
                        return [S.sb(p2a, f"{name}{i}", shape, dt, **kw) for i in range(2)]
                    xtP = mk("xt2", [128, 1024], F32, dma=True)
                    n1P = mk("n12", [128, 1024], BF16)
                    n1TP = mk("n1T2", [128, 8, 128], BF16)
                    glrP = mk("glr", [128, 32], F32)
                    glrAP = mk("glrA2", [33, 128], F32)
                    spP = mk("sp2", [128, 512], F32)
                    EpP = mk("Ep", [128, 512], F32)
                    EmP = mk("Em", [128, 512], F32)
                    qdP = mk("qd", [128, 512], BF16)
                    kdP = mk("kd", [128, 512], BF16)
                    DrMP = mk("DrM", [16, 512], F32)
                    stP = mk("st2", [128, 8], F32)
                    jkP = mk("jk2", [128, 1024], BF16)
                    for i in range(2):
                        S.op("pool", lambda e, i=i: e.memset(glrAP[i][:], 1.0), writes=[glrAP[i]])

                    def a2_tile(n):
                        p = n % 2
                        xt, n1, n1T, glr, glrA, sp_t, Ep, Em, qd, kd, DrM, st_, jk = (xtP[p], n1P[p], n1TP[p], glrP[p], glrAP[p], spP[p], EpP[p], EmP[p],
                                                                                      qdP[p], kdP[p], DrMP[p], stP[p], jkP[p])
                        X0, X1, X2, X3 = 4 * p, 4 * p + 1, 4 * p + 2, 4 * p + 3
                        load_norm_T(base + n * 128, g1b, xt, n1, n1T, X0, st_, jk)
                        for (bk, c0, w) in ((X1, 0, 512), (X2, 512, 512), (X3, 1024, 512)):
                            for kc in range(8):
                                S.op("pe", lambda e, kc=kc, bk=bk, c0=c0, w=w: e.matmul(PB[bk][:, 0:w], lhsT=n1T[:, kc, :], rhs=WinB[:, kc, c0:c0 + w],
                                                                                         start=(kc == 0), stop=(kc == 7)), reads=[n1T, WinB], writes=[PB[bk]])
                        S.op("act", lambda e: e.copy(out=GV[:, n, :], in_=PB[X2][:, :]), reads=[PB[X2]], writes=[GV])
                        S.op("act", lambda e: e.activation(out=SG[:, n, :], in_=PB[X3][:, :], func=AF.Silu), reads=[PB[X3]], writes=[SG])
                        for kc in range(8):
                            S.op("pe", lambda e, kc=kc: e.matmul(PB[X2][:, 0:32], lhsT=n1T[:, kc, :], rhs=WinB[:, kc, 1536:1568],
                                                                 start=(kc == 0), stop=(kc == 7)), reads=[n1T, WinB], writes=[PB[X2]])
                        S.op("dve", lambda e: e.tensor_copy(out=glr[:], in_=PB[X2][:, 0:32]), reads=[PB[X2]], writes=[glr])
                        S.op("pe", lambda e: e.transpose(out=PB[X3][0:32, 0:128], in_=glr[:], identity=ident_f[:]), reads=[glr, ident_f], writes=[PB[X3]])
                        S.op("act", lambda e: e.copy(out=glrA[0:32, :], in_=PB[X3][0:32, 0:128]), reads=[PB[X3]], writes=[glrA])
                        S.op("pe", lambda e: e.matmul(PB[X2][:, :], lhsT=glrA[:, :], rhs=Wg[:, :], start=True, stop=True), reads=[glrA, Wg], writes=[PB[X2]])
                        S.op("act", lambda e: e.activation(out=sp_t[:], in_=PB[X2][:, :], func=AF.Exp, scale=-1.0), reads=[PB[X2]], writes=[sp_t])
                        S.op("act", lambda e: e.activation(out=sp_t[:], in_=sp_t[:], func=AF.Ln, bias=1.0), reads=[sp_t], writes=[sp_t])
                        S.op("dve", lambda e: e.tensor_scalar_mul(out=DrM[:], in0=DrowG[:], scalar1=eye16[:, n:n + 1]), reads=[DrowG, eye16], writes=[DrM])
                        S.op("pe", lambda e: e.matmul(PB[X3][:, 0:256], lhsT=triL[:], rhs=sp_t[:, 0:256], start=True, stop=False, skip_group_check=True),
                             reads=[triL, sp_t], writes=[PB[X3]])
                        S.op("pe", lambda e: e.matmul(PB[X3][:, 256:512], lhsT=triU[:], rhs=sp_t[:, 256:512], start=False, stop=False, skip_group_check=True),
                             reads=[triU, sp_t], writes=[PB[X3]])
                        S.op("pe", lambda e: e.matmul(PB[X3][:, :], lhsT=ones16[:], rhs=DrM[:], start=False, stop=True, skip_group_check=True),
                             reads=[ones16, DrM], writes=[PB[X3]])
                        S.op("act", lambda e: e.activation(out=Ep[:], in_=PB[X3][:, :], func=AF.Exp, scale=-1.0 / 16), reads=[PB[X3]], writes=[Ep])
                        S.op("act", lambda e: e.activation(out=Em[:], in_=PB[X3][:, :], func=AF.Exp, scale=1.0 / 16), reads=[PB[X3]], writes=[Em])
                        gqv = PB[X1][:, 0:256].unsqueeze(1).to_broadcast([128, 2, 256])
                        gkv_ = PB[X1][:, 256:512].unsqueeze(1).to_broadcast([128, 2, 256])
                        S.op("dve", lambda e: e.scalar_tensor_tensor(out=qd[:].rearrange("p (d c) -> p d c", d=2), in0=gqv, scalar=0.125,
                                                                     in1=Ep[:].rearrange("p (d c) -> p d c", d=2), op0=ALU.mult, op1=ALU.mult),
                             reads=[PB[X1], Ep], writes=[qd])
                        S.op("dve", lambda e: e.tensor_tensor(out=kd[:].rearrange("p (d c) -> p d c", d=2), in0=gkv_,
                                                              in1=Em[:].rearrange("p (d c) -> p d c", d=2), op=ALU.mult),
                             reads=[PB[X1], Em], writes=[kd])
                        transpose_to(qd, lambda k: qd[:, k * 128:(k + 1) * 128], 4, X0, qdT, lambda: qdT[:, :, n * 128:(n + 1) * 128], eng="dve")
                        transpose_to(kd, lambda k: kd[:, k * 128:(k + 1) * 128], 4, X2, kdT, lambda: kdT[:, :, n * 128:(n + 1) * 128], eng="act")

                    run_pipelined(a2_tile, NTILE)
                    S.barrier()
                if stop == "A2":
                    return nc

                with ExitStack() as p2c:
                    mf = S.sb(p2c, "mf", [128, 4, 512], BF16)
                    mb = S.sb(p2c, "mb", [128, 4, 512], BF16)
                    S.op("pool", lambda e: e.memset(mf[:], 1.0), writes=[mf])
                    S.op("pool", lambda e: e.memset(mb[:], 1.0), writes=[mb])
                    for r in range(4):
                        S.op("pool", lambda e, r=r: e.affine_select(out=mf[:, r, :], in_=mf[:, r, :], pattern=[[1, 512]], compare_op=ALU.is_ge, fill=0.0,
                                                                    base=-128 * r, channel_multiplier=-1), reads=[mf], writes=[mf])
                        S.op("pool", lambda e, r=r: e.affine_select(out=mb[:, r, :], in_=mb[:, r, :], pattern=[[-1, 512]], compare_op=ALU.is_ge, fill=0.0,
                                                                    base=128 * r, channel_multiplier=1), reads=[mb], writes=[mb])
                    At = [S.sb(p2c, f"At{i}", [128, 512], BF16) for i in range(3)]
                    Ogc = S.sb(p2c, "Ogc", [128, 4, 4, 128], F32)
                    sq = S.sb(p2c, "sq", [128, 16, 128], F32)
                    ssq = S.sb(p2c, "ssq", [128, 16], F32)
                    ogt = S.sb(p2c, "ogt", [128, 4, 512], BF16)
                    for tc in range(4):
                        jobsC = []
                        for h in range(4):
                            jl = [(h, 0, jt) for jt in range(0, 4 * tc + 4)] + [(h, 1, jt) for jt in range(4 * tc, NTILE)]
                            jobsC += [(h, d, jt, k == 0, k == len(jl) - 1) for k, (h, d, jt) in enumerate(jl)]

                        def c_score(i):
                            h, d, jt, isf, isl = jobsC[i]
                            sbk = i % 2
                            hp, hl = h // 2, (h % 2) * 64
                            blk = d * 2 + hp
                            S.op("pe", lambda e: e.matmul(
                                PB[sbk][:, :], lhsT=kdT[hl:hl + 64, blk, jt * 128:(jt + 1) * 128], rhs=qdT[hl:hl + 64, blk, tc * 512:(tc + 1) * 512],
                                start=True, stop=True), reads=[kdT, qdT], writes=[PB[sbk]])
                            A = At[i % 3]
                            r = jt - 4 * tc
                            if 0 <= r < 4:
                                mk = mf if d == 0 else mb
                                S.op("dve", lambda e: e.tensor_tensor(out=A[:], in0=PB[sbk][:, :], in1=mk[:, r, :], op=ALU.mult),
                                     reads=[PB[sbk], mk], writes=[A])
                            else:
                                S.op("act", lambda e: e.copy(out=A[:], in_=PB[sbk][:, :]), reads=[PB[sbk]], writes=[A])

                        def c_pv(i):
                            h, d, jt, isf, isl = jobsC[i]
                            ob = 4 + (h % 2)
                            A = At[i % 3]
                            first = isf
                            for ii in range(4):
                                tt_ = 4 * tc + ii
                                if (d == 0 and jt > tt_) or (d == 1 and jt < tt_):
                                    continue
                                S.op("pe", lambda e, ii=ii, first=first: e.matmul(
                                    PB[ob][:, ii * 128:(ii + 1) * 128], lhsT=A[:, ii * 128:(ii + 1) * 128], rhs=GV[:, jt, h * 128:(h + 1) * 128],
                                    start=first, stop=False, skip_group_check=True), reads=[A, GV], writes=[PB[ob]])
                                first = False
                            if isl:
                                S.op("act", lambda e: e.copy(out=Ogc[:, :, h, :], in_=PB[ob][:, :].rearrange("p (i c) -> p i c", i=4)),
                                     reads=[PB[ob]], writes=[Ogc])

                        for i in range(len(jobsC) + 1):
                            if i < len(jobsC):
                                c_score(i)
                            if i >= 1:
                                c_pv(i - 1)
                        og2 = Ogc[:].rearrange("p i h c -> p (i h) c")
                        S.op("dve", lambda e: e.tensor_tensor(out=sq[:], in0=og2, in1=og2, op=ALU.mult), reads=[Ogc], writes=[sq])
                        S.op("dve", lambda e: e.tensor_reduce(out=ssq[:], in_=sq[:], axis=AX.X, op=ALU.add), reads=[sq], writes=[ssq])
                        S.op("act", lambda e: e.activation(out=ssq[:], in_=ssq[:], func=AF.Sqrt, scale=1.0 / 128, bias=EPS), reads=[ssq], writes=[ssq])
                        S.op("dve", lambda e: e.reciprocal(out=ssq[:], in_=ssq[:]), reads=[ssq], writes=[ssq])
                        S.op("dve", lambda e: e.tensor_tensor(out=sq[:], in0=og2, in1=ssq[:].unsqueeze(2).to_broadcast([128, 16, 128]), op=ALU.mult),
                             reads=[Ogc, ssq], writes=[sq])
                        S.op("dve", lambda e: e.tensor_tensor(out=sq[:], in0=sq[:], in1=glab[:].unsqueeze(1).to_broadcast([128, 16, 128]), op=ALU.mult),
                             reads=[sq, glab], writes=[sq])
                        S.op("dve", lambda e, tc=tc: e.tensor_tensor(out=ogt[:].rearrange("p i c -> p (i c)"), in0=sq[:].rearrange("p a c -> p (a c)"),
                                                                     in1=SG[:, 4 * tc:4 * tc + 4, :].rearrange("p i c -> p (i c)"), op=ALU.mult),
                             reads=[sq, SG], writes=[ogt])
                        for i in range(4):
                            n = 4 * tc + i
                            transpose_to(ogt, lambda c, i=i: ogt[:, i, c * 128:(c + 1) * 128], 4, 6 + (i % 2), oglaT,
                                         lambda n=n: oglaT[:, :, n * 128:(n + 1) * 128], eng="act")
                    S.barrier()

            if stop == "C":
                return nc
            with ExitStack() as p3:
                Wo = S.sb(p3, "Wo", [128, 8, 1024], BF16, dma="sw")
                Wq = S.sb(p3, "Wq", [128, 8, 2048], BF16, dma="sw")
                SubT = S.sb(p3, "SubT", [128, 16, 128], BF16, dma="sw")
                S.dma("pool", lambda q: q.dma_start(out=Wo[:], in_=wo_d.rearrange("(c p) n -> p c n", p=128)), writes=[Wo], semt=Wo)
                S.dma("pool", lambda q: q.dma_start(out=Wq[:], in_=wq_d.rearrange("(c p) n -> p c n", p=128)), writes=[Wq], semt=Wq)
                S.dma("pool", lambda q: q.dma_start(out=SubT[:], in_=subk_d), writes=[SubT], semt=SubT)
                g2b = bcast_load(p3, "g2b", g2_d, 1024)
                gfb = bcast_load(p3, "gfb", gf_d, 1024)
                htP = [S.sb(p3, f"ht{i}", [128, 1024], F32, dma=True) for i in range(2)]
                hnP = [S.sb(p3, f"hn{i}", [128, 1024], F32) for i in range(2)]
                idxP = [S.sb(p3, f"idxu{i}", [128, 128], U32) for i in range(2)]
                gateP = [S.sb(p3, f"gate{i}", [128, 8, 16], F32) for i in range(2)]
                hnb = S.sb(p3, "hnb", [128, 1024], BF16)
                hnT = S.sb(p3, "hnT", [128, 8, 128], BF16)
                qT = S.sb(p3, "qT", [128, 16, 128], BF16)
                ssb = S.sb(p3, "ssb", [128, 16, 128], F32)
                wk = S.sb(p3, "wk", [128, 256], F32)
                m16 = S.sb(p3, "m16", [128, 16, 16], F32)
                i16 = S.sb(p3, "i16", [128, 16, 16], U32)
                i16f = S.sb(p3, "i16f", [128, 16, 16], F32)
                cand = S.sb(p3, "cand", [128, 8, 256], F32)
                tops = S.sb(p3, "tops", [128, 8, 16], F32)
                pos = S.sb(p3, "pos", [128, 8, 16], U32)
                pa_ = S.sb(p3, "posa", [128, 8, 16], U32)
                pb_ = S.sb(p3, "posb", [128, 8, 16], U32)
                paf = S.sb(p3, "paf", [128, 8, 16], F32)
                pbf_ = S.sb(p3, "pbf", [128, 8, 16], F32)
                eq = S.sb(p3, "eq", [128, 8, 16, 16], BF16)
                sel1 = S.sb(p3, "sel1", [128, 8, 16], F32)
                sel2 = S.sb(p3, "sel2", [128, 8, 16], F32)
                idxf = S.sb(p3, "idxf", [128, 128], F32)
                gsum = S.sb(p3, "gsum", [128, 8], F32)
                stat2 = S.sb(p3, "stat2", [128, 8], F32)
                junk_a = S.sb(p3, "junk_a", [128, 1024], BF16)
                junkD = [S.sb(p3, f"junkD{i}", [128, 1024], BF16) for i in range(2)]
                actv = S.sb(p3, "actv", [128, 128], F32)
                glt = S.sb(p3, "glt", [128, 128], F32)
                wct = S.sb(p3, "wct", [128, 128], F32)
                actC = [T(actv[:, i:i + 1], f"actc{i}") for i in range(128)]
                glC = [T(glt[:, i:i + 1], f"glc{i}") for i in range(128)]
                NUV = 10
                UV = [S.sb(p3, f"UV{i}", [128, 2048], BF16, dma="sw") for i in range(NUV)]
                NDG = 4
                Dg = [S.sb(p3, f"Dg{i}", [128, 128], BF16) for i in range(NDG)]

                def rstd2(src_t, col):
                    S.op("act", lambda e: e.activation(out=junk_a[:], in_=src_t[:], func=AF.Square, accum_out=stat2[:, col:col + 1]),
                         reads=[src_t], writes=[stat2, junk_a])
                    S.op("act", lambda e: e.activation(out=stat2[:, col + 1:col + 2], in_=stat2[:, col:col + 1], func=AF.Sqrt,
                                                       scale=1.0 / 1024, bias=EPS), reads=[stat2], writes=[stat2])
                    S.op("dve", lambda e: e.reciprocal(out=stat2[:, col + 1:col + 2], in_=stat2[:, col + 1:col + 2]), reads=[stat2], writes=[stat2])
                    return stat2[:, col + 1:col + 2]

                def top16(src_ap, src_t, n_el, mv, iv, mvt, ivt):
                    S.op("dve", lambda e: e.max(out=mv[:, 0:8], in_=src_ap), reads=[src_t], writes=[mvt])
                    S.op("dve", lambda e: e.max_index(out=iv[:, 0:8], in_max=mv[:, 0:8], in_values=src_ap), reads=[src_t, mvt], writes=[ivt])
                    S.op("dve", lambda e: e.match_replace(out=wk[:, 0:n_el], in_to_replace=mv[:, 0:8], in_values=src_ap, imm_value=-1e30),
                         reads=[src_t, mvt], writes=[wk])
                    S.op("dve", lambda e: e.max(out=mv[:, 8:16], in_=wk[:, 0:n_el]), reads=[wk], writes=[mvt])
                    S.op("dve", lambda e: e.max_index(out=iv[:, 8:16], in_max=mv[:, 8:16], in_values=wk[:, 0:n_el]), reads=[wk, mvt], writes=[ivt])

                def emit_d12(n):
                    par = n % 2
                    ht, hn, idxu, gate = htP[par], hnP[par], idxP[par], gateP[par]
                    r0 = base + n * 128
                    S.dma("sp", lambda q: q.dma_start(out=ht[:], in_=x_d[r0:r0 + 128, :]), writes=[ht], semt=ht)
                    for j in range(2):
                        for c in range(8):
                            src = attnT if c < 4 else oglaT
                            S.op("pe", lambda e, j=j, c=c, src=src: e.matmul(PB[j][:, :], lhsT=src[:, c % 4, n * 128:(n + 1) * 128],
                                                                             rhs=Wo[:, c, j * 512:(j + 1) * 512], start=(c == 0), stop=(c == 7)),
                                 reads=[src, Wo], writes=[PB[j]])
                        S.op("dve", lambda e, j=j: e.tensor_tensor(out=ht[:, j * 512:(j + 1) * 512], in0=PB[j][:, :], in1=ht[:, j * 512:(j + 1) * 512], op=ALU.add),
                             reads=[PB[j], ht], writes=[ht])
                    if dbg:
                        S.dma("sp", lambda q: q.dma_start(out=dbgh_d[r0:r0 + 128, :], in_=ht[:]), reads=[ht], writes=[dbgh_t], semt=dbgh_t)
                    r = rstd2(ht, 0)
                    S.op("dve", lambda e: e.scalar_tensor_tensor(out=hn[:], in0=ht[:], scalar=r, in1=g2b[:], op0=ALU.mult, op1=ALU.mult),
                         reads=[ht, stat2, g2b], writes=[hn])
                    S.op("act", lambda e: e.copy(out=hnb[:], in_=hn[:]), reads=[hn], writes=[hnb])
                    transpose_to(hnb, lambda kc: hnb[:, kc * 128:(kc + 1) * 128], 8, 2, hnT, lambda: hnT[:])
                    for g4 in range(4):
                        bk = 3 + (g4 % 2)
                        for q4 in range(4):
                            hp = g4 * 4 + q4
                            for kc in range(8):
                                S.op("pe", lambda e, hp=hp, kc=kc, bk=bk, q4=q4: e.matmul(PB[bk][:, q4 * 128:(q4 + 1) * 128], lhsT=Wq[:, kc, hp * 128:(hp + 1) * 128],
                                                                                            rhs=hnT[:, kc, :], start=(kc == 0), stop=(kc == 7)),
                                     reads=[Wq, hnT], writes=[PB[bk]])
                        S.op("act", lambda e, g4=g4, bk=bk: e.copy(out=qT[:, g4 * 4:(g4 + 1) * 4, :].rearrange("p a b -> p (a b)"), in_=PB[bk][:, :]),
                             reads=[PB[bk]], writes=[qT])
                    for g4 in range(4):
                        bk = 5 if g4 % 2 == 0 else 2
                        for q4 in range(4):
                            hp = g4 * 4 + q4
                            S.op("pe", lambda e, hp=hp, bk=bk, q4=q4: e.matmul(PB[bk][:, q4 * 128:(q4 + 1) * 128], lhsT=qT[:, hp, :], rhs=SubT[:, hp, :],
                                                                                start=True, stop=True), reads=[qT, SubT], writes=[PB[bk]])
                        S.op("act", lambda e, g4=g4, bk=bk: e.copy(out=ssb[:, g4 * 4:(g4 + 1) * 4, :].rearrange("p a b -> p (a b)"), in_=PB[bk][:, :]),
                             reads=[PB[bk]], writes=[ssb])
                    if S.defer is not None:
                        mark[0] = len(S.defer)
                    for hp in range(16):
                        top16(ssb[:, hp, :], ssb, 128, m16[:, hp, :], i16[:, hp, :], m16, i16)
                    m4 = m16[:].rearrange("p (h a) i -> p h a i", a=2)
                    S.op("dve", lambda e: e.tensor_tensor(out=cand[:].rearrange("p h (i j) -> p h i j", i=16),
                                                          in0=m4[:, :, 0, :].unsqueeze(3).to_broadcast([128, 8, 16, 16]),
                                                          in1=m4[:, :, 1, :].unsqueeze(2).to_broadcast([128, 8, 16, 16]), op=ALU.add),
                         reads=[m16], writes=[cand])
                    for h in range(8):
                        top16(cand[:, h, :], cand, 256, tops[:, h, :], pos[:, h, :], tops, pos)
                    S.op("dve", lambda e: e.tensor_single_scalar(out=pa_[:], in_=pos[:], scalar=4, op=ALU.logical_shift_right), reads=[pos], writes=[pa_])
                    S.op("dve", lambda e: e.tensor_single_scalar(out=pb_[:], in_=pos[:], scalar=15, op=ALU.bitwise_and), reads=[pos], writes=[pb_])
                    S.op("dve", lambda e: e.tensor_copy(out=paf[:], in_=pa_[:]), reads=[pa_], writes=[paf])
                    S.op("dve", lambda e: e.tensor_copy(out=pbf_[:], in_=pb_[:]), reads=[pb_], writes=[pbf_])
                    S.op("dve", lambda e: e.tensor_copy(out=i16f[:], in_=i16[:]), reads=[i16], writes=[i16f])
                    i4 = i16f[:].rearrange("p (h a) i -> p h a i", a=2)
                    iob = iota16[:].unsqueeze(1).unsqueeze(1).to_broadcast([128, 8, 16, 16])
                    for (pf, a, sel) in ((paf, 0, sel1), (pbf_, 1, sel2)):
                        S.op("dve", lambda e, pf=pf: e.tensor_tensor(out=eq[:], in0=pf[:].unsqueeze(3).to_broadcast([128, 8, 16, 16]), in1=iob, op=ALU.is_equal),
                             reads=[pf, iota16], writes=[eq])
                        S.op("dve", lambda e, a=a: e.tensor_tensor(out=eq[:], in0=eq[:], in1=i4[:, :, a, :].unsqueeze(2).to_broadcast([128, 8, 16, 16]), op=ALU.mult),
                             reads=[eq, i16f], writes=[eq])
                        S.op("dve", lambda e, sel=sel: e.tensor_reduce(out=sel[:], in_=eq[:], axis=AX.X, op=ALU.add), reads=[eq], writes=[sel])
                    S.op("dve", lambda e: e.scalar_tensor_tensor(out=idxf[:], in0=sel1[:].rearrange("p h k -> p (h k)"), scalar=128.0,
                                                                 in1=sel2[:].rearrange("p h k -> p (h k)"), op0=ALU.mult, op1=ALU.add),
                         reads=[sel1, sel2], writes=[idxf])
                    S.op("dve", lambda e: e.tensor_copy(out=idxu[:], in_=idxf[:]), reads=[idxf], writes=[idxu])
                    S.op("dve", lambda e: e.tensor_tensor(out=gate[:], in0=tops[:], in1=tops[:, :, 0:1].to_broadcast([128, 8, 16]), op=ALU.subtract),
                         reads=[tops], writes=[gate])
                    S.op("act", lambda e: e.activation(out=gate[:], in_=gate[:], func=AF.Exp), reads=[gate], writes=[gate])
                    S.op("dve", lambda e: e.tensor_reduce(out=gsum[:], in_=gate[:], axis=AX.X, op=ALU.add), reads=[gate], writes=[gsum])
                    S.op("dve", lambda e: e.reciprocal(out=gsum[:], in_=gsum[:]), reads=[gsum], writes=[gsum])
                    S.op("dve", lambda e: e.tensor_tensor(out=gate[:], in0=gate[:], in1=gsum[:].unsqueeze(2).to_broadcast([128, 8, 16]), op=ALU.mult),
                         reads=[gate, gsum], writes=[gate])

                mark = [0]
                RATE_A = 9

                def emit_loop(n, pending):
                    tail = len(pending) - mark[0]
                    par = n % 2
                    ht, hn, idxu, gate = htP[par], hnP[par], idxP[par], gateP[par]
                    r0 = base + n * 128
                    gflat = gate[:].rearrange("p h k -> p (h k)")
                    LAG = 2

                    def emit_acc(sl):
                        U = UV[sl % NUV]
                        D = Dg[sl % NDG]
                        S.op("act", lambda e: e.activation(out=glt[:, sl:sl + 1], in_=actv[:, sl:sl + 1], func=AF.Gelu_apprx_tanh),
                             reads=[actC[sl]], writes=[glC[sl]])
                        S.op("act", lambda e: e.activation(out=wct[:, sl:sl + 1], in_=gflat[:, sl:sl + 1], func=AF.Copy, scale=glt[:, sl:sl + 1]),
                             reads=[glC[sl], gate], writes=[glC[sl]])
                        S.op("act", lambda e: e.activation(out=D[:], in_=ident_bf[:], func=AF.Copy, scale=wct[:, sl:sl + 1]),
                             reads=[ident_bf, glC[sl]], writes=[D])
                        for j in range(2):
                            S.op("pe", lambda e, j=j: e.matmul(PB[6 + j][:, :], lhsT=D[:], rhs=U[:, 1024 + j * 512:1024 + (j + 1) * 512],
                                                               start=(sl == 0), stop=(sl == 127)), reads=[D, U], writes=[PB[6 + j]])

                    for sl in range(128):
                        U = UV[sl % NUV]
                        S.dma("pool", lambda q, U=U, sl=sl: q.indirect_dma_start(out=U[:], out_offset=None, in_=uvbf_d,
                                                                                 in_offset=bass.IndirectOffsetOnAxis(ap=idxu[:, sl:sl + 1], axis=0)),
                              reads=[idxu, uv_t], writes=[U], semt=U)
                        S.op("dve", lambda e, U=U, sl=sl: e.scalar_tensor_tensor(out=junkD[sl % 2][:], in0=U[:, 0:1024], scalar=1.0, in1=hn[:], op0=ALU.mult, op1=ALU.mult,
                                                                                 accum_out=actv[:, sl:sl + 1]), reads=[U, hn], writes=[junkD[sl % 2], actC[sl]])
                        if sl >= LAG:
                            emit_acc(sl - LAG)
                        if pending:
                            if len(pending) > tail:
                                S.replay(pending, min(RATE_A, len(pending) - tail))
                            else:
                                S.replay(pending, -(-len(pending) // max(1, 124 - sl)))
                    for sl in range(128 - LAG, 128):
                        emit_acc(sl)
                    if pending:
                        S.replay(pending, len(pending))
                    for j in range(2):
                        S.op("dve", lambda e, j=j: e.tensor_tensor(out=ht[:, j * 512:(j + 1) * 512], in0=PB[6 + j][:, :], in1=ht[:, j * 512:(j + 1) * 512], op=ALU.add),
                             reads=[PB[6 + j], ht], writes=[ht])
                    r = rstd2(ht, 2)
                    S.op("dve", lambda e: e.scalar_tensor_tensor(out=hn[:], in0=ht[:], scalar=r, in1=gfb[:], op0=ALU.mult, op1=ALU.mult),
                         reads=[ht, stat2, gfb], writes=[hn])
                    S.dma("sp", lambda q: q.dma_start(out=y_d[r0:r0 + 128, :], in_=hn[:]), reads=[hn], writes=[y_t], semt=y_t)

                emit_d12(0)
                for n in range(NTILE):
                    pending = []
                    if n + 1 < NTILE:
                        S.defer = pending
                        emit_d12(n + 1)
                        S.defer = None
                    emit_loop(n, pending)
                S.barrier()
        S.barrier()
        print("ninstr", S.ninstr, "nsem", S.nsem)
    return nc


def _host_inputs(inputs):
    f = np.float32
    w_in = np.asarray(inputs["w_in"])[0]
    wA = np.ascontiguousarray(np.concatenate([w_in[:, 0:416], w_in[:, 1440:1472]], axis=1))
    wB = np.ascontiguousarray(np.concatenate([w_in[:, 416:1440], w_in[:, 1472:1984], w_in[:, 1440:1472]], axis=1))
    gw = np.zeros((33, 512), f)
    gw[0:16, 0:256] = np.asarray(inputs["gate_fwd_w"])[0]
    gw[16:32, 256:512] = np.asarray(inputs["gate_bwd_w"])[0]
    gw[32, 0:256] = np.asarray(inputs["gate_fwd_b"])[0]
    gw[32, 256:512] = np.asarray(inputs["gate_bwd_b"])[0]
    subk = np.asarray(inputs["peer_subkeys"])[0].reshape(16, 128, 128)
    subkT = np.ascontiguousarray(np.transpose(subk, (2, 0, 1)))
    half = 16
    freqs = (np.float32(10000.0) ** (-np.arange(half, dtype=f) * f(2.0) / f(32))).astype(f)
    ang = (np.arange(S_LEN, dtype=f)[:, None] * freqs[None, :]).astype(f)
    shared = {
        "w_inA": wA, "w_inB": wB,
        "norm1_g": np.ascontiguousarray(np.asarray(inputs["norm1_g"])[0]),
        "q_norm_g": np.ascontiguousarray(np.asarray(inputs["q_norm_g"])[0]),
        "w_uq": np.ascontiguousarray(np.asarray(inputs["w_uq"])[0]),
        "kv_norm_g": np.ascontiguousarray(np.asarray(inputs["kv_norm_g"])[0]),
        "w_ukv": np.ascontiguousarray(np.asarray(inputs["w_ukv"])[0]),
        "gate_w": gw,
        "gla_norm_g": np.ascontiguousarray(np.asarray(inputs["gla_norm_g"])[0]),
        "w_o": np.ascontiguousarray(np.asarray(inputs["w_o"])[0]),
        "norm2_g": np.ascontiguousarray(np.asarray(inputs["norm2_g"])[0]),
        "peer_wq": np.ascontiguousarray(np.asarray(inputs["peer_wq"])[0]),
        "subkT": subkT,
        "peer_u": np.ascontiguousarray(np.asarray(inputs["peer_u"])[0]),
        "peer_v": np.ascontiguousarray(np.asarray(inputs["peer_v"])[0]),
        "final_norm_g": np.ascontiguousarray(np.asarray(inputs["final_norm_g"])),
        "rope_cos": np.cos(ang).astype(f), "rope_sin": np.sin(ang).astype(f),
    }
    return shared


def kernel(**inputs):
    xp = np.asarray(inputs["x_prompt"], dtype=np.float32)
    xs = np.asarray(inputs["x_sample"], dtype=np.float32)
    xall = np.concatenate([xp, xs], axis=0)
    nb = xall.shape[0]
    ncore = 8
    per = nb // ncore
    shared = _host_inputs(inputs)
    nc = build_program(per, False)
    in_maps = []
    for c in range(ncore):
        m = dict(shared)
        m["x"] = np.ascontiguousarray(xall[c * per:(c + 1) * per].reshape(per * S_LEN, 1024))
        in_maps.append(m)
    res = run_bass_kernel_spmd(nc, in_maps, core_ids=list(range(ncore)))
    ys = [np.asarray(r["y"]).reshape(per, S_LEN, 1024) for r in res.results]
    yall = np.concatenate(ys, axis=0).astype(np.float32)
    return (yall[:xp.shape[0]], yall[xp.shape[0]:])
```

```python
import numpy as np
from contextlib import ExitStack
import concourse.bass as bass
import concourse.mybir as mybir
from concourse.bass_utils import run_bass_kernel_spmd

F32 = mybir.dt.float32
BF16 = mybir.dt.bfloat16
U32 = mybir.dt.uint32
AF = mybir.ActivationFunctionType
ALU = mybir.AluOpType
AX = mybir.AxisListType

NSEQ = 5
DBG = False
EPOCH = 60000
S_LEN = 2048
NTILE = 16
EPS = 1e-6


class DSem:
    def __init__(self, h):
        self.h = h
        self.cnt = 0


class T:
    def __init__(self, ap, name, dsem=None):
        self.ap = ap
        self.name = name
        self.lw = None
        self.rd = {}
        self.dsem = dsem

    def __getitem__(self, k):
        return self.ap[k]


class Sched:
    def __init__(self, nc, stack):
        self.nc = nc
        self.stack = stack
        self.eng = {"pe": nc.tensor, "act": nc.scalar, "dve": nc.vector, "pool": nc.gpsimd, "sp": nc.sync}
        self.cnt = {k: 0 for k in self.eng}
        self.sem = {}
        self.nsem = 0
        for k in self.eng:
            self.sem[k] = self._newsem(k)
        self.waited = {k: {} for k in self.eng}
        self.ninstr = 0
        self.dpool = {"sw": [], "hw": []}
        self.dall = []

    def _newsem(self, name):
        self.nsem += 1
        return self.stack.enter_context(self.nc.semaphore(f"s{self.nsem}_{name}"))

    def getd(self, kind="hw"):
        if self.dpool[kind]:
            return self.dpool[kind].pop()
        d = DSem(self._newsem("d" + kind))
        d.kind = kind
        self.dall.append(d)
        return d

    def sb(self, st, name, shape, dtype, dma=False):
        self.nalloc = getattr(self, "nalloc", 0) + 1
        name = f"{name}_{self.nalloc}"
        t = st.enter_context(self.nc.sbuf_tensor(name, shape, dtype))
        kind = "hw" if dma is True else dma
        tt = T(t, name, self.getd(kind) if dma else None)
        if dma:
            st.callback(lambda d=tt.dsem: self.dpool[d.kind].append(d))
        return tt

    def ps(self, st, name, shape, dtype):
        t = st.enter_context(self.nc.psum_tensor(name, shape, dtype))
        tt = T(t, name)
        tt.excl = True
        return tt

    def dram(self, ap, name):
        return T(ap, name, self.getd())

    def _wait(self, e, deps):
        w = self.waited[e]
        for (sem, val) in deps:
            key = id(sem)
            if w.get(key, 0) >= val:
                continue
            self.eng[e].wait_ge(sem, val)
            w[key] = val
            self.ninstr += 1

    def replay(self, lst, k):
        d, self.defer = self.defer, None
        for _ in range(min(k, len(lst))):
            kind, a = lst.pop(0)
            (self.op if kind == "op" else self.dma)(*a)
        self.defer = d

    def op(self, e, fn, reads=(), writes=()):
        if getattr(self, "defer", None) is not None:
            self.defer.append(("op", (e, fn, list(reads), list(writes))))
            return None
        ex = [t for t in reads if getattr(t, "excl", False)]
        if ex:
            reads = [t for t in reads if not getattr(t, "excl", False)]
            writes = list(writes) + ex
        deps = []
        for t in reads:
            if t.lw is not None:
                deps.append(t.lw[1:])
        strict = (e != "pe")
        for t in writes:
            if t.lw is not None and (t.lw[0] != e or strict):
                deps.append(t.lw[1:])
            for en, d in t.rd.items():
                if en != e or strict:
                    deps.append(d)
        self._wait(e, deps)
        ins = fn(self.eng[e])
        self.cnt[e] += 1
        if self.cnt[e] > EPOCH:
            self.sem[e] = self._newsem(e)
            self.cnt[e] = 1
        ins.then_inc(self.sem[e], 1)
        self.ninstr += 1
        rec = (self.sem[e], self.cnt[e])
        for t in reads:
            t.rd[e] = rec
        for t in writes:
            t.lw = (e,) + rec
            t.rd = {}
        return ins

    def dma(self, q, fn, reads=(), writes=(), semt=None):
        if getattr(self, "defer", None) is not None:
            self.defer.append(("dma", (q, fn, list(reads), list(writes), semt)))
            return None
        deps = []
        for t in reads:
            if t.lw is not None:
                deps.append(t.lw[1:])
        for t in writes:
            if t.lw is not None:
                deps.append(t.lw[1:])
            for en, d in t.rd.items():
                deps.append(d)
        self._wait(q, deps)
        ins = fn(self.eng[q])
        ds = semt.dsem
        ds.cnt += 16
        ins.then_inc(ds.h, 16)
        self.ninstr += 1
        rec = (ds.h, ds.cnt)
        for t in reads:
            t.rd[("dma", id(ds))] = rec
        for t in writes:
            t.lw = ("dma",) + rec
            t.rd = {}
        return ins

    def barrier(self):
        deps = [(self.sem[k], self.cnt[k]) for k in self.eng if self.cnt[k] > 0]
        deps += [(d.h, d.cnt) for d in self.dall if d.cnt > 0]
        for e in self.eng:
            self._wait(e, deps)


def build_program(nseq, dbg, stop=None):
    nc = bass.Bass("TRN2", target_bir_lowering=False)
    ntok = nseq * S_LEN

    def din(name, shape, dt=F32):
        return nc.dram_tensor(name, shape, dt, kind="ExternalInput").ap()

    x_d = din("x", [ntok, 1024])
    winA_d = din("w_inA", [1024, 448])
    winB_d = din("w_inB", [1024, 1568])
    g1_d = din("norm1_g", [1024])
    gq_d = din("q_norm_g", [256])
    wuq_d = din("w_uq", [256, 768])
    gkv_d = din("kv_norm_g", [128])
    wukv_d = din("w_ukv", [128, 1024])
    wg_d = din("gate_w", [33, 512])
    gla_d = din("gla_norm_g", [128])
    wo_d = din("w_o", [1024, 1024])
    g2_d = din("norm2_g", [1024])
    wq_d = din("peer_wq", [1024, 2048])
    subk_d = din("subkT", [128, 16, 128])
    u_d = din("peer_u", [16384, 1024])
    v_d = din("peer_v", [16384, 1024])
    gf_d = din("final_norm_g", [1024])
    cos_d = din("rope_cos", [S_LEN, 16])
    sin_d = din("rope_sin", [S_LEN, 16])
    y_d = nc.dram_tensor("y", [ntok, 1024], F32, kind="ExternalOutput").ap()
    uvbf_d = nc.dram_tensor("uv_bf", [16384, 2048], BF16, kind="Internal").ap()
    if dbg:
        dbgh_d = nc.dram_tensor("dbg_h", [ntok, 1024], F32, kind="ExternalOutput").ap()
        dbgi_d = nc.dram_tensor("dbg_i", [ntok, 128], U32, kind="ExternalOutput").ap()
        dbgg_d = nc.dram_tensor("dbg_g", [ntok, 128], F32, kind="ExternalOutput").ap()
        dbga_d = nc.dram_tensor("dbg_a", [ntok, 128], F32, kind="ExternalOutput").ap()

    with ExitStack() as gst:
        S = Sched(nc, gst)
        y_t = S.dram(y_d, "y")
        if dbg:
            dbgh_t = S.dram(dbgh_d, "dbgh")
            dbgi_t = S.dram(dbgi_d, "dbgi")
            dbgg_t = S.dram(dbgg_d, "dbgg")
            dbga_t = S.dram(dbga_d, "dbga")

        ident_bf = S.sb(gst, "ident_bf", [128, 128], BF16)
        ident_f = S.sb(gst, "ident_f", [128, 128], F32)
        triL = S.sb(gst, "triL", [128, 128], F32)
        triU = S.sb(gst, "triU", [128, 128], F32)
        eye16 = S.sb(gst, "eye16", [16, 16], F32)
        ones16 = S.sb(gst, "ones16", [16, 128], F32)
        iota16 = S.sb(gst, "iota16", [128, 16], F32)
        for tt in (ident_bf, ident_f):
            S.op("pool", lambda e, tt=tt: e.memset(tt[:], 0.0), writes=[tt])
            S.op("pool", lambda e, tt=tt: e.affine_select(out=tt[:], in_=tt[:], pattern=[[-1, 128]], compare_op=ALU.not_equal,
                                                          fill=1.0, base=0, channel_multiplier=1), reads=[tt], writes=[tt])
        S.op("pool", lambda e: e.memset(eye16[:], 0.0), writes=[eye16])
        S.op("pool", lambda e: e.affine_select(out=eye16[:], in_=eye16[:], pattern=[[-1, 16]], compare_op=ALU.not_equal,
                                               fill=1.0, base=0, channel_multiplier=1), reads=[eye16], writes=[eye16])
        S.op("pool", lambda e: e.memset(ones16[:], 1.0), writes=[ones16])
        S.op("pool", lambda e: e.memset(triL[:], 1.0), writes=[triL])
        S.op("pool", lambda e: e.affine_select(out=triL[:], in_=triL[:], pattern=[[1, 128]], compare_op=ALU.is_ge,
                                               fill=0.0, base=0, channel_multiplier=-1), reads=[triL], writes=[triL])
        S.op("pool", lambda e: e.memset(triU[:], 1.0), writes=[triU])
        S.op("pool", lambda e: e.affine_select(out=triU[:], in_=triU[:], pattern=[[-1, 128]], compare_op=ALU.is_ge,
                                               fill=0.0, base=0, channel_multiplier=1), reads=[triU], writes=[triU])
        S.op("pool", lambda e: e.iota(iota16[:], pattern=[[1, 16]], base=0, channel_multiplier=0,
                                      allow_small_or_imprecise_dtypes=True), writes=[iota16])
        attnT = S.sb(gst, "attnT", [128, 4, S_LEN], BF16)
        oglaT = S.sb(gst, "oglaT", [128, 4, S_LEN], BF16)
        PB = [S.ps(gst, f"pb{i}", [128, 512], F32) for i in range(8)]

        def pbf(i):
            return PB[i][:].bitcast(BF16)

        stat = S.sb(gst, "stat", [128, 8], F32)
        DrowG = S.sb(gst, "DrowG", [16, 512], F32)

        def rstd_of(src_ap, src_ts, width, junk, col, st_=None):
            stt = stat if st_ is None else st_
            jap = junk[:, 0:width]
            S.op("act", lambda e: e.activation(out=jap, in_=src_ap, func=AF.Square, accum_out=stt[:, col:col + 1]),
                 reads=src_ts, writes=[stt, junk])
            S.op("act", lambda e: e.activation(out=stt[:, col + 1:col + 2], in_=stt[:, col:col + 1], func=AF.Ln,
                                               scale=1.0 / width, bias=EPS), reads=[stt], writes=[stt])
            S.op("act", lambda e: e.activation(out=stt[:, col + 1:col + 2], in_=stt[:, col + 1:col + 2], func=AF.Exp, scale=-0.5),
                 reads=[stt], writes=[stt])
            return stt[:, col + 1:col + 2]

        def run_pipelined(tile_fn, ntile):
            lists = []
            for n in range(ntile):
                S.defer = L = []
                tile_fn(n)
                S.defer = None
                lists.append(L)
            cur = lists[0]
            S.replay(cur, len(cur) // 2)
            for n in range(1, ntile):
                nxt = lists[n]
                half = len(nxt) // 2
                k = 0
                while cur or k < half:
                    if cur:
                        S.replay(cur, 1)
                    if k < half:
                        S.replay(nxt, 1)
                        k += 1
                cur = nxt
            S.replay(cur, len(cur))

        junk_t = S.sb(gst, "junk", [128, 1024], BF16)

        def load_norm_T(row0, gb, xt, n1, n1T, bank, st_, jk):
            S.dma("sp", lambda q: q.dma_start(out=xt[:], in_=x_d[row0:row0 + 128, :]), writes=[xt], semt=xt)
            r = rstd_of(xt[:], [xt], 1024, jk, 0, st_)
            S.op("dve", lambda e: e.scalar_tensor_tensor(out=n1[:], in0=xt[:], scalar=r, in1=gb[:], op0=ALU.mult, op1=ALU.mult),
                 reads=[xt, st_, gb], writes=[n1])
            transpose_to(n1, lambda kc: n1[:, kc * 128:(kc + 1) * 128], 8, bank, n1T, lambda: n1T[:])

        def transpose_to(src_t, src_fn, nblk, bank, dst_t, dst_fn, rows=128, eng="act"):
            pv = pbf(bank)
            for k in range(nblk):
                S.op("pe", lambda e, k=k: e.transpose(out=pv[0:rows, k * 128:(k + 1) * 128], in_=src_fn(k), identity=ident_bf[:]),
                     reads=[src_t, ident_bf], writes=[PB[bank]])
            srcv = pv[0:rows, 0:nblk * 128].rearrange("p (a b) -> p a b", a=nblk)
            if eng == "act":
                S.op("act", lambda e: e.copy(out=dst_fn(), in_=srcv), reads=[PB[bank]], writes=[dst_t])
            else:
                S.op("dve", lambda e: e.tensor_copy(out=dst_fn(), in_=srcv), reads=[PB[bank]], writes=[dst_t])

        def bcast_load(st, name, vec_d, n):
            t = S.sb(st, name, [128, n], F32, dma=True)
            S.dma("sp", lambda q: q.dma_start(out=t[:], in_=vec_d.partition_broadcast(128)), writes=[t], semt=t)
            return t

        uv_t = S.dram(uvbf_d, "uvbf")
        prepass = []

        for s in range(nseq):
            base = s * S_LEN
            with ExitStack() as p1:
                WinA = S.sb(p1, "WinA", [128, 8, 448], BF16, dma="sw")
                Wuq = S.sb(p1, "Wuq", [128, 2, 768], BF16, dma="sw")
                Wukv = S.sb(p1, "Wukv", [128, 1024], BF16, dma="sw")
                Wg = S.sb(p1, "Wg", [33, 512], F32, dma=True)
                S.dma("pool", lambda q: q.dma_start(out=WinA[:], in_=winA_d.rearrange("(c p) n -> p c n", p=128)), writes=[WinA], semt=WinA)
                S.dma("pool", lambda q: q.dma_start(out=Wuq[:], in_=wuq_d.rearrange("(c p) n -> p c n", p=128)), writes=[Wuq], semt=Wuq)
                S.dma("pool", lambda q: q.dma_start(out=Wukv[:], in_=wukv_d), writes=[Wukv], semt=Wukv)
                S.dma("sp", lambda q: q.dma_start(out=Wg[:], in_=wg_d), writes=[Wg], semt=Wg)
                g1b = bcast_load(p1, "g1b", g1_d, 1024)
                gqb = bcast_load(p1, "gqb", gq_d, 256)
                gkvb = bcast_load(p1, "gkvb", gkv_d, 128)
                cosT = S.sb(p1, "cosT", [128, NTILE, 16], F32, dma=True)
                sinT = S.sb(p1, "sinT", [128, NTILE, 16], F32, dma=True)
                S.dma("sp", lambda q: q.dma_start(out=cosT[:], in_=cos_d.rearrange("(n p) d -> p n d", p=128)), writes=[cosT], semt=cosT)
                S.dma("sp", lambda q: q.dma_start(out=sinT[:], in_=sin_d.rearrange("(n p) d -> p n d", p=128)), writes=[sinT], semt=sinT)

                KT = S.sb(p1, "KT", [128, 8, S_LEN], BF16)
                Vaug = S.sb(p1, "Vaug", [128, NTILE, 8, 65], BF16)
                cqT = S.sb(p1, "cqT", [128, 2, S_LEN], BF16)
                TTs = S.sb(p1, "TTs", [128, 4, NTILE], F32)
                S.op("pool", lambda e: e.memset(Vaug[:], 1.0), writes=[Vaug])
                with ExitStack() as p1a:
                    def mk(name, shape, dt, **kw):
                        return [S.sb(p1a, f"{name}{i}", shape, dt, **kw) for i in range(2)]
                    xtP = mk("xt", [128, 1024], F32, dma=True)
                    n1P = mk("n1", [128, 1024], BF16)
                    n1TP = mk("n1T", [128, 8, 128], BF16)
                    paP = mk("pa", [128, 448], F32)
                    cqnP = mk("cqn", [128, 384], BF16)
                    latP = mk("latT", [128, 3, 128], BF16)
                    glrAP = mk("glrA", [33, 128], F32)
                    KtP = mk("Kt", [128, 8, 96], BF16)
                    rpP = mk("rp", [128, 4, 16], F32)
                    spP = mk("sp", [128, 512], F32)
                    krrP = mk("krr", [128, 32], BF16)
                    stP = mk("st", [128, 8], F32)
                    jkP = mk("jk", [128, 1024], BF16)
                    for i in range(2):
                        S.op("pool", lambda e, i=i: e.memset(glrAP[i][:], 1.0), writes=[glrAP[i]])

                    def a1_tile(n):
                        p = n % 2
                        xt, n1, n1T, pa, cqn, lat, glrA, Kt, rp, sp_t, krr, st_, jk = (xtP[p], n1P[p], n1TP[p], paP[p], cqnP[p], latP[p], glrAP[p],
                                                                                      KtP[p], rpP[p], spP[p], krrP[p], stP[p], jkP[p])
                        X0, X1, X2, X3 = 4 * p, 4 * p + 1, 4 * p + 2, 4 * p + 3
                        load_norm_T(base + n * 128, g1b, xt, n1, n1T, X0, st_, jk)
                        for kc in range(8):
                            S.op("pe", lambda e, kc=kc: e.matmul(PB[X1][:, 0:448], lhsT=n1T[:, kc, :], rhs=WinA[:, kc, :], start=(kc == 0), stop=(kc == 7)),
                                 reads=[n1T, WinA], writes=[PB[X1]])
                        S.op("act", lambda e: e.copy(out=pa[:], in_=PB[X1][:, 0:448]), reads=[PB[X1]], writes=[pa])
                        r = rstd_of(pa[:, 0:256], [pa], 256, jk, 2, st_)
                        S.op("dve", lambda e: e.scalar_tensor_tensor(out=cqn[:, 0:256], in0=pa[:, 0:256], scalar=r, in1=gqb[:], op0=ALU.mult, op1=ALU.mult),
                             reads=[pa, st_, gqb], writes=[cqn])
                        r2 = rstd_of(pa[:, 256:384], [pa], 128, jk, 4, st_)
                        S.op("dve", lambda e: e.scalar_tensor_tensor(out=cqn[:, 256:384], in0=pa[:, 256:384], scalar=r2, in1=gkvb[:], op0=ALU.mult, op1=ALU.mult),
                             reads=[pa, st_, gkvb], writes=[cqn])
                        transpose_to(cqn, lambda k: cqn[:, k * 128:(k + 1) * 128], 3, X0, lat, lambda: lat[:], eng="dve")
                        S.op("dve", lambda e: e.tensor_copy(out=cqT[:, :, n * 128:(n + 1) * 128], in_=lat[:, 0:2, :]), reads=[lat], writes=[cqT])
                        for j in range(2):
                            bk = X2 + j
                            S.op("pe", lambda e, j=j, bk=bk: e.matmul(PB[bk][:, :], lhsT=lat[:, 2, :], rhs=Wukv[:, j * 512:(j + 1) * 512], start=True, stop=True),
                                 reads=[lat, Wukv], writes=[PB[bk]])
                            kv = PB[bk][:, :].rearrange("p (h c) -> p h c", h=4)
                            S.op("act", lambda e, j=j, kv=kv, bk=bk: e.copy(out=Vaug[:, n, 4 * j:4 * j + 4, 0:64], in_=kv[:, :, 64:128]),
                                 reads=[PB[bk]], writes=[Vaug])
                            S.op("dve", lambda e, j=j, kv=kv, bk=bk: e.tensor_copy(out=Kt[:, 4 * j:4 * j + 4, 0:64], in_=kv[:, :, 0:64]),
                                 reads=[PB[bk]], writes=[Kt])
                        c_ = cosT[:, n, :]
                        s_ = sinT[:, n, :]
                        x1, x2 = pa[:, 384:400], pa[:, 400:416]
                        S.op("dve", lambda e: e.tensor_tensor(out=rp[:, 0, :], in0=x1, in1=c_, op=ALU.mult), reads=[pa, cosT], writes=[rp])
                        S.op("dve", lambda e: e.tensor_tensor(out=rp[:, 1, :], in0=x2, in1=s_, op=ALU.mult), reads=[pa, sinT], writes=[rp])
                        S.op("dve", lambda e: e.tensor_tensor(out=rp[:, 2, :], in0=x2, in1=c_, op=ALU.mult), reads=[pa, cosT], writes=[rp])
                        S.op("dve", lambda e: e.tensor_tensor(out=rp[:, 3, :], in0=x1, in1=s_, op=ALU.mult), reads=[pa, sinT], writes=[rp])
                        S.op("dve", lambda e: e.tensor_tensor(out=krr[:, 0:16], in0=rp[:, 0, :], in1=rp[:, 1, :], op=ALU.subtract), reads=[rp], writes=[krr])
                        S.op("dve", lambda e: e.tensor_tensor(out=krr[:, 16:32], in0=rp[:, 2, :], in1=rp[:, 3, :], op=ALU.add), reads=[rp], writes=[krr])
                        S.op("dve", lambda e: e.tensor_copy(out=Kt[:, :, 64:96], in_=krr[:].unsqueeze(1).to_broadcast([128, 8, 32])),
                             reads=[krr], writes=[Kt])
                        transpose_to(Kt, lambda h: Kt[:, h, :], 8, X0, KT, lambda: KT[0:96, :, n * 128:(n + 1) * 128], rows=96)
                        S.op("pe", lambda e: e.transpose(out=PB[X1][0:32, 0:128], in_=pa[:, 416:448], identity=ident_f[:]),
                             reads=[pa, ident_f], writes=[PB[X1]])
                        S.op("act", lambda e: e.copy(out=glrA[0:32, :], in_=PB[X1][0:32, 0:128]), reads=[PB[X1]], writes=[glrA])
                        S.op("pe", lambda e: e.matmul(PB[X2][:, :], lhsT=glrA[:, :], rhs=Wg[:, :], start=True, stop=True),
                             reads=[glrA, Wg], writes=[PB[X2]])
                        S.op("act", lambda e: e.activation(out=sp_t[:], in_=PB[X2][:, :], func=AF.Exp, scale=-1.0), reads=[PB[X2]], writes=[sp_t])
                        S.op("act", lambda e: e.activation(out=sp_t[:], in_=sp_t[:], func=AF.Ln, bias=1.0), reads=[sp_t], writes=[sp_t])
                        for cc in range(4):
                            S.op("pe", lambda e, cc=cc: e.matmul(PB[X1][:, 256 + cc:257 + cc], lhsT=sp_t[:, cc * 128:(cc + 1) * 128], rhs=triU[:, 0:1],
                                                                 start=True, stop=True), reads=[sp_t, triU], writes=[PB[X1]])
                        S.op("dve", lambda e: e.tensor_copy(out=TTs[:, :, n], in_=PB[X1][:, 256:260]), reads=[PB[X1]], writes=[TTs])

                    run_pipelined(a1_tile, NTILE)
                    S.barrier()
                if stop == "A1":
                    S.barrier()
                    return nc
                Inc = S.sb(p1, "Inc", [128, 4, NTILE], F32)
                Dc = S.sb(p1, "Dc", [128, 4, NTILE], F32)
                S.op("dve", lambda e: e.tensor_copy(out=Inc[:, :, 0:1], in_=TTs[:, :, 0:1]), reads=[TTs], writes=[Inc])
                for n in range(1, NTILE):
                    S.op("dve", lambda e, n=n: e.tensor_tensor(out=Inc[:, :, n:n + 1], in0=Inc[:, :, n - 1:n], in1=TTs[:, :, n:n + 1], op=ALU.add),
                         reads=[Inc, TTs], writes=[Inc])
                S.op("dve", lambda e: e.tensor_tensor(out=Dc[:, 0:2, :], in0=Inc[:, 0:2, :], in1=TTs[:, 0:2, :], op=ALU.subtract), reads=[Inc, TTs], writes=[Dc])
                S.op("dve", lambda e: e.tensor_tensor(out=Dc[:, 0:2, :], in0=Dc[:, 0:2, :], in1=Inc[:, 0:2, 7:8].to_broadcast([128, 2, NTILE]), op=ALU.subtract),
                     reads=[Dc, Inc], writes=[Dc])
                S.op("dve", lambda e: e.tensor_tensor(out=Dc[:, 2:4, :], in0=Inc[:, 2:4, 7:8].to_broadcast([128, 2, NTILE]), in1=Inc[:, 2:4, :], op=ALU.subtract),
                     reads=[Inc], writes=[Dc])
                for cc in range(4):
                    S.op("pe", lambda e, cc=cc: e.transpose(out=PB[3][0:16, cc * 128:(cc + 1) * 128], in_=Dc[:, cc, :], identity=ident_f[:]),
                         reads=[Dc, ident_f], writes=[PB[3]])
                S.op("act", lambda e: e.copy(out=DrowG[:], in_=PB[3][0:16, :]), reads=[PB[3]], writes=[DrowG])

                if stop == "A1b":
                    S.barrier()
                    return nc
                if s == 0:
                    cb = [S.sb(p1, f"cb{i}", [128, 4, 1024], BF16, dma="sw") for i in range(3)]
                    S.defer = prepass
                    k = 0
                    for (src, c0) in ((u_d, 0), (v_d, 1024)):
                        for ch in range(32):
                            b_ = cb[k % 3]
                            k += 1
                            S.dma("pool", lambda q, b_=b_, src=src, ch=ch: q.dma_start(out=b_[:], in_=src[ch * 512:(ch + 1) * 512, :].rearrange("(p r) d -> p r d", r=4)),
                                  writes=[b_], semt=b_)
                            S.dma("sp", lambda q, b_=b_, c0=c0, ch=ch: q.dma_start(out=uvbf_d[ch * 512:(ch + 1) * 512, c0:c0 + 1024].rearrange("(p r) d -> p r d", r=4), in_=b_[:]),
                                  reads=[b_], writes=[uv_t], semt=uv_t)
                    S.defer = None
                Qt = S.sb(p1, "Qt", [128, 8, 96], BF16)
                rpq = S.sb(p1, "rpq", [128, 4, 4, 16], F32)
                QTc = S.sb(p1, "QTc", [128, 8, 512], BF16)
                Eb = [S.sb(p1, f"Eb{i}", [128, 512], BF16) for i in range(2)]
                Oacc = S.sb(p1, "Oacc", [128, 4, 8, 65], F32)
                rc = S.sb(p1, "rc", [128, 4, 8], F32)
                atok = S.sb(p1, "atok", [128, 4, 512], BF16)
                sc = 96.0 ** -0.5
                for qc in range(4):
                    for i in range(4):
                        n = qc * 4 + i
                        for j in range(2):
                            for kc in range(2):
                                S.op("pe", lambda e, j=j, kc=kc, n=n: e.matmul(PB[6 + j][:, 0:384], lhsT=cqT[:, kc, n * 128:(n + 1) * 128],
                                                                                 rhs=Wuq[:, kc, j * 384:(j + 1) * 384], start=(kc == 0), stop=(kc == 1)),
                                     reads=[cqT, Wuq], writes=[PB[6 + j]])
                            qv = PB[6 + j][:, 0:384].rearrange("p (h c) -> p h c", h=4)
                            S.op("act", lambda e, j=j, qv=qv: e.copy(out=Qt[:, 4 * j:4 * j + 4, 0:64], in_=qv[:, :, 0:64]), reads=[PB[6 + j]], writes=[Qt])
                            cb_ = cosT[:, n, :].unsqueeze(1).to_broadcast([128, 4, 16])
                            sb2_ = sinT[:, n, :].unsqueeze(1).to_broadcast([128, 4, 16])
                            x1, x2 = qv[:, :, 64:80], qv[:, :, 80:96]
                            bkq = PB[6 + j]
                            S.op("dve", lambda e, x1=x1, cb_=cb_: e.tensor_tensor(out=rpq[:, 0], in0=x1, in1=cb_, op=ALU.mult), reads=[bkq, cosT], writes=[rpq])
                            S.op("dve", lambda e, x2=x2, sb2_=sb2_: e.tensor_tensor(out=rpq[:, 1], in0=x2, in1=sb2_, op=ALU.mult), reads=[bkq, sinT], writes=[rpq])
                            S.op("dve", lambda e, x2=x2, cb_=cb_: e.tensor_tensor(out=rpq[:, 2], in0=x2, in1=cb_, op=ALU.mult), reads=[bkq, cosT], writes=[rpq])
                            S.op("dve", lambda e, x1=x1, sb2_=sb2_: e.tensor_tensor(out=rpq[:, 3], in0=x1, in1=sb2_, op=ALU.mult), reads=[bkq, sinT], writes=[rpq])
                            S.op("dve", lambda e, j=j: e.tensor_tensor(out=Qt[:, 4 * j:4 * j + 4, 64:80], in0=rpq[:, 0], in1=rpq[:, 1], op=ALU.subtract), reads=[rpq], writes=[Qt])
                            S.op("dve", lambda e, j=j: e.tensor_tensor(out=Qt[:, 4 * j:4 * j + 4, 80:96], in0=rpq[:, 2], in1=rpq[:, 3], op=ALU.add), reads=[rpq], writes=[Qt])
                        transpose_to(Qt, lambda h: Qt[:, h, :], 8, 0, QTc, lambda i=i: QTc[0:96, :, i * 128:(i + 1) * 128], rows=96)
                    jobsB = [(h, kt) for h in range(8) for kt in range(NTILE)]

                    def b_score(i):
                        h, kt = jobsB[i]
                        sb_ = i % 2
                        if prepass:
                            S.replay(prepass, 1)
                        S.op("pe", lambda e: e.matmul(PB[sb_][:, :], lhsT=KT[0:96, h, kt * 128:(kt + 1) * 128], rhs=QTc[0:96, h, :],
                                                      start=True, stop=True), reads=[KT, QTc], writes=[PB[sb_]])
                        S.op("act", lambda e: e.activation(out=Eb[sb_][:], in_=PB[sb_][:, :], func=AF.Exp, scale=sc),
                             reads=[PB[sb_]], writes=[Eb[sb_]])

                    def b_pv(i):
                        h, kt = jobsB[i]
                        sb_ = i % 2
                        pv0 = 2 + 2 * (h % 2)
                        for qt in range(4):
                            bk = pv0 + qt // 2
                            c0 = (qt % 2) * 128
                            S.op("pe", lambda e, qt=qt, bk=bk, c0=c0: e.matmul(
                                PB[bk][:, c0:c0 + 65], lhsT=Eb[sb_][:, qt * 128:(qt + 1) * 128], rhs=Vaug[:, kt, h, :],
                                start=(kt == 0 and qt % 2 == 0), stop=(kt == NTILE - 1), skip_group_check=True),
                                reads=[Eb[sb_], Vaug], writes=[PB[bk]])
                        if kt == NTILE - 1:
                            for half in range(2):
                                bk = pv0 + half
                                src = PB[bk][:, 0:256].rearrange("p (a c) -> p a c", a=2)[:, :, 0:65]
                                S.op("dve", lambda e, half=half, src=src: e.tensor_copy(out=Oacc[:, 2 * half:2 * half + 2, h, :], in_=src),
                                     reads=[PB[bk]], writes=[Oacc])

                    for i in range(len(jobsB) + 1):
                        if i < len(jobsB):
                            b_score(i)
                        if i >= 1:
                            b_pv(i - 1)
                    S.op("dve", lambda e: e.reciprocal(out=rc[:], in_=Oacc[:, :, :, 64]), reads=[Oacc], writes=[rc])
                    for i in range(4):
                        S.op("dve", lambda e, i=i: e.tensor_tensor(out=atok[:, i, :].rearrange("p (h c) -> p h c", h=8), in0=Oacc[:, i, :, 0:64],
                                                                   in1=rc[:, i, :].unsqueeze(2).to_broadcast([128, 8, 64]), op=ALU.mult),
                             reads=[Oacc, rc], writes=[atok])
                        n = qc * 4 + i
                        transpose_to(atok, lambda c, i=i: atok[:, i, c * 128:(c + 1) * 128], 4, 6 + (i % 2), attnT,
                                     lambda n=n: attnT[:, :, n * 128:(n + 1) * 128], eng="dve")
                if prepass:
                    S.replay(prepass, len(prepass))
                S.barrier()
            if stop == "B":
                return nc

            with ExitStack() as p2:
                qdT = S.sb(p2, "qdT", [128, 4, S_LEN], BF16)
                kdT = S.sb(p2, "kdT", [128, 4, S_LEN], BF16)
                GV = S.sb(p2, "GV", [128, NTILE, 512], BF16)
                SG = S.sb(p2, "SG", [128, NTILE, 512], BF16)
                glab = bcast_load(p2, "glab", gla_d, 128)
                with ExitStack() as p2a:
                    WinB = S.sb(p2a, "WinB", [128, 8, 1568], BF16, dma="sw")
                    Wg = S.sb(p2a, "Wg2", [33, 512], F32, dma=True)
                    S.dma("pool", lambda q: q.dma_start(out=WinB[:], in_=winB_d.rearrange("(c p) n -> p c n", p=128)), writes=[WinB], semt=WinB)
                    S.dma("sp", lambda q: q.dma_start(out=Wg[:], in_=wg_d), writes=[Wg], semt=Wg)
                    g1b = bcast_load(p2a, "g1b2", g1_d, 1024)

                    def mk(name, shape, dt, **kw):
                        return [S.sb(p2a, f"{name}{i}", shape, dt, **kw) for i in range(2)]
                    xtP = mk("xt2", [128, 1024], F32, dma=True)
                    n1P = mk("n12", [128, 1024], BF16)
                    n1TP = mk("n1T2", [128, 8, 128], BF16)
                    glrP = mk("glr", [128, 32], F32)
                    glrAP = mk("glrA2", [33, 128], F32)
                    spP = mk("sp2", [128, 512], F32)
                    EpP = mk("Ep", [128, 512], F32)
                    EmP = mk("Em", [128, 512], F32)
                    qdP = mk("qd", [128, 512], BF16)
                    kdP = mk("kd", [128, 512], BF16)
                    DrMP = mk("DrM", [16, 512], F32)
                    stP = mk("st2", [128, 8], F32)
                    jkP = mk("jk2", [128, 1024], BF16)
                    for i in range(2):
                        S.op("pool", lambda e, i=i: e.memset(glrAP[i][:], 1.0), writes=[glrAP[i]])

                    def a2_tile(n):
                        p = n % 2
                        xt, n1, n1T, glr, glrA, sp_t, Ep, Em, qd, kd, DrM, st_, jk = (xtP[p], n1P[p], n1TP[p], glrP[p], glrAP[p], spP[p], EpP[p], EmP[p],
                                                                                      qdP[p], kdP[p], DrMP[p], stP[p], jkP[p])
                        X0, X1, X2, X3 = 4 * p, 4 * p + 1, 4 * p + 2, 4 * p + 3
                        load_norm_T(base + n * 128, g1b, xt, n1, n1T, X0, st_, jk)
                        for (bk, c0, w) in ((X1, 0, 512), (X2, 512, 512), (X3, 1024, 512)):
                            for kc in range(8):
                                S.op("pe", lambda e, kc=kc, bk=bk, c0=c0, w=w: e.matmul(PB[bk][:, 0:w], lhsT=n1T[:, kc, :], rhs=WinB[:, kc, c0:c0 + w],
                                                                                         start=(kc == 0), stop=(kc == 7)), reads=[n1T, WinB], writes=[PB[bk]])
                        S.op("act", lambda e: e.copy(out=GV[:, n, :], in_=PB[X2][:, :]), reads=[PB[X2]], writes=[GV])
                        S.op("act", lambda e: e.activation(out=SG[:, n, :], in_=PB[X3][:, :], func=AF.Silu), reads=[PB[X3]], writes=[SG])
                        for kc in range(8):
                            S.op("pe", lambda e, kc=kc: e.matmul(PB[X2][:, 0:32], lhsT=n1T[:, kc, :], rhs=WinB[:, kc, 1536:1568],
                                                                 start=(kc == 0), stop=(kc == 7)), reads=[n1T, WinB], writes=[PB[X2]])
                        S.op("dve", lambda e: e.tensor_copy(out=glr[:], in_=PB[X2][:, 0:32]), reads=[PB[X2]], writes=[glr])
                        S.op("pe", lambda e: e.transpose(out=PB[X3][0:32, 0:128], in_=glr[:], identity=ident_f[:]), reads=[glr, ident_f], writes=[PB[X3]])
                        S.op("act", lambda e: e.copy(out=glrA[0:32, :], in_=PB[X3][0:32, 0:128]), reads=[PB[X3]], writes=[glrA])
                        S.op("pe", lambda e: e.matmul(PB[X2][:, :], lhsT=glrA[:, :], rhs=Wg[:, :], start=True, stop=True), reads=[glrA, Wg], writes=[PB[X2]])
                        S.op("act", lambda e: e.activation(out=sp_t[:], in_=PB[X2][:, :], func=AF.Exp, scale=-1.0), reads=[PB[X2]], writes=[sp_t])
                        S.op("act", lambda e: e.activation(out=sp_t[:], in_=sp_t[:], func=AF.Ln, bias=1.0), reads=[sp_t], writes=[sp_t])
                        S.op("dve", lambda e: e.tensor_scalar_mul(out=DrM[:], in0=DrowG[:], scalar1=eye16[:, n:n + 1]), reads=[DrowG, eye16], writes=[DrM])
                        S.op("pe", lambda e: e.matmul(PB[X3][:, 0:256], lhsT=triL[:], rhs=sp_t[:, 0:256], start=True, stop=False, skip_group_check=True),
                             reads=[triL, sp_t], writes=[PB[X3]])
                        S.op("pe", lambda e: e.matmul(PB[X3][:, 256:512], lhsT=triU[:], rhs=sp_t[:, 256:512], start=False, stop=False, skip_group_check=True),
                             reads=[triU, sp_t], writes=[PB[X3]])
                        S.op("pe", lambda e: e.matmul(PB[X3][:, :], lhsT=ones16[:], rhs=DrM[:], start=False, stop=True, skip_group_check=True),
                             reads=[ones16, DrM], writes=[PB[X3]])
                        S.op("act", lambda e: e.activation(out=Ep[:], in_=PB[X3][:, :], func=AF.Exp, scale=-1.0 / 16), reads=[PB[X3]], writes=[Ep])
                        S.op("act", lambda e: e.activation(out=Em[:], in_=PB[X3][:, :], func=AF.Exp, scale=1.0 / 16), reads=[PB[X3]], writes=[Em])
                        gqv = PB[X1][:, 0:256].unsqueeze(1).to_broadcast([128, 2, 256])
                        gkv_ = PB[X1][:, 256:512].unsqueeze(1).to_broadcast([128, 2, 256])
                        S.op("dve", lambda e: e.scalar_tensor_tensor(out=qd[:].rearrange("p (d c) -> p d c", d=2), in0=gqv, scalar=0.125,
                                                                     in1=Ep[:].rearrange("p (d c) -> p d c", d=2), op0=ALU.mult, op1=ALU.mult),
                             reads=[PB[X1], Ep], writes=[qd])
                        S.op("dve", lambda e: e.tensor_tensor(out=kd[:].rearrange("p (d c) -> p d c", d=2), in0=gkv_,
                                                              in1=Em[:].rearrange("p (d c) -> p d c", d=2), op=ALU.mult),
                             reads=[PB[X1], Em], writes=[kd])
                        transpose_to(qd, lambda k: qd[:, k * 128:(k + 1) * 128], 4, X0, qdT, lambda: qdT[:, :, n * 128:(n + 1) * 128], eng="dve")
                        transpose_to(kd, lambda k: kd[:, k * 128:(k + 1) * 128], 4, X2, kdT, lambda: kdT[:, :, n * 128:(n + 1) * 128], eng="act")

                    run_pipelined(a2_tile, NTILE)
                    S.barrier()
                if stop == "A2":
                    return nc

                with ExitStack() as p2c:
                    mf = S.sb(p2c, "mf", [128, 4, 512], BF16)
                    mb = S.sb(p2c, "mb", [128, 4, 512], BF16)
                    S.op("pool", lambda e: e.memset(mf[:], 1.0), writes=[mf])
                    S.op("pool", lambda e: e.memset(mb[:], 1.0), writes=[mb])
                    for r in range(4):
                        S.op("pool", lambda e, r=r: e.affine_select(out=mf[:, r, :], in_=mf[:, r, :], pattern=[[1, 512]], compare_op=ALU.is_ge, fill=0.0,
                                                                    base=-128 * r, channel_multiplier=-1), reads=[mf], writes=[mf])
                        S.op("pool", lambda e, r=r: e.affine_select(out=mb[:, r, :], in_=mb[:, r, :], pattern=[[-1, 512]], compare_op=ALU.is_ge, fill=0.0,
                                                                    base=128 * r, channel_multiplier=1), reads=[mb], writes=[mb])
                    At = [S.sb(p2c, f"At{i}", [128, 512], BF16) for i in range(3)]
                    Ogc = S.sb(p2c, "Ogc", [128, 4, 4, 128], F32)
                    sq = S.sb(p2c, "sq", [128, 16, 128], F32)
                    ssq = S.sb(p2c, "ssq", [128, 16], F32)
                    ogt = S.sb(p2c, "ogt", [128, 4, 512], BF16)
                    for tc in range(4):
                        jobsC = []
                        for h in range(4):
                            jl = [(h, 0, jt) for jt in range(0, 4 * tc + 4)] + [(h, 1, jt) for jt in range(4 * tc, NTILE)]
                            jobsC += [(h, d, jt, k == 0, k == len(jl) - 1) for k, (h, d, jt) in enumerate(jl)]

                        def c_score(i):
                            h, d, jt, isf, isl = jobsC[i]
                            sbk = i % 2
                            hp, hl = h // 2, (h % 2) * 64
                            blk = d * 2 + hp
                            S.op("pe", lambda e: e.matmul(
                                PB[sbk][:, :], lhsT=kdT[hl:hl + 64, blk, jt * 128:(jt + 1) * 128], rhs=qdT[hl:hl + 64, blk, tc * 512:(tc + 1) * 512],
                                start=True, stop=True), reads=[kdT, qdT], writes=[PB[sbk]])
                            A = At[i % 3]
                            r = jt - 4 * tc
                            if 0 <= r < 4:
                                mk = mf if d == 0 else mb
                                S.op("dve", lambda e: e.tensor_tensor(out=A[:], in0=PB[sbk][:, :], in1=mk[:, r, :], op=ALU.mult),
                                     reads=[PB[sbk], mk], writes=[A])
                            else:
                                S.op("act", lambda e: e.copy(out=A[:], in_=PB[sbk][:, :]), reads=[PB[sbk]], writes=[A])

                        def c_pv(i):
                            h, d, jt, isf, isl = jobsC[i]
                            ob = 4 + (h % 2)
                            A = At[i % 3]
                            first = isf
                            for ii in range(4):
                                tt_ = 4 * tc + ii
                                if (d == 0 and jt > tt_) or (d == 1 and jt < tt_):
                                    continue
                                S.op("pe", lambda e, ii=ii, first=first: e.matmul(
                                    PB[ob][:, ii * 128:(ii + 1) * 128], lhsT=A[:, ii * 128:(ii + 1) * 128], rhs=GV[:, jt, h * 128:(h + 1) * 128],
                                    start=first, stop=False, skip_group_check=True), reads=[A, GV], writes=[PB[ob]])
                                first = False
                            if isl:
                                S.op("act", lambda e: e.copy(out=Ogc[:, :, h, :], in_=PB[ob][:, :].rearrange("p (i c) -> p i c", i=4)),
                                     reads=[PB[ob]], writes=[Ogc])

                        for i in range(len(jobsC) + 1):
                            if i < len(jobsC):
                                c_score(i)
                            if i >= 1:
                                c_pv(i - 1)
                        og2 = Ogc[:].rearrange("p i h c -> p (i h) c")
                        S.op("dve", lambda e: e.tensor_tensor(out=sq[:], in0=og2, in1=og2, op=ALU.mult), reads=[Ogc], writes=[sq])
                        S.op("dve", lambda e: e.tensor_reduce(out=ssq[:], in_=sq[:], axis=AX.X, op=ALU.add), reads=[sq], writes=[ssq])
                        S.op("act", lambda e: e.activation(out=ssq[:], in_=ssq[:], func=AF.Sqrt, scale=1.0 / 128, bias=EPS), reads=[ssq], writes=[ssq])
                        S.op("dve", lambda e: e.reciprocal(out=ssq[:], in_=ssq[:]), reads=[ssq], writes=[ssq])
                        S.op("dve", lambda e: e.tensor_tensor(out=sq[:], in0=og2, in1=ssq[:].unsqueeze(2).to_broadcast([128, 16, 128]), op=ALU.mult),
                             reads=[Ogc, ssq], writes=[sq])
                        S.op("dve", lambda e: e.tensor_tensor(out=sq[:], in0=sq[:], in1=glab[:].unsqueeze(1).to_broadcast([128, 16, 128]), op=ALU.mult),
                             reads=[sq, glab], writes=[sq])
                        S.op("dve", lambda e, tc=tc: e.tensor_tensor(out=ogt[:].rearrange("p i c -> p (i c)"), in0=sq[:].rearrange("p a c -> p (a c)"),
                                                                     in1=SG[:, 4 * tc:4 * tc + 4, :].rearrange("p i c -> p (i c)"), op=ALU.mult),
                             reads=[sq, SG], writes=[ogt])
                        for i in range(4):
                            n = 4 * tc + i
                            transpose_to(ogt, lambda c, i=i: ogt[:, i, c * 128:(c + 1) * 128], 4, 6 + (i % 2), oglaT,
                                         lambda n=n: oglaT[:, :, n * 128:(n + 1) * 128], eng="act")
                    S.barrier()

            if stop == "C":
                return nc
            with ExitStack() as p3:
                Wo = S.sb(p3, "Wo", [128, 8, 1024], BF16, dma="sw")
                Wq = S.sb(p3, "Wq", [128, 8, 2048], BF16, dma="sw")
                SubT = S.sb(p3, "SubT", [128, 16, 128], BF16, dma="sw")
                S.dma("pool", lambda q: q.dma_start(out=Wo[:], in_=wo_d.rearrange("(c p) n -> p c n", p=128)), writes=[Wo], semt=Wo)
                S.dma("pool", lambda q: q.dma_start(out=Wq[:], in_=wq_d.rearrange("(c p) n -> p c n", p=128)), writes=[Wq], semt=Wq)
                S.dma("pool", lambda q: q.dma_start(out=SubT[:], in_=subk_d), writes=[SubT], semt=SubT)
                g2b = bcast_load(p3, "g2b", g2_d, 1024)
                gfb = bcast_load(p3, "gfb", gf_d, 1024)
                htP = [S.sb(p3, f"ht{i}", [128, 1024], F32, dma=True) for i in range(2)]
                hnP = [S.sb(p3, f"hn{i}", [128, 1024], F32) for i in range(2)]
                idxP = [S.sb(p3, f"idxu{i}", [128, 128], U32) for i in range(2)]
                gateP = [S.sb(p3, f"gate{i}", [128, 8, 16], F32) for i in range(2)]
                hnb = S.sb(p3, "hnb", [128, 1024], BF16)
                hnT = S.sb(p3, "hnT", [128, 8, 128], BF16)
                qT = S.sb(p3, "qT", [128, 16, 128], BF16)
                ssb = S.sb(p3, "ssb", [128, 16, 128], F32)
                wk = S.sb(p3, "wk", [128, 256], F32)
                m16 = S.sb(p3, "m16", [128, 16, 16], F32)
                i16 = S.sb(p3, "i16", [128, 16, 16], U32)
                i16f = S.sb(p3, "i16f", [128, 16, 16], F32)
                cand = S.sb(p3, "cand", [128, 8, 256], F32)
                tops = S.sb(p3, "tops", [128, 8, 16], F32)
                pos = S.sb(p3, "pos", [128, 8, 16], U32)
                pa_ = S.sb(p3, "posa", [128, 8, 16], U32)
                pb_ = S.sb(p3, "posb", [128, 8, 16], U32)
                paf = S.sb(p3, "paf", [128, 8, 16], F32)
                pbf_ = S.sb(p3, "pbf", [128, 8, 16], F32)
                eq = S.sb(p3, "eq", [128, 8, 16, 16], BF16)
                sel1 = S.sb(p3, "sel1", [128, 8, 16], F32)
                sel2 = S.sb(p3, "sel2", [128, 8, 16], F32)
                idxf = S.sb(p3, "idxf", [128, 128], F32)
                gsum = S.sb(p3, "gsum", [128, 8], F32)
                stat2 = S.sb(p3, "stat2", [128, 8], F32)
                junk_a = S.sb(p3, "junk_a", [128, 1024], BF16)
                junkD = [S.sb(p3, f"junkD{i}", [128, 1024], BF16) for i in range(2)]
                actv = S.sb(p3, "actv", [128, 128], F32)
                glt = S.sb(p3, "glt", [128, 128], F32)
                wct = S.sb(p3, "wct", [128, 128], F32)
                actC = [T(actv[:, i:i + 1], f"actc{i}") for i in range(128)]
                glC = [T(glt[:, i:i + 1], f"glc{i}") for i in range(128)]
                NUV = 10
                UV = [S.sb(p3, f"UV{i}", [128, 2048], BF16, dma="sw") for i in range(NUV)]
                NDG = 4
                Dg = [S.sb(p3, f"Dg{i}", [128, 128], BF16) for i in range(NDG)]

                def rstd2(src_t, col):
                    S.op("act", lambda e: e.activation(out=junk_a[:], in_=src_t[:], func=AF.Square, accum_out=stat2[:, col:col + 1]),
                         reads=[src_t], writes=[stat2, junk_a])
                    S.op("act", lambda e: e.activation(out=stat2[:, col + 1:col + 2], in_=stat2[:, col:col + 1], func=AF.Sqrt,
                                                       scale=1.0 / 1024, bias=EPS), reads=[stat2], writes=[stat2])
                    S.op("dve", lambda e: e.reciprocal(out=stat2[:, col + 1:col + 2], in_=stat2[:, col + 1:col + 2]), reads=[stat2], writes=[stat2])
                    return stat2[:, col + 1:col + 2]

                def top16(src_ap, src_t, n_el, mv, iv, mvt, ivt):
                    S.op("dve", lambda e: e.max(out=mv[:, 0:8], in_=src_ap), reads=[src_t], writes=[mvt])
                    S.op("dve", lambda e: e.max_index(out=iv[:, 0:8], in_max=mv[:, 0:8], in_values=src_ap), reads=[src_t, mvt], writes=[ivt])
                    S.op("dve", lambda e: e.match_replace(out=wk[:, 0:n_el], in_to_replace=mv[:, 0:8], in_values=src_ap, imm_value=-1e30),
                         reads=[src_t, mvt], writes=[wk])
                    S.op("dve", lambda e: e.max(out=mv[:, 8:16], in_=wk[:, 0:n_el]), reads=[wk], writes=[mvt])
                    S.op("dve", lambda e: e.max_index(out=iv[:, 8:16], in_max=mv[:, 8:16], in_values=wk[:, 0:n_el]), reads=[wk, mvt], writes=[ivt])

                def emit_d12(n):
                    par = n % 2
                    ht, hn, idxu, gate = htP[par], hnP[par], idxP[par], gateP[par]
                    r0 = base + n * 128
                    S.dma("sp", lambda q: q.dma_start(out=ht[:], in_=x_d[r0:r0 + 128, :]), writes=[ht], semt=ht)
                    for j in range(2):
                        for c in range(8):
                            src = attnT if c < 4 else oglaT
                            S.op("pe", lambda e, j=j, c=c, src=src: e.matmul(PB[j][:, :], lhsT=src[:, c % 4, n * 128:(n + 1) * 128],
                                                                             rhs=Wo[:, c, j * 512:(j + 1) * 512], start=(c == 0), stop=(c == 7)),
                                 reads=[src, Wo], writes=[PB[j]])
                        S.op("dve", lambda e, j=j: e.tensor_tensor(out=ht[:, j * 512:(j + 1) * 512], in0=PB[j][:, :], in1=ht[:, j * 512:(j + 1) * 512], op=ALU.add),
                             reads=[PB[j], ht], writes=[ht])
                    if dbg:
                        S.dma("sp", lambda q: q.dma_start(out=dbgh_d[r0:r0 + 128, :], in_=ht[:]), reads=[ht], writes=[dbgh_t], semt=dbgh_t)
                    r = rstd2(ht, 0)
                    S.op("dve", lambda e: e.scalar_tensor_tensor(out=hn[:], in0=ht[:], scalar=r, in1=g2b[:], op0=ALU.mult, op1=ALU.mult),
                         reads=[ht, stat2, g2b], writes=[hn])
                    S.op("act", lambda e: e.copy(out=hnb[:], in_=hn[:]), reads=[hn], writes=[hnb])
                    transpose_to(hnb, lambda kc: hnb[:, kc * 128:(kc + 1) * 128], 8, 2, hnT, lambda: hnT[:])
                    for g4 in range(4):
                        bk = 3 + (g4 % 2)
                        for q4 in range(4):
                            hp = g4 * 4 + q4
                            for kc in range(8):
                                S.op("pe", lambda e, hp=hp, kc=kc, bk=bk, q4=q4: e.matmul(PB[bk][:, q4 * 128:(q4 + 1) * 128], lhsT=Wq[:, kc, hp * 128:(hp + 1) * 128],
                                                                                            rhs=hnT[:, kc, :], start=(kc == 0), stop=(kc == 7)),
                                     reads=[Wq, hnT], writes=[PB[bk]])
                        S.op("act", lambda e, g4=g4, bk=bk: e.copy(out=qT[:, g4 * 4:(g4 + 1) * 4, :].rearrange("p a b -> p (a b)"), in_=PB[bk][:, :]),
                             reads=[PB[bk]], writes=[qT])
                    for g4 in range(4):
                        bk = 5 if g4 % 2 == 0 else 2
                        for q4 in range(4):
                            hp = g4 * 4 + q4
                            S.op("pe", lambda e, hp=hp, bk=bk, q4=q4: e.matmul(PB[bk][:, q4 * 128:(q4 + 1) * 128], lhsT=qT[:, hp, :], rhs=SubT[:, hp, :],
                                                                                start=True, stop=True), reads=[qT, SubT], writes=[PB[bk]])
                        S.op("act", lambda e, g4=g4, bk=bk: e.copy(out=ssb[:, g4 * 4:(g4 + 1) * 4, :].rearrange("p a b -> p (a b)"), in_=PB[bk][:, :]),
                             reads=[PB[bk]], writes=[ssb])
                    if S.defer is not None:
                        mark[0] = len(S.defer)
                    for hp in range(16):
                        top16(ssb[:, hp, :], ssb, 128, m16[:, hp, :], i16[:, hp, :], m16, i16)
                    m4 = m16[:].rearrange("p (h a) i -> p h a i", a=2)
                    S.op("dve", lambda e: e.tensor_tensor(out=cand[:].rearrange("p h (i j) -> p h i j", i=16),
                                                          in0=m4[:, :, 0, :].unsqueeze(3).to_broadcast([128, 8, 16, 16]),
                                                          in1=m4[:, :, 1, :].unsqueeze(2).to_broadcast([128, 8, 16, 16]), op=ALU.add),
                         reads=[m16], writes=[cand])
                    for h in range(8):
                        top16(cand[:, h, :], cand, 256, tops[:, h, :], pos[:, h, :], tops, pos)
                    S.op("dve", lambda e: e.tensor_single_scalar(out=pa_[:], in_=pos[:], scalar=4, op=ALU.logical_shift_right), reads=[pos], writes=[pa_])
                    S.op("dve", lambda e: e.tensor_single_scalar(out=pb_[:], in_=pos[:], scalar=15, op=ALU.bitwise_and), reads=[pos], writes=[pb_])
                    S.op("dve", lambda e: e.tensor_copy(out=paf[:], in_=pa_[:]), reads=[pa_], writes=[paf])
                    S.op("dve", lambda e: e.tensor_copy(out=pbf_[:], in_=pb_[:]), reads=[pb_], writes=[pbf_])
                    S.op("dve", lambda e: e.tensor_copy(out=i16f[:], in_=i16[:]), reads=[i16], writes=[i16f])
                    i4 = i16f[:].rearrange("p (h a) i -> p h a i", a=2)
                    iob = iota16[:].unsqueeze(1).unsqueeze(1).to_broadcast([128, 8, 16, 16])
                    for (pf, a, sel) in ((paf, 0, sel1), (pbf_, 1, sel2)):
                        S.op("dve", lambda e, pf=pf: e.tensor_tensor(out=eq[:], in0=pf[:].unsqueeze(3).to_broadcast([128, 8, 16, 16]), in1=iob, op=ALU.is_equal),
                             reads=[pf, iota16], writes=[eq])
                        S.op("dve", lambda e, a=a: e.tensor_tensor(out=eq[:], in0=eq[:], in1=i4[:, :, a, :].unsqueeze(2).to_broadcast([128, 8, 16, 16]), op=ALU.mult),
                             reads=[eq, i16f], writes=[eq])
                        S.op("dve", lambda e, sel=sel: e.tensor_reduce(out=sel[:], in_=eq[:], axis=AX.X, op=ALU.add), reads=[eq], writes=[sel])
                    S.op("dve", lambda e: e.scalar_tensor_tensor(out=idxf[:], in0=sel1[:].rearrange("p h k -> p (h k)"), scalar=128.0,
                                                                 in1=sel2[:].rearrange("p h k -> p (h k)"), op0=ALU.mult, op1=ALU.add),
                         reads=[sel1, sel2], writes=[idxf])
                    S.op("dve", lambda e: e.tensor_copy(out=idxu[:], in_=idxf[:]), reads=[idxf], writes=[idxu])
                    S.op("dve", lambda e: e.tensor_tensor(out=gate[:], in0=tops[:], in1=tops[:, :, 0:1].to_broadcast([128, 8, 16]), op=ALU.subtract),
                         reads=[tops], writes=[gate])
                    S.op("act", lambda e: e.activation(out=gate[:], in_=gate[:], func=AF.Exp), reads=[gate], writes=[gate])
                    S.op("dve", lambda e: e.tensor_reduce(out=gsum[:], in_=gate[:], axis=AX.X, op=ALU.add), reads=[gate], writes=[gsum])
                    S.op("dve", lambda e: e.reciprocal(out=gsum[:], in_=gsum[:]), reads=[gsum], writes=[gsum])
                    S.op("dve", lambda e: e.tensor_tensor(out=gate[:], in0=gate[:], in1=gsum[:].unsqueeze(2).to_broadcast([128, 8, 16]), op=ALU.mult),
                         reads=[gate, gsum], writes=[gate])

                mark = [0]
                RATE_A = 9

                def emit_loop(n, pending):
                    tail = len(pending) - mark[0]
                    par = n % 2
                    ht, hn, idxu, gate = htP[par], hnP[par], idxP[par], gateP[par]
                    r0 = base + n * 128
                    gflat = gate[:].rearrange("p h k -> p (h k)")
                    LAG = 2

                    def emit_acc(sl):
                        U = UV[sl % NUV]
                        D = Dg[sl % NDG]
                        S.op("act", lambda e: e.activation(out=glt[:, sl:sl + 1], in_=actv[:, sl:sl + 1], func=AF.Gelu_apprx_tanh),
                             reads=[actC[sl]], writes=[glC[sl]])
                        S.op("act", lambda e: e.activation(out=wct[:, sl:sl + 1], in_=gflat[:, sl:sl + 1], func=AF.Copy, scale=glt[:, sl:sl + 1]),
                             reads=[glC[sl], gate], writes=[glC[sl]])
                        S.op("act", lambda e: e.activation(out=D[:], in_=ident_bf[:], func=AF.Copy, scale=wct[:, sl:sl + 1]),
                             reads=[ident_bf, glC[sl]], writes=[D])
                        for j in range(2):
                            S.op("pe", lambda e, j=j: e.matmul(PB[6 + j][:, :], lhsT=D[:], rhs=U[:, 1024 + j * 512:1024 + (j + 1) * 512],
                                                               start=(sl == 0), stop=(sl == 127)), reads=[D, U], writes=[PB[6 + j]])

                    for sl in range(128):
                        U = UV[sl % NUV]
                        S.dma("pool", lambda q, U=U, sl=sl: q.indirect_dma_start(out=U[:], out_offset=None, in_=uvbf_d,
                                                                                 in_offset=bass.IndirectOffsetOnAxis(ap=idxu[:, sl:sl + 1], axis=0)),
                              reads=[idxu, uv_t], writes=[U], semt=U)
                        S.op("dve", lambda e, U=U, sl=sl: e.scalar_tensor_tensor(out=junkD[sl % 2][:], in0=U[:, 0:1024], scalar=1.0, in1=hn[:], op0=ALU.mult, op1=ALU.mult,
                                                                                 accum_out=actv[:, sl:sl + 1]), reads=[U, hn], writes=[junkD[sl % 2], actC[sl]])
                        if sl >= LAG:
                            emit_acc(sl - LAG)
                        if pending:
                            if len(pending) > tail:
                                S.replay(pending, min(RATE_A, len(pending) - tail))
                            else:
                                S.replay(pending, -(-len(pending) // max(1, 124 - sl)))
                    for sl in range(128 - LAG, 128):
                        emit_acc(sl)
                    if pending:
                        S.replay(pending, len(pending))
                    for j in range(2):
                        S.op("dve", lambda e, j=j: e.tensor_tensor(out=ht[:, j * 512:(j + 1) * 512], in0=PB[6 + j][:, :], in1=ht[:, j * 512:(j + 1) * 512], op=ALU.add),
                             reads=[PB[6 + j], ht], writes=[ht])
                    r = rstd2(ht, 2)
                    S.op("dve", lambda e: e.scalar_tensor_tensor(out=hn[:], in0=ht[:], scalar=r, in1=gfb[:], op0=ALU.mult, op1=ALU.mult),
                         reads=[ht, stat2, gfb], writes=[hn])
                    S.dma("sp", lambda q: q.dma_start(out=y_d[r0:r0 + 128, :], in_=hn[:]), reads=[hn], writes=[y_t], semt=y_t)

                emit_d12(0)
                for n in range(NTILE):
                    pending = []
                    if n + 1 < NTILE:
                        S.defer = pending
                        emit_d12(n + 1)
                        S.defer = None
                    emit_loop(n, pending)
                S.barrier()
        S.barrier()
        print("ninstr", S.ninstr, "nsem", S.nsem)
    return nc


def _host_inputs(inputs):
    f = np.float32
    w_in = np.asarray(inputs["w_in"])[0]
    wA = np.ascontiguousarray(np.concatenate([w_in[:, 0:416], w_in[:, 1440:1472]], axis=1))
    wB = np.ascontiguousarray(np.concatenate([w_in[:, 416:1440], w_in[:, 1472:1984], w_in[:, 1440:1472]], axis=1))
    gw = np.zeros((33, 512), f)
    gw[0:16, 0:256] = np.asarray(inputs["gate_fwd_w"])[0]
    gw[16:32, 256:512] = np.asarray(inputs["gate_bwd_w"])[0]
    gw[32, 0:256] = np.asarray(inputs["gate_fwd_b"])[0]
    gw[32, 256:512] = np.asarray(inputs["gate_bwd_b"])[0]
    subk = np.asarray(inputs["peer_subkeys"])[0].reshape(16, 128, 128)
    subkT = np.ascontiguousarray(np.transpose(subk, (2, 0, 1)))
    half = 16
    freqs = (np.float32(10000.0) ** (-np.arange(half, dtype=f) * f(2.0) / f(32))).astype(f)
    ang = (np.arange(S_LEN, dtype=f)[:, None] * freqs[None, :]).astype(f)
    shared = {
        "w_inA": wA, "w_inB": wB,
        "norm1_g": np.ascontiguousarray(np.asarray(inputs["norm1_g"])[0]),
        "q_norm_g": np.ascontiguousarray(np.asarray(inputs["q_norm_g"])[0]),
        "w_uq": np.ascontiguousarray(np.asarray(inputs["w_uq"])[0]),
        "kv_norm_g": np.ascontiguousarray(np.asarray(inputs["kv_norm_g"])[0]),
        "w_ukv": np.ascontiguousarray(np.asarray(inputs["w_ukv"])[0]),
        "gate_w": gw,
        "gla_norm_g": np.ascontiguousarray(np.asarray(inputs["gla_norm_g"])[0]),
        "w_o": np.ascontiguousarray(np.asarray(inputs["w_o"])[0]),
        "norm2_g": np.ascontiguousarray(np.asarray(inputs["norm2_g"])[0]),
        "peer_wq": np.ascontiguousarray(np.asarray(inputs["peer_wq"])[0]),
        "subkT": subkT,
        "peer_u": np.ascontiguousarray(np.asarray(inputs["peer_u"])[0]),
        "peer_v": np.ascontiguousarray(np.asarray(inputs["peer_v"])[0]),
        "final_norm_g": np.ascontiguousarray(np.asarray(inputs["final_norm_g"])),
        "rope_cos": np.cos(ang).astype(f), "rope_sin": np.sin(ang).astype(f),
    }
    return shared


def kernel(**inputs):
    xp = np.asarray(inputs["x_prompt"], dtype=np.float32)
    xs = np.asarray(inputs["x_sample"], dtype=np.float32)
    xall = np.concatenate([xp, xs], axis=0)
    nb = xall.shape[0]
    ncore = 8
    per = nb // ncore
    shared = _host_inputs(inputs)
    nc = build_program(per, False)
    in_maps = []
    for c in range(ncore):
        m = dict(shared)
        m["x"] = np.ascontiguousarray(xall[c * per:(c + 1) * per].reshape(per * S_LEN, 1024))
        in_maps.append(m)
    res = run_bass_kernel_spmd(nc, in_maps, core_ids=list(range(ncore)))
    ys = [np.asarray(r["y"]).reshape(per, S_LEN, 1024) for r in res.results]
    yall = np.concatenate(ys, axis=0).astype(np.float32)
    return (yall[:xp.shape[0]], yall[xp.shape[0]:])
```

```python
import numpy as np
from contextlib import ExitStack
import concourse.bass as bass
import concourse.mybir as mybir
from concourse.bass_utils import run_bass_kernel_spmd

F32 = mybir.dt.float32
BF16 = mybir.dt.bfloat16
U32 = mybir.dt.uint32
AF = mybir.ActivationFunctionType
ALU = mybir.AluOpType
AX = mybir.AxisListType

NSEQ = 5
DBG = False
EPOCH = 60000
S_LEN = 2048
NTILE = 16
EPS = 1e-6


class DSem:
    def __init__(self, h):
        self.h = h
        self.cnt = 0


class T:
    def __init__(self, ap, name, dsem=None):
        self.ap = ap
        self.name = name
        self.lw = None
        self.rd = {}
        self.dsem = dsem

    def __getitem__(self, k):
        return self.ap[k]


class Sched:
    def __init__(self, nc, stack):
        self.nc = nc
        self.stack = stack
        self.eng = {"pe": nc.tensor, "act": nc.scalar, "dve": nc.vector, "pool": nc.gpsimd, "sp": nc.sync}
        self.cnt = {k: 0 for k in self.eng}
        self.sem = {}
        self.nsem = 0
        for k in self.eng:
            self.sem[k] = self._newsem(k)
        self.waited = {k: {} for k in self.eng}
        self.ninstr = 0
        self.dpool = {"sw": [], "hw": []}
        self.dall = []

    def _newsem(self, name):
        self.nsem += 1
        return self.stack.enter_context(self.nc.semaphore(f"s{self.nsem}_{name}"))

    def getd(self, kind="hw"):
        if self.dpool[kind]:
            return self.dpool[kind].pop()
        d = DSem(self._newsem("d" + kind))
        d.kind = kind
        self.dall.append(d)
        return d

    def sb(self, st, name, shape, dtype, dma=False):
        self.nalloc = getattr(self, "nalloc", 0) + 1
        name = f"{name}_{self.nalloc}"
        t = st.enter_context(self.nc.sbuf_tensor(name, shape, dtype))
        kind = "hw" if dma is True else dma
        tt = T(t, name, self.getd(kind) if dma else None)
        if dma:
            st.callback(lambda d=tt.dsem: self.dpool[d.kind].append(d))
        return tt

    def ps(self, st, name, shape, dtype):
        t = st.enter_context(self.nc.psum_tensor(name, shape, dtype))
        tt = T(t, name)
        tt.excl = True
        return tt

    def dram(self, ap, name):
        return T(ap, name, self.getd())

    def _wait(self, e, deps):
        w = self.waited[e]
        for (sem, val) in deps:
            key = id(sem)
            if w.get(key, 0) >= val:
                continue
            self.eng[e].wait_ge(sem, val)
            w[key] = val
            self.ninstr += 1

    def replay(self, lst, k):
        d, self.defer = self.defer, None
        for _ in range(min(k, len(lst))):
            kind, a = lst.pop(0)
            (self.op if kind == "op" else self.dma)(*a)
        self.defer = d

    def op(self, e, fn, reads=(), writes=()):
        if getattr(self, "defer", None) is not None:
            self.defer.append(("op", (e, fn, list(reads), list(writes))))
            return None
        ex = [t for t in reads if getattr(t, "excl", False)]
        if ex:
            reads = [t for t in reads if not getattr(t, "excl", False)]
            writes = list(writes) + ex
        deps = []
        for t in reads:
            if t.lw is not None:
                deps.append(t.lw[1:])
        strict = (e != "pe")
        for t in writes:
            if t.lw is not None and (t.lw[0] != e or strict):
                deps.append(t.lw[1:])
            for en, d in t.rd.items():
                if en != e or strict:
                    deps.append(d)
        self._wait(e, deps)
        ins = fn(self.eng[e])
        self.cnt[e] += 1
        if self.cnt[e] > EPOCH:
            self.sem[e] = self._newsem(e)
            self.cnt[e] = 1
        ins.then_inc(self.sem[e], 1)
        self.ninstr += 1
        rec = (self.sem[e], self.cnt[e])
        for t in reads:
            t.rd[e] = rec
        for t in writes:
            t.lw = (e,) + rec
            t.rd = {}
        return ins

    def dma(self, q, fn, reads=(), writes=(), semt=None):
        if getattr(self, "defer", None) is not None:
            self.defer.append(("dma", (q, fn, list(reads), list(writes), semt)))
            return None
        deps = []
        for t in reads:
            if t.lw is not None:
                deps.append(t.lw[1:])
        for t in writes:
            if t.lw is not None:
                deps.append(t.lw[1:])
            for en, d in t.rd.items():
                deps.append(d)
        self._wait(q, deps)
        ins = fn(self.eng[q])
        ds = semt.dsem
        ds.cnt += 16
        ins.then_inc(ds.h, 16)
        self.ninstr += 1
        rec = (ds.h, ds.cnt)
        for t in reads:
            t.rd[("dma", id(ds))] = rec
        for t in writes:
            t.lw = ("dma",) + rec
            t.rd = {}
        return ins

    def barrier(self):
        deps = [(self.sem[k], self.cnt[k]) for k in self.eng if self.cnt[k] > 0]
        deps += [(d.h, d.cnt) for d in self.dall if d.cnt > 0]
        for e in self.eng:
            self._wait(e, deps)


def build_program(nseq, dbg, stop=None):
    nc = bass.Bass("TRN2", target_bir_lowering=False)
    ntok = nseq * S_LEN

    def din(name, shape, dt=F32):
        return nc.dram_tensor(name, shape, dt, kind="ExternalInput").ap()

    x_d = din("x", [ntok, 1024])
    winA_d = din("w_inA", [1024, 448])
    winB_d = din("w_inB", [1024, 1568])
    g1_d = din("norm1_g", [1024])
    gq_d = din("q_norm_g", [256])
    wuq_d = din("w_uq", [256, 768])
    gkv_d = din("kv_norm_g", [128])
    wukv_d = din("w_ukv", [128, 1024])
    wg_d = din("gate_w", [33, 512])
    gla_d = din("gla_norm_g", [128])
    wo_d = din("w_o", [1024, 1024])
    g2_d = din("norm2_g", [1024])
    wq_d = din("peer_wq", [1024, 2048])
    subk_d = din("subkT", [128, 16, 128])
    u_d = din("peer_u", [16384, 1024])
    v_d = din("peer_v", [16384, 1024])
    gf_d = din("final_norm_g", [1024])
    cos_d = din("rope_cos", [S_LEN, 16])
    sin_d = din("rope_sin", [S_LEN, 16])
    y_d = nc.dram_tensor("y", [ntok, 1024], F32, kind="ExternalOutput").ap()
    uvbf_d = nc.dram_tensor("uv_bf", [16384, 2048], BF16, kind="Internal").ap()
    if dbg:
        dbgh_d = nc.dram_tensor("dbg_h", [ntok, 1024], F32, kind="ExternalOutput").ap()
        dbgi_d = nc.dram_tensor("dbg_i", [ntok, 128], U32, kind="ExternalOutput").ap()
        dbgg_d = nc.dram_tensor("dbg_g", [ntok, 128], F32, kind="ExternalOutput").ap()
        dbga_d = nc.dram_tensor("dbg_a", [ntok, 128], F32, kind="ExternalOutput").ap()

    with ExitStack() as gst:
        S = Sched(nc, gst)
        y_t = S.dram(y_d, "y")
        if dbg:
            dbgh_t = S.dram(dbgh_d, "dbgh")
            dbgi_t = S.dram(dbgi_d, "dbgi")
            dbgg_t = S.dram(dbgg_d, "dbgg")
            dbga_t = S.dram(dbga_d, "dbga")

        ident_bf = S.sb(gst, "ident_bf", [128, 128], BF16)
        ident_f = S.sb(gst, "ident_f", [128, 128], F32)
        triL = S.sb(gst, "triL", [128, 128], F32)
        triU = S.sb(gst, "triU", [128, 128], F32)
        eye16 = S.sb(gst, "eye16", [16, 16], F32)
        ones16 = S.sb(gst, "ones16", [16, 128], F32)
        iota16 = S.sb(gst, "iota16", [128, 16], F32)
        for tt in (ident_bf, ident_f):
            S.op("pool", lambda e, tt=tt: e.memset(tt[:], 0.0), writes=[tt])
            S.op("pool", lambda e, tt=tt: e.affine_select(out=tt[:], in_=tt[:], pattern=[[-1, 128]], compare_op=ALU.not_equal,
                                                          fill=1.0, base=0, channel_multiplier=1), reads=[tt], writes=[tt])
        S.op("pool", lambda e: e.memset(eye16[:], 0.0), writes=[eye16])
        S.op("pool", lambda e: e.affine_select(out=eye16[:], in_=eye16[:], pattern=[[-1, 16]], compare_op=ALU.not_equal,
                                               fill=1.0, base=0, channel_multiplier=1), reads=[eye16], writes=[eye16])
        S.op("pool", lambda e: e.memset(ones16[:], 1.0), writes=[ones16])
        S.op("pool", lambda e: e.memset(triL[:], 1.0), writes=[triL])
        S.op("pool", lambda e: e.affine_select(out=triL[:], in_=triL[:], pattern=[[1, 128]], compare_op=ALU.is_ge,
                                               fill=0.0, base=0, channel_multiplier=-1), reads=[triL], writes=[triL])
        S.op("pool", lambda e: e.memset(triU[:], 1.0), writes=[triU])
        S.op("pool", lambda e: e.affine_select(out=triU[:], in_=triU[:], pattern=[[-1, 128]], compare_op=ALU.is_ge,
                                               fill=0.0, base=0, channel_multiplier=1), reads=[triU], writes=[triU])
        S.op("pool", lambda e: e.iota(iota16[:], pattern=[[1, 16]], base=0, channel_multiplier=0,
                                      allow_small_or_imprecise_dtypes=True), writes=[iota16])
        attnT = S.sb(gst, "attnT", [128, 4, S_LEN], BF16)
        oglaT = S.sb(gst, "oglaT", [128, 4, S_LEN], BF16)
        PB = [S.ps(gst, f"pb{i}", [128, 512], F32) for i in range(8)]

        def pbf(i):
            return PB[i][:].bitcast(BF16)

        stat = S.sb(gst, "stat", [128, 8], F32)
        DrowG = S.sb(gst, "DrowG", [16, 512], F32)

        def rstd_of(src_ap, src_ts, width, junk, col, st_=None):
            stt = stat if st_ is None else st_
            jap = junk[:, 0:width]
            S.op("act", lambda e: e.activation(out=jap, in_=src_ap, func=AF.Square, accum_out=stt[:, col:col + 1]),
                 reads=src_ts, writes=[stt, junk])
            S.op("act", lambda e: e.activation(out=stt[:, col + 1:col + 2], in_=stt[:, col:col + 1], func=AF.Ln,
                                               scale=1.0 / width, bias=EPS), reads=[stt], writes=[stt])
            S.op("act", lambda e: e.activation(out=stt[:, col + 1:col + 2], in_=stt[:, col + 1:col + 2], func=AF.Exp, scale=-0.5),
                 reads=[stt], writes=[stt])
            return stt[:, col + 1:col + 2]

        def run_pipelined(tile_fn, ntile):
            lists = []
            for n in range(ntile):
                S.defer = L = []
                tile_fn(n)
                S.defer = None
                lists.append(L)
            cur = lists[0]
            S.replay(cur, len(cur) // 2)
            for n in range(1, ntile):
                nxt = lists[n]
                half = len(nxt) // 2
                k = 0
                while cur or k < half:
                    if cur:
                        S.replay(cur, 1)
                    if k < half:
                        S.replay(nxt, 1)
                        k += 1
                cur = nxt
            S.replay(cur, len(cur))

        junk_t = S.sb(gst, "junk", [128, 1024], BF16)

        def load_norm_T(row0, gb, xt, n1, n1T, bank, st_, jk):
            S.dma("sp", lambda q: q.dma_start(out=xt[:], in_=x_d[row0:row0 + 128, :]), writes=[xt], semt=xt)
            r = rstd_of(xt[:], [xt], 1024, jk, 0, st_)
            S.op("dve", lambda e: e.scalar_tensor_tensor(out=n1[:], in0=xt[:], scalar=r, in1=gb[:], op0=ALU.mult, op1=ALU.mult),
                 reads=[xt, st_, gb], writes=[n1])
            transpose_to(n1, lambda kc: n1[:, kc * 128:(kc + 1) * 128], 8, bank, n1T, lambda: n1T[:])

        def transpose_to(src_t, src_fn, nblk, bank, dst_t, dst_fn, rows=128, eng="act"):
            pv = pbf(bank)
            for k in range(nblk):
                S.op("pe", lambda e, k=k: e.transpose(out=pv[0:rows, k * 128:(k + 1) * 128], in_=src_fn(k), identity=ident_bf[:]),
                     reads=[src_t, ident_bf], writes=[PB[bank]])
            srcv = pv[0:rows, 0:nblk * 128].rearrange("p (a b) -> p a b", a=nblk)
            if eng == "act":
                S.op("act", lambda e: e.copy(out=dst_fn(), in_=srcv), reads=[PB[bank]], writes=[dst_t])
            else:
                S.op("dve", lambda e: e.tensor_copy(out=dst_fn(), in_=srcv), reads=[PB[bank]], writes=[dst_t])

        def bcast_load(st, name, vec_d, n):
            t = S.sb(st, name, [128, n], F32, dma=True)
            S.dma("sp", lambda q: q.dma_start(out=t[:], in_=vec_d.partition_broadcast(128)), writes=[t], semt=t)
            return t

        uv_t = S.dram(uvbf_d, "uvbf")
        prepass = []

        for s in range(nseq):
            base = s * S_LEN
            with ExitStack() as p1:
                WinA = S.sb(p1, "WinA", [128, 8, 448], BF16, dma="sw")
                Wuq = S.sb(p1, "Wuq", [128, 2, 768], BF16, dma="sw")
                Wukv = S.sb(p1, "Wukv", [128, 1024], BF16, dma="sw")
                Wg = S.sb(p1, "Wg", [33, 512], F32, dma=True)
                S.dma("pool", lambda q: q.dma_start(out=WinA[:], in_=winA_d.rearrange("(c p) n -> p c n", p=128)), writes=[WinA], semt=WinA)
                S.dma("pool", lambda q: q.dma_start(out=Wuq[:], in_=wuq_d.rearrange("(c p) n -> p c n", p=128)), writes=[Wuq], semt=Wuq)
                S.dma("pool", lambda q: q.dma_start(out=Wukv[:], in_=wukv_d), writes=[Wukv], semt=Wukv)
                S.dma("sp", lambda q: q.dma_start(out=Wg[:], in_=wg_d), writes=[Wg], semt=Wg)
                g1b = bcast_load(p1, "g1b", g1_d, 1024)
                gqb = bcast_load(p1, "gqb", gq_d, 256)
                gkvb = bcast_load(p1, "gkvb", gkv_d, 128)
                cosT = S.sb(p1, "cosT", [128, NTILE, 16], F32, dma=True)
                sinT = S.sb(p1, "sinT", [128, NTILE, 16], F32, dma=True)
                S.dma("sp", lambda q: q.dma_start(out=cosT[:], in_=cos_d.rearrange("(n p) d -> p n d", p=128)), writes=[cosT], semt=cosT)
                S.dma("sp", lambda q: q.dma_start(out=sinT[:], in_=sin_d.rearrange("(n p) d -> p n d", p=128)), writes=[sinT], semt=sinT)

                KT = S.sb(p1, "KT", [128, 8, S_LEN], BF16)
                Vaug = S.sb(p1, "Vaug", [128, NTILE, 8, 65], BF16)
                cqT = S.sb(p1, "cqT", [128, 2, S_LEN], BF16)
                TTs = S.sb(p1, "TTs", [128, 4, NTILE], F32)
                S.op("pool", lambda e: e.memset(Vaug[:], 1.0), writes=[Vaug])
                with ExitStack() as p1a:
                    def mk(name, shape, dt, **kw):
                        return [S.sb(p1a, f"{name}{i}", shape, dt, **kw) for i in range(2)]
                    xtP = mk("xt", [128, 1024], F32, dma=True)
                    n1P = mk("n1", [128, 1024], BF16)
                    n1TP = mk("n1T", [128, 8, 128], BF16)
                    paP = mk("pa", [128, 448], F32)
                    cqnP = mk("cqn", [128, 384], BF16)
                    latP = mk("latT", [128, 3, 128], BF16)
                    glrAP = mk("glrA", [33, 128], F32)
                    KtP = mk("Kt", [128, 8, 96], BF16)
                    rpP = mk("rp", [128, 4, 16], F32)
                    spP = mk("sp", [128, 512], F32)
                    krrP = mk("krr", [128, 32], BF16)
                    stP = mk("st", [128, 8], F32)
                    jkP = mk("jk", [128, 1024], BF16)
                    for i in range(2):
                        S.op("pool", lambda e, i=i: e.memset(glrAP[i][:], 1.0), writes=[glrAP[i]])

                    def a1_tile(n):
                        p = n % 2
                        xt, n1, n1T, pa, cqn, lat, glrA, Kt, rp, sp_t, krr, st_, jk = (xtP[p], n1P[p], n1TP[p], paP[p], cqnP[p], latP[p], glrAP[p],
                                                                                      KtP[p], rpP[p], spP[p], krrP[p], stP[p], jkP[p])
                        X0, X1, X2, X3 = 4 * p, 4 * p + 1, 4 * p + 2, 4 * p + 3
                        load_norm_T(base + n * 128, g1b, xt, n1, n1T, X0, st_, jk)
                        for kc in range(8):
                            S.op("pe", lambda e, kc=kc: e.matmul(PB[X1][:, 0:448], lhsT=n1T[:, kc, :], rhs=WinA[:, kc, :], start=(kc == 0), stop=(kc == 7)),
                                 reads=[n1T, WinA], writes=[PB[X1]])
                        S.op("act", lambda e: e.copy(out=pa[:], in_=PB[X1][:, 0:448]), reads=[PB[X1]], writes=[pa])
                        r = rstd_of(pa[:, 0:256], [pa], 256, jk, 2, st_)
                        S.op("dve", lambda e: e.scalar_tensor_tensor(out=cqn[:, 0:256], in0=pa[:, 0:256], scalar=r, in1=gqb[:], op0=ALU.mult, op1=ALU.mult),
                             reads=[pa, st_, gqb], writes=[cqn])
                        r2 = rstd_of(pa[:, 256:384], [pa], 128, jk, 4, st_)
                        S.op("dve", lambda e: e.scalar_tensor_tensor(out=cqn[:, 256:384], in0=pa[:, 256:384], scalar=r2, in1=gkvb[:], op0=ALU.mult, op1=ALU.mult),
                             reads=[pa, st_, gkvb], writes=[cqn])
                        transpose_to(cqn, lambda k: cqn[:, k * 128:(k + 1) * 128], 3, X0, lat, lambda: lat[:], eng="dve")
                        S.op("dve", lambda e: e.tensor_copy(out=cqT[:, :, n * 128:(n + 1) * 128], in_=lat[:, 0:2, :]), reads=[lat], writes=[cqT])
                        for j in range(2):
                            bk = X2 + j
                            S.op("pe", lambda e, j=j, bk=bk: e.matmul(PB[bk][:, :], lhsT=lat[:, 2, :], rhs=Wukv[:, j * 512:(j + 1) * 512], start=True, stop=True),
                                 reads=[lat, Wukv], writes=[PB[bk]])
                            kv = PB[bk][:, :].rearrange("p (h c) -> p h c", h=4)
                            S.op("act", lambda e, j=j, kv=kv, bk=bk: e.copy(out=Vaug[:, n, 4 * j:4 * j + 4, 0:64], in_=kv[:, :, 64:128]),
                                 reads=[PB[bk]], writes=[Vaug])
                            S.op("dve", lambda e, j=j, kv=kv, bk=bk: e.tensor_copy(out=Kt[:, 4 * j:4 * j + 4, 0:64], in_=kv[:, :, 0:64]),
                                 reads=[PB[bk]], writes=[Kt])
                        c_ = cosT[:, n, :]
                        s_ = sinT[:, n, :]
                        x1, x2 = pa[:, 384:400], pa[:, 400:416]
                        S.op("dve", lambda e: e.tensor_tensor(out=rp[:, 0, :], in0=x1, in1=c_, op=ALU.mult), reads=[pa, cosT], writes=[rp])
                        S.op("dve", lambda e: e.tensor_tensor(out=rp[:, 1, :], in0=x2, in1=s_, op=ALU.mult), reads=[pa, sinT], writes=[rp])
                        S.op("dve", lambda e: e.tensor_tensor(out=rp[:, 2, :], in0=x2, in1=c_, op=ALU.mult), reads=[pa, cosT], writes=[rp])
                        S.op("dve", lambda e: e.tensor_tensor(out=rp[:, 3, :], in0=x1, in1=s_, op=ALU.mult), reads=[pa, sinT], writes=[rp])
                        S.op("dve", lambda e: e.tensor_tensor(out=krr[:, 0:16], in0=rp[:, 0, :], in1=rp[:, 1, :], op=ALU.subtract), reads=[rp], writes=[krr])
                        S.op("dve", lambda e: e.tensor_tensor(out=krr[:, 16:32], in0=rp[:, 2, :], in1=rp[:, 3, :], op=ALU.add), reads=[rp], writes=[krr])
                        S.op("dve", lambda e: e.tensor_copy(out=Kt[:, :, 64:96], in_=krr[:].unsqueeze(1).to_broadcast([128, 8, 32])),
                             reads=[krr], writes=[Kt])
                        transpose_to(Kt, lambda h: Kt[:, h, :], 8, X0, KT, lambda: KT[0:96, :, n * 128:(n + 1) * 128], rows=96)
                        S.op("pe", lambda e: e.transpose(out=PB[X1][0:32, 0:128], in_=pa[:, 416:448], identity=ident_f[:]),
                             reads=[pa, ident_f], writes=[PB[X1]])
                        S.op("act", lambda e: e.copy(out=glrA[0:32, :], in_=PB[X1][0:32, 0:128]), reads=[PB[X1]], writes=[glrA])
                        S.op("pe", lambda e: e.matmul(PB[X2][:, :], lhsT=glrA[:, :], rhs=Wg[:, :], start=True, stop=True),
                             reads=[glrA, Wg], writes=[PB[X2]])
                        S.op("act", lambda e: e.activation(out=sp_t[:], in_=PB[X2][:, :], func=AF.Exp, scale=-1.0), reads=[PB[X2]], writes=[sp_t])
                        S.op("act", lambda e: e.activation(out=sp_t[:], in_=sp_t[:], func=AF.Ln, bias=1.0), reads=[sp_t], writes=[sp_t])
                        for cc in range(4):
                            S.op("pe", lambda e, cc=cc: e.matmul(PB[X1][:, 256 + cc:257 + cc], lhsT=sp_t[:, cc * 128:(cc + 1) * 128], rhs=triU[:, 0:1],
                                                                 start=True, stop=True), reads=[sp_t, triU], writes=[PB[X1]])
                        S.op("dve", lambda e: e.tensor_copy(out=TTs[:, :, n], in_=PB[X1][:, 256:260]), reads=[PB[X1]], writes=[TTs])

                    run_pipelined(a1_tile, NTILE)
                    S.barrier()
                if stop == "A1":
                    S.barrier()
                    return nc
                Inc = S.sb(p1, "Inc", [128, 4, NTILE], F32)
                Dc = S.sb(p1, "Dc", [128, 4, NTILE], F32)
                S.op("dve", lambda e: e.tensor_copy(out=Inc[:, :, 0:1], in_=TTs[:, :, 0:1]), reads=[TTs], writes=[Inc])
                for n in range(1, NTILE):
                    S.op("dve", lambda e, n=n: e.tensor_tensor(out=Inc[:, :, n:n + 1], in0=Inc[:, :, n - 1:n], in1=TTs[:, :, n:n + 1], op=ALU.add),
                         reads=[Inc, TTs], writes=[Inc])
                S.op("dve", lambda e: e.tensor_tensor(out=Dc[:, 0:2, :], in0=Inc[:, 0:2, :], in1=TTs[:, 0:2, :], op=ALU.subtract), reads=[Inc, TTs], writes=[Dc])
                S.op("dve", lambda e: e.tensor_tensor(out=Dc[:, 0:2, :], in0=Dc[:, 0:2, :], in1=Inc[:, 0:2, 7:8].to_broadcast([128, 2, NTILE]), op=ALU.subtract),
                     reads=[Dc, Inc], writes=[Dc])
                S.op("dve", lambda e: e.tensor_tensor(out=Dc[:, 2:4, :], in0=Inc[:, 2:4, 7:8].to_broadcast([128, 2, NTILE]), in1=Inc[:, 2:4, :], op=ALU.subtract),
                     reads=[Inc], writes=[Dc])
                for cc in range(4):
                    S.op("pe", lambda e, cc=cc: e.transpose(out=PB[3][0:16, cc * 128:(cc + 1) * 128], in_=Dc[:, cc, :], identity=ident_f[:]),
                         reads=[Dc, ident_f], writes=[PB[3]])
                S.op("act", lambda e: e.copy(out=DrowG[:], in_=PB[3][0:16, :]), reads=[PB[3]], writes=[DrowG])

                if stop == "A1b":
                    S.barrier()
                    return nc
                if s == 0:
                    cb = [S.sb(p1, f"cb{i}", [128, 4, 1024], BF16, dma="sw") for i in range(3)]
                    S.defer = prepass
                    k = 0
                    for (src, c0) in ((u_d, 0), (v_d, 1024)):
                        for ch in range(32):
                            b_ = cb[k % 3]
                            k += 1
                            S.dma("pool", lambda q, b_=b_, src=src, ch=ch: q.dma_start(out=b_[:], in_=src[ch * 512:(ch + 1) * 512, :].rearrange("(p r) d -> p r d", r=4)),
                                  writes=[b_], semt=b_)
                            S.dma("sp", lambda q, b_=b_, c0=c0, ch=ch: q.dma_start(out=uvbf_d[ch * 512:(ch + 1) * 512, c0:c0 + 1024].rearrange("(p r) d -> p r d", r=4), in_=b_[:]),
                                  reads=[b_], writes=[uv_t], semt=uv_t)
                    S.defer = None
                Qt = S.sb(p1, "Qt", [128, 8, 96], BF16)
                rpq = S.sb(p1, "rpq", [128, 4, 4, 16], F32)
                QTcP = [S.sb(p1, f"QTc{i}", [128, 8, 512], BF16) for i in range(2)]
                Eb = [S.sb(p1, f"Eb{i}", [128, 512], BF16) for i in range(2)]
                Oacc = S.sb(p1, "Oacc", [128, 4, 8, 65], F32)
                rc = S.sb(p1, "rc", [128, 4, 8], F32)
                atok = S.sb(p1, "atok", [128, 4, 512], BF16)
                sc = 96.0 ** -0.5
                def emit_qproj(qc, QTc_t):
                    for i in range(4):
                        n = qc * 4 + i
                        for j in range(2):
                            for kc in range(2):
                                S.op("pe", lambda e, j=j, kc=kc, n=n: e.matmul(PB[6 + j][:, 0:384], lhsT=cqT[:, kc, n * 128:(n + 1) * 128],
                                                                                 rhs=Wuq[:, kc, j * 384:(j + 1) * 384], start=(kc == 0), stop=(kc == 1)),
                                     reads=[cqT, Wuq], writes=[PB[6 + j]])
                            qv = PB[6 + j][:, 0:384].rearrange("p (h c) -> p h c", h=4)
                            S.op("act", lambda e, j=j, qv=qv: e.copy(out=Qt[:, 4 * j:4 * j + 4, 0:64], in_=qv[:, :, 0:64]), reads=[PB[6 + j]], writes=[Qt])
                            cb_ = cosT[:, n, :].unsqueeze(1).to_broadcast([128, 4, 16])
                            sb2_ = sinT[:, n, :].unsqueeze(1).to_broadcast([128, 4, 16])
                            x1, x2 = qv[:, :, 64:80], qv[:, :, 80:96]
                            bkq = PB[6 + j]
                            S.op("dve", lambda e, x1=x1, cb_=cb_: e.tensor_tensor(out=rpq[:, 0], in0=x1, in1=cb_, op=ALU.mult), reads=[bkq, cosT], writes=[rpq])
                            S.op("dve", lambda e, x2=x2, sb2_=sb2_: e.tensor_tensor(out=rpq[:, 1], in0=x2, in1=sb2_, op=ALU.mult), reads=[bkq, sinT], writes=[rpq])
                            S.op("dve", lambda e, x2=x2, cb_=cb_: e.tensor_tensor(out=rpq[:, 2], in0=x2, in1=cb_, op=ALU.mult), reads=[bkq, cosT], writes=[rpq])
                            S.op("dve", lambda e, x1=x1, sb2_=sb2_: e.tensor_tensor(out=rpq[:, 3], in0=x1, in1=sb2_, op=ALU.mult), reads=[bkq, sinT], writes=[rpq])
                            S.op("dve", lambda e, j=j: e.tensor_tensor(out=Qt[:, 4 * j:4 * j + 4, 64:80], in0=rpq[:, 0], in1=rpq[:, 1], op=ALU.subtract), reads=[rpq], writes=[Qt])
                            S.op("dve", lambda e, j=j: e.tensor_tensor(out=Qt[:, 4 * j:4 * j + 4, 80:96], in0=rpq[:, 2], in1=rpq[:, 3], op=ALU.add), reads=[rpq], writes=[Qt])
                        transpose_to(Qt, lambda h: Qt[:, h, :], 8, 6, QTc_t, lambda i=i: QTc_t[0:96, :, i * 128:(i + 1) * 128], rows=96)
                for qc in range(4):
                    QTc = QTcP[qc % 2]
                    if qc == 0:
                        emit_qproj(0, QTc)
                    pend_q = []
                    if qc + 1 < 4:
                        S.defer = pend_q
                        emit_qproj(qc + 1, QTcP[(qc + 1) % 2])
                        S.defer = None
                    jobsB = [(h, kt) for h in range(8) for kt in range(NTILE)]

                    def b_score(i):
                        h, kt = jobsB[i]
                        sb_ = i % 2
                        if prepass:
                            S.replay(prepass, 1)
                        if pend_q:
                            S.replay(pend_q, 1)
                        S.op("pe", lambda e: e.matmul(PB[sb_][:, :], lhsT=KT[0:96, h, kt * 128:(kt + 1) * 128], rhs=QTc[0:96, h, :],
                                                      start=True, stop=True), reads=[KT, QTc], writes=[PB[sb_]])
                        S.op("act", lambda e: e.activation(out=Eb[sb_][:], in_=PB[sb_][:, :], func=AF.Exp, scale=sc),
                             reads=[PB[sb_]], writes=[Eb[sb_]])

                    def b_pv(i):
                        h, kt = jobsB[i]
                        sb_ = i % 2
                        pv0 = 2 + 2 * (h % 2)
                        for qt in range(4):
                            bk = pv0 + qt // 2
                            c0 = (qt % 2) * 128
                            S.op("pe", lambda e, qt=qt, bk=bk, c0=c0: e.matmul(
                                PB[bk][:, c0:c0 + 65], lhsT=Eb[sb_][:, qt * 128:(qt + 1) * 128], rhs=Vaug[:, kt, h, :],
                                start=(kt == 0 and qt % 2 == 0), stop=(kt == NTILE - 1), skip_group_check=True),
                                reads=[Eb[sb_], Vaug], writes=[PB[bk]])
                        if kt == NTILE - 1:
                            for half in range(2):
                                bk = pv0 + half
                                src = PB[bk][:, 0:256].rearrange("p (a c) -> p a c", a=2)[:, :, 0:65]
                                S.op("dve", lambda e, half=half, src=src: e.tensor_copy(out=Oacc[:, 2 * half:2 * half + 2, h, :], in_=src),
                                     reads=[PB[bk]], writes=[Oacc])

                    for i in range(len(jobsB) + 1):
                        if i < len(jobsB):
                            b_score(i)
                        if i >= 1:
                            b_pv(i - 1)
                    if pend_q:
                        S.replay(pend_q, len(pend_q))
                    S.op("dve", lambda e: e.reciprocal(out=rc[:], in_=Oacc[:, :, :, 64]), reads=[Oacc], writes=[rc])
                    for i in range(4):
                        S.op("dve", lambda e, i=i: e.tensor_tensor(out=atok[:, i, :].rearrange("p (h c) -> p h c", h=8), in0=Oacc[:, i, :, 0:64],
                                                                   in1=rc[:, i, :].unsqueeze(2).to_broadcast([128, 8, 64]), op=ALU.mult),
                             reads=[Oacc, rc], writes=[atok])
                        n = qc * 4 + i
                        transpose_to(atok, lambda c, i=i: atok[:, i, c * 128:(c + 1) * 128], 4, 6 + (i % 2), attnT,
                                     lambda n=n: attnT[:, :, n * 128:(n + 1) * 128], eng="dve")
                if prepass:
                    S.replay(prepass, len(prepass))
                S.barrier()
            if stop == "B":
                return nc

            with ExitStack() as p2:
                qdT = S.sb(p2, "qdT", [128, 4, S_LEN], BF16)
                kdT = S.sb(p2, "kdT", [128, 4, S_LEN], BF16)
                GV = S.sb(p2, "GV", [128, NTILE, 512], BF16)
                SG = S.sb(p2, "SG", [128, NTILE, 512], BF16)
                glab = bcast_load(p2, "glab", gla_d, 128)
                with ExitStack() as p2a:
                    WinB = S.sb(p2a, "WinB", [128, 8, 1568], BF16, dma="sw")
                    Wg = S.sb(p2a, "Wg2", [33, 512], F32, dma=True)
                    S.dma("pool", lambda q: q.dma_start(out=WinB[:], in_=winB_d.rearrange("(c p) n -> p c n", p=128)), writes=[WinB], semt=WinB)
                    S.dma("sp", lambda q: q.dma_start(out=Wg[:], in_=wg_d), writes=[Wg], semt=Wg)
                    g1b = bcast_load(p2a, "g1b2", g1_d, 1024)

                    def mk(name, shape, dt, **kw):
                        return [S.sb(p2a, f"{name}{i}", shape, dt, **kw) for i in range(2)]
                    xtP = mk("xt2", [128, 1024], F32, dma=True)
                    n1P = mk("n12", [128, 1024], BF16)
                    n1TP = mk("n1T2", [128, 8, 128], BF16)
                    glrP = mk("glr", [128, 32], F32)
                    glrAP = mk("glrA2", [33, 128], F32)
                    spP = mk("sp2", [128, 512], F32)
                    EpP = mk("Ep", [128, 512], F32)
                    EmP = mk("Em", [128, 512], F32)
                    qdP = mk("qd", [128, 512], BF16)
                    kdP = mk("kd", [128, 512], BF16)
                    DrMP = mk("DrM", [16, 512], F32)
                    stP = mk("st2", [128, 8], F32)
                    jkP = mk("jk2", [128, 1024], BF16)
                    for i in range(2):
                        S.op("pool", lambda e, i=i: e.memset(glrAP[i][:], 1.0), writes=[glrAP[i]])

                    def a2_tile(n):
                        p = n % 2
                        xt, n1, n1T, glr, glrA, sp_t, Ep, Em, qd, kd, DrM, st_, jk = (xtP[p], n1P[p], n1TP[p], glrP[p], glrAP[p], spP[p], EpP[p], EmP[p],
                                                                                      qdP[p], kdP[p], DrMP[p], stP[p], jkP[p])
                        X0, X1, X2, X3 = 4 * p, 4 * p + 1, 4 * p + 2, 4 * p + 3
                        load_norm_T(base + n * 128, g1b, xt, n1, n1T, X0, st_, jk)
                        for (bk, c0, w) in ((X1, 0, 512), (X2, 512, 512), (X3, 1024, 512)):
                            for kc in range(8):
                                S.op("pe", lambda e, kc=kc, bk=bk, c0=c0, w=w: e.matmul(PB[bk][:, 0:w], lhsT=n1T[:, kc, :], rhs=WinB[:, kc, c0:c0 + w],
                                                                                         start=(kc == 0), stop=(kc == 7)), reads=[n1T, WinB], writes=[PB[bk]])
                        S.op("act", lambda e: e.copy(out=GV[:, n, :], in_=PB[X2][:, :]), reads=[PB[X2]], writes=[GV])
                        S.op("act", lambda e: e.activation(out=SG[:, n, :], in_=PB[X3][:, :], func=AF.Silu), reads=[PB[X3]], writes=[SG])
                        for kc in range(8):
                            S.op("pe", lambda e, kc=kc: e.matmul(PB[X2][:, 0:32], lhsT=n1T[:, kc, :], rhs=WinB[:, kc, 1536:1568],
                                                                 start=(kc == 0), stop=(kc == 7)), reads=[n1T, WinB], writes=[PB[X2]])
                        S.op("dve", lambda e: e.tensor_copy(out=glr[:], in_=PB[X2][:, 0:32]), reads=[PB[X2]], writes=[glr])
                        S.op("pe", lambda e: e.transpose(out=PB[X3][0:32, 0:128], in_=glr[:], identity=ident_f[:]), reads=[glr, ident_f], writes=[PB[X3]])
                        S.op("act", lambda e: e.copy(out=glrA[0:32, :], in_=PB[X3][0:32, 0:128]), reads=[PB[X3]], writes=[glrA])
                        S.op("pe", lambda e: e.matmul(PB[X2][:, :], lhsT=glrA[:, :], rhs=Wg[:, :], start=True, stop=True), reads=[glrA, Wg], writes=[PB[X2]])
                        S.op("act", lambda e: e.activation(out=sp_t[:], in_=PB[X2][:, :], func=AF.Exp, scale=-1.0), reads=[PB[X2]], writes=[sp_t])
                        S.op("act", lambda e: e.activation(out=sp_t[:], in_=sp_t[:], func=AF.Ln, bias=1.0), reads=[sp_t], writes=[sp_t])
                        S.op("dve", lambda e: e.tensor_scalar_mul(out=DrM[:], in0=DrowG[:], scalar1=eye16[:, n:n + 1]), reads=[DrowG, eye16], writes=[DrM])
                        S.op("pe", lambda e: e.matmul(PB[X3][:, 0:256], lhsT=triL[:], rhs=sp_t[:, 0:256], start=True, stop=False, skip_group_check=True),
                             reads=[triL, sp_t], writes=[PB[X3]])
                        S.op("pe", lambda e: e.matmul(PB[X3][:, 256:512], lhsT=triU[:], rhs=sp_t[:, 256:512], start=False, stop=False, skip_group_check=True),
                             reads=[triU, sp_t], writes=[PB[X3]])
                        S.op("pe", lambda e: e.matmul(PB[X3][:, :], lhsT=ones16[:], rhs=DrM[:], start=False, stop=True, skip_group_check=True),
                             reads=[ones16, DrM], writes=[PB[X3]])
                        S.op("act", lambda e: e.activation(out=Ep[:], in_=PB[X3][:, :], func=AF.Exp, scale=-1.0 / 16), reads=[PB[X3]], writes=[Ep])
                        S.op("act", lambda e: e.activation(out=Em[:], in_=PB[X3][:, :], func=AF.Exp, scale=1.0 / 16), reads=[PB[X3]], writes=[Em])
                        gqv = PB[X1][:, 0:256].unsqueeze(1).to_broadcast([128, 2, 256])
                        gkv_ = PB[X1][:, 256:512].unsqueeze(1).to_broadcast([128, 2, 256])
                        S.op("dve", lambda e: e.scalar_tensor_tensor(out=qd[:].rearrange("p (d c) -> p d c", d=2), in0=gqv, scalar=0.125,
                                                                     in1=Ep[:].rearrange("p (d c) -> p d c", d=2), op0=ALU.mult, op1=ALU.mult),
                             reads=[PB[X1], Ep], writes=[qd])
                        S.op("dve", lambda e: e.tensor_tensor(out=kd[:].rearrange("p (d c) -> p d c", d=2), in0=gkv_,
                                                              in1=Em[:].rearrange("p (d c) -> p d c", d=2), op=ALU.mult),
                             reads=[PB[X1], Em], writes=[kd])
                        transpose_to(qd, lambda k: qd[:, k * 128:(k + 1) * 128], 4, X0, qdT, lambda: qdT[:, :, n * 128:(n + 1) * 128], eng="dve")
                        transpose_to(kd, lambda k: kd[:, k * 128:(k + 1) * 128], 4, X2, kdT, lambda: kdT[:, :, n * 128:(n + 1) * 128], eng="act")

                    run_pipelined(a2_tile, NTILE)
                    S.barrier()
                if stop == "A2":
                    return nc

                with ExitStack() as p2c:
                    mf = S.sb(p2c, "mf", [128, 4, 512], BF16)
                    mb = S.sb(p2c, "mb", [128, 4, 512], BF16)
                    S.op("pool", lambda e: e.memset(mf[:], 1.0), writes=[mf])
                    S.op("pool", lambda e: e.memset(mb[:], 1.0), writes=[mb])
                    for r in range(4):
                        S.op("pool", lambda e, r=r: e.affine_select(out=mf[:, r, :], in_=mf[:, r, :], pattern=[[1, 512]], compare_op=ALU.is_ge, fill=0.0,
                                                                    base=-128 * r, channel_multiplier=-1), reads=[mf], writes=[mf])
                        S.op("pool", lambda e, r=r: e.affine_select(out=mb[:, r, :], in_=mb[:, r, :], pattern=[[-1, 512]], compare_op=ALU.is_ge, fill=0.0,
                                                                    base=128 * r, channel_multiplier=1), reads=[mb], writes=[mb])
                    At = [S.sb(p2c, f"At{i}", [128, 512], BF16) for i in range(3)]
                    OgcP = [S.sb(p2c, f"Ogc{i}", [128, 4, 4, 128], F32) for i in range(2)]
                    pend_n = []
                    sq = S.sb(p2c, "sq", [128, 16, 128], F32)
                    ssq = S.sb(p2c, "ssq", [128, 16], F32)
                    ogt = S.sb(p2c, "ogt", [128, 4, 512], BF16)
                    for tc in range(4):
                        Ogc = OgcP[tc % 2]
                        jobsC = []
                        for h in range(4):
                            jl = [(h, 0, jt) for jt in range(0, 4 * tc + 4)] + [(h, 1, jt) for jt in range(4 * tc, NTILE)]
                            jobsC += [(h, d, jt, k == 0, k == len(jl) - 1) for k, (h, d, jt) in enumerate(jl)]

                        def c_score(i):
                            h, d, jt, isf, isl = jobsC[i]
                            sbk = i % 2
                            hp, hl = h // 2, (h % 2) * 64
                            blk = d * 2 + hp
                            S.op("pe", lambda e: e.matmul(
                                PB[sbk][:, :], lhsT=kdT[hl:hl + 64, blk, jt * 128:(jt + 1) * 128], rhs=qdT[hl:hl + 64, blk, tc * 512:(tc + 1) * 512],
                                start=True, stop=True), reads=[kdT, qdT], writes=[PB[sbk]])
                            A = At[i % 3]
                            r = jt - 4 * tc
                            if 0 <= r < 4:
                                mk = mf if d == 0 else mb
                                S.op("dve", lambda e: e.tensor_tensor(out=A[:], in0=PB[sbk][:, :], in1=mk[:, r, :], op=ALU.mult),
                                     reads=[PB[sbk], mk], writes=[A])
                            else:
                                S.op("act", lambda e: e.copy(out=A[:], in_=PB[sbk][:, :]), reads=[PB[sbk]], writes=[A])

                        def c_pv(i):
                            h, d, jt, isf, isl = jobsC[i]
                            ob = 4 + (h % 2)
                            A = At[i % 3]
                            first = isf
                            for ii in range(4):
                                tt_ = 4 * tc + ii
                                if (d == 0 and jt > tt_) or (d == 1 and jt < tt_):
                                    continue
                                S.op("pe", lambda e, ii=ii, first=first: e.matmul(
                                    PB[ob][:, ii * 128:(ii + 1) * 128], lhsT=A[:, ii * 128:(ii + 1) * 128], rhs=GV[:, jt, h * 128:(h + 1) * 128],
                                    start=first, stop=False, skip_group_check=True), reads=[A, GV], writes=[PB[ob]])
                                first = False
                            if isl:
                                S.op("act", lambda e: e.copy(out=Ogc[:, :, h, :], in_=PB[ob][:, :].rearrange("p (i c) -> p i c", i=4)),
                                     reads=[PB[ob]], writes=[Ogc])

                        for i in range(len(jobsC) + 1):
                            if i < len(jobsC):
                                c_score(i)
                            if i >= 1:
                                c_pv(i - 1)
                            if pend_n and i >= 4:
                                S.replay(pend_n, 1)
                        if pend_n:
                            S.replay(pend_n, len(pend_n))
                        if tc < 3:
                            S.defer = pend_n
                        og2 = Ogc[:].rearrange("p i h c -> p (i h) c")
                        S.op("dve", lambda e: e.tensor_tensor(out=sq[:], in0=og2, in1=og2, op=ALU.mult), reads=[Ogc], writes=[sq])
                        S.op("dve", lambda e: e.tensor_reduce(out=ssq[:], in_=sq[:], axis=AX.X, op=ALU.add), reads=[sq], writes=[ssq])
                        S.op("act", lambda e: e.activation(out=ssq[:], in_=ssq[:], func=AF.Sqrt, scale=1.0 / 128, bias=EPS), reads=[ssq], writes=[ssq])
                        S.op("dve", lambda e: e.reciprocal(out=ssq[:], in_=ssq[:]), reads=[ssq], writes=[ssq])
                        S.op("dve", lambda e: e.tensor_tensor(out=sq[:], in0=og2, in1=ssq[:].unsqueeze(2).to_broadcast([128, 16, 128]), op=ALU.mult),
                             reads=[Ogc, ssq], writes=[sq])
                        S.op("dve", lambda e: e.tensor_tensor(out=sq[:], in0=sq[:], in1=glab[:].unsqueeze(1).to_broadcast([128, 16, 128]), op=ALU.mult),
                             reads=[sq, glab], writes=[sq])
                        S.op("dve", lambda e, tc=tc: e.tensor_tensor(out=ogt[:].rearrange("p i c -> p (i c)"), in0=sq[:].rearrange("p a c -> p (a c)"),
                                                                     in1=SG[:, 4 * tc:4 * tc + 4, :].rearrange("p i c -> p (i c)"), op=ALU.mult),
                             reads=[sq, SG], writes=[ogt])
                        for i in range(4):
                            n = 4 * tc + i
                            transpose_to(ogt, lambda c, i=i: ogt[:, i, c * 128:(c + 1) * 128], 4, 6 + (i % 2), oglaT,
                                         lambda n=n: oglaT[:, :, n * 128:(n + 1) * 128], eng="act")
                        S.defer = None
                    S.barrier()

            if stop == "C":
                return nc
            with ExitStack() as p3:
                Wo = S.sb(p3, "Wo", [128, 8, 1024], BF16, dma="sw")
                Wq = S.sb(p3, "Wq", [128, 8, 2048], BF16, dma="sw")
                SubT = S.sb(p3, "SubT", [128, 16, 128], BF16, dma="sw")
                S.dma("pool", lambda q: q.dma_start(out=Wo[:], in_=wo_d.rearrange("(c p) n -> p c n", p=128)), writes=[Wo], semt=Wo)
                S.dma("pool", lambda q: q.dma_start(out=Wq[:], in_=wq_d.rearrange("(c p) n -> p c n", p=128)), writes=[Wq], semt=Wq)
                S.dma("pool", lambda q: q.dma_start(out=SubT[:], in_=subk_d), writes=[SubT], semt=SubT)
                g2b = bcast_load(p3, "g2b", g2_d, 1024)
                gfb = bcast_load(p3, "gfb", gf_d, 1024)
                htP = [S.sb(p3, f"ht{i}", [128, 1024], F32, dma=True) for i in range(2)]
                hnP = [S.sb(p3, f"hn{i}", [128, 1024], F32) for i in range(2)]
                idxP = [S.sb(p3, f"idxu{i}", [128, 128], U32) for i in range(2)]
                gateP = [S.sb(p3, f"gate{i}", [128, 8, 16], F32) for i in range(2)]
                hnb = S.sb(p3, "hnb", [128, 1024], BF16)
                hnT = S.sb(p3, "hnT", [128, 8, 128], BF16)
                qT = S.sb(p3, "qT", [128, 16, 128], BF16)
                ssb = S.sb(p3, "ssb", [128, 16, 128], F32)
                wk = S.sb(p3, "wk", [128, 256], F32)
                m16 = S.sb(p3, "m16", [128, 16, 16], F32)
                i16 = S.sb(p3, "i16", [128, 16, 16], U32)
                i16f = S.sb(p3, "i16f", [128, 16, 16], F32)
                cand = S.sb(p3, "cand", [128, 8, 256], F32)
                tops = S.sb(p3, "tops", [128, 8, 16], F32)
                pos = S.sb(p3, "pos", [128, 8, 16], U32)
                pa_ = S.sb(p3, "posa", [128, 8, 16], U32)
                pb_ = S.sb(p3, "posb", [128, 8, 16], U32)
                paf = S.sb(p3, "paf", [128, 8, 16], F32)
                pbf_ = S.sb(p3, "pbf", [128, 8, 16], F32)
                eq = S.sb(p3, "eq", [128, 8, 16, 16], BF16)
                sel1 = S.sb(p3, "sel1", [128, 8, 16], F32)
                sel2 = S.sb(p3, "sel2", [128, 8, 16], F32)
                idxf = S.sb(p3, "idxf", [128, 128], F32)
                gsum = S.sb(p3, "gsum", [128, 8], F32)
                stat2 = S.sb(p3, "stat2", [128, 8], F32)
                junk_a = S.sb(p3, "junk_a", [128, 1024], BF16)
                junkD = [S.sb(p3, f"junkD{i}", [128, 1024], BF16) for i in range(2)]
                actv = S.sb(p3, "actv", [128, 128], F32)
                glt = S.sb(p3, "glt", [128, 128], F32)
                wct = S.sb(p3, "wct", [128, 128], F32)
                actC = [T(actv[:, i:i + 1], f"actc{i}") for i in range(128)]
                glC = [T(glt[:, i:i + 1], f"glc{i}") for i in range(128)]
                NUV = 10
                UV = [S.sb(p3, f"UV{i}", [128, 2048], BF16, dma="sw") for i in range(NUV)]
                NDG = 4
                Dg = [S.sb(p3, f"Dg{i}", [128, 128], BF16) for i in range(NDG)]

                def rstd2(src_t, col):
                    S.op("act", lambda e: e.activation(out=junk_a[:], in_=src_t[:], func=AF.Square, accum_out=stat2[:, col:col + 1]),
                         reads=[src_t], writes=[stat2, junk_a])
                    S.op("act", lambda e: e.activation(out=stat2[:, col + 1:col + 2], in_=stat2[:, col:col + 1], func=AF.Sqrt,
                                                       scale=1.0 / 1024, bias=EPS), reads=[stat2], writes=[stat2])
                    S.op("dve", lambda e: e.reciprocal(out=stat2[:, col + 1:col + 2], in_=stat2[:, col + 1:col + 2]), reads=[stat2], writes=[stat2])
                    return stat2[:, col + 1:col + 2]

                def top16(src_ap, src_t, n_el, mv, iv, mvt, ivt):
                    S.op("dve", lambda e: e.max(out=mv[:, 0:8], in_=src_ap), reads=[src_t], writes=[mvt])
                    S.op("dve", lambda e: e.max_index(out=iv[:, 0:8], in_max=mv[:, 0:8], in_values=src_ap), reads=[src_t, mvt], writes=[ivt])
                    S.op("dve", lambda e: e.match_replace(out=wk[:, 0:n_el], in_to_replace=mv[:, 0:8], in_values=src_ap, imm_value=-1e30),
                         reads=[src_t, mvt], writes=[wk])
                    S.op("dve", lambda e: e.max(out=mv[:, 8:16], in_=wk[:, 0:n_el]), reads=[wk], writes=[mvt])
                    S.op("dve", lambda e: e.max_index(out=iv[:, 8:16], in_max=mv[:, 8:16], in_values=wk[:, 0:n_el]), reads=[wk, mvt], writes=[ivt])

                def emit_d12(n):
                    par = n % 2
                    ht, hn, idxu, gate = htP[par], hnP[par], idxP[par], gateP[par]
                    r0 = base + n * 128
                    S.dma("sp", lambda q: q.dma_start(out=ht[:], in_=x_d[r0:r0 + 128, :]), writes=[ht], semt=ht)
                    for j in range(2):
                        for c in range(8):
                            src = attnT if c < 4 else oglaT
                            S.op("pe", lambda e, j=j, c=c, src=src: e.matmul(PB[j][:, :], lhsT=src[:, c % 4, n * 128:(n + 1) * 128],
                                                                             rhs=Wo[:, c, j * 512:(j + 1) * 512], start=(c == 0), stop=(c == 7)),
                                 reads=[src, Wo], writes=[PB[j]])
                        S.op("dve", lambda e, j=j: e.tensor_tensor(out=ht[:, j * 512:(j + 1) * 512], in0=PB[j][:, :], in1=ht[:, j * 512:(j + 1) * 512], op=ALU.add),
                             reads=[PB[j], ht], writes=[ht])
                    if dbg:
                        S.dma("sp", lambda q: q.dma_start(out=dbgh_d[r0:r0 + 128, :], in_=ht[:]), reads=[ht], writes=[dbgh_t], semt=dbgh_t)
                    r = rstd2(ht, 0)
                    S.op("dve", lambda e: e.scalar_tensor_tensor(out=hn[:], in0=ht[:], scalar=r, in1=g2b[:], op0=ALU.mult, op1=ALU.mult),
                         reads=[ht, stat2, g2b], writes=[hn])
                    S.op("act", lambda e: e.copy(out=hnb[:], in_=hn[:]), reads=[hn], writes=[hnb])
                    transpose_to(hnb, lambda kc: hnb[:, kc * 128:(kc + 1) * 128], 8, 2, hnT, lambda: hnT[:])
                    for g4 in range(4):
                        bk = 3 + (g4 % 2)
                        for q4 in range(4):
                            hp = g4 * 4 + q4
                            for kc in range(8):
                                S.op("pe", lambda e, hp=hp, kc=kc, bk=bk, q4=q4: e.matmul(PB[bk][:, q4 * 128:(q4 + 1) * 128], lhsT=Wq[:, kc, hp * 128:(hp + 1) * 128],
                                                                                            rhs=hnT[:, kc, :], start=(kc == 0), stop=(kc == 7)),
                                     reads=[Wq, hnT], writes=[PB[bk]])
                        S.op("act", lambda e, g4=g4, bk=bk: e.copy(out=qT[:, g4 * 4:(g4 + 1) * 4, :].rearrange("p a b -> p (a b)"), in_=PB[bk][:, :]),
                             reads=[PB[bk]], writes=[qT])
                    for g4 in range(4):
                        bk = 5 if g4 % 2 == 0 else 2
                        for q4 in range(4):
                            hp = g4 * 4 + q4
                            S.op("pe", lambda e, hp=hp, bk=bk, q4=q4: e.matmul(PB[bk][:, q4 * 128:(q4 + 1) * 128], lhsT=qT[:, hp, :], rhs=SubT[:, hp, :],
                                                                                start=True, stop=True), reads=[qT, SubT], writes=[PB[bk]])
                        S.op("act", lambda e, g4=g4, bk=bk: e.copy(out=ssb[:, g4 * 4:(g4 + 1) * 4, :].rearrange("p a b -> p (a b)"), in_=PB[bk][:, :]),
                             reads=[PB[bk]], writes=[ssb])
                    if S.defer is not None:
                        mark[0] = len(S.defer)
                    for hp in range(16):
                        top16(ssb[:, hp, :], ssb, 128, m16[:, hp, :], i16[:, hp, :], m16, i16)
                    m4 = m16[:].rearrange("p (h a) i -> p h a i", a=2)
                    S.op("dve", lambda e: e.tensor_tensor(out=cand[:].rearrange("p h (i j) -> p h i j", i=16),
                                                          in0=m4[:, :, 0, :].unsqueeze(3).to_broadcast([128, 8, 16, 16]),
                                                          in1=m4[:, :, 1, :].unsqueeze(2).to_broadcast([128, 8, 16, 16]), op=ALU.add),
                         reads=[m16], writes=[cand])
                    for h in range(8):
                        top16(cand[:, h, :], cand, 256, tops[:, h, :], pos[:, h, :], tops, pos)
                    S.op("dve", lambda e: e.tensor_single_scalar(out=pa_[:], in_=pos[:], scalar=4, op=ALU.logical_shift_right), reads=[pos], writes=[pa_])
                    S.op("dve", lambda e: e.tensor_single_scalar(out=pb_[:], in_=pos[:], scalar=15, op=ALU.bitwise_and), reads=[pos], writes=[pb_])
                    S.op("dve", lambda e: e.tensor_copy(out=paf[:], in_=pa_[:]), reads=[pa_], writes=[paf])
                    S.op("dve", lambda e: e.tensor_copy(out=pbf_[:], in_=pb_[:]), reads=[pb_], writes=[pbf_])
                    S.op("dve", lambda e: e.tensor_copy(out=i16f[:], in_=i16[:]), reads=[i16], writes=[i16f])
                    i4 = i16f[:].rearrange("p (h a) i -> p h a i", a=2)
                    iob = iota16[:].unsqueeze(1).unsqueeze(1).to_broadcast([128, 8, 16, 16])
                    for (pf, a, sel) in ((paf, 0, sel1), (pbf_, 1, sel2)):
                        S.op("dve", lambda e, pf=pf: e.tensor_tensor(out=eq[:], in0=pf[:].unsqueeze(3).to_broadcast([128, 8, 16, 16]), in1=iob, op=ALU.is_equal),
                             reads=[pf, iota16], writes=[eq])
                        S.op("dve", lambda e, a=a: e.tensor_tensor(out=eq[:], in0=eq[:], in1=i4[:, :, a, :].unsqueeze(2).to_broadcast([128, 8, 16, 16]), op=ALU.mult),
                             reads=[eq, i16f], writes=[eq])
                        S.op("dve", lambda e, sel=sel: e.tensor_reduce(out=sel[:], in_=eq[:], axis=AX.X, op=ALU.add), reads=[eq], writes=[sel])
                    S.op("dve", lambda e: e.scalar_tensor_tensor(out=idxf[:], in0=sel1[:].rearrange("p h k -> p (h k)"), scalar=128.0,
                                                                 in1=sel2[:].rearrange("p h k -> p (h k)"), op0=ALU.mult, op1=ALU.add),
                         reads=[sel1, sel2], writes=[idxf])
                    S.op("dve", lambda e: e.tensor_copy(out=idxu[:], in_=idxf[:]), reads=[idxf], writes=[idxu])
                    S.op("dve", lambda e: e.tensor_tensor(out=gate[:], in0=tops[:], in1=tops[:, :, 0:1].to_broadcast([128, 8, 16]), op=ALU.subtract),
                         reads=[tops], writes=[gate])
                    S.op("act", lambda e: e.activation(out=gate[:], in_=gate[:], func=AF.Exp), reads=[gate], writes=[gate])
                    S.op("dve", lambda e: e.tensor_reduce(out=gsum[:], in_=gate[:], axis=AX.X, op=ALU.add), reads=[gate], writes=[gsum])
                    S.op("dve", lambda e: e.reciprocal(out=gsum[:], in_=gsum[:]), reads=[gsum], writes=[gsum])
                    S.op("dve", lambda e: e.tensor_tensor(out=gate[:], in0=gate[:], in1=gsum[:].unsqueeze(2).to_broadcast([128, 8, 16]), op=ALU.mult),
                         reads=[gate, gsum], writes=[gate])

                mark = [0]
                RATE_A = 9

                def emit_loop(n, pending):
                    tail = len(pending) - mark[0]
                    par = n % 2
                    ht, hn, idxu, gate = htP[par], hnP[par], idxP[par], gateP[par]
                    r0 = base + n * 128
                    gflat = gate[:].rearrange("p h k -> p (h k)")
                    LAG = 2

                    def emit_acc(sl):
                        U = UV[sl % NUV]
                        D = Dg[sl % NDG]
                        S.op("act", lambda e: e.activation(out=glt[:, sl:sl + 1], in_=actv[:, sl:sl + 1], func=AF.Gelu_apprx_tanh),
                             reads=[actC[sl]], writes=[glC[sl]])
                        S.op("act", lambda e: e.activation(out=wct[:, sl:sl + 1], in_=gflat[:, sl:sl + 1], func=AF.Copy, scale=glt[:, sl:sl + 1]),
                             reads=[glC[sl], gate], writes=[glC[sl]])
                        S.op("act", lambda e: e.activation(out=D[:], in_=ident_bf[:], func=AF.Copy, scale=wct[:, sl:sl + 1]),
                             reads=[ident_bf, glC[sl]], writes=[D])
                        for j in range(2):
                            S.op("pe", lambda e, j=j: e.matmul(PB[6 + j][:, :], lhsT=D[:], rhs=U[:, 1024 + j * 512:1024 + (j + 1) * 512],
                                                               start=(sl == 0), stop=(sl == 127)), reads=[D, U], writes=[PB[6 + j]])

                    for sl in range(128):
                        U = UV[sl % NUV]
                        S.dma("pool", lambda q, U=U, sl=sl: q.indirect_dma_start(out=U[:], out_offset=None, in_=uvbf_d,
                                                                                 in_offset=bass.IndirectOffsetOnAxis(ap=idxu[:, sl:sl + 1], axis=0)),
                              reads=[idxu, uv_t], writes=[U], semt=U)
                        S.op("dve", lambda e, U=U, sl=sl: e.scalar_tensor_tensor(out=junkD[sl % 2][:], in0=U[:, 0:1024], scalar=1.0, in1=hn[:], op0=ALU.mult, op1=ALU.mult,
                                                                                 accum_out=actv[:, sl:sl + 1]), reads=[U, hn], writes=[junkD[sl % 2], actC[sl]])
                        if sl >= LAG:
                            emit_acc(sl - LAG)
                        if pending:
                            if len(pending) > tail:
                                S.replay(pending, min(RATE_A, len(pending) - tail))
                            else:
                                S.replay(pending, -(-len(pending) // max(1, 124 - sl)))
                    for sl in range(128 - LAG, 128):
                        emit_acc(sl)
                    if pending:
                        S.replay(pending, len(pending))
                    for j in range(2):
                        S.op("dve", lambda e, j=j: e.tensor_tensor(out=ht[:, j * 512:(j + 1) * 512], in0=PB[6 + j][:, :], in1=ht[:, j * 512:(j + 1) * 512], op=ALU.add),
                             reads=[PB[6 + j], ht], writes=[ht])
                    r = rstd2(ht, 2)
                    S.op("dve", lambda e: e.scalar_tensor_tensor(out=hn[:], in0=ht[:], scalar=r, in1=gfb[:], op0=ALU.mult, op1=ALU.mult),
                         reads=[ht, stat2, gfb], writes=[hn])
                    S.dma("sp", lambda q: q.dma_start(out=y_d[r0:r0 + 128, :], in_=hn[:]), reads=[hn], writes=[y_t], semt=y_t)

                emit_d12(0)
                for n in range(NTILE):
                    pending = []
                    if n + 1 < NTILE:
                        S.defer = pending
                        emit_d12(n + 1)
                        S.defer = None
                    emit_loop(n, pending)
                S.barrier()
        S.barrier()
        print("ninstr", S.ninstr, "nsem", S.nsem)
    return nc


def _host_inputs(inputs):
    f = np.float32
    w_in = np.asarray(inputs["w_in"])[0]
    wA = np.ascontiguousarray(np.concatenate([w_in[:, 0:416], w_in[:, 1440:1472]], axis=1))
    wB = np.ascontiguousarray(np.concatenate([w_in[:, 416:1440], w_in[:, 1472:1984], w_in[:, 1440:1472]], axis=1))
    gw = np.zeros((33, 512), f)
    gw[0:16, 0:256] = np.asarray(inputs["gate_fwd_w"])[0]
    gw[16:32, 256:512] = np.asarray(inputs["gate_bwd_w"])[0]
    gw[32, 0:256] = np.asarray(inputs["gate_fwd_b"])[0]
    gw[32, 256:512] = np.asarray(inputs["gate_bwd_b"])[0]
    subk = np.asarray(inputs["peer_subkeys"])[0].reshape(16, 128, 128)
    subkT = np.ascontiguousarray(np.transpose(subk, (2, 0, 1)))
    half = 16
    freqs = (np.float32(10000.0) ** (-np.arange(half, dtype=f) * f(2.0) / f(32))).astype(f)
    ang = (np.arange(S_LEN, dtype=f)[:, None] * freqs[None, :]).astype(f)
    shared = {
        "w_inA": wA, "w_inB": wB,
        "norm1_g": np.ascontiguousarray(np.asarray(inputs["norm1_g"])[0]),
        "q_norm_g": np.ascontiguousarray(np.asarray(inputs["q_norm_g"])[0]),
        "w_uq": np.ascontiguousarray(np.asarray(inputs["w_uq"])[0]),
        "kv_norm_g": np.ascontiguousarray(np.asarray(inputs["kv_norm_g"])[0]),
        "w_ukv": np.ascontiguousarray(np.asarray(inputs["w_ukv"])[0]),
        "gate_w": gw,
        "gla_norm_g": np.ascontiguousarray(np.asarray(inputs["gla_norm_g"])[0]),
        "w_o": np.ascontiguousarray(np.asarray(inputs["w_o"])[0]),
        "norm2_g": np.ascontiguousarray(np.asarray(inputs["norm2_g"])[0]),
        "peer_wq": np.ascontiguousarray(np.asarray(inputs["peer_wq"])[0]),
        "subkT": subkT,
        "peer_u": np.ascontiguousarray(np.asarray(inputs["peer_u"])[0]),
        "peer_v": np.ascontiguousarray(np.asarray(inputs["peer_v"])[0]),
        "final_norm_g": np.ascontiguousarray(np.asarray(inputs["final_norm_g"])),
        "rope_cos": np.cos(ang).astype(f), "rope_sin": np.sin(ang).astype(f),
    }
    return shared


def kernel(**inputs):
    xp = np.asarray(inputs["x_prompt"], dtype=np.float32)
    xs = np.asarray(inputs["x_sample"], dtype=np.float32)
    xall = np.concatenate([xp, xs], axis=0)
    nb = xall.shape[0]
    ncore = 8
    per = nb // ncore
    shared = _host_inputs(inputs)
    nc = build_program(per, False)
    in_maps = []
    for c in range(ncore):
        m = dict(shared)
        m["x"] = np.ascontiguousarray(xall[c * per:(c + 1) * per].reshape(per * S_LEN, 1024))
        in_maps.append(m)
    res = run_bass_kernel_spmd(nc, in_maps, core_ids=list(range(ncore)))
    ys = [np.asarray(r["y"]).reshape(per, S_LEN, 1024) for r in res.results]
    yall = np.concatenate(ys, axis=0).astype(np.float32)
    return (yall[:xp.shape[0]], yall[xp.shape[0]:])
```

```python
import numpy as np
from contextlib import ExitStack
import concourse.bass as bass
import concourse.mybir as mybir
from concourse.bass_utils import run_bass_kernel_spmd

F32 = mybir.dt.float32
BF16 = mybir.dt.bfloat16
U32 = mybir.dt.uint32
AF = mybir.ActivationFunctionType
ALU = mybir.AluOpType
AX = mybir.AxisListType

NSEQ = 5
DBG = False
EPOCH = 60000
S_LEN = 2048
NTILE = 16
EPS = 1e-6


class DSem:
    def __init__(self, h):
        self.h = h
        self.cnt = 0


class T:
    def __init__(self, ap, name, dsem=None):
        self.ap = ap
        self.name = name
        self.lw = None
        self.rd = {}
        self.dsem = dsem

    def __getitem__(self, k):
        return self.ap[k]


class Sched:
    def __init__(self, nc, stack):
        self.nc = nc
        self.stack = stack
        self.eng = {"pe": nc.tensor, "act": nc.scalar, "dve": nc.vector, "pool": nc.gpsimd, "sp": nc.sync}
        self.cnt = {k: 0 for k in self.eng}
        self.sem = {}
        self.nsem = 0
        for k in self.eng:
            self.sem[k] = self._newsem(k)
        self.waited = {k: {} for k in self.eng}
        self.ninstr = 0
        self.dpool = {"sw": [], "hw": []}
        self.dall = []

    def _newsem(self, name):
        self.nsem += 1
        return self.stack.enter_context(self.nc.semaphore(f"s{self.nsem}_{name}"))

    def getd(self, kind="hw"):
        if self.dpool[kind]:
            return self.dpool[kind].pop()
        d = DSem(self._newsem("d" + kind))
        d.kind = kind
        self.dall.append(d)
        return d

    def sb(self, st, name, shape, dtype, dma=False):
        self.nalloc = getattr(self, "nalloc", 0) + 1
        name = f"{name}_{self.nalloc}"
        t = st.enter_context(self.nc.sbuf_tensor(name, shape, dtype))
        kind = "hw" if dma is True else dma
        tt = T(t, name, self.getd(kind) if dma else None)
        if dma:
            st.callback(lambda d=tt.dsem: self.dpool[d.kind].append(d))
        return tt

    def ps(self, st, name, shape, dtype):
        t = st.enter_context(self.nc.psum_tensor(name, shape, dtype))
        tt = T(t, name)
        tt.excl = True
        return tt

    def dram(self, ap, name):
        return T(ap, name, self.getd())

    def _wait(self, e, deps):
        w = self.waited[e]
        for (sem, val) in deps:
            key = id(sem)
            if w.get(key, 0) >= val:
                continue
            self.eng[e].wait_ge(sem, val)
            w[key] = val
            self.ninstr += 1

    def replay(self, lst, k):
        d, self.defer = self.defer, None
        for _ in range(min(k, len(lst))):
            kind, a = lst.pop(0)
            (self.op if kind == "op" else self.dma)(*a)
        self.defer = d

    def op(self, e, fn, reads=(), writes=()):
        if getattr(self, "defer", None) is not None:
            self.defer.append(("op", (e, fn, list(reads), list(writes))))
            return None
        ex = [t for t in reads if getattr(t, "excl", False)]
        if ex:
            reads = [t for t in reads if not getattr(t, "excl", False)]
            writes = list(writes) + ex
        deps = []
        for t in reads:
            if t.lw is not None:
                deps.append(t.lw[1:])
        strict = (e != "pe")
        for t in writes:
            if t.lw is not None and (t.lw[0] != e or strict):
                deps.append(t.lw[1:])
            for en, d in t.rd.items():
                if en != e or strict:
                    deps.append(d)
        self._wait(e, deps)
        ins = fn(self.eng[e])
        self.cnt[e] += 1
        if self.cnt[e] > EPOCH:
            self.sem[e] = self._newsem(e)
            self.cnt[e] = 1
        ins.then_inc(self.sem[e], 1)
        self.ninstr += 1
        rec = (self.sem[e], self.cnt[e])
        for t in reads:
            t.rd[e] = rec
        for t in writes:
            t.lw = (e,) + rec
            t.rd = {}
        return ins

    def dma(self, q, fn, reads=(), writes=(), semt=None):
        if getattr(self, "defer", None) is not None:
            self.defer.append(("dma", (q, fn, list(reads), list(writes), semt)))
            return None
        deps = []
        for t in reads:
            if t.lw is not None:
                deps.append(t.lw[1:])
        for t in writes:
            if t.lw is not None:
                deps.append(t.lw[1:])
            for en, d in t.rd.items():
                deps.append(d)
        self._wait(q, deps)
        ins = fn(self.eng[q])
        ds = semt.dsem
        ds.cnt += 16
        ins.then_inc(ds.h, 16)
        self.ninstr += 1
        rec = (ds.h, ds.cnt)
        for t in reads:
            t.rd[("dma", id(ds))] = rec
        for t in writes:
            t.lw = ("dma",) + rec
            t.rd = {}
        return ins

    def barrier(self):
        deps = [(self.sem[k], self.cnt[k]) for k in self.eng if self.cnt[k] > 0]
        deps += [(d.h, d.cnt) for d in self.dall if d.cnt > 0]
        for e in self.eng:
            self._wait(e, deps)


def build_program(nseq, dbg, stop=None):
    nc = bass.Bass("TRN2", target_bir_lowering=False)
    ntok = nseq * S_LEN

    def din(name, shape, dt=F32):
        return nc.dram_tensor(name, shape, dt, kind="ExternalInput").ap()

    x_d = din("x", [ntok, 1024])
    winA_d = din("w_inA", [1024, 448])
    winB_d = din("w_inB", [1024, 1568])
    g1_d = din("norm1_g", [1024])
    gq_d = din("q_norm_g", [256])
    wuq_d = din("w_uq", [256, 768])
    gkv_d = din("kv_norm_g", [128])
    wukv_d = din("w_ukv", [128, 1024])
    wg_d = din("gate_w", [33, 512])
    gla_d = din("gla_norm_g", [128])
    wo_d = din("w_o", [1024, 1024])
    g2_d = din("norm2_g", [1024])
    wq_d = din("peer_wq", [1024, 2048])
    subk_d = din("subkT", [128, 16, 128])
    u_d = din("peer_u", [16384, 1024])
    v_d = din("peer_v", [16384, 1024])
    gf_d = din("final_norm_g", [1024])
    cos_d = din("rope_cos", [S_LEN, 16])
    sin_d = din("rope_sin", [S_LEN, 16])
    y_d = nc.dram_tensor("y", [ntok, 1024], F32, kind="ExternalOutput").ap()
    uvbf_d = nc.dram_tensor("uv_bf", [16384, 2048], BF16, kind="Internal").ap()
    if dbg:
        dbgh_d = nc.dram_tensor("dbg_h", [ntok, 1024], F32, kind="ExternalOutput").ap()
        dbgi_d = nc.dram_tensor("dbg_i", [ntok, 128], U32, kind="ExternalOutput").ap()
        dbgg_d = nc.dram_tensor("dbg_g", [ntok, 128], F32, kind="ExternalOutput").ap()
        dbga_d = nc.dram_tensor("dbg_a", [ntok, 128], F32, kind="ExternalOutput").ap()

    with ExitStack() as gst:
        S = Sched(nc, gst)
        y_t = S.dram(y_d, "y")
        if dbg:
            dbgh_t = S.dram(dbgh_d, "dbgh")
            dbgi_t = S.dram(dbgi_d, "dbgi")
            dbgg_t = S.dram(dbgg_d, "dbgg")
            dbga_t = S.dram(dbga_d, "dbga")

        ident_bf = S.sb(gst, "ident_bf", [128, 128], BF16)
        ident_f = S.sb(gst, "ident_f", [128, 128], F32)
        triL = S.sb(gst, "triL", [128, 128], F32)
        triU = S.sb(gst, "triU", [128, 128], F32)
        eye16 = S.sb(gst, "eye16", [16, 16], F32)
        ones16 = S.sb(gst, "ones16", [16, 128], F32)
        iota16 = S.sb(gst, "iota16", [128, 16], F32)
        for tt in (ident_bf, ident_f):
            S.op("pool", lambda e, tt=tt: e.memset(tt[:], 0.0), writes=[tt])
            S.op("pool", lambda e, tt=tt: e.affine_select(out=tt[:], in_=tt[:], pattern=[[-1, 128]], compare_op=ALU.not_equal,
                                                          fill=1.0, base=0, channel_multiplier=1), reads=[tt], writes=[tt])
        S.op("pool", lambda e: e.memset(eye16[:], 0.0), writes=[eye16])
        S.op("pool", lambda e: e.affine_select(out=eye16[:], in_=eye16[:], pattern=[[-1, 16]], compare_op=ALU.not_equal,
                                               fill=1.0, base=0, channel_multiplier=1), reads=[eye16], writes=[eye16])
        S.op("pool", lambda e: e.memset(ones16[:], 1.0), writes=[ones16])
        S.op("pool", lambda e: e.memset(triL[:], 1.0), writes=[triL])
        S.op("pool", lambda e: e.affine_select(out=triL[:], in_=triL[:], pattern=[[1, 128]], compare_op=ALU.is_ge,
                                               fill=0.0, base=0, channel_multiplier=-1), reads=[triL], writes=[triL])
        S.op("pool", lambda e: e.memset(triU[:], 1.0), writes=[triU])
        S.op("pool", lambda e: e.affine_select(out=triU[:], in_=triU[:], pattern=[[-1, 128]], compare_op=ALU.is_ge,
                                               fill=0.0, base=0, channel_multiplier=1), reads=[triU], writes=[triU])
        S.op("pool", lambda e: e.iota(iota16[:], pattern=[[1, 16]], base=0, channel_multiplier=0,
                                      allow_small_or_imprecise_dtypes=True), writes=[iota16])
        attnT = S.sb(gst, "attnT", [128, 4, S_LEN], BF16)
        oglaT = S.sb(gst, "oglaT", [128, 4, S_LEN], BF16)
        PB = [S.ps(gst, f"pb{i}", [128, 512], F32) for i in range(8)]

        def pbf(i):
            return PB[i][:].bitcast(BF16)

        stat = S.sb(gst, "stat", [128, 8], F32)
        DrowG = S.sb(gst, "DrowG", [16, 512], F32)

        def rstd_of(src_ap, src_ts, width, junk, col, st_=None):
            stt = stat if st_ is None else st_
            jap = junk[:, 0:width]
            S.op("act", lambda e: e.activation(out=jap, in_=src_ap, func=AF.Square, accum_out=stt[:, col:col + 1]),
                 reads=src_ts, writes=[stt, junk])
            S.op("act", lambda e: e.activation(out=stt[:, col + 1:col + 2], in_=stt[:, col:col + 1], func=AF.Ln,
                                               scale=1.0 / width, bias=EPS), reads=[stt], writes=[stt])
            S.op("act", lambda e: e.activation(out=stt[:, col + 1:col + 2], in_=stt[:, col + 1:col + 2], func=AF.Exp, scale=-0.5),
                 reads=[stt], writes=[stt])
            return stt[:, col + 1:col + 2]

        def run_pipelined(tile_fn, ntile):
            lists = []
            for n in range(ntile):
                S.defer = L = []
                tile_fn(n)
                S.defer = None
                lists.append(L)
            cur = lists[0]
            S.replay(cur, len(cur) // 2)
            for n in range(1, ntile):
                nxt = lists[n]
                half = len(nxt) // 2
                k = 0
                while cur or k < half:
                    if cur:
                        S.replay(cur, 1)
                    if k < half:
                        S.replay(nxt, 1)
                        k += 1
                cur = nxt
            S.replay(cur, len(cur))

        junk_t = S.sb(gst, "junk", [128, 1024], BF16)

        def load_norm_T(row0, gb, xt, n1, n1T, bank, st_, jk):
            S.dma("sp", lambda q: q.dma_start(out=xt[:], in_=x_d[row0:row0 + 128, :]), writes=[xt], semt=xt)
            r = rstd_of(xt[:], [xt], 1024, jk, 0, st_)
            S.op("dve", lambda e: e.scalar_tensor_tensor(out=n1[:], in0=xt[:], scalar=r, in1=gb[:], op0=ALU.mult, op1=ALU.mult),
                 reads=[xt, st_, gb], writes=[n1])
            transpose_to(n1, lambda kc: n1[:, kc * 128:(kc + 1) * 128], 8, bank, n1T, lambda: n1T[:])

        def transpose_to(src_t, src_fn, nblk, bank, dst_t, dst_fn, rows=128, eng="act"):
            pv = pbf(bank)
            for k in range(nblk):
                S.op("pe", lambda e, k=k: e.transpose(out=pv[0:rows, k * 128:(k + 1) * 128], in_=src_fn(k), identity=ident_bf[:]),
                     reads=[src_t, ident_bf], writes=[PB[bank]])
            srcv = pv[0:rows, 0:nblk * 128].rearrange("p (a b) -> p a b", a=nblk)
            if eng == "act":
                S.op("act", lambda e: e.copy(out=dst_fn(), in_=srcv), reads=[PB[bank]], writes=[dst_t])
            else:
                S.op("dve", lambda e: e.tensor_copy(out=dst_fn(), in_=srcv), reads=[PB[bank]], writes=[dst_t])

        def bcast_load(st, name, vec_d, n):
            t = S.sb(st, name, [128, n], F32, dma=True)
            S.dma("sp", lambda q: q.dma_start(out=t[:], in_=vec_d.partition_broadcast(128)), writes=[t], semt=t)
            return t

        uv_t = S.dram(uvbf_d, "uvbf")
        prepass = []

        for s in range(nseq):
            base = s * S_LEN
            with ExitStack() as p1:
                WinA = S.sb(p1, "WinA", [128, 8, 448], BF16, dma="sw")
                Wuq = S.sb(p1, "Wuq", [128, 2, 768], BF16, dma="sw")
                Wukv = S.sb(p1, "Wukv", [128, 1024], BF16, dma="sw")
                Wg = S.sb(p1, "Wg", [33, 512], F32, dma=True)
                S.dma("pool", lambda q: q.dma_start(out=WinA[:], in_=winA_d.rearrange("(c p) n -> p c n", p=128)), writes=[WinA], semt=WinA)
                S.dma("pool", lambda q: q.dma_start(out=Wuq[:], in_=wuq_d.rearrange("(c p) n -> p c n", p=128)), writes=[Wuq], semt=Wuq)
                S.dma("pool", lambda q: q.dma_start(out=Wukv[:], in_=wukv_d), writes=[Wukv], semt=Wukv)
                S.dma("sp", lambda q: q.dma_start(out=Wg[:], in_=wg_d), writes=[Wg], semt=Wg)
                g1b = bcast_load(p1, "g1b", g1_d, 1024)
                gqb = bcast_load(p1, "gqb", gq_d, 256)
                gkvb = bcast_load(p1, "gkvb", gkv_d, 128)
                cosT = S.sb(p1, "cosT", [128, NTILE, 16], F32, dma=True)
                sinT = S.sb(p1, "sinT", [128, NTILE, 16], F32, dma=True)
                S.dma("sp", lambda q: q.dma_start(out=cosT[:], in_=cos_d.rearrange("(n p) d -> p n d", p=128)), writes=[cosT], semt=cosT)
                S.dma("sp", lambda q: q.dma_start(out=sinT[:], in_=sin_d.rearrange("(n p) d -> p n d", p=128)), writes=[sinT], semt=sinT)

                KT = S.sb(p1, "KT", [128, 8, S_LEN], BF16)
                Vaug = S.sb(p1, "Vaug", [128, NTILE, 8, 65], BF16)
                cqT = S.sb(p1, "cqT", [128, 2, S_LEN], BF16)
                TTs = S.sb(p1, "TTs", [128, 4, NTILE], F32)
                S.op("pool", lambda e: e.memset(Vaug[:], 1.0), writes=[Vaug])
                with ExitStack() as p1a:
                    def mk(name, shape, dt, **kw):
                        return [S.sb(p1a, f"{name}{i}", shape, dt, **kw) for i in range(2)]
                    xtP = mk("xt", [128, 1024], F32, dma=True)
                    n1P = mk("n1", [128, 1024], BF16)
                    n1TP = mk("n1T", [128, 8, 128], BF16)
                    paP = mk("pa", [128, 448], F32)
                    cqnP = mk("cqn", [128, 384], BF16)
                    latP = mk("latT", [128, 3, 128], BF16)
                    glrAP = mk("glrA", [33, 128], F32)
                    KtP = mk("Kt", [128, 8, 96], BF16)
                    rpP = mk("rp", [128, 4, 16], F32)
                    spP = mk("sp", [128, 512], F32)
                    krrP = mk("krr", [128, 32], BF16)
                    stP = mk("st", [128, 8], F32)
                    jkP = mk("jk", [128, 1024], BF16)
                    for i in range(2):
                        S.op("pool", lambda e, i=i: e.memset(glrAP[i][:], 1.0), writes=[glrAP[i]])

                    def a1_tile(n):
                        p = n % 2
                        xt, n1, n1T, pa, cqn, lat, glrA, Kt, rp, sp_t, krr, st_, jk = (xtP[p], n1P[p], n1TP[p], paP[p], cqnP[p], latP[p], glrAP[p],
                                                                                      KtP[p], rpP[p], spP[p], krrP[p], stP[p], jkP[p])
                        X0, X1, X2, X3 = 4 * p, 4 * p + 1, 4 * p + 2, 4 * p + 3
                        load_norm_T(base + n * 128, g1b, xt, n1, n1T, X0, st_, jk)
                        for kc in range(8):
                            S.op("pe", lambda e, kc=kc: e.matmul(PB[X1][:, 0:448], lhsT=n1T[:, kc, :], rhs=WinA[:, kc, :], start=(kc == 0), stop=(kc == 7)),
                                 reads=[n1T, WinA], writes=[PB[X1]])
                        S.op("act", lambda e: e.copy(out=pa[:], in_=PB[X1][:, 0:448]), reads=[PB[X1]], writes=[pa])
                        r = rstd_of(pa[:, 0:256], [pa], 256, jk, 2, st_)
                        S.op("dve", lambda e: e.scalar_tensor_tensor(out=cqn[:, 0:256], in0=pa[:, 0:256], scalar=r, in1=gqb[:], op0=ALU.mult, op1=ALU.mult),
                             reads=[pa, st_, gqb], writes=[cqn])
                        r2 = rstd_of(pa[:, 256:384], [pa], 128, jk, 4, st_)
                        S.op("dve", lambda e: e.scalar_tensor_tensor(out=cqn[:, 256:384], in0=pa[:, 256:384], scalar=r2, in1=gkvb[:], op0=ALU.mult, op1=ALU.mult),
                             reads=[pa, st_, gkvb], writes=[cqn])
                        transpose_to(cqn, lambda k: cqn[:, k * 128:(k + 1) * 128], 3, X0, lat, lambda: lat[:], eng="dve")
                        S.op("dve", lambda e: e.tensor_copy(out=cqT[:, :, n * 128:(n + 1) * 128], in_=lat[:, 0:2, :]), reads=[lat], writes=[cqT])
                        for j in range(2):
                            bk = X2 + j
                            S.op("pe", lambda e, j=j, bk=bk: e.matmul(PB[bk][:, :], lhsT=lat[:, 2, :], rhs=Wukv[:, j * 512:(j + 1) * 512], start=True, stop=True),
                                 reads=[lat, Wukv], writes=[PB[bk]])
                            kv = PB[bk][:, :].rearrange("p (h c) -> p h c", h=4)
                            S.op("act", lambda e, j=j, kv=kv, bk=bk: e.copy(out=Vaug[:, n, 4 * j:4 * j + 4, 0:64], in_=kv[:, :, 64:128]),
                                 reads=[PB[bk]], writes=[Vaug])
                            S.op("dve", lambda e, j=j, kv=kv, bk=bk: e.tensor_copy(out=Kt[:, 4 * j:4 * j + 4, 0:64], in_=kv[:, :, 0:64]),
                                 reads=[PB[bk]], writes=[Kt])
                        c_ = cosT[:, n, :]
                        s_ = sinT[:, n, :]
                        x1, x2 = pa[:, 384:400], pa[:, 400:416]
                        S.op("dve", lambda e: e.tensor_tensor(out=rp[:, 0, :], in0=x1, in1=c_, op=ALU.mult), reads=[pa, cosT], writes=[rp])
                        S.op("dve", lambda e: e.tensor_tensor(out=rp[:, 1, :], in0=x2, in1=s_, op=ALU.mult), reads=[pa, sinT], writes=[rp])
                        S.op("dve", lambda e: e.tensor_tensor(out=rp[:, 2, :], in0=x2, in1=c_, op=ALU.mult), reads=[pa, cosT], writes=[rp])
                        S.op("dve", lambda e: e.tensor_tensor(out=rp[:, 3, :], in0=x1, in1=s_, op=ALU.mult), reads=[pa, sinT], writes=[rp])
                        S.op("dve", lambda e: e.tensor_tensor(out=krr[:, 0:16], in0=rp[:, 0, :], in1=rp[:, 1, :], op=ALU.subtract), reads=[rp], writes=[krr])
                        S.op("dve", lambda e: e.tensor_tensor(out=krr[:, 16:32], in0=rp[:, 2, :], in1=rp[:, 3, :], op=ALU.add), reads=[rp], writes=[krr])
                        S.op("dve", lambda e: e.tensor_copy(out=Kt[:, :, 64:96], in_=krr[:].unsqueeze(1).to_broadcast([128, 8, 32])),
                             reads=[krr], writes=[Kt])
                        transpose_to(Kt, lambda h: Kt[:, h, :], 8, X0, KT, lambda: KT[0:96, :, n * 128:(n + 1) * 128], rows=96)
                        S.op("pe", lambda e: e.transpose(out=PB[X1][0:32, 0:128], in_=pa[:, 416:448], identity=ident_f[:]),
                             reads=[pa, ident_f], writes=[PB[X1]])
                        S.op("act", lambda e: e.copy(out=glrA[0:32, :], in_=PB[X1][0:32, 0:128]), reads=[PB[X1]], writes=[glrA])
                        S.op("pe", lambda e: e.matmul(PB[X2][:, :], lhsT=glrA[:, :], rhs=Wg[:, :], start=True, stop=True),
                             reads=[glrA, Wg], writes=[PB[X2]])
                        S.op("act", lambda e: e.activation(out=sp_t[:], in_=PB[X2][:, :], func=AF.Exp, scale=-1.0), reads=[PB[X2]], writes=[sp_t])
                        S.op("act", lambda e: e.activation(out=sp_t[:], in_=sp_t[:], func=AF.Ln, bias=1.0), reads=[sp_t], writes=[sp_t])
                        for cc in range(4):
                            S.op("pe", lambda e, cc=cc: e.matmul(PB[X1][:, 256 + cc:257 + cc], lhsT=sp_t[:, cc * 128:(cc + 1) * 128], rhs=triU[:, 0:1],
                                                                 start=True, stop=True), reads=[sp_t, triU], writes=[PB[X1]])
                        S.op("dve", lambda e: e.tensor_copy(out=TTs[:, :, n], in_=PB[X1][:, 256:260]), reads=[PB[X1]], writes=[TTs])

                    run_pipelined(a1_tile, NTILE)
                    S.barrier()
                if stop == "A1":
                    S.barrier()
                    return nc
                Inc = S.sb(p1, "Inc", [128, 4, NTILE], F32)
                Dc = S.sb(p1, "Dc", [128, 4, NTILE], F32)
                S.op("dve", lambda e: e.tensor_copy(out=Inc[:, :, 0:1], in_=TTs[:, :, 0:1]), reads=[TTs], writes=[Inc])
                for n in range(1, NTILE):
                    S.op("dve", lambda e, n=n: e.tensor_tensor(out=Inc[:, :, n:n + 1], in0=Inc[:, :, n - 1:n], in1=TTs[:, :, n:n + 1], op=ALU.add),
                         reads=[Inc, TTs], writes=[Inc])
                S.op("dve", lambda e: e.tensor_tensor(out=Dc[:, 0:2, :], in0=Inc[:, 0:2, :], in1=TTs[:, 0:2, :], op=ALU.subtract), reads=[Inc, TTs], writes=[Dc])
                S.op("dve", lambda e: e.tensor_tensor(out=Dc[:, 0:2, :], in0=Dc[:, 0:2, :], in1=Inc[:, 0:2, 7:8].to_broadcast([128, 2, NTILE]), op=ALU.subtract),
                     reads=[Dc, Inc], writes=[Dc])
                S.op("dve", lambda e: e.tensor_tensor(out=Dc[:, 2:4, :], in0=Inc[:, 2:4, 7:8].to_broadcast([128, 2, NTILE]), in1=Inc[:, 2:4, :], op=ALU.subtract),
                     reads=[Inc], writes=[Dc])
                for cc in range(4):
                    S.op("pe", lambda e, cc=cc: e.transpose(out=PB[3][0:16, cc * 128:(cc + 1) * 128], in_=Dc[:, cc, :], identity=ident_f[:]),
                         reads=[Dc, ident_f], writes=[PB[3]])
                S.op("act", lambda e: e.copy(out=DrowG[:], in_=PB[3][0:16, :]), reads=[PB[3]], writes=[DrowG])

                if stop == "A1b":
                    S.barrier()
                    return nc
                if s == 0:
                    cb = [S.sb(p1, f"cb{i}", [128, 4, 1024], BF16, dma="sw") for i in range(3)]
                    S.defer = prepass
                    k = 0
                    for (src, c0) in ((u_d, 0), (v_d, 1024)):
                        for ch in range(32):
                            b_ = cb[k % 3]
                            k += 1
                            S.dma("pool", lambda q, b_=b_, src=src, ch=ch: q.dma_start(out=b_[:], in_=src[ch * 512:(ch + 1) * 512, :].rearrange("(p r) d -> p r d", r=4)),
                                  writes=[b_], semt=b_)
                            S.dma("sp", lambda q, b_=b_, c0=c0, ch=ch: q.dma_start(out=uvbf_d[ch * 512:(ch + 1) * 512, c0:c0 + 1024].rearrange("(p r) d -> p r d", r=4), in_=b_[:]),
                                  reads=[b_], writes=[uv_t], semt=uv_t)
                    S.defer = None
                Qt = S.sb(p1, "Qt", [128, 8, 96], BF16)
                rpq = S.sb(p1, "rpq", [128, 4, 4, 16], F32)
                QTcP = [S.sb(p1, f"QTc{i}", [128, 8, 512], BF16) for i in range(2)]
                Eb = [S.sb(p1, f"Eb{i}", [128, 512], BF16) for i in range(2)]
                Oacc = S.sb(p1, "Oacc", [128, 4, 8, 65], F32)
                rc = S.sb(p1, "rc", [128, 4, 8], F32)
                atok = S.sb(p1, "atok", [128, 4, 512], BF16)
                sc = 96.0 ** -0.5
                def emit_qproj(qc, QTc_t):
                    for i in range(4):
                        n = qc * 4 + i
                        for j in range(2):
                            for kc in range(2):
                                S.op("pe", lambda e, j=j, kc=kc, n=n: e.matmul(PB[6 + j][:, 0:384], lhsT=cqT[:, kc, n * 128:(n + 1) * 128],
                                                                                 rhs=Wuq[:, kc, j * 384:(j + 1) * 384], start=(kc == 0), stop=(kc == 1)),
                                     reads=[cqT, Wuq], writes=[PB[6 + j]])
                            qv = PB[6 + j][:, 0:384].rearrange("p (h c) -> p h c", h=4)
                            S.op("act", lambda e, j=j, qv=qv: e.copy(out=Qt[:, 4 * j:4 * j + 4, 0:64], in_=qv[:, :, 0:64]), reads=[PB[6 + j]], writes=[Qt])
                            cb_ = cosT[:, n, :].unsqueeze(1).to_broadcast([128, 4, 16])
                            sb2_ = sinT[:, n, :].unsqueeze(1).to_broadcast([128, 4, 16])
                            x1, x2 = qv[:, :, 64:80], qv[:, :, 80:96]
                            bkq = PB[6 + j]
                            S.op("dve", lambda e, x1=x1, cb_=cb_: e.tensor_tensor(out=rpq[:, 0], in0=x1, in1=cb_, op=ALU.mult), reads=[bkq, cosT], writes=[rpq])
                            S.op("dve", lambda e, x2=x2, sb2_=sb2_: e.tensor_tensor(out=rpq[:, 1], in0=x2, in1=sb2_, op=ALU.mult), reads=[bkq, sinT], writes=[rpq])
                            S.op("dve", lambda e, x2=x2, cb_=cb_: e.tensor_tensor(out=rpq[:, 2], in0=x2, in1=cb_, op=ALU.mult), reads=[bkq, cosT], writes=[rpq])
                            S.op("dve", lambda e, x1=x1, sb2_=sb2_: e.tensor_tensor(out=rpq[:, 3], in0=x1, in1=sb2_, op=ALU.mult), reads=[bkq, sinT], writes=[rpq])
                            S.op("dve", lambda e, j=j: e.tensor_tensor(out=Qt[:, 4 * j:4 * j + 4, 64:80], in0=rpq[:, 0], in1=rpq[:, 1], op=ALU.subtract), reads=[rpq], writes=[Qt])
                            S.op("dve", lambda e, j=j: e.tensor_tensor(out=Qt[:, 4 * j:4 * j + 4, 80:96], in0=rpq[:, 2], in1=rpq[:, 3], op=ALU.add), reads=[rpq], writes=[Qt])
                        transpose_to(Qt, lambda h: Qt[:, h, :], 8, 6, QTc_t, lambda i=i: QTc_t[0:96, :, i * 128:(i + 1) * 128], rows=96)
                for qc in range(4):
                    QTc = QTcP[qc % 2]
                    if qc == 0:
                        emit_qproj(0, QTc)
                    pend_q = []
                    if qc + 1 < 4:
                        S.defer = pend_q
                        emit_qproj(qc + 1, QTcP[(qc + 1) % 2])
                        S.defer = None
                    jobsB = [(h, kt) for h in range(8) for kt in range(NTILE)]

                    def b_score(i):
                        h, kt = jobsB[i]
                        sb_ = i % 2
                        if prepass:
                            S.replay(prepass, 1)
                        if pend_q:
                            S.replay(pend_q, 1)
                        S.op("pe", lambda e: e.matmul(PB[sb_][:, :], lhsT=KT[0:96, h, kt * 128:(kt + 1) * 128], rhs=QTc[0:96, h, :],
                                                      start=True, stop=True), reads=[KT, QTc], writes=[PB[sb_]])
                        S.op("act", lambda e: e.activation(out=Eb[sb_][:], in_=PB[sb_][:, :], func=AF.Exp, scale=sc),
                             reads=[PB[sb_]], writes=[Eb[sb_]])

                    def b_pv(i):
                        h, kt = jobsB[i]
                        sb_ = i % 2
                        pv0 = 2 + 2 * (h % 2)
                        for qt in range(4):
                            bk = pv0 + qt // 2
                            c0 = (qt % 2) * 128
                            S.op("pe", lambda e, qt=qt, bk=bk, c0=c0: e.matmul(
                                PB[bk][:, c0:c0 + 65], lhsT=Eb[sb_][:, qt * 128:(qt + 1) * 128], rhs=Vaug[:, kt, h, :],
                                start=(kt == 0 and qt % 2 == 0), stop=(kt == NTILE - 1), skip_group_check=True),
                                reads=[Eb[sb_], Vaug], writes=[PB[bk]])
                        if kt == NTILE - 1:
                            for half in range(2):
                                bk = pv0 + half
                                src = PB[bk][:, 0:256].rearrange("p (a c) -> p a c", a=2)[:, :, 0:65]
                                S.op("dve", lambda e, half=half, src=src: e.tensor_copy(out=Oacc[:, 2 * half:2 * half + 2, h, :], in_=src),
                                     reads=[PB[bk]], writes=[Oacc])

                    for i in range(len(jobsB) + 1):
                        if i < len(jobsB):
                            b_score(i)
                        if i >= 1:
                            b_pv(i - 1)
                    if pend_q:
                        S.replay(pend_q, len(pend_q))
                    S.op("dve", lambda e: e.reciprocal(out=rc[:], in_=Oacc[:, :, :, 64]), reads=[Oacc], writes=[rc])
                    for i in range(4):
                        S.op("dve", lambda e, i=i: e.tensor_tensor(out=atok[:, i, :].rearrange("p (h c) -> p h c", h=8), in0=Oacc[:, i, :, 0:64],
                                                                   in1=rc[:, i, :].unsqueeze(2).to_broadcast([128, 8, 64]), op=ALU.mult),
                             reads=[Oacc, rc], writes=[atok])
                        n = qc * 4 + i
                        transpose_to(atok, lambda c, i=i: atok[:, i, c * 128:(c + 1) * 128], 4, 6 + (i % 2), attnT,
                                     lambda n=n: attnT[:, :, n * 128:(n + 1) * 128], eng="dve")
                if prepass:
                    S.replay(prepass, len(prepass))
                S.barrier()
            if stop == "B":
                return nc

            with ExitStack() as p2:
                qdT = S.sb(p2, "qdT", [128, 4, S_LEN], BF16)
                kdT = S.sb(p2, "kdT", [128, 4, S_LEN], BF16)
                GV = S.sb(p2, "GV", [128, NTILE, 512], BF16)
                SG = S.sb(p2, "SG", [128, NTILE, 512], BF16)
                glab = bcast_load(p2, "glab", gla_d, 128)
                with ExitStack() as p2a:
                    WinB = S.sb(p2a, "WinB", [128, 8, 1568], BF16, dma="sw")
                    Wg = S.sb(p2a, "Wg2", [33, 512], F32, dma=True)
                    S.dma("pool", lambda q: q.dma_start(out=WinB[:], in_=winB_d.rearrange("(c p) n -> p c n", p=128)), writes=[WinB], semt=WinB)
                    S.dma("sp", lambda q: q.dma_start(out=Wg[:], in_=wg_d), writes=[Wg], semt=Wg)
                    g1b = bcast_load(p2a, "g1b2", g1_d, 1024)

                    def mk(name, shape, dt, **kw):
                        return [S.sb(p2a, f"{name}{i}", shape, dt, **kw) for i in range(2)]
                    xtP = mk("xt2", [128, 1024], F32, dma=True)
                    n1P = mk("n12", [128, 1024], BF16)
                    n1TP = mk("n1T2", [128, 8, 128], BF16)
                    glrP = mk("glr", [128, 32], F32)
                    glrAP = mk("glrA2", [33, 128], F32)
                    spP = mk("sp2", [128, 512], F32)
                    EpP = mk("Ep", [128, 512], F32)
                    EmP = mk("Em", [128, 512], F32)
                    qdP = mk("qd", [128, 512], BF16)
                    kdP = mk("kd", [128, 512], BF16)
                    DrMP = mk("DrM", [16, 512], F32)
                    stP = mk("st2", [128, 8], F32)
                    jkP = mk("jk2", [128, 1024], BF16)
                    for i in range(2):
                        S.op("pool", lambda e, i=i: e.memset(glrAP[i][:], 1.0), writes=[glrAP[i]])

                    def a2_tile(n):
                        p = n % 2
                        xt, n1, n1T, glr, glrA, sp_t, Ep, Em, qd, kd, DrM, st_, jk = (xtP[p], n1P[p], n1TP[p], glrP[p], glrAP[p], spP[p], EpP[p], EmP[p],
                                                                                      qdP[p], kdP[p], DrMP[p], stP[p], jkP[p])
                        X0, X1, X2, X3 = 4 * p, 4 * p + 1, 4 * p + 2, 4 * p + 3
                        load_norm_T(base + n * 128, g1b, xt, n1, n1T, X0, st_, jk)
                        for (bk, c0, w) in ((X1, 0, 512), (X2, 512, 512), (X3, 1024, 512)):
                            for kc in range(8):
                                S.op("pe", lambda e, kc=kc, bk=bk, c0=c0, w=w: e.matmul(PB[bk][:, 0:w], lhsT=n1T[:, kc, :], rhs=WinB[:, kc, c0:c0 + w],
                                                                                         start=(kc == 0), stop=(kc == 7)), reads=[n1T, WinB], writes=[PB[bk]])
                        S.op("act", lambda e: e.copy(out=GV[:, n, :], in_=PB[X2][:, :]), reads=[PB[X2]], writes=[GV])
                        S.op("act", lambda e: e.activation(out=SG[:, n, :], in_=PB[X3][:, :], func=AF.Silu), reads=[PB[X3]], writes=[SG])
                        for kc in range(8):
                            S.op("pe", lambda e, kc=kc: e.matmul(PB[X2][:, 0:32], lhsT=n1T[:, kc, :], rhs=WinB[:, kc, 1536:1568],
                                                                 start=(kc == 0), stop=(kc == 7)), reads=[n1T, WinB], writes=[PB[X2]])
                        S.op("dve", lambda e: e.tensor_copy(out=glr[:], in_=PB[X2][:, 0:32]), reads=[PB[X2]], writes=[glr])
                        S.op("pe", lambda e: e.transpose(out=PB[X3][0:32, 0:128], in_=glr[:], identity=ident_f[:]), reads=[glr, ident_f], writes=[PB[X3]])
                        S.op("act", lambda e: e.copy(out=glrA[0:32, :], in_=PB[X3][0:32, 0:128]), reads=[PB[X3]], writes=[glrA])
                        S.op("pe", lambda e: e.matmul(PB[X2][:, :], lhsT=glrA[:, :], rhs=Wg[:, :], start=True, stop=True), reads=[glrA, Wg], writes=[PB[X2]])
                        S.op("act", lambda e: e.activation(out=sp_t[:], in_=PB[X2][:, :], func=AF.Exp, scale=-1.0), reads=[PB[X2]], writes=[sp_t])
                        S.op("act", lambda e: e.activation(out=sp_t[:], in_=sp_t[:], func=AF.Ln, bias=1.0), reads=[sp_t], writes=[sp_t])
                        S.op("dve", lambda e: e.tensor_scalar_mul(out=DrM[:], in0=DrowG[:], scalar1=eye16[:, n:n + 1]), reads=[DrowG, eye16], writes=[DrM])
                        S.op("pe", lambda e: e.matmul(PB[X3][:, 0:256], lhsT=triL[:], rhs=sp_t[:, 0:256], start=True, stop=False, skip_group_check=True),
                             reads=[triL, sp_t], writes=[PB[X3]])
                        S.op("pe", lambda e: e.matmul(PB[X3][:, 256:512], lhsT=triU[:], rhs=sp_t[:, 256:512], start=False, stop=False, skip_group_check=True),
                             reads=[triU, sp_t], writes=[PB[X3]])
                        S.op("pe", lambda e: e.matmul(PB[X3][:, :], lhsT=ones16[:], rhs=DrM[:], start=False, stop=True, skip_group_check=True),
                             reads=[ones16, DrM], writes=[PB[X3]])
                        S.op("act", lambda e: e.activation(out=Ep[:], in_=PB[X3][:, :], func=AF.Exp, scale=-1.0 / 16), reads=[PB[X3]], writes=[Ep])
                        S.op("act", lambda e: e.activation(out=Em[:], in_=PB[X3][:, :], func=AF.Exp, scale=1.0 / 16), reads=[PB[X3]], writes=[Em])
                        gqv = PB[X1][:, 0:256].unsqueeze(1).to_broadcast([128, 2, 256])
                        gkv_ = PB[X1][:, 256:512].unsqueeze(1).to_broadcast([128, 2, 256])
                        S.op("dve", lambda e: e.scalar_tensor_tensor(out=qd[:].rearrange("p (d c) -> p d c", d=2), in0=gqv, scalar=0.125,
                                                                     in1=Ep[:].rearrange("p (d c) -> p d c", d=2), op0=ALU.mult, op1=ALU.mult),
                             reads=[PB[X1], Ep], writes=[qd])
                        S.op("dve", lambda e: e.tensor_tensor(out=kd[:].rearrange("p (d c) -> p d c", d=2), in0=gkv_,
                                                              in1=Em[:].rearrange("p (d c) -> p d c", d=2), op=ALU.mult),
                             reads=[PB[X1], Em], writes=[kd])
                        transpose_to(qd, lambda k: qd[:, k * 128:(k + 1) * 128], 4, X0, qdT, lambda: qdT[:, :, n * 128:(n + 1) * 128], eng="dve")
                        transpose_to(kd, lambda k: kd[:, k * 128:(k + 1) * 128], 4, X2, kdT, lambda: kdT[:, :, n * 128:(n + 1) * 128], eng="act")

                    run_pipelined(a2_tile, NTILE)
                    S.barrier()
                if stop == "A2":
                    return nc

                with ExitStack() as p2c:
                    mf = S.sb(p2c, "mf", [128, 4, 512], BF16)
                    mb = S.sb(p2c, "mb", [128, 4, 512], BF16)
                    S.op("pool", lambda e: e.memset(mf[:], 1.0), writes=[mf])
                    S.op("pool", lambda e: e.memset(mb[:], 1.0), writes=[mb])
                    for r in range(4):
                        S.op("pool", lambda e, r=r: e.affine_select(out=mf[:, r, :], in_=mf[:, r, :], pattern=[[1, 512]], compare_op=ALU.is_ge, fill=0.0,
                                                                    base=-128 * r, channel_multiplier=-1), reads=[mf], writes=[mf])
                        S.op("pool", lambda e, r=r: e.affine_select(out=mb[:, r, :], in_=mb[:, r, :], pattern=[[-1, 512]], compare_op=ALU.is_ge, fill=0.0,
                                                                    base=128 * r, channel_multiplier=1), reads=[mb], writes=[mb])
                    At = [S.sb(p2c, f"At{i}", [128, 512], BF16) for i in range(3)]
                    OgcP = [S.sb(p2c, f"Ogc{i}", [128, 4, 4, 128], F32) for i in range(2)]
                    pend_n = []
                    sq = S.sb(p2c, "sq", [128, 16, 128], F32)
                    ssq = S.sb(p2c, "ssq", [128, 16], F32)
                    ogt = S.sb(p2c, "ogt", [128, 4, 512], BF16)
                    for tc in range(4):
                        Ogc = OgcP[tc % 2]
                        jobsC = []
                        for h in range(4):
                            jl = [(h, 0, jt) for jt in range(0, 4 * tc + 4)] + [(h, 1, jt) for jt in range(4 * tc, NTILE)]
                            jobsC += [(h, d, jt, k == 0, k == len(jl) - 1) for k, (h, d, jt) in enumerate(jl)]

                        def c_score(i):
                            h, d, jt, isf, isl = jobsC[i]
                            sbk = i % 2
                            hp, hl = h // 2, (h % 2) * 64
                            blk = d * 2 + hp
                            S.op("pe", lambda e: e.matmul(
                                PB[sbk][:, :], lhsT=kdT[hl:hl + 64, blk, jt * 128:(jt + 1) * 128], rhs=qdT[hl:hl + 64, blk, tc * 512:(tc + 1) * 512],
                                start=True, stop=True), reads=[kdT, qdT], writes=[PB[sbk]])
                            A = At[i % 3]
                            r = jt - 4 * tc
                            if 0 <= r < 4:
                                mk = mf if d == 0 else mb
                                S.op("dve", lambda e: e.tensor_tensor(out=A[:], in0=PB[sbk][:, :], in1=mk[:, r, :], op=ALU.mult),
                                     reads=[PB[sbk], mk], writes=[A])
                            else:
                                S.op("act", lambda e: e.copy(out=A[:], in_=PB[sbk][:, :]), reads=[PB[sbk]], writes=[A])

                        def c_pv(i):
                            h, d, jt, isf, isl = jobsC[i]
                            ob = 4 + (h % 2)
                            A = At[i % 3]
                            first = isf
                            for ii in range(4):
                                tt_ = 4 * tc + ii
                                if (d == 0 and jt > tt_) or (d == 1 and jt < tt_):
                                    continue
                                S.op("pe", lambda e, ii=ii, first=first: e.matmul(
                                    PB[ob][:, ii * 128:(ii + 1) * 128], lhsT=A[:, ii * 128:(ii + 1) * 128], rhs=GV[:, jt, h * 128:(h + 1) * 128],
                                    start=first, stop=False, skip_group_check=True), reads=[A, GV], writes=[PB[ob]])
                                first = False
                            if isl:
                                S.op("act", lambda e: e.copy(out=Ogc[:, :, h, :], in_=PB[ob][:, :].rearrange("p (i c) -> p i c", i=4)),
                                     reads=[PB[ob]], writes=[Ogc])

                        for i in range(len(jobsC) + 1):
                            if i < len(jobsC):
                                c_score(i)
                            if i >= 1:
                                c_pv(i - 1)
                            if pend_n and i >= 4:
                                S.replay(pend_n, 1)
                        if pend_n:
                            S.replay(pend_n, len(pend_n))
                        if tc < 3:
                            S.defer = pend_n
                        og2 = Ogc[:].rearrange("p i h c -> p (i h) c")
                        S.op("dve", lambda e: e.tensor_tensor(out=sq[:], in0=og2, in1=og2, op=ALU.mult), reads=[Ogc], writes=[sq])
                        S.op("dve", lambda e: e.tensor_reduce(out=ssq[:], in_=sq[:], axis=AX.X, op=ALU.add), reads=[sq], writes=[ssq])
                        S.op("act", lambda e: e.activation(out=ssq[:], in_=ssq[:], func=AF.Sqrt, scale=1.0 / 128, bias=EPS), reads=[ssq], writes=[ssq])
                        S.op("dve", lambda e: e.reciprocal(out=ssq[:], in_=ssq[:]), reads=[ssq], writes=[ssq])
                        S.op("dve", lambda e: e.tensor_tensor(out=sq[:], in0=og2, in1=ssq[:].unsqueeze(2).to_broadcast([128, 16, 128]), op=ALU.mult),
                             reads=[Ogc, ssq], writes=[sq])
                        S.op("dve", lambda e: e.tensor_tensor(out=sq[:], in0=sq[:], in1=glab[:].unsqueeze(1).to_broadcast([128, 16, 128]), op=ALU.mult),
                             reads=[sq, glab], writes=[sq])
                        S.op("dve", lambda e, tc=tc: e.tensor_tensor(out=ogt[:].rearrange("p i c -> p (i c)"), in0=sq[:].rearrange("p a c -> p (a c)"),
                                                                     in1=SG[:, 4 * tc:4 * tc + 4, :].rearrange("p i c -> p (i c)"), op=ALU.mult),
                             reads=[sq, SG], writes=[ogt])
                        for i in range(4):
                            n = 4 * tc + i
                            transpose_to(ogt, lambda c, i=i: ogt[:, i, c * 128:(c + 1) * 128], 4, 6 + (i % 2), oglaT,
                                         lambda n=n: oglaT[:, :, n * 128:(n + 1) * 128], eng="act")
                        S.defer = None
                    S.barrier()

            if stop == "C":
                return nc
            with ExitStack() as p3:
                Wo = S.sb(p3, "Wo", [128, 8, 1024], BF16, dma="sw")
                Wq = S.sb(p3, "Wq", [128, 8, 2048], BF16, dma="sw")
                SubT = S.sb(p3, "SubT", [128, 16, 128], BF16, dma="sw")
                S.dma("pool", lambda q: q.dma_start(out=Wo[:], in_=wo_d.rearrange("(c p) n -> p c n", p=128)), writes=[Wo], semt=Wo)
                S.dma("pool", lambda q: q.dma_start(out=Wq[:], in_=wq_d.rearrange("(c p) n -> p c n", p=128)), writes=[Wq], semt=Wq)
                S.dma("pool", lambda q: q.dma_start(out=SubT[:], in_=subk_d), writes=[SubT], semt=SubT)
                g2b = bcast_load(p3, "g2b", g2_d, 1024)
                gfb = bcast_load(p3, "gfb", gf_d, 1024)
                htP = [S.sb(p3, f"ht{i}", [128, 1024], F32, dma=True) for i in range(2)]
                hnP = [S.sb(p3, f"hn{i}", [128, 1024], F32) for i in range(2)]
                idxP = [S.sb(p3, f"idxu{i}", [128, 128], U32) for i in range(2)]
                gateP = [S.sb(p3, f"gate{i}", [128, 8, 16], F32) for i in range(2)]
                hnb = S.sb(p3, "hnb", [128, 1024], BF16)
                hnT = S.sb(p3, "hnT", [128, 8, 128], BF16)
                qT = S.sb(p3, "qT", [128, 16, 128], BF16)
                ssb = S.sb(p3, "ssb", [128, 16, 128], F32)
                wk = S.sb(p3, "wk", [128, 256], F32)
                m16 = S.sb(p3, "m16", [128, 16, 16], F32)
                i16 = S.sb(p3, "i16", [128, 16, 16], U32)
                i16f = S.sb(p3, "i16f", [128, 16, 16], F32)
                cand = S.sb(p3, "cand", [128, 8, 256], F32)
                tops = S.sb(p3, "tops", [128, 8, 16], F32)
                pos = S.sb(p3, "pos", [128, 8, 16], U32)
                pa_ = S.sb(p3, "posa", [128, 8, 16], U32)
                pb_ = S.sb(p3, "posb", [128, 8, 16], U32)
                paf = S.sb(p3, "paf", [128, 8, 16], F32)
                pbf_ = S.sb(p3, "pbf", [128, 8, 16], F32)
                eq = S.sb(p3, "eq", [128, 8, 16, 16], BF16)
                sel1 = S.sb(p3, "sel1", [128, 8, 16], F32)
                sel2 = S.sb(p3, "sel2", [128, 8, 16], F32)
                idxf = S.sb(p3, "idxf", [128, 128], F32)
                gsum = S.sb(p3, "gsum", [128, 8], F32)
                stat2 = S.sb(p3, "stat2", [128, 8], F32)
                junk_a = S.sb(p3, "junk_a", [128, 1024], BF16)
                junkD = [S.sb(p3, f"junkD{i}", [128, 1024], BF16) for i in range(2)]
                actv = S.sb(p3, "actv", [128, 128], F32)
                glt = S.sb(p3, "glt", [128, 128], F32)
                wct = S.sb(p3, "wct", [128, 128], F32)
                actC = [T(actv[:, i:i + 1], f"actc{i}") for i in range(128)]
                glC = [T(glt[:, i:i + 1], f"glc{i}") for i in range(128)]
                NUV = 10
                UV = [S.sb(p3, f"UV{i}", [128, 2048], BF16, dma="sw") for i in range(NUV)]
                NDG = 4
                Dg = [S.sb(p3, f"Dg{i}", [128, 128], BF16) for i in range(NDG)]

                def rstd2(src_t, col):
                    S.op("act", lambda e: e.activation(out=junk_a[:], in_=src_t[:], func=AF.Square, accum_out=stat2[:, col:col + 1]),
                         reads=[src_t], writes=[stat2, junk_a])
                    S.op("act", lambda e: e.activation(out=stat2[:, col + 1:col + 2], in_=stat2[:, col:col + 1], func=AF.Sqrt,
                                                       scale=1.0 / 1024, bias=EPS), reads=[stat2], writes=[stat2])
                    S.op("dve", lambda e: e.reciprocal(out=stat2[:, col + 1:col + 2], in_=stat2[:, col + 1:col + 2]), reads=[stat2], writes=[stat2])
                    return stat2[:, col + 1:col + 2]

                def top16(src_ap, src_t, n_el, mv, iv, mvt, ivt):
                    S.op("dve", lambda e: e.max(out=mv[:, 0:8], in_=src_ap), reads=[src_t], writes=[mvt])
                    S.op("dve", lambda e: e.max_index(out=iv[:, 0:8], in_max=mv[:, 0:8], in_values=src_ap), reads=[src_t, mvt], writes=[ivt])
                    S.op("dve", lambda e: e.match_replace(out=wk[:, 0:n_el], in_to_replace=mv[:, 0:8], in_values=src_ap, imm_value=-1e30),
                         reads=[src_t, mvt], writes=[wk])
                    S.op("dve", lambda e: e.max(out=mv[:, 8:16], in_=wk[:, 0:n_el]), reads=[wk], writes=[mvt])
                    S.op("dve", lambda e: e.max_index(out=iv[:, 8:16], in_max=mv[:, 8:16], in_values=wk[:, 0:n_el]), reads=[wk, mvt], writes=[ivt])

                def emit_d12(n):
                    par = n % 2
                    ht, hn, idxu, gate = htP[par], hnP[par], idxP[par], gateP[par]
                    r0 = base + n * 128
                    S.dma("sp", lambda q: q.dma_start(out=ht[:], in_=x_d[r0:r0 + 128, :]), writes=[ht], semt=ht)
                    for j in range(2):
                        for c in range(8):
                            src = attnT if c < 4 else oglaT
                            S.op("pe", lambda e, j=j, c=c, src=src: e.matmul(PB[j][:, :], lhsT=src[:, c % 4, n * 128:(n + 1) * 128],
                                                                             rhs=Wo[:, c, j * 512:(j + 1) * 512], start=(c == 0), stop=(c == 7)),
                                 reads=[src, Wo], writes=[PB[j]])
                        S.op("dve", lambda e, j=j: e.tensor_tensor(out=ht[:, j * 512:(j + 1) * 512], in0=PB[j][:, :], in1=ht[:, j * 512:(j + 1) * 512], op=ALU.add),
                             reads=[PB[j], ht], writes=[ht])
                    if dbg:
                        S.dma("sp", lambda q: q.dma_start(out=dbgh_d[r0:r0 + 128, :], in_=ht[:]), reads=[ht], writes=[dbgh_t], semt=dbgh_t)
                    r = rstd2(ht, 0)
                    S.op("dve", lambda e: e.scalar_tensor_tensor(out=hn[:], in0=ht[:], scalar=r, in1=g2b[:], op0=ALU.mult, op1=ALU.mult),
                         reads=[ht, stat2, g2b], writes=[hn])
                    S.op("act", lambda e: e.copy(out=hnb[:], in_=hn[:]), reads=[hn], writes=[hnb])
                    transpose_to(hnb, lambda kc: hnb[:, kc * 128:(kc + 1) * 128], 8, 2, hnT, lambda: hnT[:])
                    for g4 in range(4):
                        bk = 3 + (g4 % 2)
                        for q4 in range(4):
                            hp = g4 * 4 + q4
                            for kc in range(8):
                                S.op("pe", lambda e, hp=hp, kc=kc, bk=bk, q4=q4: e.matmul(PB[bk][:, q4 * 128:(q4 + 1) * 128], lhsT=Wq[:, kc, hp * 128:(hp + 1) * 128],
                                                                                            rhs=hnT[:, kc, :], start=(kc == 0), stop=(kc == 7)),
                                     reads=[Wq, hnT], writes=[PB[bk]])
                        S.op("act", lambda e, g4=g4, bk=bk: e.copy(out=qT[:, g4 * 4:(g4 + 1) * 4, :].rearrange("p a b -> p (a b)"), in_=PB[bk][:, :]),
                             reads=[PB[bk]], writes=[qT])
                    for g4 in range(4):
                        bk = 5 if g4 % 2 == 0 else 2
                        for q4 in range(4):
                            hp = g4 * 4 + q4
                            S.op("pe", lambda e, hp=hp, bk=bk, q4=q4: e.matmul(PB[bk][:, q4 * 128:(q4 + 1) * 128], lhsT=qT[:, hp, :], rhs=SubT[:, hp, :],
                                                                                start=True, stop=True), reads=[qT, SubT], writes=[PB[bk]])
                        S.op("act", lambda e, g4=g4, bk=bk: e.copy(out=ssb[:, g4 * 4:(g4 + 1) * 4, :].rearrange("p a b -> p (a b)"), in_=PB[bk][:, :]),
                             reads=[PB[bk]], writes=[ssb])
                    if S.defer is not None:
                        mark[0] = len(S.defer)
                    for hp in range(16):
                        top16(ssb[:, hp, :], ssb, 128, m16[:, hp, :], i16[:, hp, :], m16, i16)
                    m4 = m16[:].rearrange("p (h a) i -> p h a i", a=2)
                    S.op("dve", lambda e: e.tensor_tensor(out=cand[:].rearrange("p h (i j) -> p h i j", i=16),
                                                          in0=m4[:, :, 0, :].unsqueeze(3).to_broadcast([128, 8, 16, 16]),
                                                          in1=m4[:, :, 1, :].unsqueeze(2).to_broadcast([128, 8, 16, 16]), op=ALU.add),
                         reads=[m16], writes=[cand])
                    for h in range(8):
                        top16(cand[:, h, :], cand, 256, tops[:, h, :], pos[:, h, :], tops, pos)
                    S.op("dve", lambda e: e.tensor_single_scalar(out=pa_[:], in_=pos[:], scalar=4, op=ALU.logical_shift_right), reads=[pos], writes=[pa_])
                    S.op("dve", lambda e: e.tensor_single_scalar(out=pb_[:], in_=pos[:], scalar=15, op=ALU.bitwise_and), reads=[pos], writes=[pb_])
                    S.op("dve", lambda e: e.tensor_copy(out=paf[:], in_=pa_[:]), reads=[pa_], writes=[paf])
                    S.op("dve", lambda e: e.tensor_copy(out=pbf_[:], in_=pb_[:]), reads=[pb_], writes=[pbf_])
                    S.op("dve", lambda e: e.tensor_copy(out=i16f[:], in_=i16[:]), reads=[i16], writes=[i16f])
                    i4 = i16f[:].rearrange("p (h a) i -> p h a i", a=2)
                    iob = iota16[:].unsqueeze(1).unsqueeze(1).to_broadcast([128, 8, 16, 16])
                    for (pf, a, sel) in ((paf, 0, sel1), (pbf_, 1, sel2)):
                        S.op("dve", lambda e, pf=pf: e.tensor_tensor(out=eq[:], in0=pf[:].unsqueeze(3).to_broadcast([128, 8, 16, 16]), in1=iob, op=ALU.is_equal),
                             reads=[pf, iota16], writes=[eq])
                        S.op("dve", lambda e, a=a: e.tensor_tensor(out=eq[:], in0=eq[:], in1=i4[:, :, a, :].unsqueeze(2).to_broadcast([128, 8, 16, 16]), op=ALU.mult),
                             reads=[eq, i16f], writes=[eq])
                        S.op("dve", lambda e, sel=sel: e.tensor_reduce(out=sel[:], in_=eq[:], axis=AX.X, op=ALU.add), reads=[eq], writes=[sel])
                    S.op("dve", lambda e: e.scalar_tensor_tensor(out=idxf[:], in0=sel1[:].rearrange("p h k -> p (h k)"), scalar=128.0,
                                                                 in1=sel2[:].rearrange("p h k -> p (h k)"), op0=ALU.mult, op1=ALU.add),
                         reads=[sel1, sel2], writes=[idxf])
                    S.op("dve", lambda e: e.tensor_copy(out=idxu[:], in_=idxf[:]), reads=[idxf], writes=[idxu])
                    S.op("dve", lambda e: e.tensor_tensor(out=gate[:], in0=tops[:], in1=tops[:, :, 0:1].to_broadcast([128, 8, 16]), op=ALU.subtract),
                         reads=[tops], writes=[gate])
                    S.op("act", lambda e: e.activation(out=gate[:], in_=gate[:], func=AF.Exp), reads=[gate], writes=[gate])
                    S.op("dve", lambda e: e.tensor_reduce(out=gsum[:], in_=gate[:], axis=AX.X, op=ALU.add), reads=[gate], writes=[gsum])
                    S.op("dve", lambda e: e.reciprocal(out=gsum[:], in_=gsum[:]), reads=[gsum], writes=[gsum])
                    S.op("dve", lambda e: e.tensor_tensor(out=gate[:], in0=gate[:], in1=gsum[:].unsqueeze(2).to_broadcast([128, 8, 16]), op=ALU.mult),
                         reads=[gate, gsum], writes=[gate])

                mark = [0]
                RATE_A = 14

                def emit_loop(n, pending):
                    tail = len(pending) - mark[0]
                    par = n % 2
                    ht, hn, idxu, gate = htP[par], hnP[par], idxP[par], gateP[par]
                    r0 = base + n * 128
                    gflat = gate[:].rearrange("p h k -> p (h k)")
                    LAG = 2

                    def emit_acc(sl):
                        U = UV[sl % NUV]
                        D = Dg[sl % NDG]
                        S.op("act", lambda e: e.activation(out=glt[:, sl:sl + 1], in_=actv[:, sl:sl + 1], func=AF.Gelu_apprx_tanh),
                             reads=[actC[sl]], writes=[glC[sl]])
                        S.op("act", lambda e: e.activation(out=wct[:, sl:sl + 1], in_=gflat[:, sl:sl + 1], func=AF.Copy, scale=glt[:, sl:sl + 1]),
                             reads=[glC[sl], gate], writes=[glC[sl]])
                        S.op("act", lambda e: e.activation(out=D[:], in_=ident_bf[:], func=AF.Copy, scale=wct[:, sl:sl + 1]),
                             reads=[ident_bf, glC[sl]], writes=[D])
                        for j in range(2):
                            S.op("pe", lambda e, j=j: e.matmul(PB[6 + j][:, :], lhsT=D[:], rhs=U[:, 1024 + j * 512:1024 + (j + 1) * 512],
                                                               start=(sl == 0), stop=(sl == 127)), reads=[D, U], writes=[PB[6 + j]])

                    for sl in range(128):
                        U = UV[sl % NUV]
                        S.dma("pool", lambda q, U=U, sl=sl: q.indirect_dma_start(out=U[:], out_offset=None, in_=uvbf_d,
                                                                                 in_offset=bass.IndirectOffsetOnAxis(ap=idxu[:, sl:sl + 1], axis=0)),
                              reads=[idxu, uv_t], writes=[U], semt=U)
                        S.op("dve", lambda e, U=U, sl=sl: e.scalar_tensor_tensor(out=junkD[sl % 2][:], in0=U[:, 0:1024], scalar=1.0, in1=hn[:], op0=ALU.mult, op1=ALU.mult,
                                                                                 accum_out=actv[:, sl:sl + 1]), reads=[U, hn], writes=[junkD[sl % 2], actC[sl]])
                        if sl >= LAG:
                            emit_acc(sl - LAG)
                        if pending:
                            if len(pending) > tail:
                                S.replay(pending, min(RATE_A, len(pending) - tail))
                            else:
                                S.replay(pending, -(-len(pending) // max(1, 124 - sl)))
                    for sl in range(128 - LAG, 128):
                        emit_acc(sl)
                    if pending:
                        S.replay(pending, len(pending))
                    for j in range(2):
                        S.op("dve", lambda e, j=j: e.tensor_tensor(out=ht[:, j * 512:(j + 1) * 512], in0=PB[6 + j][:, :], in1=ht[:, j * 512:(j + 1) * 512], op=ALU.add),
                             reads=[PB[6 + j], ht], writes=[ht])
                    r = rstd2(ht, 2)
                    S.op("dve", lambda e: e.scalar_tensor_tensor(out=hn[:], in0=ht[:], scalar=r, in1=gfb[:], op0=ALU.mult, op1=ALU.mult),
                         reads=[ht, stat2, gfb], writes=[hn])
                    S.dma("sp", lambda q: q.dma_start(out=y_d[r0:r0 + 128, :], in_=hn[:]), reads=[hn], writes=[y_t], semt=y_t)

                emit_d12(0)
                for n in range(NTILE):
                    pending = []
                    if n + 1 < NTILE:
                        S.defer = pending
                        emit_d12(n + 1)
                        S.defer = None
                    emit_loop(n, pending)
                S.barrier()
        S.barrier()
        print("ninstr", S.ninstr, "nsem", S.nsem)
    return nc


def _host_inputs(inputs):
    f = np.float32
    w_in = np.asarray(inputs["w_in"])[0]
    wA = np.ascontiguousarray(np.concatenate([w_in[:, 0:416], w_in[:, 1440:1472]], axis=1))
    wB = np.ascontiguousarray(np.concatenate([w_in[:, 416:1440], w_in[:, 1472:1984], w_in[:, 1440:1472]], axis=1))
    gw = np.zeros((33, 512), f)
    gw[0:16, 0:256] = np.asarray(inputs["gate_fwd_w"])[0]
    gw[16:32, 256:512] = np.asarray(inputs["gate_bwd_w"])[0]
    gw[32, 0:256] = np.asarray(inputs["gate_fwd_b"])[0]
    gw[32, 256:512] = np.asarray(inputs["gate_bwd_b"])[0]
    subk = np.asarray(inputs["peer_subkeys"])[0].reshape(16, 128, 128)
    subkT = np.ascontiguousarray(np.transpose(subk, (2, 0, 1)))
    half = 16
    freqs = (np.float32(10000.0) ** (-np.arange(half, dtype=f) * f(2.0) / f(32))).astype(f)
    ang = (np.arange(S_LEN, dtype=f)[:, None] * freqs[None, :]).astype(f)
    shared = {
        "w_inA": wA, "w_inB": wB,
        "norm1_g": np.ascontiguousarray(np.asarray(inputs["norm1_g"])[0]),
        "q_norm_g": np.ascontiguousarray(np.asarray(inputs["q_norm_g"])[0]),
        "w_uq": np.ascontiguousarray(np.asarray(inputs["w_uq"])[0]),
        "kv_norm_g": np.ascontiguousarray(np.asarray(inputs["kv_norm_g"])[0]),
        "w_ukv": np.ascontiguousarray(np.asarray(inputs["w_ukv"])[0]),
        "gate_w": gw,
        "gla_norm_g": np.ascontiguousarray(np.asarray(inputs["gla_norm_g"])[0]),
        "w_o": np.ascontiguousarray(np.asarray(inputs["w_o"])[0]),
        "norm2_g": np.ascontiguousarray(np.asarray(inputs["norm2_g"])[0]),
        "peer_wq": np.ascontiguousarray(np.asarray(inputs["peer_wq"])[0]),
        "subkT": subkT,
        "peer_u": np.ascontiguousarray(np.asarray(inputs["peer_u"])[0]),
        "peer_v": np.ascontiguousarray(np.asarray(inputs["peer_v"])[0]),
        "final_norm_g": np.ascontiguousarray(np.asarray(inputs["final_norm_g"])),
        "rope_cos": np.cos(ang).astype(f), "rope_sin": np.sin(ang).astype(f),
    }
    return shared


def kernel(**inputs):
    xp = np.asarray(inputs["x_prompt"], dtype=np.float32)
    xs = np.asarray(inputs["x_sample"], dtype=np.float32)
    xall = np.concatenate([xp, xs], axis=0)
    nb = xall.shape[0]
    ncore = 8
    per = nb // ncore
    shared = _host_inputs(inputs)
    nc = build_program(per, False)
    in_maps = []
    for c in range(ncore):
        m = dict(shared)
        m["x"] = np.ascontiguousarray(xall[c * per:(c + 1) * per].reshape(per * S_LEN, 1024))
        in_maps.append(m)
    res = run_bass_kernel_spmd(nc, in_maps, core_ids=list(range(ncore)))
    ys = [np.asarray(r["y"]).reshape(per, S_LEN, 1024) for r in res.results]
    yall = np.concatenate(ys, axis=0).astype(np.float32)
    return (yall[:xp.shape[0]], yall[xp.shape[0]:])
```
